# Optimizing a Trainium2 kernel written in Bass

```python
import math
import jax, jax.numpy as jnp
from jax import lax
import numpy as np

D_MODEL = 1024
BATCH = 4
SEQ = 8192
DEPTH = 1

N_META = 16
CHUNK = 128
META_PAD = (-N_META) % CHUNK
RET_HEADS = 8
RET_DK = 64
RET_DV = 128
RET_QK_W = RET_HEADS * RET_DK
RET_V_W = RET_HEADS * RET_DV
MLA_HEADS = 8
MLA_Q_RANK = 384
MLA_KV_RANK = 256
MLA_NOPE = 64
MLA_ROPE = 32
MLA_DV = 64
MLA_QK = MLA_NOPE + MLA_ROPE
D_FF = ((-(-8 * D_MODEL // 3) + 255) // 256) * 256
ROPE_BASE = 10000.0
RMS_EPS = 1e-6
GN_EPS = 1e-5

SPLIT_SIZES = (RET_QK_W, RET_QK_W, RET_V_W, RET_V_W, MLA_Q_RANK, MLA_KV_RANK, MLA_ROPE, D_MODEL, D_MODEL)
SPLIT_IDX = tuple(int(s) for s in np.cumsum(SPLIT_SIZES)[:-1])
D_IN = int(sum(SPLIT_SIZES))

kernel_name = "hybrid_retention_mla_gated_encoder"


def _rmsnorm(x, w):
    xf = x.astype(jnp.float32)
    y = xf * lax.rsqrt(jnp.mean(xf * xf, axis=-1, keepdims=True) + RMS_EPS)
    return (y * w.astype(jnp.float32)).astype(x.dtype)


def _rope_tables(pos, dim):
    half = dim // 2
    inv = ROPE_BASE ** (-jnp.arange(half, dtype=jnp.float32) / half)
    ang = pos.astype(jnp.float32)[:, None] * inv[None, :]
    return jnp.cos(ang), jnp.sin(ang)


def _apply_rope(x, cos, sin):
    half = x.shape[-1] // 2
    xf = x.astype(jnp.float32)
    x1, x2 = xf[..., :half], xf[..., half:]
    return jnp.concatenate([x1 * cos - x2 * sin, x1 * sin + x2 * cos], axis=-1).astype(x.dtype)


def _retention_dir(q, k, v, log_gamma, inclusive):
    B, Lp, H, dk = q.shape
    dv = v.shape[-1]
    n = Lp // CHUNK
    qc = q.reshape(B, n, CHUNK, H, dk)
    kc = k.reshape(B, n, CHUNK, H, dk)
    vc = v.reshape(B, n, CHUNK, H, dv)
    lg = log_gamma.astype(jnp.float32)
    idx = jnp.arange(CHUNK, dtype=jnp.float32)
    rel = idx[:, None] - idx[None, :]
    mask = rel >= 0 if inclusive else rel > 0
    dmask = jnp.exp(jnp.where(mask[None], lg[:, None, None] * rel[None], -jnp.inf)).astype(q.dtype)
    scores = jnp.einsum('bnihd,bnjhd->bnhij', qc, kc) * dmask
    inner = jnp.einsum('bnhij,bnjhe->bnihe', scores, vc)
    w_k = jnp.exp(lg[None, :] * (CHUNK - 1 - idx)[:, None]).astype(q.dtype)
    incr = jnp.einsum('bnjhd,jh,bnjhe->nbhde', kc, w_k, vc)
    g_chunk = jnp.exp(lg * CHUNK).astype(incr.dtype)[None, :, None, None]

    def step(state, u):
        return g_chunk * state + u, state

    state0 = jnp.zeros((B, H, dk, dv), incr.dtype)
    _, states_prev = lax.scan(step, state0, incr)
    w_q = jnp.exp(lg[None, :] * (idx + 1.0)[:, None]).astype(q.dtype)
    cross = jnp.einsum('bnihd,ih,nbhde->bnihe', qc, w_q, states_prev)
    return (inner + cross).reshape(B, Lp, H, dv)


def _bidir_retention(q, k, v, decay_f, decay_b):
    lg_f = -jnp.exp(decay_f.astype(jnp.float32))
    lg_b = -jnp.exp(decay_b.astype(jnp.float32))
    pad = ((0, 0), (META_PAD, 0), (0, 0), (0, 0))
    qp, kp, vp = jnp.pad(q, pad), jnp.pad(k, pad), jnp.pad(v, pad)
    fwd = _retention_dir(qp, kp, vp, lg_f, True)
    bwd = jnp.flip(_retention_dir(jnp.flip(qp, 1), jnp.flip(kp, 1), jnp.flip(vp, 1), lg_b, False), 1)
    return (fwd + bwd)[:, META_PAD:]


def _head_group_norm(y, w):
    B, L, H, dv = y.shape
    yf = y.astype(jnp.float32)
    mu = jnp.mean(yf, axis=-1, keepdims=True)
    var = jnp.mean(jnp.square(yf - mu), axis=-1, keepdims=True)
    yn = ((yf - mu) * lax.rsqrt(var + GN_EPS)).reshape(B, L, H * dv)
    return (yn * w.astype(jnp.float32)).astype(y.dtype)


def _mla_attention(q_nope, q_rope, k_nope, k_rope, v):
    B, L, H, _ = q_nope.shape
    Lp = L + META_PAD
    nb = Lp // CHUNK
    scale = MLA_QK ** -0.5

    def to_blocks(t):
        t = jnp.pad(t, ((0, 0), (META_PAD, 0), (0, 0), (0, 0)))
        return t.reshape(B, nb, CHUNK, H, t.shape[-1]).swapaxes(0, 1)

    def attend(qb):
        qn, qr = qb
        s = jnp.einsum('bqhd,bkhd->bhqk', qn, k_nope) + jnp.einsum('bqhr,bkr->bhqk', qr, k_rope)
        p = jax.nn.softmax(s.astype(jnp.float32) * scale, axis=-1).astype(v.dtype)
        return jnp.einsum('bhqk,bkhd->bqhd', p, v)

    out = lax.map(attend, (to_blocks(q_nope), to_blocks(q_rope)))
    return out.swapaxes(0, 1).reshape(B, Lp, H, MLA_DV)[:, META_PAD:]


def _mixer(u, w_in, decay_f, decay_b, gn_w, w_ret_out, q_norm_w, w_uq, kv_norm_w, w_uk, w_uv,
           w_mla_out, w_o, ret_cos, ret_sin, mla_cos, mla_sin):
    B, L, _ = u.shape
    proj = u @ w_in
    rq, rk, rv, rg, a_cq, a_ckv, a_kr, a_gret, a_gmla = jnp.split(proj, SPLIT_IDX, axis=-1)

    rq = _apply_rope(rq.reshape(B, L, RET_HEADS, RET_DK), ret_cos[:, None, :], ret_sin[:, None, :])
    rk = _apply_rope(rk.reshape(B, L, RET_HEADS, RET_DK), ret_cos[:, None, :], ret_sin[:, None, :]) * (RET_DK ** -0.5)
    rv = rv.reshape(B, L, RET_HEADS, RET_DV)
    y_ret = _head_group_norm(_bidir_retention(rq, rk, rv, decay_f, decay_b), gn_w)
    y_ret = (jax.nn.silu(rg) * y_ret) @ w_ret_out

    c_q = _rmsnorm(a_cq, q_norm_w)
    q = (c_q @ w_uq).reshape(B, L, MLA_HEADS, MLA_QK)
    q_nope = q[..., :MLA_NOPE]
    q_rope = _apply_rope(q[..., MLA_NOPE:], mla_cos[:, None, :], mla_sin[:, None, :])
    c_kv = _rmsnorm(a_ckv, kv_norm_w)
    k_nope = (c_kv @ w_uk).reshape(B, L, MLA_HEADS, MLA_NOPE)
    v = (c_kv @ w_uv).reshape(B, L, MLA_HEADS, MLA_DV)
    k_rope = _apply_rope(a_kr, mla_cos, mla_sin)
    y_mla = _mla_attention(q_nope, q_rope, k_nope, k_rope, v).reshape(B, L, MLA_HEADS * MLA_DV) @ w_mla_out

    merged = jax.nn.sigmoid(a_gret) * y_ret + jax.nn.sigmoid(a_gmla) * y_mla
    return merged @ w_o


def _swiglu(u, w_gate, w_up, w_down):
    return (jax.nn.silu(u @ w_gate) * (u @ w_up)) @ w_down


def setup_inputs(seed: int = 0) -> dict:
    key = jax.random.key(seed)
    ks = jax.random.split(key, 24)
    f32 = jnp.float32

    def nrm(k, shape, fan_in):
        return jax.random.normal(k, shape, f32) * (fan_in ** -0.5)

    def gain(k, shape):
        return 1.0 + 0.02 * jax.random.normal(k, shape, f32)

    h_idx = jnp.arange(RET_HEADS, dtype=f32)
    gamma = 1.0 - 2.0 ** (-5.0 - h_idx)
    decay_base = jnp.log(-jnp.log(gamma))
    return {
        'x': jax.random.normal(ks[0], (BATCH, SEQ, D_MODEL), f32),
        'meta_tokens': jax.random.normal(ks[1], (N_META, D_MODEL), f32),
        'norm_mix_w': gain(ks[2], (DEPTH, D_MODEL)),
        'w_in': nrm(ks[3], (DEPTH, D_MODEL, D_IN), D_MODEL),
        'ret_decay_fwd': decay_base[None] + 0.05 * jax.random.normal(ks[4], (DEPTH, RET_HEADS), f32),
        'ret_decay_bwd': decay_base[None] + 0.05 * jax.random.normal(ks[5], (DEPTH, RET_HEADS), f32),
        'ret_gn_w': gain(ks[6], (DEPTH, RET_V_W)),
        'w_ret_out': nrm(ks[7], (DEPTH, RET_V_W, D_MODEL), RET_V_W),
        'mla_q_norm_w': gain(ks[8], (DEPTH, MLA_Q_RANK)),
        'w_uq': nrm(ks[9], (DEPTH, MLA_Q_RANK, MLA_HEADS * MLA_QK), MLA_Q_RANK),
        'mla_kv_norm_w': gain(ks[10], (DEPTH, MLA_KV_RANK)),
        'w_uk': nrm(ks[11], (DEPTH, MLA_KV_RANK, MLA_HEADS * MLA_NOPE), MLA_KV_RANK),
        'w_uv': nrm(ks[12], (DEPTH, MLA_KV_RANK, MLA_HEADS * MLA_DV), MLA_KV_RANK),
        'w_mla_out': nrm(ks[13], (DEPTH, MLA_HEADS * MLA_DV, D_MODEL), MLA_HEADS * MLA_DV),
        'w_o': nrm(ks[14], (DEPTH, D_MODEL, D_MODEL), D_MODEL),
        'norm_ffn_w': gain(ks[15], (DEPTH, D_MODEL)),
        'w_ffn_gate': nrm(ks[16], (DEPTH, D_MODEL, D_FF), D_MODEL),
        'w_ffn_up': nrm(ks[17], (DEPTH, D_MODEL, D_FF), D_MODEL),
        'w_ffn_down': nrm(ks[18], (DEPTH, D_FF, D_MODEL), D_FF),
        'norm_final_w': gain(ks[19], (D_MODEL,)),
    }


def reference(x, meta_tokens, norm_mix_w, w_in, ret_decay_fwd, ret_decay_bwd, ret_gn_w, w_ret_out,
              mla_q_norm_w, w_uq, mla_kv_norm_w, w_uk, w_uv, w_mla_out, w_o, norm_ffn_w,
              w_ffn_gate, w_ffn_up, w_ffn_down, norm_final_w):
    B, S, D = x.shape
    L = S + N_META
    meta = jnp.broadcast_to(meta_tokens.astype(x.dtype)[None], (B, N_META, D))
    h = jnp.concatenate([meta, x], axis=1)
    pos = jnp.arange(L)
    ret_cos, ret_sin = _rope_tables(pos, RET_DK)
    mla_cos, mla_sin = _rope_tables(pos, MLA_ROPE)
    for l in range(DEPTH):
        h = h + _mixer(_rmsnorm(h, norm_mix_w[l]), w_in[l], ret_decay_fwd[l], ret_decay_bwd[l], ret_gn_w[l],
                       w_ret_out[l], mla_q_norm_w[l], w_uq[l], mla_kv_norm_w[l], w_uk[l], w_uv[l],
                       w_mla_out[l], w_o[l], ret_cos, ret_sin, mla_cos, mla_sin)
        h = h + _swiglu(_rmsnorm(h, norm_ffn_w[l]), w_ffn_gate[l], w_ffn_up[l], w_ffn_down[l])
    h = _rmsnorm(h, norm_final_w)
    return h[:, N_META:]
```

```python
from contextlib import ExitStack
import os
import numpy as np
import concourse.bass as bass
import concourse.mybir as mybir
from concourse.bass_utils import run_bass_kernel_spmd

F32 = mybir.dt.float32
BF16 = mybir.dt.bfloat16
AF = mybir.ActivationFunctionType
ALU = mybir.AluOpType
AX = mybir.AxisListType

D = 1024
SEQ = 8192
NB = 4
NMETA = 16
TOK = 4096
NT = 32
NOT_ = 33
NSLOT = 65
NKEY = NSLOT * 128
NKEYP = 17 * 512
DFF = 2816
NFC = DFF // 128
RMS_EPS = 1e-6
GN_EPS = 1e-5
SC_ATT = 96.0 ** -0.5

ENGS = ("pe", "act", "dve", "pool", "sp")
EPOCH = 24000


class _Op:
    __slots__ = ("eng", "fn", "signal", "deps", "dma", "dma_n", "sem", "cnt")

    def __init__(self, eng, fn, dma):
        self.eng = eng
        self.fn = fn
        self.signal = False
        self.deps = []
        self.dma = dma
        self.dma_n = 0
        self.sem = None
        self.cnt = 0


class Sched:
    def __init__(self):
        self.ops = []
        self.last_w = {}
        self.readers = {}
        self.dma_cnt = {}
        self._bar = []
        self._bar_seen = set()

    def op(self, eng, fn, reads=(), writes=(), dma=None):
        o = _Op(eng, fn, dma)
        deps = set()
        if eng not in self._bar_seen:
            self._bar_seen.add(eng)
            deps.update(self._bar)
        for t in reads:
            w = self.last_w.get(t)
            if w is not None:
                deps.add(w)
            if isinstance(t, str) and t[0] == "b" and t[1:].isdigit():
                for k, r in self.readers.get(t, {}).items():
                    if r.eng != eng:
                        deps.add(r)
        for t in writes:
            w = self.last_w.get(t)
            if w is not None:
                deps.add(w)
            for r in self.readers.get(t, {}).values():
                deps.add(r)
        if dma is not None:
            n = self.dma_cnt.get(dma, 0) + 1
            self.dma_cnt[dma] = n
            o.dma_n = n
        for d in deps:
            if d is o:
                continue
            if d.dma is None and d.eng == "pe" and eng == "pe" and dma is None:
                continue
            o.deps.append(d)
            if d.dma is None:
                d.signal = True
        for t in reads:
            rd = self.readers.setdefault(t, {})
            rd[eng if dma is None else ("dma", id(o))] = o
        for t in writes:
            self.last_w[t] = o
            self.readers[t] = {}
        self.ops.append(o)
        return o

    def barrier(self):
        last = {}
        for o in self.ops:
            last[o.eng if o.dma is None else ("dma", o.dma)] = o
        self._bar = list(last.values())
        self._bar_seen = set()
        for o in self._bar:
            if o.dma is None:
                o.signal = True

    def emit(self, nc, final_eng="sp"):
        per = {e: [] for e in ENGS}
        for o in self.ops:
            per[o.eng].append(o)
        nsems = {}
        for e in ENGS:
            c = 0
            ep = 0
            for o in per[e]:
                if o.signal and o.dma is None:
                    c += 1
                    if c > EPOCH:
                        ep += 1
                        c = 1
                    o.sem = (e, ep)
                    o.cnt = c
            nsems[e] = ep + 1
        with ExitStack() as es:
            sems = {}
            for e in ENGS:
                for ep in range(nsems[e]):
                    sems[(e, ep)] = es.enter_context(nc.semaphore(f"s_{e}_{ep}"))
            dsems = {}
            for k in self.dma_cnt:
                dsems[k] = es.enter_context(nc.semaphore("d_" + str(len(dsems))))
            block = es.enter_context(nc.Block())
            dma_cnt = self.dma_cnt

            def run(e, eng):
                waited = {}
                for o in per[e]:
                    need = {}
                    for d in o.deps:
                        if d.dma is not None:
                            key = ("d", d.dma)
                            v = (0, 16 * d.dma_n)
                        else:
                            key = ("e", d.eng)
                            v = (d.sem[1], d.cnt)
                        if v > need.get(key, (-1, -1)):
                            need[key] = v
                    for key, v in need.items():
                        if v <= waited.get(key, (-1, -1)):
                            continue
                        waited[key] = v
                        if key[0] == "d":
                            eng.wait_ge(dsems[key[1]], v[1])
                        else:
                            eng.wait_ge(sems[(key[1], v[0])], v[1])
                    ins = o.fn(eng)
                    if o.dma is not None:
                        ins.then_inc(dsems[o.dma], 16)
                    elif o.signal:
                        ins.then_inc(sems[o.sem], 1)
                if e == final_eng:
                    for k, n in dma_cnt.items():
                        if 16 * n > waited.get(("d", k), (-1, -1))[1]:
                            eng.wait_ge(dsems[k], 16 * n)

            @block.tensor
            def _(eng):
                run("pe", eng)

            @block.scalar
            def _(eng):
                run("act", eng)

            @block.vector
            def _(eng):
                run("dve", eng)

            @block.gpsimd
            def _(eng):
                run("pool", eng)

            @block.sync
            def _(eng):
                run("sp", eng)


class Arena:
    def __init__(self, ap, ncols):
        self.ap = ap
        self.n = ncols
        self.pos = 0

    def alloc(self, cols, dt=BF16):
        n = cols * 2 if dt == F32 else cols
        n = (n + 1) // 2 * 2
        v = self.ap[:, self.pos:self.pos + (cols * 2 if dt == F32 else cols)]
        self.pos += n
        assert self.pos <= self.n, ("arena overflow", self.pos, self.n)
        return v.bitcast(F32) if dt == F32 else v


V_NMW, V_NFW, V_GNW, V_QNW, V_KVNW, V_DFP, V_DBP, V_DF8, V_DB8 = 0, 8, 16, 24, 27, 29, 33, 37, 45
NVEC = 53
C_POS, C_NEG, C_I1, C_128MI, C_127MJ, C_J = 0, 128, 256, 384, 512, 513
NCTAB = 514
W1_CQ, W1_CKV, W1_KR, W1_RK, W1_RKS, W1_RV, W1_N = 0, 384, 640, 672, 1184, 1696, 2720
W3_RQ, W3_RQS, W3_RK, W3_RKS, W3_RV, W3_RG, W3_GR, W3_N = 0, 512, 1024, 1536, 2048, 3072, 4096, 5120
ARENA_COLS = 106400


def build_program(stages=99, dbg=False):
    nc = bass.Bass("TRN2", target_bir_lowering=False)

    def din(name, shape, dt=F32):
        return nc.dram_tensor(name, list(shape), dt, kind="ExternalInput").ap()

    xo = din("xo", [TOK, D])
    xr = din("xr", [NOT_ * 128, D])
    w1_d = din("w1", [128, 8, W1_N])
    w3a_d = din("w3a", [128, 8, W3_N])
    wgm_d = din("wgm", [128, 8, 1024])
    wmo_d = din("wmo", [128, 4, 1024])
    wo_d = din("wo", [128, 8, 1024])
    wuq_d = din("wuq", [128, 3, 768])
    wuk_d = din("wuk", [128, 2, 512])
    wuv_d = din("wuv", [128, 2, 512])
    wro_d = din("wro", [128, 8, 1024])
    wg_d = din("wg", [128, 8, DFF])
    wu_d = din("wu", [128, 8, DFF])
    wd_d = din("wd", [128, NFC, 1024])
    vec_d = din("vec", [128, NVEC])
    ctab_d = din("ctab", [128, NCTAB])
    dist_d = din("dist", [128, 2 * NOT_])
    ident_d = din("ident", [128, 128])
    nfin_d = din("nfin", [128, D])
    ropeR_own_d = din("ropeR_own", [128, NT, 256])
    ropeR_oth_d = din("ropeR_oth", [128, NOT_, 256])
    tabM_own_d = din("tabM_own", [128, NT, 64])
    tabM_oth_d = din("tabM_oth", [128, NOT_, 64])
    out_d = nc.dram_tensor("out", [TOK, D], F32, kind="ExternalOutput").ap()
    SCR_KIND = "ExternalOutput" if dbg else "Internal"
    Qs_d = nc.dram_tensor("Qs", [8, 96, TOK], BF16, kind=SCR_KIND).ap()
    m1_d = nc.dram_tensor("m1s", [TOK, D], BF16, kind=SCR_KIND).ap()
    h1_d = nc.dram_tensor("h1s", [TOK, D], F32, kind=SCR_KIND).ap()
    ckvT_d = nc.dram_tensor("ckvTs", [128, 2 * NKEYP], BF16, kind=SCR_KIND).ap()
    krope_d = nc.dram_tensor("kropes", [32, NKEYP], BF16, kind=SCR_KIND).ap()
    SbAll_d = nc.dram_tensor("SbAlls", [NT, 128, 512], BF16, kind=SCR_KIND).ap()

    S = Sched()
    es = ExitStack()
    arena_t = es.enter_context(nc.sbuf_tensor("arena", [128, ARENA_COLS], BF16))
    A = Arena(arena_t, ARENA_COLS)
    PS2 = [es.enter_context(nc.psum_tensor(f"ps2_{i}", [128, 1024], F32)) for i in range(4)]

    def bank(i):
        return PS2[i // 2][:, (i % 2) * 512:(i % 2 + 1) * 512]

    def bankb(i):
        return bank(i).bitcast(BF16)

    def B(i):
        return "b%d" % i

    def MM(out, lhsT, rhs, start, stop, r, w):
        S.op("pe", lambda e: e.matmul(out, lhsT=lhsT, rhs=rhs, start=start, stop=stop), r, w)

    def TR(out, in_, idn, r, w):
        S.op("pe", lambda e: e.transpose(out=out, in_=in_, identity=idn), r, w)

    def ACT(out, in_, func, r, w, scale=None, bias=None, accum=None):
        kw = {}
        if scale is not None:
            kw["scale"] = scale
        if bias is not None:
            kw["bias"] = bias
        if accum is not None:
            kw["accum_out"] = accum
        S.op("act", lambda e: e.activation(out=out, in_=in_, func=func, **kw), r, w)

    def TT(eng, out, in0, in1, op, r, w):
        S.op(eng, lambda e: e.tensor_tensor(out=out, in0=in0, in1=in1, op=op), r, w)

    def TS(eng, out, in0, s1, op0, r, w, s2=None, op1=None):
        if op1 is None:
            S.op(eng, lambda e: e.tensor_scalar(out=out, in0=in0, scalar1=s1, scalar2=None, op0=op0), r, w)
        else:
            S.op(eng, lambda e: e.tensor_scalar(out=out, in0=in0, scalar1=s1, scalar2=s2, op0=op0, op1=op1), r, w)

    def STT(out, in0, scalar, in1, op0, op1, r, w):
        S.op("dve", lambda e: e.scalar_tensor_tensor(out=out, in0=in0, scalar=scalar, in1=in1, op0=op0, op1=op1), r, w)

    def CP(eng, out, in_, r, w):
        if eng == "act":
            S.op("act", lambda e: e.copy(out=out, in_=in_), r, w)
        else:
            S.op(eng, lambda e: e.tensor_copy(out=out, in_=in_), r, w)

    def RED(out, in_, r, w):
        S.op("dve", lambda e: e.tensor_reduce(out=out, in_=in_, axis=AX.X, op=ALU.add), r, w)

    def MSET(eng, ap, val, w):
        S.op(eng, lambda e: e.memset(ap, val), (), w)

    def DMA(eng, out, in_, r, w, key):
        S.op(eng, lambda e: e.dma_start(out=out, in_=in_), r, w, dma=key)

    chain_i = [0]

    def LOAD(eng, out, in_, w):
        k = "chain_%s%d" % (eng, chain_i[0] % 3)
        chain_i[0] += 1
        S.op(eng, lambda e: e.dma_start(out=out, in_=in_), (), list(w) + [k], dma=k)

    def v3(ap, a):
        return ap.rearrange("p (a b) -> p a b", a=a)

    vec = A.alloc(NVEC + 1, F32)
    identf = A.alloc(128, F32)
    identb = A.alloc(128, BF16)
    lg8 = A.alloc(16, F32)
    lgP = A.alloc(8, F32)
    mhalf = A.alloc(16, F32)
    rstd_own = A.alloc(NT, F32)
    P0_END = A.pos
    ctab = A.alloc(NCTAB, F32)
    c128 = A.alloc(128, F32)
    Sf = A.alloc(512, F32)
    Sb = A.alloc(512, F32)
    P1_END = A.pos

    LOAD("sp", vec[:, 0:NVEC], vec_d, ["vec"])
    LOAD("sp", ctab, ctab_d, ["ctab"])
    LOAD("sp", identf, ident_d, ["identf"])
    CP("dve", identb, identf, ["identf"], ["identb"])
    MSET("pool", c128, 128.0, ["c128"])
    MSET("pool", mhalf, -0.5, ["mhalf"])
    MSET("pool", Sf, 0.0, ["Sf"])
    MSET("pool", Sb, 0.0, ["Sb"])
    ACT(lg8, vec[:, V_DF8:V_DF8 + 16], AF.Exp, ["vec"], ["lg8"])
    TS("dve", lg8, lg8, -1.0, ALU.mult, ["lg8"], ["lg8"])
    ACT(lgP, vec[:, V_DFP:V_DFP + 8], AF.Exp, ["vec"], ["lgP"])
    TS("dve", lgP, lgP, -1.0, ALU.mult, ["lgP"], ["lgP"])

    if stages >= 1:
        A.pos = P1_END
        w1 = A.alloc(8 * W1_N, BF16)
        wuq = A.alloc(3 * 768, BF16)
        w13 = v3(w1, 8)
        wuq3 = v3(wuq, 3)
        wfb = A.alloc(16, F32)
        GbT = A.alloc(512, F32)
        wo_fb = A.alloc(2 * NOT_ * 8, F32)
        dist = A.alloc(2 * NOT_, F32)
        tabM_own = A.alloc(NT * 64, F32)
        tabM_oth = A.alloc(NOT_ * 64, F32)
        NXB = 2
        xbuf = [A.alloc(D, F32) for _ in range(NXB)]
        ropeb = [A.alloc(256, F32) for _ in range(NXB)]
        xs = [A.alloc(D, BF16) for _ in range(2)]
        junk = A.alloc(D, BF16)
        uT = [A.alloc(D, BF16) for _ in range(2)]
        st = A.alloc(8, F32)
        ckvn = A.alloc(256, BF16)
        kaug = A.alloc(96, BF16)
        ropeA = A.alloc(32, F32)
        ropeBv = A.alloc(32, F32)
        t1 = A.alloc(512, F32)
        t2 = A.alloc(512, F32)
        kT = A.alloc(512, BF16)
        kwf = A.alloc(512, BF16)
        kwb = A.alloc(512, BF16)
        vtok = A.alloc(1024, BF16)
        cqn = A.alloc(384, BF16)
        cqT = A.alloc(384, BF16)
        qA = A.alloc(256, F32)
        qB = A.alloc(256, F32)
        qtok = A.alloc(768, BF16)
        Qst = [A.alloc(8 * 512, BF16) for _ in range(2)]
        ckst = [A.alloc(2 * 512, BF16) for _ in range(2)]
        krst = [A.alloc(512, BF16) for _ in range(2)]
        sbst = [A.alloc(512, BF16) for _ in range(2)]

        LOAD("sp", dist, dist_d, ["dist"])
        LOAD("sp", tabM_own, tabM_own_d.rearrange("p a b -> p (a b)"), ["tabM_own"])
        LOAD("sp", tabM_oth, tabM_oth_d.rearrange("p a b -> p (a b)"), ["tabM_oth"])
        TS("dve", wfb[:, 0:8], lg8[:, 0:8], ctab[:, C_127MJ:C_127MJ + 1], ALU.mult, ["lg8", "ctab"], ["wfb"])
        TS("dve", wfb[:, 8:16], lg8[:, 8:16], ctab[:, C_J:C_J + 1], ALU.mult, ["lg8", "ctab", "wfb"], ["wfb"])
        ACT(wfb, wfb, AF.Exp, ["wfb"], ["wfb"])
        TS("dve", wfb, wfb, 0.125, ALU.mult, ["wfb"], ["wfb"])
        GbT3 = v3(GbT, 4)
        for hp in range(4):
            ACT(GbT3[:, hp, :], c128, AF.Exp, ["c128", "lgP"], ["GT"], scale=lgP[:, 4 + hp:5 + hp])
        wo3 = v3(wo_fb, 2 * NOT_)
        for h in range(8):
            TS("dve", wo3[:, 0:NOT_, h], dist[:, 0:NOT_], lg8[:, h:h + 1], ALU.mult, ["dist", "lg8"], ["wo_fb"])
            TS("dve", wo3[:, NOT_:2 * NOT_, h], dist[:, NOT_:2 * NOT_], lg8[:, 8 + h:9 + h], ALU.mult, ["dist", "lg8"], ["wo_fb"])
        ACT(wo_fb, wo_fb, AF.Exp, ["wo_fb"], ["wo_fb"])
        TS("dve", wo_fb, wo_fb, 0.125, ALU.mult, ["wo_fb"], ["wo_fb"])

        for kc in range(8):
            LOAD("pool", w1[:, kc * W1_N:(kc + 1) * W1_N], w1_d[:, kc, :], ["w1_%d" % kc])
        for kc in range(3):
            LOAD("pool", wuq[:, kc * 768:(kc + 1) * 768], wuq_d[:, kc, :], ["wuq"])
        for kc in range(8):
            sl = slice(kc * W1_N, (kc + 1) * W1_N)
            TS("dve", w1[:, sl], w1[:, sl], vec[:, V_NMW + kc:V_NMW + kc + 1], ALU.mult, ["w1_%d" % kc, "vec"], ["w1_%d" % kc])
        for kc in range(3):
            sl = slice(kc * 768, (kc + 1) * 768)
            TS("dve", wuq[:, sl], wuq[:, sl], vec[:, V_QNW + kc:V_QNW + kc + 1], ALU.mult, ["wuq", "vec"], ["wuq"])
        W1T = ["w1_%d" % kc for kc in range(8)]
        MSET("pool", kaug, 0.0, ["kaug"])
        for i in range(2):
            MSET("pool", ckst[i], 0.0, ["ckst%d" % i])
            MSET("pool", krst[i], 0.0, ["krst%d" % i])
        gcount = 0

        seq = [("o", t) for t in range(NOT_)] + [("s", c) for c in range(NT - 1, -1, -1)]
        import os
        if os.environ.get("K_METAFIRST"):
            seq = [("o", NOT_ - 1)] + [("o", t) for t in range(NOT_ - 1)] + [("s", c) for c in range(NT - 1, -1, -1)]
        if os.environ.get("K_S1N"):
            seq = seq[:int(os.environ["K_S1N"])]
        def front1(it):
            kind, ti = seq[it]
            own = kind == "s"
            xsrc = xo[ti * 128:(ti + 1) * 128, :] if own else xr[ti * 128:(ti + 1) * 128, :]
            rsrc = ropeR_own_d[:, ti, :] if own else ropeR_oth_d[:, ti, :]
            xb, xbt = xbuf[it % NXB], "xbuf%d" % (it % NXB)
            rb, rbt = ropeb[it % NXB], "ropeb%d" % (it % NXB)
            xsb, xst = xs[it % 2], "xs%d" % (it % 2)
            u, ut = uT[it % 2], "uT%d" % (it % 2)
            DMA("sp", xb, xsrc, (), [xbt], xbt)
            DMA("sp", rb, rsrc, (), [rbt], rbt)
            ACT(junk, xb, AF.Square, [xbt], ["junk", "x_ss"], accum=st[:, 0:1])
            rs = rstd_own[:, ti:ti + 1] if own else st[:, 1:2]
            TS("dve", st[:, 0:1], st[:, 0:1], 1.0 / D, ALU.mult, ["x_ss"], ["x_ss"], s2=RMS_EPS, op1=ALU.add)
            TT("pool", rs, st[:, 0:1], mhalf[:, 0:1], ALU.pow, ["x_ss", "mhalf"], ["x_rstd"])
            ACT(xsb, xb, AF.Identity, [xbt, "x_rstd"], [xst], scale=rs)
            tb = bankb(0)
            for kc in range(8):
                TR(tb[:, kc * 128:(kc + 1) * 128], xsb[:, kc * 128:(kc + 1) * 128], identb, [xst, "identb"], [B(0)])
            CP("dve", u, tb[:, 0:1024], [B(0)], [ut])

        lat = [A.alloc(672, F32) for _ in range(2)]
        kT2 = [kT, A.alloc(512, BF16)]
        vtok2 = [vtok, A.alloc(1024, BF16)]
        qtok2 = [qtok, A.alloc(768, BF16)]
        gstate = {"gcount": 0}

        def proj1(it):
            kind, ti = seq[it]
            own = kind == "s"
            par = it % 2
            rb, rbt = ropeb[it % NXB], "ropeb%d" % (it % NXB)
            u, ut = uT[par], "uT%d" % par
            u3 = v3(u, 8)
            la, lat_t = lat[par], "lat%d" % par
            for kc in range(8):
                MM(bank(1)[:, 0:288], u3[:, kc, :], w13[:, kc, W1_CKV:W1_CKV + 288], kc == 0, kc == 7, [ut, W1T[kc]], [B(1)])
            if own:
                for kc in range(8):
                    MM(bank(2)[:, 0:384], u3[:, kc, :], w13[:, kc, W1_CQ:W1_CQ + 384], kc == 0, kc == 7, [ut, W1T[kc]], [B(2)])
            for hp in range(4):
                for kc in range(8):
                    MM(bank(3)[:, hp * 128:(hp + 1) * 128], w13[:, kc, W1_RK + hp * 128:W1_RK + (hp + 1) * 128], u3[:, kc, :],
                       kc == 0, kc == 7, [ut, W1T[kc]], [B(3)])
            CP("act", la[:, 0:288], bank(1)[:, 0:288], [B(1)], [lat_t])
            if own:
                CP("act", la[:, 288:672], bank(2)[:, 0:384], [B(2)], [lat_t])
            for hp in range(4):
                for kc in range(8):
                    MM(bank(4)[:, hp * 128:(hp + 1) * 128], w13[:, kc, W1_RKS + hp * 128:W1_RKS + (hp + 1) * 128], u3[:, kc, :],
                       kc == 0, kc == 7, [ut, W1T[kc]], [B(4)])
            cosR = rb[:, 0:128].unsqueeze(1).to_broadcast([128, 4, 128])
            sinR = rb[:, 128:256].unsqueeze(1).to_broadcast([128, 4, 128])
            TT("dve", v3(t1, 4), v3(bank(3), 4), cosR, ALU.mult, [B(3), rbt], ["t1"])
            for hf in range(2):
                for kc in range(8):
                    MM(bank(5 + hf), u3[:, kc, :], w13[:, kc, W1_RV + hf * 512:W1_RV + (hf + 1) * 512], kc == 0, kc == 7,
                       [ut, W1T[kc]], [B(5 + hf)])
            TT("dve", v3(t2, 4), v3(bank(4), 4), sinR, ALU.mult, [B(4), rbt], ["t2"])
            TT("pool", kT2[par], t1, t2, ALU.add, ["t1", "t2"], ["kT%d" % par])
            CP("act", vtok2[par][:, 0:512], bank(5), [B(5)], ["vtok%d" % par])
            CP("act", vtok2[par][:, 512:1024], bank(6), [B(6)], ["vtok%d" % par])

        def back1(it):
            kind, ti = seq[it]
            own = kind == "s"
            par = it % 2
            slot = ti if own else (32 + ti)
            tabM = (tabM_own if own else tabM_oth)[:, ti * 64:(ti + 1) * 64]
            tabMt = "tabM_own" if own else "tabM_oth"
            la, lat_t = lat[par], "lat%d" % par
            kTp, kTt = kT2[par], "kT%d" % par
            vtp, vtt = vtok2[par], "vtok%d" % par
            ACT(junk[:, 0:256], la[:, 0:256], AF.Square, [lat_t], ["junk", "kv_ss"], accum=st[:, 2:3])
            TS("dve", st[:, 2:3], st[:, 2:3], 1.0 / 256, ALU.mult, ["kv_ss"], ["kv_ss"], s2=RMS_EPS, op1=ALU.add)
            TT("pool", st[:, 3:4], st[:, 2:3], mhalf[:, 0:1], ALU.pow, ["kv_ss", "mhalf"], ["kv_rstd"])
            ACT(ckvn, la[:, 0:256], AF.Identity, [lat_t, "kv_rstd"], ["ckvn"], scale=st[:, 3:4])
            if own:
                ACT(junk[:, 0:384], la[:, 288:672], AF.Square, [lat_t], ["junk", "q_ss"], accum=st[:, 4:5])
                TS("dve", st[:, 4:5], st[:, 4:5], 1.0 / 384, ALU.mult, ["q_ss"], ["q_ss"], s2=RMS_EPS, op1=ALU.add)
                TT("pool", st[:, 5:6], st[:, 4:5], mhalf[:, 0:1], ALU.pow, ["q_ss", "mhalf"], ["q_rstd"])
                ACT(cqn, la[:, 288:672], AF.Identity, [lat_t, "q_rstd"], ["cqn"], scale=st[:, 5:6])
            TT("dve", ropeA, la[:, 256:288], tabM[:, 0:32], ALU.mult, [lat_t, tabMt], ["ropeA"])
            TT("dve", ropeBv, la[:, 256:288], tabM[:, 32:64], ALU.mult, [lat_t, tabMt], ["ropeB"])
            TT("pool", kaug[:, 64:80], ropeA[:, 0:16], ropeBv[:, 16:32], ALU.subtract, ["ropeA", "ropeB"], ["kaug"])
            TT("pool", kaug[:, 80:96], ropeBv[:, 0:16], ropeA[:, 16:32], ALU.add, ["ropeA", "ropeB"], ["kaug"])
            hb = bankb(7)
            for hp in range(4):
                TR(hb[:, 384 + hp * 128:384 + (hp + 1) * 128], kTp[:, hp * 128:(hp + 1) * 128], identb, [kTt, "identb"], [B(7)])
            for c2 in range(2):
                TR(hb[:, c2 * 128:(c2 + 1) * 128], ckvn[:, c2 * 128:(c2 + 1) * 128], identb, ["ckvn", "identb"], [B(7)])
            TR(hb[0:96, 256:384], kaug, identb, ["kaug", "identb"], [B(7)])
            if own:
                tb0 = bankb(0)
                for c3 in range(3):
                    TR(tb0[:, c3 * 128:(c3 + 1) * 128], cqn[:, c3 * 128:(c3 + 1) * 128], identb, ["cqn", "identb"], [B(0)])
                CP("dve", cqT, tb0[:, 0:384], [B(0)], ["cqT"])
            ktok3 = hb[:, 384:896].rearrange("p (h d) -> p h d", h=8)
            if own:
                wbb = wfb[:, 8:16].unsqueeze(2).to_broadcast([128, 8, 64])
                TT("dve", kwb.rearrange("p (h d) -> p h d", h=8), ktok3, wbb, ALU.mult, [B(7), "wfb"], ["kwb"])
                dirs = [("b", kwb, "kwb", 3)]
            else:
                wof = wo3[:, ti, :].unsqueeze(2).to_broadcast([128, 8, 64])
                wob = wo3[:, NOT_ + ti, :].unsqueeze(2).to_broadcast([128, 8, 64])
                TT("dve", kwf.rearrange("p (h d) -> p h d", h=8), ktok3, wof, ALU.mult, [B(7), "wo_fb"], ["kwf"])
                TT("dve", kwb.rearrange("p (h d) -> p h d", h=8), ktok3, wob, ALU.mult, [B(7), "wo_fb"], ["kwb"])
                dirs = [("f", kwf, "kwf", 1), ("b", kwb, "kwb", 3)]
            gb_i = gstate["gcount"] % 2
            cks, ckt = ckst[gb_i], "ckst%d" % gb_i
            krs, krt = krst[gb_i], "krst%d" % gb_i
            sp_ = slot % 4
            CP("dve", v3(cks, 2)[:, :, sp_ * 128:(sp_ + 1) * 128], hb[:, 0:256].rearrange("p (a b) -> p a b", a=2), [B(7)], [ckt])
            CP("dve", krs[64:96, sp_ * 128:(sp_ + 1) * 128], hb[64:96, 256:384], [B(7)], [krt])
            flush = (slot == 64) or (own and ti % 4 == 0) or ((not own) and slot < 64 and slot % 4 == 3)
            if flush:
                g0 = (slot // 4) * 512
                DMA("pool", ckvT_d.rearrange("p (a n) -> p a n", a=2)[:, :, g0:g0 + 512], v3(cks, 2)[:, :, 0:512], [ckt], [("ckvT_d", slot // 4)], ckt)
                DMA("pool", krope_d[:, g0:g0 + 512], krs[64:96, 0:512], [krt], [("krope_d", slot // 4)], krt)
                gstate["gcount"] += 1
            if own:
                cqT3 = v3(cqT, 3)
                for kc in range(3):
                    MM(bank(5)[:, 0:480], cqT3[:, kc, :], wuq3[:, kc, 0:480], kc == 0, kc == 2, ["cqT", "wuq"], [B(5)])
                for kc in range(3):
                    MM(bank(6)[:, 0:288], cqT3[:, kc, :], wuq3[:, kc, 480:768], kc == 0, kc == 2, ["cqT", "wuq"], [B(6)])
            for (dname, kw_, kwt, b0) in dirs:
                for hp in range(4):
                    bk = bank(b0 + hp // 2)
                    MM(bk[:, (hp % 2) * 256:(hp % 2) * 256 + 256], kw_[:, hp * 128:(hp + 1) * 128], vtp[:, hp * 256:(hp + 1) * 256],
                       True, True, [kwt, vtt], [B(b0 + hp // 2)])
            Sb3 = v3(Sb, 4)
            Sf3 = v3(Sf, 4)
            if own:
                sbs, sbt = sbst[ti % 2], "sbst%d" % (ti % 2)
                CP("pool", sbs, Sb, ["Sb"], [sbt])
                DMA("pool", SbAll_d[ti, :, :], sbs, [sbt], [("SbAll", ti)], sbt)
                TT("pool", Sb, Sb, GbT, ALU.mult, ["Sb", "GT", sbt], ["Sb"])
            for (dname, kw_, kwt, b0) in dirs:
                Sx3 = Sb3 if dname == "b" else Sf3
                Sxt = "Sb" if dname == "b" else "Sf"
                for hh in range(2):
                    bk = v3(bank(b0 + hh), 2)
                    TT("dve", Sx3[0:64, 2 * hh:2 * hh + 2, :], Sx3[0:64, 2 * hh:2 * hh + 2, :], bk[0:64, :, 0:128], ALU.add,
                       [Sxt, B(b0 + hh)], [Sxt])
                    TT("dve", Sx3[64:128, 2 * hh:2 * hh + 2, :], Sx3[64:128, 2 * hh:2 * hh + 2, :], bk[64:128, :, 128:256], ALU.add,
                       [Sxt, B(b0 + hh)], [Sxt])
            if own:
                qtk, qtt = qtok2[par], "qtok%d" % par
                q3 = qtk.rearrange("p (h c) -> p h c", h=8)
                for (bk_i, h0, nh) in ((5, 0, 5), (6, 5, 3)):
                    src = bank(bk_i)[:, 0:nh * 96].rearrange("p (h c) -> p h c", h=nh)
                    CP("act", q3[:, h0:h0 + nh, 0:64], src[:, :, 0:64], [B(bk_i)], [qtt])
                    ccq = tabM[:, 0:32].unsqueeze(1).to_broadcast([128, nh, 32])
                    ssq = tabM[:, 32:64].unsqueeze(1).to_broadcast([128, nh, 32])
                    qA3 = qA[:, h0 * 32:(h0 + nh) * 32].rearrange("p (h c) -> p h c", h=nh)
                    qB3 = qB[:, h0 * 32:(h0 + nh) * 32].rearrange("p (h c) -> p h c", h=nh)
                    TT("dve", qA3, src[:, :, 64:96], ccq, ALU.mult, [B(bk_i), tabMt], ["qA"])
                    TT("dve", qB3, src[:, :, 64:96], ssq, ALU.mult, [B(bk_i), tabMt], ["qB"])
                qA3 = qA.rearrange("p (h c) -> p h c", h=8)
                qB3 = qB.rearrange("p (h c) -> p h c", h=8)
                TT("pool", q3[:, :, 64:80], qA3[:, :, 0:16], qB3[:, :, 16:32], ALU.subtract, ["qA", "qB"], [qtt])
                TT("pool", q3[:, :, 80:96], qB3[:, :, 0:16], qA3[:, :, 16:32], ALU.add, ["qA", "qB"], [qtt])

        def back2(it):
            kind, ti = seq[it]
            if kind != "s":
                return
            par = it % 2
            qtk, qtt = qtok2[par], "qtok%d" % par
            q3 = qtk.rearrange("p (h c) -> p h c", h=8)
            tb2 = bankb(2)
            for h in range(8):
                TR(tb2[0:96, h * 128:(h + 1) * 128], q3[:, h, :], identb, [qtt, "identb"], [B(2)])
            g = ti // 4
            qs = Qst[g % 2]
            qst = "Qst%d" % (g % 2)
            qs3 = v3(qs, 8)
            CP("dve", qs3[0:96, :, (ti % 4) * 128:(ti % 4 + 1) * 128], tb2[0:96, 0:1024].rearrange("p (h t) -> p h t", h=8),
               [B(2)], [qst])
            if ti % 4 == 0:
                DMA("pool", Qs_d[:, :, g * 512:(g + 1) * 512].rearrange("h d t -> d h t"), qs3[0:96, :, :], [qst], [("Qs", g)], qst)

        n1 = len(seq)
        if n1:
            front1(0)
        for it in range(n1):
            proj1(it)
            if it + 1 < n1:
                front1(it + 1)
            if it >= 1:
                back1(it - 1)
            if it >= 2:
                back2(it - 2)
        if n1:
            back1(n1 - 1)
            if n1 >= 2:
                back2(n1 - 2)
            back2(n1 - 1)
        S.barrier()

    if stages >= 2:
        A.pos = P1_END
        w3 = A.alloc(8 * W3_N, BF16)
        wro = A.alloc(8 * 1024, BF16)
        w33 = v3(w3, 8)
        wro3 = v3(wro, 8)
        DT = A.alloc(1024, F32)
        wfb = A.alloc(16, F32)
        wqfd = A.alloc(1024, F32)
        wqbd = A.alloc(1024, F32)
        GfT = A.alloc(512, F32)
        Sf_bf = A.alloc(512, BF16)
        NXB = 2
        xbuf = [A.alloc(D, F32) for _ in range(NXB)]
        ropeb = [A.alloc(256, F32) for _ in range(NXB)]
        sbl = [A.alloc(512, BF16) for _ in range(2)]
        xs = [A.alloc(D, BF16) for _ in range(2)]
        uT = [A.alloc(D, BF16) for _ in range(2)]
        t1 = A.alloc(512, F32)
        t2 = A.alloc(512, F32)
        qTd = A.alloc(1024, BF16)
        kT = A.alloc(512, BF16)
        qfd = A.alloc(1024, BF16)
        qbd = A.alloc(1024, BF16)
        kwf = A.alloc(512, BF16)
        vtok = A.alloc(1024, BF16)
        sig = A.alloc(1024, F32)
        silu = sig
        sgr = A.alloc(1024, F32)
        PT = A.alloc(1024, BF16)
        sq = A.alloc(1024, F32)
        gst = A.alloc(64, F32)
        zn = sq
        z = A.alloc(1024, BF16)
        zT = A.alloc(1024, BF16)
        m1b = [A.alloc(1024, BF16) for _ in range(2)]
        DT3 = v3(DT, 8)
        for h in range(8):
            TS("dve", DT3[:, h, :], ctab[:, C_POS:C_POS + 128], lg8[:, h:h + 1], ALU.mult, ["ctab", "lg8"], ["DT"])
            STT(DT3[:, h, :], ctab[:, C_NEG:C_NEG + 128], lg8[:, 8 + h:9 + h], DT3[:, h, :], ALU.mult, ALU.add,
                ["ctab", "lg8", "DT"], ["DT"])
        ACT(DT, DT, AF.Exp, ["DT"], ["DT"])
        TS("dve", DT, DT, 0.125, ALU.mult, ["DT"], ["DT"])
        TS("dve", wfb[:, 0:8], lg8[:, 0:8], ctab[:, C_127MJ:C_127MJ + 1], ALU.mult, ["lg8", "ctab"], ["wfb"])
        TS("dve", wfb[:, 8:16], lg8[:, 8:16], ctab[:, C_J:C_J + 1], ALU.mult, ["lg8", "ctab", "wfb"], ["wfb"])
        ACT(wfb, wfb, AF.Exp, ["wfb"], ["wfb"])
        TS("dve", wfb, wfb, 0.125, ALU.mult, ["wfb"], ["wfb"])
        wqfd3, wqbd3, GfT3 = v3(wqfd, 4), v3(wqbd, 4), v3(GfT, 4)
        MSET("pool", wqfd, 0.0, ["wq"])
        MSET("pool", wqbd, 0.0, ["wq"])
        MSET("pool", qTd, 0.0, ["qT"])
        for hp in range(4):
            for par in range(2):
                rs_ = slice(par * 64, (par + 1) * 64)
                cs_ = slice(par * 128, (par + 1) * 128)
                ACT(wqfd3[rs_, hp, cs_], ctab[rs_, C_I1:C_I1 + 128], AF.Exp, ["ctab", "lgP", "wq"], ["wq"], scale=lgP[rs_, hp:hp + 1])
                ACT(wqbd3[rs_, hp, cs_], ctab[rs_, C_128MI:C_128MI + 128], AF.Exp, ["ctab", "lgP", "wq"], ["wq"], scale=lgP[rs_, 4 + hp:5 + hp])
            ACT(GfT3[:, hp, :], c128, AF.Exp, ["c128", "lgP"], ["GT"], scale=lgP[:, hp:hp + 1])
        for kc in range(8):
            LOAD("pool", w3[:, kc * W3_N:(kc + 1) * W3_N], w3a_d[:, kc, :], ["w3_%d" % kc])
            LOAD("pool", wro[:, kc * 1024:(kc + 1) * 1024], wro_d[:, kc, :], ["wro_%d" % kc])
        for kc in range(8):
            sl = slice(kc * W3_N, (kc + 1) * W3_N)
            TS("dve", w3[:, sl], w3[:, sl], vec[:, V_NMW + kc:V_NMW + kc + 1], ALU.mult, ["w3_%d" % kc, "vec"], ["w3_%d" % kc])
            sl = slice(kc * 1024, (kc + 1) * 1024)
            TS("dve", wro[:, sl], wro[:, sl], vec[:, V_GNW + kc:V_GNW + kc + 1], ALU.mult, ["wro_%d" % kc, "vec"], ["wro_%d" % kc])
        W3T = ["w3_%d" % kc for kc in range(8)]
        WRT = ["wro_%d" % kc for kc in range(8)]
        CP("pool", Sf_bf, Sf, ["Sf"], ["Sf_bf"])
        Sf3 = v3(Sf, 4)
        Sfb3 = v3(Sf_bf, 4)
        PT3 = v3(PT, 8)
        N3A = int(os.environ.get("K_S3N", NT))

        def front3a(c):
            xb, xbt = xbuf[c % NXB], "xbuf%d" % (c % NXB)
            rb, rbt = ropeb[c % NXB], "ropeb%d" % (c % NXB)
            xsb, xst = xs[c % 2], "xs%d" % (c % 2)
            u, ut = uT[c % 2], "uT%d" % (c % 2)
            DMA("sp", xb, xo[c * 128:(c + 1) * 128, :], (), [xbt], xbt)
            DMA("sp", rb, ropeR_own_d[:, c, :], (), [rbt], rbt)
            DMA("sp", sbl[c % 2], SbAll_d[c, :, :], [("SbAll", c)], ["sbl%d" % (c % 2)], "sbl%d" % (c % 2))
            ACT(xsb, xb, AF.Identity, [xbt], [xst], scale=rstd_own[:, c:c + 1])
            tb = bankb(0)
            for kc in range(8):
                TR(tb[:, kc * 128:(kc + 1) * 128], xsb[:, kc * 128:(kc + 1) * 128], identb, [xst, "identb"], [B(0)])
            CP("dve", u, tb[:, 0:1024], [B(0)], [ut])

        def tail3a(c):
            tb7 = bankb(7)
            for kc in range(8):
                TR(tb7[:, kc * 128:(kc + 1) * 128], z[:, kc * 128:(kc + 1) * 128], identb, ["z", "identb"], [B(7)])
            CP("dve", zT, tb7[:, 0:1024], [B(7)], ["zT"])
            zT3 = v3(zT, 8)
            for hf in range(2):
                for kc in range(8):
                    MM(bank(5 + hf), zT3[:, kc, :], wro3[:, kc, hf * 512:(hf + 1) * 512], kc == 0, kc == 7, ["zT", WRT[kc]], [B(5 + hf)])
            mb = m1b[c % 2]
            mbt = "m1b%d" % (c % 2)
            for hf in range(2):
                TT("dve", mb[:, hf * 512:(hf + 1) * 512], bank(5 + hf), sgr[:, hf * 512:(hf + 1) * 512], ALU.mult, [B(5 + hf), "sgr"], [mbt])
            DMA("pool", m1_d[c * 128:(c + 1) * 128, :], mb, [mbt], [("m1", c)], mbt)

        if N3A:
            front3a(0)
        for c in range(N3A):
            rb = ropeb[c % NXB]
            rbt = "ropeb%d" % (c % NXB)
            u = uT[c % 2]
            ut = "uT%d" % (c % 2)
            u3 = v3(u, 8)
            sbc, sbct = sbl[c % 2], "sbl%d" % (c % 2)
            sbc3 = v3(sbc, 4)
            for (bk_i, c0) in ((1, W3_RQ), (2, W3_RQS), (3, W3_RK), (4, W3_RKS)):
                for hp in range(4):
                    for kc in range(8):
                        MM(bank(bk_i)[:, hp * 128:(hp + 1) * 128], w33[:, kc, c0 + hp * 128:c0 + (hp + 1) * 128], u3[:, kc, :],
                           kc == 0, kc == 7, [ut, W3T[kc]], [B(bk_i)])
            for hf in range(2):
                for kc in range(8):
                    MM(bank(5 + hf), u3[:, kc, :], w33[:, kc, W3_RV + hf * 512:W3_RV + (hf + 1) * 512], kc == 0, kc == 7,
                       [ut, W3T[kc]], [B(5 + hf)])
            cosR = rb[:, 0:128].unsqueeze(1).to_broadcast([128, 4, 128])
            sinR = rb[:, 128:256].unsqueeze(1).to_broadcast([128, 4, 128])
            TT("dve", v3(t1, 4), v3(bank(1), 4), cosR, ALU.mult, [B(1), rbt], ["t1"])
            TT("dve", v3(t2, 4), v3(bank(2), 4), sinR, ALU.mult, [B(2), rbt], ["t2"])
            qTd3 = v3(qTd, 4)
            TT("pool", qTd3[0:64, :, 0:128], v3(t1, 4)[0:64, :, :], v3(t2, 4)[0:64, :, :], ALU.add, ["t1", "t2", "qT"], ["qT"])
            TT("pool", qTd3[64:128, :, 128:256], v3(t1, 4)[64:128, :, :], v3(t2, 4)[64:128, :, :], ALU.add, ["t1", "t2", "qT"], ["qT"])
            TT("dve", v3(t1, 4), v3(bank(3), 4), cosR, ALU.mult, [B(3), rbt, "qT"], ["t1"])
            TT("dve", v3(t2, 4), v3(bank(4), 4), sinR, ALU.mult, [B(4), rbt, "qT"], ["t2"])
            TT("pool", kT, t1, t2, ALU.add, ["t1", "t2"], ["kT"])
            TT("pool", qfd, qTd, wqfd, ALU.mult, ["qT", "wq"], ["qf"])
            TT("pool", qbd, qTd, wqbd, ALU.mult, ["qT", "wq"], ["qb"])
            CP("act", vtok[:, 0:512], bank(5), [B(5)], ["vtok"])
            CP("act", vtok[:, 512:1024], bank(6), [B(6)], ["vtok"])
            if c >= 1:
                tail3a(c - 1)
            for hp in range(4):
                MM(bank(1 + hp // 2)[:, (hp % 2) * 256:(hp % 2) * 256 + 256], kT[:, hp * 128:(hp + 1) * 128], qTd3[:, hp, :],
                   True, True, ["kT", "qT"], [B(1 + hp // 2)])
            hb = bankb(7)
            for hp in range(4):
                TR(hb[:, hp * 128:(hp + 1) * 128], kT[:, hp * 128:(hp + 1) * 128], identb, ["kT", "identb"], [B(7)])
            for hf in range(2):
                for kc in range(8):
                    MM(bank(3 + hf), u3[:, kc, :], w33[:, kc, W3_RG + hf * 512:W3_RG + (hf + 1) * 512], kc == 0, kc == 7,
                       [ut, W3T[kc]], [B(3 + hf)])
            for hf in range(2):
                TT("dve", PT[:, hf * 512:(hf + 1) * 512], bank(1 + hf), DT[:, hf * 512:(hf + 1) * 512], ALU.mult, [B(1 + hf), "DT"], ["PT"])
            wfbb = wfb[:, 0:8].unsqueeze(2).to_broadcast([128, 8, 64])
            TT("dve", kwf.rearrange("p (h d) -> p h d", h=8), hb[:, 0:512].rearrange("p (h d) -> p h d", h=8), wfbb, ALU.mult,
               [B(7), "wfb"], ["kwf"])
            for hf in range(2):
                sl = slice(hf * 512, (hf + 1) * 512)
                ACT(sig[:, sl], bank(3 + hf), AF.Sigmoid, [B(3 + hf)], ["sig"])
                TT("dve", silu[:, sl], bank(3 + hf), sig[:, sl], ALU.mult, [B(3 + hf), "sig"], ["sig"])
            for h in range(8):
                hp, hb0 = h // 2, (h % 2) * 64
                o_ap = bank(5 + h // 4)[:, (h % 4) * 128:(h % 4 + 1) * 128]
                MM(o_ap, PT3[:, h, :], vtok[:, h * 128:(h + 1) * 128], True, False, ["PT", "vtok"], [B(5 + h // 4)])
                pc = slice((h % 2) * 128, (h % 2) * 128 + 128)
                MM(o_ap, v3(qfd, 4)[:, hp, pc], Sfb3[:, hp, :], False, False, ["qf", "Sf_bf"], [B(5 + h // 4)])
                MM(o_ap, v3(qbd, 4)[:, hp, pc], sbc3[:, hp, :], False, True, ["qb", sbct], [B(5 + h // 4)])
            for hp in range(4):
                bk = bank(1 + hp // 2)
                MM(bk[:, (hp % 2) * 256:(hp % 2) * 256 + 256], kwf[:, hp * 128:(hp + 1) * 128], vtok[:, hp * 256:(hp + 1) * 256],
                   True, True, ["kwf", "vtok"], [B(1 + hp // 2)])
            TT("pool", Sf, Sf, GfT, ALU.mult, ["Sf", "GT", "Sf_bf"], ["Sf"])
            for hh in range(2):
                bk = v3(bank(1 + hh), 2)
                TT("dve", Sf3[0:64, 2 * hh:2 * hh + 2, :], Sf3[0:64, 2 * hh:2 * hh + 2, :], bk[0:64, :, 0:128], ALU.add, ["Sf", B(1 + hh)], ["Sf"])
                TT("dve", Sf3[64:128, 2 * hh:2 * hh + 2, :], Sf3[64:128, 2 * hh:2 * hh + 2, :], bk[64:128, :, 128:256], ALU.add, ["Sf", B(1 + hh)], ["Sf"])
            CP("pool", Sf_bf, Sf, ["Sf"], ["Sf_bf"])
            for hf in range(2):
                for kc in range(8):
                    MM(bank(3 + hf), u3[:, kc, :], w33[:, kc, W3_GR + hf * 512:W3_GR + (hf + 1) * 512], kc == 0, kc == 7,
                       [ut, W3T[kc]], [B(3 + hf)])
            if c + 1 < N3A:
                front3a(c + 1)
            for hf in range(2):
                RED(gst[:, hf * 4:(hf + 1) * 4], v3(bank(5 + hf), 4), [B(5 + hf)], ["gs1"])
                ACT(sq[:, hf * 512:(hf + 1) * 512], bank(5 + hf), AF.Square, [B(5 + hf)], ["sq"])
            RED(gst[:, 8:16], v3(sq, 8), ["sq"], ["gs2"])
            TS("dve", gst[:, 16:24], gst[:, 0:8], 1.0 / 128, ALU.mult, ["gs1"], ["gmean"])
            TT("dve", gst[:, 48:56], gst[:, 16:24], gst[:, 16:24], ALU.mult, ["gmean"], ["gmsq"])
            TS("dve", gst[:, 24:32], gst[:, 8:16], 1.0 / 128, ALU.mult, ["gs2"], ["gvar"], s2=GN_EPS, op1=ALU.add)
            TT("dve", gst[:, 24:32], gst[:, 24:32], gst[:, 48:56], ALU.subtract, ["gvar", "gmsq"], ["gvar"])
            TT("pool", gst[:, 32:40], gst[:, 24:32], mhalf[:, 0:8], ALU.pow, ["gvar", "mhalf"], ["grstd"])
            TT("pool", gst[:, 40:48], gst[:, 16:24], gst[:, 32:40], ALU.mult, ["gmean", "grstd"], ["gnmr"])
            TS("pool", gst[:, 40:48], gst[:, 40:48], -1.0, ALU.mult, ["gnmr"], ["gnmr"], s2=1.0, op1=ALU.mult)
            for h in range(8):
                ACT(zn[:, h * 128:(h + 1) * 128], bank(5 + h // 4)[:, (h % 4) * 128:(h % 4 + 1) * 128], AF.Identity,
                    [B(5 + h // 4), "grstd", "gnmr", "gs2"], ["sq"], scale=gst[:, 32 + h:33 + h], bias=gst[:, 40 + h:41 + h])
            TT("dve", z, zn, silu, ALU.mult, ["sq", "sig"], ["z"])
            for hf in range(2):
                ACT(sgr[:, hf * 512:(hf + 1) * 512], bank(3 + hf), AF.Sigmoid, [B(3 + hf)], ["sgr"])
        if N3A:
            tail3a(N3A - 1)
        S.barrier()

    if stages >= 3:
        A.pos = P0_END
        Ymla = A.alloc(NT * 512, BF16)
        ckvT = A.alloc(2 * NKEY, BF16)
        ckvT3 = v3(ckvT, 2)
        Kb = [A.alloc(NKEY, BF16), A.alloc(NKEY, BF16)]
        wuk = A.alloc(2 * 512, BF16)
        wuv = A.alloc(2 * 512, BF16)
        Vb = [A.alloc(NSLOT * 128, BF16) for _ in range(2)]
        Qb = [A.alloc(TOK, BF16) for _ in range(2)]
        Pb = [A.alloc(1024, BF16) for _ in range(2)]
        oT = A.alloc(512, F32)
        rcp = A.alloc(4, F32)
        CKD = [("ckvT_d", g) for g in range(17)]
        KRD = [("krope_d", g) for g in range(17)]
        for kc in range(2):
            DMA("sp", ckvT[:, kc * NKEY:(kc + 1) * NKEY], ckvT_d[:, kc * NKEYP:kc * NKEYP + NKEY], CKD, ["ckvT%d" % kc], "ckvT%d" % kc)
        for i in range(2):
            DMA("sp", Kb[i][64:96, :], krope_d[:, 0:NKEY], KRD, [("Kr", i)], "Kr%d" % i)
            Vi3 = v3(Vb[i], NSLOT)
            MSET("pool", Vb[i], 0.0, [("V", i)])
            MSET("pool", Vi3[:, :, 64:65], 1.0, [("V", i)])
        for kc in range(2):
            LOAD("pool", wuk[:, kc * 512:(kc + 1) * 512], wuk_d[:, kc, :], ["wuk"])
            LOAD("pool", wuv[:, kc * 512:(kc + 1) * 512], wuv_d[:, kc, :], ["wuv"])
        for kc in range(2):
            sl = slice(kc * 512, (kc + 1) * 512)
            TS("dve", wuk[:, sl], wuk[:, sl], vec[:, V_KVNW + kc:V_KVNW + kc + 1], ALU.mult, ["wuk", "vec"], ["wuk"])
            TS("dve", wuv[:, sl], wuv[:, sl], vec[:, V_KVNW + kc:V_KVNW + kc + 1], ALU.mult, ["wuv", "vec"], ["wuv"])
        Ym3 = v3(Ymla, NT)

        def gen_groups(h):
            hbuf = h % 2
            Kh, Vh, Qh = Kb[hbuf], Vb[hbuf], Qb[hbuf]
            Kt_, Vt_, Qt_ = ("K", hbuf), ("V", hbuf), ("Q", hbuf)
            Vh3 = v3(Vh, NSLOT)
            out = []

            def qload():
                DMA("sp", Qh[0:96, :], Qs_d[h, :, :], [("Qs", g) for g in range(8)], [Qt_], "Qb%d" % hbuf)
            out.append(qload)
            for n in range(17):
                def kgen(n=n):
                    n0 = n * 512
                    w = 512 if n < 16 else 128
                    for kc in range(2):
                        MM(bank(7)[0:64, 0:w], wuk[:, kc * 512 + h * 64:kc * 512 + (h + 1) * 64], ckvT3[:, kc, n0:n0 + w],
                           kc == 0, kc == 1, ["wuk", "ckvT%d" % kc], [B(7)])
                    CP("dve", Kh[0:64, n0:n0 + w], bank(7)[0:64, 0:w], [B(7)], [Kt_])
                out.append(kgen)
            for g0 in range(0, NSLOT, 8):
                def vgen(g0=g0):
                    ng = min(8, NSLOT - g0)
                    for j in range(ng):
                        s_ = g0 + j
                        for kc in range(2):
                            MM(bank(7)[:, j * 64:(j + 1) * 64], ckvT3[:, kc, s_ * 128:(s_ + 1) * 128],
                               wuv[:, kc * 512 + h * 64:kc * 512 + (h + 1) * 64], kc == 0, kc == 1, ["wuv", "ckvT%d" % kc], [B(7)])
                    CP("dve", Vh3[:, g0:g0 + ng, 0:64], bank(7)[:, 0:ng * 64].rearrange("p (a b) -> p a b", a=ng), [B(7)], [Vt_])
                out.append(vgen)
            return out

        for g_ in gen_groups(0):
            g_()
        units = []
        for qb in range(8):
            for s0 in range(0, 64, 2):
                units.append((qb, (s0, s0 + 1)))
            units.append((qb, (64,)))
        nu = len(units)
        it = 0
        for h in range(8):
            hbuf = h % 2
            Kh, Vh, Qh = Kb[hbuf], Vb[hbuf], Qb[hbuf]
            Kt_, Vt_, Qt_ = ("K", hbuf), ("V", hbuf), ("Q", hbuf)
            Vh3 = v3(Vh, NSLOT)
            pending = gen_groups(h + 1) if h < 7 else []
            stride = max(1, nu // (len(pending) + 1)) if pending else nu

            def s_mm(i):
                qb, sl = units[i]
                pb_i = (it + i) % 2
                pst = PS2[pb_i]
                for j, s_ in enumerate(sl):
                    kt = 128 if s_ < 64 else NMETA
                    MM(pst[0:kt, j * 512:(j + 1) * 512], Kh[0:96, s_ * 128:s_ * 128 + kt], Qh[0:96, qb * 512:(qb + 1) * 512], True, True,
                       [Kt_, Qt_, ("Kr", hbuf)], [B(2 * pb_i + j)])
                kt = 128 if sl[0] < 64 else NMETA
                if os.environ.get("K_WIDEEXP"):
                    w = 512 * len(sl)
                    ACT(Pb[pb_i][0:kt, 0:w], pst[0:kt, 0:w], AF.Exp, [B(2 * pb_i + j) for j in range(len(sl))], ["Pb%d" % pb_i], scale=SC_ATT)
                else:
                    for j in range(len(sl)):
                        ACT(Pb[pb_i][0:kt, j * 512:(j + 1) * 512], pst[0:kt, j * 512:(j + 1) * 512], AF.Exp, [B(2 * pb_i + j)],
                            [("Pb", pb_i, j)], scale=SC_ATT)

            s_mm(0)
            deferred = []
            for i in range(nu):
                if i + 1 < nu:
                    s_mm(i + 1)
                qb, sl = units[i]
                pb_i = (it + i) % 2
                ob = 4 + qb % 2
                for j, s_ in enumerate(sl):
                    kt = 128 if s_ < 64 else NMETA
                    MM(bank(ob)[:, :], Vh3[0:kt, s_, :], Pb[pb_i][0:kt, j * 512:(j + 1) * 512], s_ == 0, s_ == NSLOT - 1,
                       [Vt_, "Pb%d" % pb_i, ("Pb", pb_i, j)], [B(ob)])
                if pending and i % stride == stride - 1:
                    pending.pop(0)()
                if deferred and deferred[0][0] <= i:
                    deferred.pop(0)[1]()
                if sl[-1] == NSLOT - 1:
                    CP("dve", oT[0:65, :], bank(ob)[0:65, :], [B(ob)], ["oT"])

                    def norm(qb=qb):
                        for qi in range(4):
                            TR(bank(6)[:, qi * 65:(qi + 1) * 65], oT[0:65, qi * 128:(qi + 1) * 128], identf[0:65, 0:65], ["oT", "identf"], [B(6)])
                        pn = bank(6)[:, 0:260].rearrange("p (a b) -> p a b", a=4)
                        S.op("dve", lambda e, o_=rcp[:, 0:4], i_=pn[:, :, 64]: e.reciprocal(out=o_, in_=i_), [B(6)], ["rcp"])
                        TT("dve", Ym3[:, qb * 4:(qb + 1) * 4, h * 64:(h + 1) * 64], pn[:, :, 0:64],
                           rcp[:, 0:4].unsqueeze(2).to_broadcast([128, 4, 64]), ALU.mult, [B(6), "rcp"], [("Ymla", qb)])
                    deferred.append((i + 3, norm))
            while deferred:
                deferred.pop(0)[1]()
            while pending:
                pending.pop(0)()
            it += nu
        if dbg:
            ydbg = nc.dram_tensor("ymla_dbg", [128, NT * 512], BF16, kind="ExternalOutput").ap()
            DMA("sp", ydbg, Ymla, [("Ymla", q_) for q_ in range(8)], ["ydbg"], "ydbg")
        S.barrier()

    if stages >= 4:
        A.pos = P0_END
        Ymla = A.alloc(NT * 512, BF16)
        Ym3 = v3(Ymla, NT)
        wgm = A.alloc(8 * 1024, BF16)
        wmo = A.alloc(4 * 1024, BF16)
        wo = A.alloc(8 * 1024, BF16)
        wgm3, wmo3, wo3_ = v3(wgm, 8), v3(wmo, 4), v3(wo, 8)
        xbuf = [A.alloc(D, F32) for _ in range(2)]
        m1l = [A.alloc(D, BF16) for _ in range(2)]
        xs = [A.alloc(D, BF16) for _ in range(2)]
        uT = [A.alloc(D, BF16) for _ in range(2)]
        sg = A.alloc(1024, F32)
        ymT = A.alloc(512, BF16)
        mg = A.alloc(1024, F32)
        mg2 = A.alloc(1024, BF16)
        mT = A.alloc(1024, BF16)
        h1b = [A.alloc(D, F32) for _ in range(2)]
        for kc in range(8):
            LOAD("pool", wgm[:, kc * 1024:(kc + 1) * 1024], wgm_d[:, kc, :], ["wgm_%d" % kc])
            LOAD("pool", wo[:, kc * 1024:(kc + 1) * 1024], wo_d[:, kc, :], ["wo_%d" % kc])
        for kc in range(4):
            LOAD("pool", wmo[:, kc * 1024:(kc + 1) * 1024], wmo_d[:, kc, :], ["wmo"])
        for kc in range(8):
            sl = slice(kc * 1024, (kc + 1) * 1024)
            TS("dve", wgm[:, sl], wgm[:, sl], vec[:, V_NMW + kc:V_NMW + kc + 1], ALU.mult, ["wgm_%d" % kc, "vec"], ["wgm_%d" % kc])
        xbuf = xbuf + [A.alloc(D, F32)]

        def front3b(c):
            xb, xbt = xbuf[c % 3], "xbuf%d" % (c % 3)
            ml, mlt = m1l[c % 2], "m1l%d" % (c % 2)
            xsb, xst = xs[c % 2], "xs%d" % (c % 2)
            u, ut = uT[c % 2], "uT%d" % (c % 2)
            DMA("sp", xb, xo[c * 128:(c + 1) * 128, :], (), [xbt], xbt)
            DMA("sp", ml, m1_d[c * 128:(c + 1) * 128, :], [("m1", c)], [mlt], mlt)
            ACT(xsb, xb, AF.Identity, [xbt], [xst], scale=rstd_own[:, c:c + 1])
            tb = bankb(0)
            for kc in range(8):
                TR(tb[:, kc * 128:(kc + 1) * 128], xsb[:, kc * 128:(kc + 1) * 128], identb, [xst, "identb"], [B(0)])
            CP("dve", u, tb[:, 0:1024], [B(0)], [ut])

        def tailT3b(c):
            tb6 = bankb(6)
            for kc in range(8):
                TR(tb6[:, kc * 128:(kc + 1) * 128], mg2[:, kc * 128:(kc + 1) * 128], identb, ["mg2", "identb"], [B(6)])
            CP("dve", mT, tb6[:, 0:1024], [B(6)], ["mT"])

        def tailO3b(c):
            xb, xbt = xbuf[c % 3], "xbuf%d" % (c % 3)
            hb_, hbt = h1b[c % 2], "h1b%d" % (c % 2)
            mT3 = v3(mT, 8)
            for hf in range(2):
                for kc in range(8):
                    MM(bank(3 + hf), mT3[:, kc, :], wo3_[:, kc, hf * 512:(hf + 1) * 512], kc == 0, kc == 7, ["mT", "wo_%d" % kc], [B(3 + hf)])
            for hf in range(2):
                sl = slice(hf * 512, (hf + 1) * 512)
                TT("dve", hb_[:, sl], bank(3 + hf), xb[:, sl], ALU.add, [B(3 + hf), xbt], [hbt])
            DMA("pool", h1_d[c * 128:(c + 1) * 128, :], hb_, [hbt], [("h1", c)], hbt)

        front3b(0)
        YB = (5, 7)
        for c in range(NT):
            ml, mlt = m1l[c % 2], "m1l%d" % (c % 2)
            u, ut = uT[c % 2], "uT%d" % (c % 2)
            u3 = v3(u, 8)
            for hf in range(2):
                for kc in range(8):
                    MM(bank(1 + hf), u3[:, kc, :], wgm3[:, kc, hf * 512:(hf + 1) * 512], kc == 0, kc == 7, [ut, "wgm_%d" % kc], [B(1 + hf)])
            if c >= 1:
                tailT3b(c - 1)
            if c + 1 < NT:
                front3b(c + 1)
            tb7 = bankb(7)
            for kc in range(4):
                TR(tb7[:, kc * 128:(kc + 1) * 128], Ym3[:, c, kc * 128:(kc + 1) * 128], identb, [("Ymla", c // 4), "identb"], [B(7)])
            CP("dve", ymT, tb7[:, 0:512], [B(7)], ["ymT"])
            if c >= 1:
                tailO3b(c - 1)
            ymT3 = v3(ymT, 4)
            for hf in range(2):
                for kc in range(4):
                    MM(bank(YB[hf]), ymT3[:, kc, :], wmo3[:, kc, hf * 512:(hf + 1) * 512], kc == 0, kc == 3, ["ymT", "wmo"], [B(YB[hf])])
            for hf in range(2):
                sl = slice(hf * 512, (hf + 1) * 512)
                ACT(sg[:, sl], bank(1 + hf), AF.Sigmoid, [B(1 + hf)], ["sg"])
                TT("dve", mg[:, sl], bank(YB[hf]), sg[:, sl], ALU.mult, [B(YB[hf]), "sg"], ["mg"])
            TT("pool", mg2, mg, ml, ALU.add, ["mg", mlt], ["mg2"])
        tailT3b(NT - 1)
        tailO3b(NT - 1)
        S.barrier()

    if stages >= 5:
        A.pos = P0_END
        wg = A.alloc(8 * DFF, BF16)
        wu = A.alloc(8 * DFF, BF16)
        wd = A.alloc(NFC * 1024, BF16)
        wg3, wu3, wd3 = v3(wg, 8), v3(wu, 8), v3(wd, NFC)
        nfin = A.alloc(D, F32)
        hbuf_ = [A.alloc(D, F32) for _ in range(2)]
        hs = [A.alloc(D, BF16) for _ in range(2)]
        junk = A.alloc(D, BF16)
        uT2 = A.alloc(8 * 512, BF16)
        actT = A.alloc(NFC * 512, BF16)
        sgb = [A.alloc(512, F32) for _ in range(2)]
        gsb = sgb
        h2 = [A.alloc(D, F32) for _ in range(1)]
        ob_ = [A.alloc(D, F32) for _ in range(2)]
        st = A.alloc(8, F32)
        uT23 = v3(uT2, 8)
        actT3 = v3(actT, NFC)
        LOAD("sp", nfin, nfin_d, ["nfin"])
        for kc in range(8):
            LOAD("pool", wg[:, kc * DFF:(kc + 1) * DFF], wg_d[:, kc, :], ["wg_%d" % kc])
            LOAD("pool", wu[:, kc * DFF:(kc + 1) * DFF], wu_d[:, kc, :], ["wu_%d" % kc])
        for fc in range(NFC):
            LOAD("pool", wd[:, fc * 1024:(fc + 1) * 1024], wd_d[:, fc, :], ["wd_%d" % fc])
        for kc in range(8):
            sl = slice(kc * DFF, (kc + 1) * DFF)
            TS("dve", wg[:, sl], wg[:, sl], vec[:, V_NFW + kc:V_NFW + kc + 1], ALU.mult, ["wg_%d" % kc, "vec"], ["wg_%d" % kc])
            TS("pool", wu[:, sl], wu[:, sl], vec[:, V_NFW + kc:V_NFW + kc + 1], ALU.mult, ["wu_%d" % kc, "vec"], ["wu_%d" % kc], s2=1.0, op1=ALU.mult)
        cnt = 0
        for blk in range(TOK // 512):
            for t in range(4):
                c = blk * 4 + t
                hbf, hbft = hbuf_[c % 2], "hbuf%d" % (c % 2)
                hsb, hst = hs[c % 2], "hs%d" % (c % 2)
                DMA("sp", hbf, h1_d[c * 128:(c + 1) * 128, :], [("h1", c)], [hbft], hbft)
                ACT(junk, hbf, AF.Square, [hbft], ["junk", "f_ss"], accum=st[:, 0:1])
                TS("dve", st[:, 0:1], st[:, 0:1], 1.0 / D, ALU.mult, ["f_ss"], ["f_ss"], s2=RMS_EPS, op1=ALU.add)
                TT("pool", st[:, 1:2], st[:, 0:1], mhalf[:, 0:1], ALU.pow, ["f_ss", "mhalf"], ["f_rstd"])
                ACT(hsb, hbf, AF.Identity, [hbft, "f_rstd"], [hst], scale=st[:, 1:2])
                tb = bankb(t % 2)
                for kc in range(8):
                    TR(tb[:, kc * 128:(kc + 1) * 128], hsb[:, kc * 128:(kc + 1) * 128], identb, [hst, "identb"], [B(t % 2)])
                CP("dve", uT23[:, :, t * 128:(t + 1) * 128], tb[:, 0:1024].rearrange("p (a b) -> p a b", a=8), [B(t % 2)], ["uT2"])
            for fc in range(NFC):
                gb_, ub_ = 2 + 2 * (fc % 2), 3 + 2 * (fc % 2)
                for kc in range(8):
                    MM(bank(gb_), wg3[:, kc, fc * 128:(fc + 1) * 128], uT23[:, kc, :], kc == 0, kc == 7, ["uT2", "wg_%d" % kc], [B(gb_)])
                for kc in range(8):
                    MM(bank(ub_), wu3[:, kc, fc * 128:(fc + 1) * 128], uT23[:, kc, :], kc == 0, kc == 7, ["uT2", "wu_%d" % kc], [B(ub_)])
                sgt, gst_ = "sgb%d" % (fc % 2), "sgb%d" % (fc % 2)
                ACT(sgb[fc % 2], bank(gb_), AF.Sigmoid, [B(gb_)], [sgt])
                TT("dve", gsb[fc % 2], bank(gb_), sgb[fc % 2], ALU.mult, [B(gb_), sgt], [gst_])
                TT("dve", actT3[:, fc, :], bank(ub_), gsb[fc % 2], ALU.mult, [B(ub_), gst_], [("actT", fc)])
            for t in range(4):
                c = blk * 4 + t
                d0, d1 = 6, 7
                for hf in range(2):
                    for fc in range(NFC):
                        MM(bank(d0 + hf), actT3[:, fc, t * 128:(t + 1) * 128], wd3[:, fc, hf * 512:(hf + 1) * 512], fc == 0, fc == NFC - 1,
                           [("actT", fc), "wd_%d" % fc], [B(d0 + hf)])
                hbf, hbft = hbuf_[cnt % 2], "hbuf%d" % (cnt % 2)
                cnt += 1
                DMA("sp", hbf, h1_d[c * 128:(c + 1) * 128, :], [("h1", c)], [hbft], hbft)
                h2b, h2t = h2[0], "h2_0"
                obb, obt = ob_[c % 2], "ob%d" % (c % 2)
                for hf in range(2):
                    sl = slice(hf * 512, (hf + 1) * 512)
                    TT("dve", h2b[:, sl], bank(d0 + hf), hbf[:, sl], ALU.add, [B(d0 + hf), hbft], [h2t])
                ACT(junk, h2b, AF.Square, [h2t], ["junk", "o_ss"], accum=st[:, 2:3])
                TS("dve", st[:, 2:3], st[:, 2:3], 1.0 / D, ALU.mult, ["o_ss"], ["o_ss"], s2=RMS_EPS, op1=ALU.add)
                TT("pool", st[:, 3:4], st[:, 2:3], mhalf[:, 0:1], ALU.pow, ["o_ss", "mhalf"], ["o_rstd"])
                STT(obb, h2b, st[:, 3:4], nfin, ALU.mult, ALU.mult, [h2t, "o_rstd", "nfin"], [obt])
                DMA("pool", out_d[c * 128:(c + 1) * 128, :], obb, [obt], [("out", c)], obt)

    S.emit(nc)
    es.close()
    return nc


def _kc_layout(w):
    K, N = w.shape
    return np.ascontiguousarray(w.reshape(K // 128, 128, N).transpose(1, 0, 2))


def _swap_cols(w, hd):
    K, N = w.shape
    w4 = w.reshape(K, N // hd, 2, hd // 2)
    return np.ascontiguousarray(w4[:, :, ::-1, :]).reshape(K, N)


def _rope_tab(pos, half, base=10000.0):
    inv = (np.float32(base) ** (-(np.arange(half, dtype=np.float32) / np.float32(half)))).astype(np.float32)
    ang = (pos.astype(np.float32)[:, None] * inv[None, :]).astype(np.float32)
    return np.cos(ang.astype(np.float64)).astype(np.float32), np.sin(ang.astype(np.float64)).astype(np.float32)


_PROGRAM = {}


def _prep_inputs(x, meta_tokens, norm_mix_w, w_in, ret_decay_fwd, ret_decay_bwd, ret_gn_w, w_ret_out,
                 mla_q_norm_w, w_uq, mla_kv_norm_w, w_uk, w_uv, w_mla_out, w_o, norm_ffn_w,
                 w_ffn_gate, w_ffn_up, w_ffn_down, norm_final_w):
    f = np.float32
    x = np.asarray(x, f)
    W = np.asarray(w_in, f)[0]
    rq, rk, rv, rg = W[:, 0:512], W[:, 512:1024], W[:, 1024:2048], W[:, 2048:3072]
    cq, ckv, kr = W[:, 3072:3456], W[:, 3456:3712], W[:, 3712:3744]
    gret, gmla = W[:, 3744:4768], W[:, 4768:5792]
    rks, rqs = _swap_cols(rk, 64), _swap_cols(rq, 64)
    shared = {
        "w1": _kc_layout(np.concatenate([cq, ckv, kr, rk, rks, rv], axis=1)),
        "w3a": _kc_layout(np.concatenate([rq, rqs, rk, rks, rv, rg, gret], axis=1)),
        "wgm": _kc_layout(gmla),
        "wmo": _kc_layout(np.asarray(w_mla_out, f)[0]),
        "wo": _kc_layout(np.asarray(w_o, f)[0]),
        "wuq": _kc_layout(np.asarray(w_uq, f)[0]),
        "wuk": _kc_layout(np.asarray(w_uk, f)[0]),
        "wuv": _kc_layout(np.asarray(w_uv, f)[0]),
        "wro": _kc_layout(np.asarray(w_ret_out, f)[0]),
        "wg": _kc_layout(np.asarray(w_ffn_gate, f)[0]),
        "wu": _kc_layout(np.asarray(w_ffn_up, f)[0]),
        "wd": _kc_layout(np.asarray(w_ffn_down, f)[0]),
        "ident": np.eye(128, dtype=f),
        "nfin": np.ascontiguousarray(np.broadcast_to(np.asarray(norm_final_w, f)[None, :], (128, D))),
    }
    vec = np.zeros((128, NVEC), f)

    def pk(v):
        v = np.asarray(v, f).reshape(-1)
        return v.reshape(-1, 128).T

    vec[:, V_NMW:V_NMW + 8] = pk(norm_mix_w)
    vec[:, V_NFW:V_NFW + 8] = pk(norm_ffn_w)
    vec[:, V_GNW:V_GNW + 8] = pk(ret_gn_w)
    vec[:, V_QNW:V_QNW + 3] = pk(mla_q_norm_w)
    vec[:, V_KVNW:V_KVNW + 2] = pk(mla_kv_norm_w)
    df = np.asarray(ret_decay_fwd, f).reshape(8)
    db = np.asarray(ret_decay_bwd, f).reshape(8)
    par = (np.arange(128) >= 64).astype(np.int64)
    for hp in range(4):
        vec[:, V_DFP + hp] = df[2 * hp + par]
        vec[:, V_DBP + hp] = db[2 * hp + par]
    vec[:, V_DF8:V_DF8 + 8] = df[None, :]
    vec[:, V_DB8:V_DB8 + 8] = db[None, :]
    shared["vec"] = vec
    ctab = np.zeros((128, NCTAB), f)
    j = np.arange(128, dtype=f)[:, None]
    i = np.arange(128, dtype=f)[None, :]
    ctab[:, C_POS:C_POS + 128] = np.maximum(i - j, 0)
    ctab[:, C_NEG:C_NEG + 128] = np.maximum(j - i, 0)
    ctab[:, C_I1:C_I1 + 128] = np.broadcast_to(i + 1, (128, 128))
    ctab[:, C_128MI:C_128MI + 128] = np.broadcast_to(128 - i, (128, 128))
    ctab[:, C_127MJ] = 127 - j[:, 0]
    ctab[:, C_J] = j[:, 0]
    shared["ctab"] = ctab

    meta = np.asarray(meta_tokens, f)
    BIG = f(1.0e9)
    sgn = np.where((np.arange(128) % 64) < 32, -1.0, 1.0).astype(f)[:, None]
    fidx = (np.arange(128) % 64) % 32
    in_maps = []
    for core in range(8):
        b, half = core // 2, core % 2
        oth = 1 - half
        m = dict(shared)
        m["xo"] = np.ascontiguousarray(x[b, half * TOK:(half + 1) * TOK])
        xr = np.zeros((NOT_ * 128, D), f)
        xr[0:TOK] = x[b, oth * TOK:(oth + 1) * TOK]
        xr[TOK:TOK + NMETA] = meta
        m["xr"] = xr
        pos_own = (NMETA + half * TOK + np.arange(TOK)).astype(np.int64)
        pos_oth = np.zeros(NOT_ * 128, np.int64)
        pos_oth[0:TOK] = NMETA + oth * TOK + np.arange(TOK)
        pos_oth[TOK:TOK + NMETA] = np.arange(NMETA)
        valid_oth = np.zeros(NOT_ * 128, bool)
        valid_oth[0:TOK + NMETA] = True
        for nm, pos, ntile in (("ropeR_own", pos_own, NT), ("ropeR_oth", pos_oth, NOT_)):
            c, s = _rope_tab(pos, 32)
            cfm = c[:, fidx].T
            sfm = s[:, fidx].T * sgn
            tab = np.stack([cfm.reshape(128, ntile, 128), sfm.reshape(128, ntile, 128)], axis=2)
            m[nm] = np.ascontiguousarray(tab.reshape(128, ntile, 256)).astype(f)
        for nm, pos, ntile in (("tabM_own", pos_own, NT), ("tabM_oth", pos_oth, NOT_)):
            c, s = _rope_tab(pos, 16)
            tab = np.concatenate([c, c, s, s], axis=1).reshape(ntile, 128, 64).transpose(1, 0, 2)
            m[nm] = np.ascontiguousarray(tab).astype(f)
        own_first = NMETA + half * TOK
        own_last = own_first + TOK - 1
        dfw = np.where(valid_oth & (pos_oth < own_first), own_first - 1 - pos_oth, BIG).astype(f)
        dbw = np.where(valid_oth & (pos_oth > own_last), pos_oth - own_last - 1, BIG).astype(f)
        dist = np.concatenate([dfw.reshape(NOT_, 128).T, dbw.reshape(NOT_, 128).T], axis=1)
        m["dist"] = np.ascontiguousarray(dist).astype(f)
        in_maps.append(m)
    return in_maps


def kernel(**inputs):
    in_maps = _prep_inputs(**inputs)
    if "nc" not in _PROGRAM:
        _PROGRAM["nc"] = build_program()
    nc = _PROGRAM["nc"]
    res = run_bass_kernel_spmd(nc, in_maps, core_ids=list(range(8)))
    out = np.zeros((NB, SEQ, D), np.float32)
    for core in range(8):
        b, half = core // 2, core % 2
        out[b, half * TOK:(half + 1) * TOK] = res.results[core]["out"]
    return out
```

```python
from contextlib import ExitStack
import os
import numpy as np
import concourse.bass as bass
import concourse.mybir as mybir
from concourse.bass_utils import run_bass_kernel_spmd

F32 = mybir.dt.float32
BF16 = mybir.dt.bfloat16
AF = mybir.ActivationFunctionType
ALU = mybir.AluOpType
AX = mybir.AxisListType

D = 1024
SEQ = 8192
NB = 4
NMETA = 16
TOK = 4096
NT = 32
NOT_ = 33
NSLOT = 65
NKEY = NSLOT * 128
NKEYP = 17 * 512
DFF = 2816
NFC = DFF // 128
RMS_EPS = 1e-6
GN_EPS = 1e-5
SC_ATT = 96.0 ** -0.5

ENGS = ("pe", "act", "dve", "pool", "sp")
EPOCH = 24000


class _Op:
    __slots__ = ("eng", "fn", "signal", "deps", "dma", "dma_n", "sem", "cnt")

    def __init__(self, eng, fn, dma):
        self.eng = eng
        self.fn = fn
        self.signal = False
        self.deps = []
        self.dma = dma
        self.dma_n = 0
        self.sem = None
        self.cnt = 0


class Sched:
    def __init__(self):
        self.ops = []
        self.last_w = {}
        self.readers = {}
        self.dma_cnt = {}
        self._bar = []
        self._bar_seen = set()

    def op(self, eng, fn, reads=(), writes=(), dma=None):
        o = _Op(eng, fn, dma)
        deps = set()
        if eng not in self._bar_seen:
            self._bar_seen.add(eng)
            deps.update(self._bar)
        for t in reads:
            w = self.last_w.get(t)
            if w is not None:
                deps.add(w)
            if isinstance(t, str) and t[0] == "b" and t[1:].isdigit():
                for k, r in self.readers.get(t, {}).items():
                    if r.eng != eng:
                        deps.add(r)
        for t in writes:
            w = self.last_w.get(t)
            if w is not None:
                deps.add(w)
            for r in self.readers.get(t, {}).values():
                deps.add(r)
        if dma is not None:
            n = self.dma_cnt.get(dma, 0) + 1
            self.dma_cnt[dma] = n
            o.dma_n = n
        for d in deps:
            if d is o:
                continue
            if d.dma is None and d.eng == "pe" and eng == "pe" and dma is None:
                continue
            o.deps.append(d)
            if d.dma is None:
                d.signal = True
        for t in reads:
            rd = self.readers.setdefault(t, {})
            rd[eng if dma is None else ("dma", id(o))] = o
        for t in writes:
            self.last_w[t] = o
            self.readers[t] = {}
        self.ops.append(o)
        return o

    def barrier(self):
        last = {}
        for o in self.ops:
            last[o.eng if o.dma is None else ("dma", o.dma)] = o
        self._bar = list(last.values())
        self._bar_seen = set()
        for o in self._bar:
            if o.dma is None:
                o.signal = True

    def emit(self, nc, final_eng="sp"):
        per = {e: [] for e in ENGS}
        for o in self.ops:
            per[o.eng].append(o)
        nsems = {}
        for e in ENGS:
            c = 0
            ep = 0
            for o in per[e]:
                if o.signal and o.dma is None:
                    c += 1
                    if c > EPOCH:
                        ep += 1
                        c = 1
                    o.sem = (e, ep)
                    o.cnt = c
            nsems[e] = ep + 1
        with ExitStack() as es:
            sems = {}
            for e in ENGS:
                for ep in range(nsems[e]):
                    sems[(e, ep)] = es.enter_context(nc.semaphore(f"s_{e}_{ep}"))
            dsems = {}
            for k in self.dma_cnt:
                dsems[k] = es.enter_context(nc.semaphore("d_" + str(len(dsems))))
            block = es.enter_context(nc.Block())
            dma_cnt = self.dma_cnt

            def run(e, eng):
                waited = {}
                for o in per[e]:
                    need = {}
                    for d in o.deps:
                        if d.dma is not None:
                            key = ("d", d.dma)
                            v = (0, 16 * d.dma_n)
                        else:
                            key = ("e", d.eng)
                            v = (d.sem[1], d.cnt)
                        if v > need.get(key, (-1, -1)):
                            need[key] = v
                    for key, v in need.items():
                        if v <= waited.get(key, (-1, -1)):
                            continue
                        waited[key] = v
                        if key[0] == "d":
                            eng.wait_ge(dsems[key[1]], v[1])
                        else:
                            eng.wait_ge(sems[(key[1], v[0])], v[1])
                    ins = o.fn(eng)
                    if o.dma is not None:
                        ins.then_inc(dsems[o.dma], 16)
                    elif o.signal:
                        ins.then_inc(sems[o.sem], 1)
                if e == final_eng:
                    for k, n in dma_cnt.items():
                        if 16 * n > waited.get(("d", k), (-1, -1))[1]:
                            eng.wait_ge(dsems[k], 16 * n)

            @block.tensor
            def _(eng):
                run("pe", eng)

            @block.scalar
            def _(eng):
                run("act", eng)

            @block.vector
            def _(eng):
                run("dve", eng)

            @block.gpsimd
            def _(eng):
                run("pool", eng)

            @block.sync
            def _(eng):
                run("sp", eng)


class Arena:
    def __init__(self, ap, ncols):
        self.ap = ap
        self.n = ncols
        self.pos = 0

    def alloc(self, cols, dt=BF16):
        n = cols * 2 if dt == F32 else cols
        n = (n + 1) // 2 * 2
        v = self.ap[:, self.pos:self.pos + (cols * 2 if dt == F32 else cols)]
        self.pos += n
        assert self.pos <= self.n, ("arena overflow", self.pos, self.n)
        return v.bitcast(F32) if dt == F32 else v


V_NMW, V_NFW, V_GNW, V_QNW, V_KVNW, V_DFP, V_DBP, V_DF8, V_DB8 = 0, 8, 16, 24, 27, 29, 33, 37, 45
NVEC = 53
C_POS, C_NEG, C_I1, C_128MI, C_127MJ, C_J = 0, 128, 256, 384, 512, 513
NCTAB = 514
W1_CQ, W1_CKV, W1_KR, W1_RK, W1_RKS, W1_RV, W1_N = 0, 384, 640, 672, 1184, 1696, 2720
W3_RQ, W3_RQS, W3_RK, W3_RKS, W3_RV, W3_RG, W3_GR, W3_N = 0, 512, 1024, 1536, 2048, 3072, 4096, 5120
ARENA_COLS = 106400


def build_program(stages=99, dbg=False):
    nc = bass.Bass("TRN2", target_bir_lowering=False)

    def din(name, shape, dt=F32):
        return nc.dram_tensor(name, list(shape), dt, kind="ExternalInput").ap()

    xo = din("xo", [TOK, D])
    xr = din("xr", [NOT_ * 128, D])
    w1_d = din("w1", [128, 8, W1_N])
    w3a_d = din("w3a", [128, 8, W3_N])
    wgm_d = din("wgm", [128, 8, 1024])
    wmo_d = din("wmo", [128, 4, 1024])
    wo_d = din("wo", [128, 8, 1024])
    wuq_d = din("wuq", [128, 3, 768])
    wuk_d = din("wuk", [128, 2, 512])
    wuv_d = din("wuv", [128, 2, 512])
    wro_d = din("wro", [128, 8, 1024])
    wg_d = din("wg", [128, 8, DFF])
    wu_d = din("wu", [128, 8, DFF])
    wd_d = din("wd", [128, NFC, 1024])
    vec_d = din("vec", [128, NVEC])
    ctab_d = din("ctab", [128, NCTAB])
    dist_d = din("dist", [128, 2 * NOT_])
    ident_d = din("ident", [128, 128])
    nfin_d = din("nfin", [128, D])
    ropeR_own_d = din("ropeR_own", [128, NT, 256])
    ropeR_oth_d = din("ropeR_oth", [128, NOT_, 256])
    tabM_own_d = din("tabM_own", [128, NT, 64])
    tabM_oth_d = din("tabM_oth", [128, NOT_, 64])
    out_d = nc.dram_tensor("out", [TOK, D], F32, kind="ExternalOutput").ap()
    SCR_KIND = "ExternalOutput" if dbg else "Internal"
    Qs_d = nc.dram_tensor("Qs", [8, 96, TOK], BF16, kind=SCR_KIND).ap()
    m1_d = nc.dram_tensor("m1s", [TOK, D], BF16, kind=SCR_KIND).ap()
    h1_d = nc.dram_tensor("h1s", [TOK, D], F32, kind=SCR_KIND).ap()
    ckvT_d = nc.dram_tensor("ckvTs", [128, 2 * NKEYP], BF16, kind=SCR_KIND).ap()
    krope_d = nc.dram_tensor("kropes", [32, NKEYP], BF16, kind=SCR_KIND).ap()
    SbAll_d = nc.dram_tensor("SbAlls", [NT, 128, 512], BF16, kind=SCR_KIND).ap()

    S = Sched()
    es = ExitStack()
    arena_t = es.enter_context(nc.sbuf_tensor("arena", [128, ARENA_COLS], BF16))
    A = Arena(arena_t, ARENA_COLS)
    PS2 = [es.enter_context(nc.psum_tensor(f"ps2_{i}", [128, 1024], F32)) for i in range(4)]

    def bank(i):
        return PS2[i // 2][:, (i % 2) * 512:(i % 2 + 1) * 512]

    def bankb(i):
        return bank(i).bitcast(BF16)

    def B(i):
        return "b%d" % i

    def MM(out, lhsT, rhs, start, stop, r, w):
        S.op("pe", lambda e: e.matmul(out, lhsT=lhsT, rhs=rhs, start=start, stop=stop), r, w)

    def TR(out, in_, idn, r, w):
        S.op("pe", lambda e: e.transpose(out=out, in_=in_, identity=idn), r, w)

    def ACT(out, in_, func, r, w, scale=None, bias=None, accum=None):
        kw = {}
        if scale is not None:
            kw["scale"] = scale
        if bias is not None:
            kw["bias"] = bias
        if accum is not None:
            kw["accum_out"] = accum
        S.op("act", lambda e: e.activation(out=out, in_=in_, func=func, **kw), r, w)

    def TT(eng, out, in0, in1, op, r, w):
        S.op(eng, lambda e: e.tensor_tensor(out=out, in0=in0, in1=in1, op=op), r, w)

    def TS(eng, out, in0, s1, op0, r, w, s2=None, op1=None):
        if op1 is None:
            S.op(eng, lambda e: e.tensor_scalar(out=out, in0=in0, scalar1=s1, scalar2=None, op0=op0), r, w)
        else:
            S.op(eng, lambda e: e.tensor_scalar(out=out, in0=in0, scalar1=s1, scalar2=s2, op0=op0, op1=op1), r, w)

    def STT(out, in0, scalar, in1, op0, op1, r, w):
        S.op("dve", lambda e: e.scalar_tensor_tensor(out=out, in0=in0, scalar=scalar, in1=in1, op0=op0, op1=op1), r, w)

    def CP(eng, out, in_, r, w):
        if eng == "act":
            S.op("act", lambda e: e.copy(out=out, in_=in_), r, w)
        else:
            S.op(eng, lambda e: e.tensor_copy(out=out, in_=in_), r, w)

    def RED(out, in_, r, w):
        S.op("dve", lambda e: e.tensor_reduce(out=out, in_=in_, axis=AX.X, op=ALU.add), r, w)

    def MSET(eng, ap, val, w):
        S.op(eng, lambda e: e.memset(ap, val), (), w)

    def DMA(eng, out, in_, r, w, key):
        S.op(eng, lambda e: e.dma_start(out=out, in_=in_), r, w, dma=key)

    chain_i = [0]

    def LOAD(eng, out, in_, w):
        k = "chain_%s%d" % (eng, chain_i[0] % 3)
        chain_i[0] += 1
        S.op(eng, lambda e: e.dma_start(out=out, in_=in_), (), list(w) + [k], dma=k)

    def v3(ap, a):
        return ap.rearrange("p (a b) -> p a b", a=a)

    vec = A.alloc(NVEC + 1, F32)
    identf = A.alloc(128, F32)
    identb = A.alloc(128, BF16)
    lg8 = A.alloc(16, F32)
    lgP = A.alloc(8, F32)
    mhalf = A.alloc(16, F32)
    rstd_own = A.alloc(NT, F32)
    P0_END = A.pos
    ctab = A.alloc(NCTAB, F32)
    c128 = A.alloc(128, F32)
    Sf = A.alloc(512, F32)
    Sb = A.alloc(512, F32)
    P1_END = A.pos

    LOAD("sp", vec[:, 0:NVEC], vec_d, ["vec"])
    LOAD("sp", ctab, ctab_d, ["ctab"])
    LOAD("sp", identf, ident_d, ["identf"])
    CP("dve", identb, identf, ["identf"], ["identb"])
    MSET("pool", c128, 128.0, ["c128"])
    MSET("pool", mhalf, -0.5, ["mhalf"])
    MSET("pool", Sf, 0.0, ["Sf"])
    MSET("pool", Sb, 0.0, ["Sb"])
    ACT(lg8, vec[:, V_DF8:V_DF8 + 16], AF.Exp, ["vec"], ["lg8"])
    TS("dve", lg8, lg8, -1.0, ALU.mult, ["lg8"], ["lg8"])
    ACT(lgP, vec[:, V_DFP:V_DFP + 8], AF.Exp, ["vec"], ["lgP"])
    TS("dve", lgP, lgP, -1.0, ALU.mult, ["lgP"], ["lgP"])

    if stages >= 1:
        A.pos = P1_END
        w1 = A.alloc(8 * W1_N, BF16)
        wuq = A.alloc(3 * 768, BF16)
        w13 = v3(w1, 8)
        wuq3 = v3(wuq, 3)
        wfb = A.alloc(16, F32)
        GbT = A.alloc(512, F32)
        wo_fb = A.alloc(2 * NOT_ * 8, F32)
        dist = A.alloc(2 * NOT_, F32)
        tabM_own = A.alloc(NT * 64, F32)
        tabM_oth = A.alloc(NOT_ * 64, F32)
        NXB = 2
        xbuf = [A.alloc(D, F32) for _ in range(NXB)]
        ropeb = [A.alloc(256, F32) for _ in range(NXB)]
        xs = [A.alloc(D, BF16) for _ in range(2)]
        junk = A.alloc(D, BF16)
        uT = [A.alloc(D, BF16) for _ in range(2)]
        st = A.alloc(8, F32)
        ckvn = A.alloc(256, BF16)
        kaug = A.alloc(96, BF16)
        ropeA = A.alloc(32, F32)
        ropeBv = A.alloc(32, F32)
        t1 = A.alloc(512, F32)
        t2 = A.alloc(512, F32)
        kT = A.alloc(512, BF16)
        kwf = A.alloc(512, BF16)
        kwb = A.alloc(512, BF16)
        vtok = A.alloc(1024, BF16)
        cqn = A.alloc(384, BF16)
        cqT = A.alloc(384, BF16)
        qA = A.alloc(256, F32)
        qB = A.alloc(256, F32)
        qtok = A.alloc(768, BF16)
        Qst = [A.alloc(8 * 512, BF16) for _ in range(2)]
        ckst = [A.alloc(2 * 512, BF16) for _ in range(2)]
        krst = [A.alloc(512, BF16) for _ in range(2)]
        sbst = [A.alloc(512, BF16) for _ in range(2)]

        LOAD("sp", dist, dist_d, ["dist"])
        LOAD("sp", tabM_own, tabM_own_d.rearrange("p a b -> p (a b)"), ["tabM_own"])
        LOAD("sp", tabM_oth, tabM_oth_d.rearrange("p a b -> p (a b)"), ["tabM_oth"])
        TS("dve", wfb[:, 0:8], lg8[:, 0:8], ctab[:, C_127MJ:C_127MJ + 1], ALU.mult, ["lg8", "ctab"], ["wfb"])
        TS("dve", wfb[:, 8:16], lg8[:, 8:16], ctab[:, C_J:C_J + 1], ALU.mult, ["lg8", "ctab", "wfb"], ["wfb"])
        ACT(wfb, wfb, AF.Exp, ["wfb"], ["wfb"])
        TS("dve", wfb, wfb, 0.125, ALU.mult, ["wfb"], ["wfb"])
        GbT3 = v3(GbT, 4)
        for hp in range(4):
            ACT(GbT3[:, hp, :], c128, AF.Exp, ["c128", "lgP"], ["GT"], scale=lgP[:, 4 + hp:5 + hp])
        wo3 = v3(wo_fb, 2 * NOT_)
        for h in range(8):
            TS("dve", wo3[:, 0:NOT_, h], dist[:, 0:NOT_], lg8[:, h:h + 1], ALU.mult, ["dist", "lg8"], ["wo_fb"])
            TS("dve", wo3[:, NOT_:2 * NOT_, h], dist[:, NOT_:2 * NOT_], lg8[:, 8 + h:9 + h], ALU.mult, ["dist", "lg8"], ["wo_fb"])
        ACT(wo_fb, wo_fb, AF.Exp, ["wo_fb"], ["wo_fb"])
        TS("dve", wo_fb, wo_fb, 0.125, ALU.mult, ["wo_fb"], ["wo_fb"])

        for kc in range(8):
            LOAD("pool", w1[:, kc * W1_N:(kc + 1) * W1_N], w1_d[:, kc, :], ["w1_%d" % kc])
        for kc in range(3):
            LOAD("pool", wuq[:, kc * 768:(kc + 1) * 768], wuq_d[:, kc, :], ["wuq"])
        for kc in range(8):
            sl = slice(kc * W1_N, (kc + 1) * W1_N)
            TS("dve", w1[:, sl], w1[:, sl], vec[:, V_NMW + kc:V_NMW + kc + 1], ALU.mult, ["w1_%d" % kc, "vec"], ["w1_%d" % kc])
        for kc in range(3):
            sl = slice(kc * 768, (kc + 1) * 768)
            TS("dve", wuq[:, sl], wuq[:, sl], vec[:, V_QNW + kc:V_QNW + kc + 1], ALU.mult, ["wuq", "vec"], ["wuq"])
        W1T = ["w1_%d" % kc for kc in range(8)]
        MSET("pool", kaug, 0.0, ["kaug"])
        for i in range(2):
            MSET("pool", ckst[i], 0.0, ["ckst%d" % i])
            MSET("pool", krst[i], 0.0, ["krst%d" % i])
        gcount = 0

        seq = [("o", t) for t in range(NOT_)] + [("s", c) for c in range(NT - 1, -1, -1)]
        import os
        if os.environ.get("K_METAFIRST"):
            seq = [("o", NOT_ - 1)] + [("o", t) for t in range(NOT_ - 1)] + [("s", c) for c in range(NT - 1, -1, -1)]
        if os.environ.get("K_S1N"):
            seq = seq[:int(os.environ["K_S1N"])]
        def front1(it):
            kind, ti = seq[it]
            own = kind == "s"
            xsrc = xo[ti * 128:(ti + 1) * 128, :] if own else xr[ti * 128:(ti + 1) * 128, :]
            rsrc = ropeR_own_d[:, ti, :] if own else ropeR_oth_d[:, ti, :]
            xb, xbt = xbuf[it % NXB], "xbuf%d" % (it % NXB)
            rb, rbt = ropeb[it % NXB], "ropeb%d" % (it % NXB)
            xsb, xst = xs[it % 2], "xs%d" % (it % 2)
            u, ut = uT[it % 2], "uT%d" % (it % 2)
            DMA("sp", xb, xsrc, (), [xbt], xbt)
            DMA("sp", rb, rsrc, (), [rbt], rbt)
            ACT(junk, xb, AF.Square, [xbt], ["junk", "x_ss"], accum=st[:, 0:1])
            rs = rstd_own[:, ti:ti + 1] if own else st[:, 1:2]
            TS("dve", st[:, 0:1], st[:, 0:1], 1.0 / D, ALU.mult, ["x_ss"], ["x_ss"], s2=RMS_EPS, op1=ALU.add)
            TT("pool", rs, st[:, 0:1], mhalf[:, 0:1], ALU.pow, ["x_ss", "mhalf"], ["x_rstd"])
            ACT(xsb, xb, AF.Identity, [xbt, "x_rstd"], [xst], scale=rs)
            tb = bankb(0)
            for kc in range(8):
                TR(tb[:, kc * 128:(kc + 1) * 128], xsb[:, kc * 128:(kc + 1) * 128], identb, [xst, "identb"], [B(0)])
            CP("dve", u, tb[:, 0:1024], [B(0)], [ut])

        lat = [A.alloc(672, F32) for _ in range(2)]
        kT2 = [kT, A.alloc(512, BF16)]
        vtok2 = [vtok, A.alloc(1024, BF16)]
        qtok2 = [qtok, A.alloc(768, BF16)]
        gstate = {"gcount": 0}

        def proj1(it):
            kind, ti = seq[it]
            own = kind == "s"
            par = it % 2
            rb, rbt = ropeb[it % NXB], "ropeb%d" % (it % NXB)
            u, ut = uT[par], "uT%d" % par
            u3 = v3(u, 8)
            la, lat_t = lat[par], "lat%d" % par
            for kc in range(8):
                MM(bank(1)[:, 0:288], u3[:, kc, :], w13[:, kc, W1_CKV:W1_CKV + 288], kc == 0, kc == 7, [ut, W1T[kc]], [B(1)])
            if own:
                for kc in range(8):
                    MM(bank(2)[:, 0:384], u3[:, kc, :], w13[:, kc, W1_CQ:W1_CQ + 384], kc == 0, kc == 7, [ut, W1T[kc]], [B(2)])
            for hp in range(4):
                for kc in range(8):
                    MM(bank(3)[:, hp * 128:(hp + 1) * 128], w13[:, kc, W1_RK + hp * 128:W1_RK + (hp + 1) * 128], u3[:, kc, :],
                       kc == 0, kc == 7, [ut, W1T[kc]], [B(3)])
            CP("act", la[:, 0:288], bank(1)[:, 0:288], [B(1)], [lat_t])
            if own:
                CP("act", la[:, 288:672], bank(2)[:, 0:384], [B(2)], [lat_t])
            for hp in range(4):
                for kc in range(8):
                    MM(bank(4)[:, hp * 128:(hp + 1) * 128], w13[:, kc, W1_RKS + hp * 128:W1_RKS + (hp + 1) * 128], u3[:, kc, :],
                       kc == 0, kc == 7, [ut, W1T[kc]], [B(4)])
            cosR = rb[:, 0:128].unsqueeze(1).to_broadcast([128, 4, 128])
            sinR = rb[:, 128:256].unsqueeze(1).to_broadcast([128, 4, 128])
            TT("dve", v3(t1, 4), v3(bank(3), 4), cosR, ALU.mult, [B(3), rbt], ["t1"])
            for hf in range(2):
                for kc in range(8):
                    MM(bank(5 + hf), u3[:, kc, :], w13[:, kc, W1_RV + hf * 512:W1_RV + (hf + 1) * 512], kc == 0, kc == 7,
                       [ut, W1T[kc]], [B(5 + hf)])
            TT("dve", v3(t2, 4), v3(bank(4), 4), sinR, ALU.mult, [B(4), rbt], ["t2"])
            TT("pool", kT2[par], t1, t2, ALU.add, ["t1", "t2"], ["kT%d" % par])
            CP("act", vtok2[par][:, 0:512], bank(5), [B(5)], ["vtok%d" % par])
            CP("act", vtok2[par][:, 512:1024], bank(6), [B(6)], ["vtok%d" % par])

        def back1(it):
            kind, ti = seq[it]
            own = kind == "s"
            par = it % 2
            slot = ti if own else (32 + ti)
            tabM = (tabM_own if own else tabM_oth)[:, ti * 64:(ti + 1) * 64]
            tabMt = "tabM_own" if own else "tabM_oth"
            la, lat_t = lat[par], "lat%d" % par
            kTp, kTt = kT2[par], "kT%d" % par
            vtp, vtt = vtok2[par], "vtok%d" % par
            ACT(junk[:, 0:256], la[:, 0:256], AF.Square, [lat_t], ["junk", "kv_ss"], accum=st[:, 2:3])
            TS("dve", st[:, 2:3], st[:, 2:3], 1.0 / 256, ALU.mult, ["kv_ss"], ["kv_ss"], s2=RMS_EPS, op1=ALU.add)
            TT("pool", st[:, 3:4], st[:, 2:3], mhalf[:, 0:1], ALU.pow, ["kv_ss", "mhalf"], ["kv_rstd"])
            ACT(ckvn, la[:, 0:256], AF.Identity, [lat_t, "kv_rstd"], ["ckvn"], scale=st[:, 3:4])
            if own:
                ACT(junk[:, 0:384], la[:, 288:672], AF.Square, [lat_t], ["junk", "q_ss"], accum=st[:, 4:5])
                TS("dve", st[:, 4:5], st[:, 4:5], 1.0 / 384, ALU.mult, ["q_ss"], ["q_ss"], s2=RMS_EPS, op1=ALU.add)
                TT("pool", st[:, 5:6], st[:, 4:5], mhalf[:, 0:1], ALU.pow, ["q_ss", "mhalf"], ["q_rstd"])
                ACT(cqn, la[:, 288:672], AF.Identity, [lat_t, "q_rstd"], ["cqn"], scale=st[:, 5:6])
            TT("dve", ropeA, la[:, 256:288], tabM[:, 0:32], ALU.mult, [lat_t, tabMt], ["ropeA"])
            TT("dve", ropeBv, la[:, 256:288], tabM[:, 32:64], ALU.mult, [lat_t, tabMt], ["ropeB"])
            TT("pool", kaug[:, 64:80], ropeA[:, 0:16], ropeBv[:, 16:32], ALU.subtract, ["ropeA", "ropeB"], ["kaug"])
            TT("pool", kaug[:, 80:96], ropeBv[:, 0:16], ropeA[:, 16:32], ALU.add, ["ropeA", "ropeB"], ["kaug"])
            hb = bankb(7)
            for hp in range(4):
                TR(hb[:, 384 + hp * 128:384 + (hp + 1) * 128], kTp[:, hp * 128:(hp + 1) * 128], identb, [kTt, "identb"], [B(7)])
            for c2 in range(2):
                TR(hb[:, c2 * 128:(c2 + 1) * 128], ckvn[:, c2 * 128:(c2 + 1) * 128], identb, ["ckvn", "identb"], [B(7)])
            TR(hb[0:96, 256:384], kaug, identb, ["kaug", "identb"], [B(7)])
            if own:
                tb0 = bankb(0)
                for c3 in range(3):
                    TR(tb0[:, c3 * 128:(c3 + 1) * 128], cqn[:, c3 * 128:(c3 + 1) * 128], identb, ["cqn", "identb"], [B(0)])
                CP("dve", cqT, tb0[:, 0:384], [B(0)], ["cqT"])
            ktok3 = hb[:, 384:896].rearrange("p (h d) -> p h d", h=8)
            if own:
                wbb = wfb[:, 8:16].unsqueeze(2).to_broadcast([128, 8, 64])
                TT("dve", kwb.rearrange("p (h d) -> p h d", h=8), ktok3, wbb, ALU.mult, [B(7), "wfb"], ["kwb"])
                dirs = [("b", kwb, "kwb", 3)]
            else:
                wof = wo3[:, ti, :].unsqueeze(2).to_broadcast([128, 8, 64])
                wob = wo3[:, NOT_ + ti, :].unsqueeze(2).to_broadcast([128, 8, 64])
                TT("dve", kwf.rearrange("p (h d) -> p h d", h=8), ktok3, wof, ALU.mult, [B(7), "wo_fb"], ["kwf"])
                TT("dve", kwb.rearrange("p (h d) -> p h d", h=8), ktok3, wob, ALU.mult, [B(7), "wo_fb"], ["kwb"])
                dirs = [("f", kwf, "kwf", 1), ("b", kwb, "kwb", 3)]
            gb_i = gstate["gcount"] % 2
            cks, ckt = ckst[gb_i], "ckst%d" % gb_i
            krs, krt = krst[gb_i], "krst%d" % gb_i
            sp_ = slot % 4
            CP("dve", v3(cks, 2)[:, :, sp_ * 128:(sp_ + 1) * 128], hb[:, 0:256].rearrange("p (a b) -> p a b", a=2), [B(7)], [ckt])
            CP("dve", krs[64:96, sp_ * 128:(sp_ + 1) * 128], hb[64:96, 256:384], [B(7)], [krt])
            flush = (slot == 64) or (own and ti % 4 == 0) or ((not own) and slot < 64 and slot % 4 == 3)
            if flush:
                g0 = (slot // 4) * 512
                DMA("pool", ckvT_d.rearrange("p (a n) -> p a n", a=2)[:, :, g0:g0 + 512], v3(cks, 2)[:, :, 0:512], [ckt], [("ckvT_d", slot // 4)], ckt)
                DMA("pool", krope_d[:, g0:g0 + 512], krs[64:96, 0:512], [krt], [("krope_d", slot // 4)], krt)
                gstate["gcount"] += 1
            if own:
                cqT3 = v3(cqT, 3)
                for kc in range(3):
                    MM(bank(5)[:, 0:480], cqT3[:, kc, :], wuq3[:, kc, 0:480], kc == 0, kc == 2, ["cqT", "wuq"], [B(5)])
                for kc in range(3):
                    MM(bank(6)[:, 0:288], cqT3[:, kc, :], wuq3[:, kc, 480:768], kc == 0, kc == 2, ["cqT", "wuq"], [B(6)])
            for (dname, kw_, kwt, b0) in dirs:
                for hp in range(4):
                    bk = bank(b0 + hp // 2)
                    MM(bk[:, (hp % 2) * 256:(hp % 2) * 256 + 256], kw_[:, hp * 128:(hp + 1) * 128], vtp[:, hp * 256:(hp + 1) * 256],
                       True, True, [kwt, vtt], [B(b0 + hp // 2)])
            Sb3 = v3(Sb, 4)
            Sf3 = v3(Sf, 4)
            if own:
                sbs, sbt = sbst[ti % 2], "sbst%d" % (ti % 2)
                CP("pool", sbs, Sb, ["Sb"], [sbt])
                DMA("pool", SbAll_d[ti, :, :], sbs, [sbt], [("SbAll", ti)], sbt)
                TT("pool", Sb, Sb, GbT, ALU.mult, ["Sb", "GT", sbt], ["Sb"])
            for (dname, kw_, kwt, b0) in dirs:
                Sx3 = Sb3 if dname == "b" else Sf3
                Sxt = "Sb" if dname == "b" else "Sf"
                for hh in range(2):
                    bk = v3(bank(b0 + hh), 2)
                    TT("dve", Sx3[0:64, 2 * hh:2 * hh + 2, :], Sx3[0:64, 2 * hh:2 * hh + 2, :], bk[0:64, :, 0:128], ALU.add,
                       [Sxt, B(b0 + hh)], [Sxt])
                    TT("dve", Sx3[64:128, 2 * hh:2 * hh + 2, :], Sx3[64:128, 2 * hh:2 * hh + 2, :], bk[64:128, :, 128:256], ALU.add,
                       [Sxt, B(b0 + hh)], [Sxt])
            if own:
                qtk, qtt = qtok2[par], "qtok%d" % par
                q3 = qtk.rearrange("p (h c) -> p h c", h=8)
                for (bk_i, h0, nh) in ((5, 0, 5), (6, 5, 3)):
                    src = bank(bk_i)[:, 0:nh * 96].rearrange("p (h c) -> p h c", h=nh)
                    CP("act", q3[:, h0:h0 + nh, 0:64], src[:, :, 0:64], [B(bk_i)], [qtt])
                    ccq = tabM[:, 0:32].unsqueeze(1).to_broadcast([128, nh, 32])
                    ssq = tabM[:, 32:64].unsqueeze(1).to_broadcast([128, nh, 32])
                    qA3 = qA[:, h0 * 32:(h0 + nh) * 32].rearrange("p (h c) -> p h c", h=nh)
                    qB3 = qB[:, h0 * 32:(h0 + nh) * 32].rearrange("p (h c) -> p h c", h=nh)
                    TT("dve", qA3, src[:, :, 64:96], ccq, ALU.mult, [B(bk_i), tabMt], ["qA"])
                    TT("dve", qB3, src[:, :, 64:96], ssq, ALU.mult, [B(bk_i), tabMt], ["qB"])
                qA3 = qA.rearrange("p (h c) -> p h c", h=8)
                qB3 = qB.rearrange("p (h c) -> p h c", h=8)
                TT("pool", q3[:, :, 64:80], qA3[:, :, 0:16], qB3[:, :, 16:32], ALU.subtract, ["qA", "qB"], [qtt])
                TT("pool", q3[:, :, 80:96], qB3[:, :, 0:16], qA3[:, :, 16:32], ALU.add, ["qA", "qB"], [qtt])

        def back2(it):
            kind, ti = seq[it]
            if kind != "s":
                return
            par = it % 2
            qtk, qtt = qtok2[par], "qtok%d" % par
            q3 = qtk.rearrange("p (h c) -> p h c", h=8)
            tb2 = bankb(2)
            for h in range(8):
                TR(tb2[0:96, h * 128:(h + 1) * 128], q3[:, h, :], identb, [qtt, "identb"], [B(2)])
            g = ti // 4
            qs = Qst[g % 2]
            qst = "Qst%d" % (g % 2)
            qs3 = v3(qs, 8)
            CP("dve", qs3[0:96, :, (ti % 4) * 128:(ti % 4 + 1) * 128], tb2[0:96, 0:1024].rearrange("p (h t) -> p h t", h=8),
               [B(2)], [qst])
            if ti % 4 == 0:
                DMA("pool", Qs_d[:, :, g * 512:(g + 1) * 512].rearrange("h d t -> d h t"), qs3[0:96, :, :], [qst], [("Qs", g)], qst)

        n1 = len(seq)
        if n1:
            front1(0)
        for it in range(n1):
            proj1(it)
            if it + 1 < n1:
                front1(it + 1)
            if it >= 1:
                back1(it - 1)
            if it >= 2:
                back2(it - 2)
        if n1:
            back1(n1 - 1)
            if n1 >= 2:
                back2(n1 - 2)
            back2(n1 - 1)
        S.barrier()

    if stages >= 2:
        A.pos = P1_END
        w3 = A.alloc(8 * W3_N, BF16)
        wro = A.alloc(8 * 1024, BF16)
        w33 = v3(w3, 8)
        wro3 = v3(wro, 8)
        DT = A.alloc(1024, F32)
        wfb = A.alloc(16, F32)
        wqfd = A.alloc(1024, F32)
        wqbd = A.alloc(1024, F32)
        GfT = A.alloc(512, F32)
        Sf_bf = A.alloc(512, BF16)
        NXB = 2
        xbuf = [A.alloc(D, F32) for _ in range(NXB)]
        ropeb = [A.alloc(256, F32) for _ in range(NXB)]
        sbl = [A.alloc(512, BF16) for _ in range(2)]
        xs = [A.alloc(D, BF16) for _ in range(2)]
        uT = [A.alloc(D, BF16) for _ in range(2)]
        t1 = A.alloc(512, F32)
        t2 = A.alloc(512, F32)
        qTd = A.alloc(1024, BF16)
        kT = A.alloc(512, BF16)
        qfd = A.alloc(1024, BF16)
        qbd = A.alloc(1024, BF16)
        kwf = A.alloc(512, BF16)
        vtok = A.alloc(1024, BF16)
        sig = A.alloc(1024, F32)
        silu = sig
        sgr = A.alloc(1024, F32)
        PT = A.alloc(1024, BF16)
        sq = A.alloc(1024, F32)
        gst = A.alloc(64, F32)
        zn = sq
        z = A.alloc(1024, BF16)
        zT = A.alloc(1024, BF16)
        m1b = [A.alloc(1024, BF16) for _ in range(2)]
        DT3 = v3(DT, 8)
        for h in range(8):
            TS("dve", DT3[:, h, :], ctab[:, C_POS:C_POS + 128], lg8[:, h:h + 1], ALU.mult, ["ctab", "lg8"], ["DT"])
            STT(DT3[:, h, :], ctab[:, C_NEG:C_NEG + 128], lg8[:, 8 + h:9 + h], DT3[:, h, :], ALU.mult, ALU.add,
                ["ctab", "lg8", "DT"], ["DT"])
        ACT(DT, DT, AF.Exp, ["DT"], ["DT"])
        TS("dve", DT, DT, 0.125, ALU.mult, ["DT"], ["DT"])
        TS("dve", wfb[:, 0:8], lg8[:, 0:8], ctab[:, C_127MJ:C_127MJ + 1], ALU.mult, ["lg8", "ctab"], ["wfb"])
        TS("dve", wfb[:, 8:16], lg8[:, 8:16], ctab[:, C_J:C_J + 1], ALU.mult, ["lg8", "ctab", "wfb"], ["wfb"])
        ACT(wfb, wfb, AF.Exp, ["wfb"], ["wfb"])
        TS("dve", wfb, wfb, 0.125, ALU.mult, ["wfb"], ["wfb"])
        wqfd3, wqbd3, GfT3 = v3(wqfd, 4), v3(wqbd, 4), v3(GfT, 4)
        MSET("pool", wqfd, 0.0, ["wq"])
        MSET("pool", wqbd, 0.0, ["wq"])
        MSET("pool", qTd, 0.0, ["qT"])
        for hp in range(4):
            for par in range(2):
                rs_ = slice(par * 64, (par + 1) * 64)
                cs_ = slice(par * 128, (par + 1) * 128)
                ACT(wqfd3[rs_, hp, cs_], ctab[rs_, C_I1:C_I1 + 128], AF.Exp, ["ctab", "lgP", "wq"], ["wq"], scale=lgP[rs_, hp:hp + 1])
                ACT(wqbd3[rs_, hp, cs_], ctab[rs_, C_128MI:C_128MI + 128], AF.Exp, ["ctab", "lgP", "wq"], ["wq"], scale=lgP[rs_, 4 + hp:5 + hp])
            ACT(GfT3[:, hp, :], c128, AF.Exp, ["c128", "lgP"], ["GT"], scale=lgP[:, hp:hp + 1])
        for kc in range(8):
            LOAD("pool", w3[:, kc * W3_N:(kc + 1) * W3_N], w3a_d[:, kc, :], ["w3_%d" % kc])
            LOAD("pool", wro[:, kc * 1024:(kc + 1) * 1024], wro_d[:, kc, :], ["wro_%d" % kc])
        for kc in range(8):
            sl = slice(kc * W3_N, (kc + 1) * W3_N)
            TS("dve", w3[:, sl], w3[:, sl], vec[:, V_NMW + kc:V_NMW + kc + 1], ALU.mult, ["w3_%d" % kc, "vec"], ["w3_%d" % kc])
            sl = slice(kc * 1024, (kc + 1) * 1024)
            TS("dve", wro[:, sl], wro[:, sl], vec[:, V_GNW + kc:V_GNW + kc + 1], ALU.mult, ["wro_%d" % kc, "vec"], ["wro_%d" % kc])
        W3T = ["w3_%d" % kc for kc in range(8)]
        WRT = ["wro_%d" % kc for kc in range(8)]
        CP("pool", Sf_bf, Sf, ["Sf"], ["Sf_bf"])
        Sf3 = v3(Sf, 4)
        Sfb3 = v3(Sf_bf, 4)
        PT3 = v3(PT, 8)
        N3A = int(os.environ.get("K_S3N", NT))

        def front3a(c):
            xb, xbt = xbuf[c % NXB], "xbuf%d" % (c % NXB)
            rb, rbt = ropeb[c % NXB], "ropeb%d" % (c % NXB)
            xsb, xst = xs[c % 2], "xs%d" % (c % 2)
            u, ut = uT[c % 2], "uT%d" % (c % 2)
            DMA("sp", xb, xo[c * 128:(c + 1) * 128, :], (), [xbt], xbt)
            DMA("sp", rb, ropeR_own_d[:, c, :], (), [rbt], rbt)
            DMA("sp", sbl[c % 2], SbAll_d[c, :, :], [("SbAll", c)], ["sbl%d" % (c % 2)], "sbl%d" % (c % 2))
            ACT(xsb, xb, AF.Identity, [xbt], [xst], scale=rstd_own[:, c:c + 1])
            tb = bankb(0)
            for kc in range(8):
                TR(tb[:, kc * 128:(kc + 1) * 128], xsb[:, kc * 128:(kc + 1) * 128], identb, [xst, "identb"], [B(0)])
            CP("dve", u, tb[:, 0:1024], [B(0)], [ut])

        def tail3a(c):
            tb7 = bankb(7)
            for kc in range(8):
                TR(tb7[:, kc * 128:(kc + 1) * 128], z[:, kc * 128:(kc + 1) * 128], identb, ["z", "identb"], [B(7)])
            CP("dve", zT, tb7[:, 0:1024], [B(7)], ["zT"])
            zT3 = v3(zT, 8)
            for hf in range(2):
                for kc in range(8):
                    MM(bank(5 + hf), zT3[:, kc, :], wro3[:, kc, hf * 512:(hf + 1) * 512], kc == 0, kc == 7, ["zT", WRT[kc]], [B(5 + hf)])
            mb = m1b[c % 2]
            mbt = "m1b%d" % (c % 2)
            for hf in range(2):
                TT("dve", mb[:, hf * 512:(hf + 1) * 512], bank(5 + hf), sgr[:, hf * 512:(hf + 1) * 512], ALU.mult, [B(5 + hf), "sgr"], [mbt])
            DMA("pool", m1_d[c * 128:(c + 1) * 128, :], mb, [mbt], [("m1", c)], mbt)

        if N3A:
            front3a(0)
        for c in range(N3A):
            rb = ropeb[c % NXB]
            rbt = "ropeb%d" % (c % NXB)
            u = uT[c % 2]
            ut = "uT%d" % (c % 2)
            u3 = v3(u, 8)
            sbc, sbct = sbl[c % 2], "sbl%d" % (c % 2)
            sbc3 = v3(sbc, 4)
            for (bk_i, c0) in ((1, W3_RQ), (2, W3_RQS), (3, W3_RK), (4, W3_RKS)):
                for hp in range(4):
                    for kc in range(8):
                        MM(bank(bk_i)[:, hp * 128:(hp + 1) * 128], w33[:, kc, c0 + hp * 128:c0 + (hp + 1) * 128], u3[:, kc, :],
                           kc == 0, kc == 7, [ut, W3T[kc]], [B(bk_i)])
            for hf in range(2):
                for kc in range(8):
                    MM(bank(5 + hf), u3[:, kc, :], w33[:, kc, W3_RV + hf * 512:W3_RV + (hf + 1) * 512], kc == 0, kc == 7,
                       [ut, W3T[kc]], [B(5 + hf)])
            cosR = rb[:, 0:128].unsqueeze(1).to_broadcast([128, 4, 128])
            sinR = rb[:, 128:256].unsqueeze(1).to_broadcast([128, 4, 128])
            TT("dve", v3(t1, 4), v3(bank(1), 4), cosR, ALU.mult, [B(1), rbt], ["t1"])
            TT("dve", v3(t2, 4), v3(bank(2), 4), sinR, ALU.mult, [B(2), rbt], ["t2"])
            qTd3 = v3(qTd, 4)
            TT("pool", qTd3[0:64, :, 0:128], v3(t1, 4)[0:64, :, :], v3(t2, 4)[0:64, :, :], ALU.add, ["t1", "t2", "qT"], ["qT"])
            TT("pool", qTd3[64:128, :, 128:256], v3(t1, 4)[64:128, :, :], v3(t2, 4)[64:128, :, :], ALU.add, ["t1", "t2", "qT"], ["qT"])
            TT("dve", v3(t1, 4), v3(bank(3), 4), cosR, ALU.mult, [B(3), rbt, "qT"], ["t1"])
            TT("dve", v3(t2, 4), v3(bank(4), 4), sinR, ALU.mult, [B(4), rbt, "qT"], ["t2"])
            TT("pool", kT, t1, t2, ALU.add, ["t1", "t2"], ["kT"])
            TT("pool", qfd, qTd, wqfd, ALU.mult, ["qT", "wq"], ["qf"])
            TT("pool", qbd, qTd, wqbd, ALU.mult, ["qT", "wq"], ["qb"])
            CP("act", vtok[:, 0:512], bank(5), [B(5)], ["vtok"])
            CP("act", vtok[:, 512:1024], bank(6), [B(6)], ["vtok"])
            if c >= 1:
                tail3a(c - 1)
            for hp in range(4):
                MM(bank(1 + hp // 2)[:, (hp % 2) * 256:(hp % 2) * 256 + 256], kT[:, hp * 128:(hp + 1) * 128], qTd3[:, hp, :],
                   True, True, ["kT", "qT"], [B(1 + hp // 2)])
            hb = bankb(7)
            for hp in range(4):
                TR(hb[:, hp * 128:(hp + 1) * 128], kT[:, hp * 128:(hp + 1) * 128], identb, ["kT", "identb"], [B(7)])
            for hf in range(2):
                for kc in range(8):
                    MM(bank(3 + hf), u3[:, kc, :], w33[:, kc, W3_RG + hf * 512:W3_RG + (hf + 1) * 512], kc == 0, kc == 7,
                       [ut, W3T[kc]], [B(3 + hf)])
            for hf in range(2):
                TT("dve", PT[:, hf * 512:(hf + 1) * 512], bank(1 + hf), DT[:, hf * 512:(hf + 1) * 512], ALU.mult, [B(1 + hf), "DT"], ["PT"])
            wfbb = wfb[:, 0:8].unsqueeze(2).to_broadcast([128, 8, 64])
            TT("dve", kwf.rearrange("p (h d) -> p h d", h=8), hb[:, 0:512].rearrange("p (h d) -> p h d", h=8), wfbb, ALU.mult,
               [B(7), "wfb"], ["kwf"])
            for hf in range(2):
                sl = slice(hf * 512, (hf + 1) * 512)
                ACT(sig[:, sl], bank(3 + hf), AF.Sigmoid, [B(3 + hf)], ["sig"])
                TT("dve", silu[:, sl], bank(3 + hf), sig[:, sl], ALU.mult, [B(3 + hf), "sig"], ["sig"])
            for h in range(8):
                hp, hb0 = h // 2, (h % 2) * 64
                o_ap = bank(5 + h // 4)[:, (h % 4) * 128:(h % 4 + 1) * 128]
                MM(o_ap, PT3[:, h, :], vtok[:, h * 128:(h + 1) * 128], True, False, ["PT", "vtok"], [B(5 + h // 4)])
                pc = slice((h % 2) * 128, (h % 2) * 128 + 128)
                MM(o_ap, v3(qfd, 4)[:, hp, pc], Sfb3[:, hp, :], False, False, ["qf", "Sf_bf"], [B(5 + h // 4)])
                MM(o_ap, v3(qbd, 4)[:, hp, pc], sbc3[:, hp, :], False, True, ["qb", sbct], [B(5 + h // 4)])
            for hp in range(4):
                bk = bank(1 + hp // 2)
                MM(bk[:, (hp % 2) * 256:(hp % 2) * 256 + 256], kwf[:, hp * 128:(hp + 1) * 128], vtok[:, hp * 256:(hp + 1) * 256],
                   True, True, ["kwf", "vtok"], [B(1 + hp // 2)])
            TT("pool", Sf, Sf, GfT, ALU.mult, ["Sf", "GT", "Sf_bf"], ["Sf"])
            for hh in range(2):
                bk = v3(bank(1 + hh), 2)
                TT("dve", Sf3[0:64, 2 * hh:2 * hh + 2, :], Sf3[0:64, 2 * hh:2 * hh + 2, :], bk[0:64, :, 0:128], ALU.add, ["Sf", B(1 + hh)], ["Sf"])
                TT("dve", Sf3[64:128, 2 * hh:2 * hh + 2, :], Sf3[64:128, 2 * hh:2 * hh + 2, :], bk[64:128, :, 128:256], ALU.add, ["Sf", B(1 + hh)], ["Sf"])
            CP("pool", Sf_bf, Sf, ["Sf"], ["Sf_bf"])
            for hf in range(2):
                for kc in range(8):
                    MM(bank(3 + hf), u3[:, kc, :], w33[:, kc, W3_GR + hf * 512:W3_GR + (hf + 1) * 512], kc == 0, kc == 7,
                       [ut, W3T[kc]], [B(3 + hf)])
            if c + 1 < N3A:
                front3a(c + 1)
            for hf in range(2):
                RED(gst[:, hf * 4:(hf + 1) * 4], v3(bank(5 + hf), 4), [B(5 + hf)], ["gs1"])
                ACT(sq[:, hf * 512:(hf + 1) * 512], bank(5 + hf), AF.Square, [B(5 + hf)], ["sq"])
            RED(gst[:, 8:16], v3(sq, 8), ["sq"], ["gs2"])
            TS("dve", gst[:, 16:24], gst[:, 0:8], 1.0 / 128, ALU.mult, ["gs1"], ["gmean"])
            TT("dve", gst[:, 48:56], gst[:, 16:24], gst[:, 16:24], ALU.mult, ["gmean"], ["gmsq"])
            TS("dve", gst[:, 24:32], gst[:, 8:16], 1.0 / 128, ALU.mult, ["gs2"], ["gvar"], s2=GN_EPS, op1=ALU.add)
            TT("dve", gst[:, 24:32], gst[:, 24:32], gst[:, 48:56], ALU.subtract, ["gvar", "gmsq"], ["gvar"])
            TT("pool", gst[:, 32:40], gst[:, 24:32], mhalf[:, 0:8], ALU.pow, ["gvar", "mhalf"], ["grstd"])
            TT("pool", gst[:, 40:48], gst[:, 16:24], gst[:, 32:40], ALU.mult, ["gmean", "grstd"], ["gnmr"])
            TS("pool", gst[:, 40:48], gst[:, 40:48], -1.0, ALU.mult, ["gnmr"], ["gnmr"], s2=1.0, op1=ALU.mult)
            for h in range(8):
                ACT(zn[:, h * 128:(h + 1) * 128], bank(5 + h // 4)[:, (h % 4) * 128:(h % 4 + 1) * 128], AF.Identity,
                    [B(5 + h // 4), "grstd", "gnmr", "gs2"], ["sq"], scale=gst[:, 32 + h:33 + h], bias=gst[:, 40 + h:41 + h])
            TT("dve", z, zn, silu, ALU.mult, ["sq", "sig"], ["z"])
            for hf in range(2):
                ACT(sgr[:, hf * 512:(hf + 1) * 512], bank(3 + hf), AF.Sigmoid, [B(3 + hf)], ["sgr"])
        if N3A:
            tail3a(N3A - 1)
        S.barrier()

    if stages >= 3:
        A.pos = P0_END
        Ymla = A.alloc(NT * 512, BF16)
        ckvT = A.alloc(2 * NKEY, BF16)
        ckvT3 = v3(ckvT, 2)
        Kb = [A.alloc(NKEY, BF16), A.alloc(NKEY, BF16)]
        wuk = A.alloc(2 * 512, BF16)
        wuv = A.alloc(2 * 512, BF16)
        Vb = [A.alloc(NSLOT * 128, BF16) for _ in range(2)]
        Qb = [A.alloc(TOK, BF16) for _ in range(2)]
        Pb = [A.alloc(1024, BF16) for _ in range(2)]
        oT = A.alloc(512, F32)
        rcp = A.alloc(4, F32)
        CKD = [("ckvT_d", g) for g in range(17)]
        KRD = [("krope_d", g) for g in range(17)]
        for kc in range(2):
            DMA("sp", ckvT[:, kc * NKEY:(kc + 1) * NKEY], ckvT_d[:, kc * NKEYP:kc * NKEYP + NKEY], CKD, ["ckvT%d" % kc], "ckvT%d" % kc)
        for i in range(2):
            DMA("sp", Kb[i][64:96, :], krope_d[:, 0:NKEY], KRD, [("Kr", i)], "Kr%d" % i)
            Vi3 = v3(Vb[i], NSLOT)
            MSET("pool", Vb[i], 0.0, [("V", i)])
            MSET("pool", Vi3[:, :, 64:65], 1.0, [("V", i)])
        for kc in range(2):
            LOAD("pool", wuk[:, kc * 512:(kc + 1) * 512], wuk_d[:, kc, :], ["wuk"])
            LOAD("pool", wuv[:, kc * 512:(kc + 1) * 512], wuv_d[:, kc, :], ["wuv"])
        for kc in range(2):
            sl = slice(kc * 512, (kc + 1) * 512)
            TS("dve", wuk[:, sl], wuk[:, sl], vec[:, V_KVNW + kc:V_KVNW + kc + 1], ALU.mult, ["wuk", "vec"], ["wuk"])
            TS("dve", wuv[:, sl], wuv[:, sl], vec[:, V_KVNW + kc:V_KVNW + kc + 1], ALU.mult, ["wuv", "vec"], ["wuv"])
        Ym3 = v3(Ymla, NT)

        def gen_groups(h):
            hbuf = h % 2
            Kh, Vh, Qh = Kb[hbuf], Vb[hbuf], Qb[hbuf]
            Kt_, Vt_, Qt_ = ("K", hbuf), ("V", hbuf), ("Q", hbuf)
            Vh3 = v3(Vh, NSLOT)
            out = []

            def qload():
                DMA("sp", Qh[0:96, :], Qs_d[h, :, :], [("Qs", g) for g in range(8)], [Qt_], "Qb%d" % hbuf)
            out.append(qload)
            for n in range(17):
                def kgen(n=n):
                    n0 = n * 512
                    w = 512 if n < 16 else 128
                    for kc in range(2):
                        MM(bank(7)[0:64, 0:w], wuk[:, kc * 512 + h * 64:kc * 512 + (h + 1) * 64], ckvT3[:, kc, n0:n0 + w],
                           kc == 0, kc == 1, ["wuk", "ckvT%d" % kc], [B(7)])
                    CP("dve", Kh[0:64, n0:n0 + w], bank(7)[0:64, 0:w], [B(7)], [Kt_])
                out.append(kgen)
            for g0 in range(0, NSLOT, 8):
                def vgen(g0=g0):
                    ng = min(8, NSLOT - g0)
                    for j in range(ng):
                        s_ = g0 + j
                        for kc in range(2):
                            MM(bank(7)[:, j * 64:(j + 1) * 64], ckvT3[:, kc, s_ * 128:(s_ + 1) * 128],
                               wuv[:, kc * 512 + h * 64:kc * 512 + (h + 1) * 64], kc == 0, kc == 1, ["wuv", "ckvT%d" % kc], [B(7)])
                    CP("dve", Vh3[:, g0:g0 + ng, 0:64], bank(7)[:, 0:ng * 64].rearrange("p (a b) -> p a b", a=ng), [B(7)], [Vt_])
                out.append(vgen)
            return out

        for g_ in gen_groups(0):
            g_()
        units = []
        for qb in range(8):
            for s0 in range(0, 64, 2):
                units.append((qb, (s0, s0 + 1)))
            units.append((qb, (64,)))
        nu = len(units)
        it = 0
        for h in range(8):
            hbuf = h % 2
            Kh, Vh, Qh = Kb[hbuf], Vb[hbuf], Qb[hbuf]
            Kt_, Vt_, Qt_ = ("K", hbuf), ("V", hbuf), ("Q", hbuf)
            Vh3 = v3(Vh, NSLOT)
            pending = gen_groups(h + 1) if h < 7 else []
            stride = max(1, nu // (len(pending) + 1)) if pending else nu

            def s_mm(i):
                qb, sl = units[i]
                pb_i = (it + i) % 2
                pst = PS2[pb_i]
                for j, s_ in enumerate(sl):
                    kt = 128 if s_ < 64 else NMETA
                    MM(pst[0:kt, j * 512:(j + 1) * 512], Kh[0:96, s_ * 128:s_ * 128 + kt], Qh[0:96, qb * 512:(qb + 1) * 512], True, True,
                       [Kt_, Qt_, ("Kr", hbuf)], [B(2 * pb_i + j)])
                kt = 128 if sl[0] < 64 else NMETA
                if os.environ.get("K_WIDEEXP"):
                    w = 512 * len(sl)
                    ACT(Pb[pb_i][0:kt, 0:w], pst[0:kt, 0:w], AF.Exp, [B(2 * pb_i + j) for j in range(len(sl))], ["Pb%d" % pb_i], scale=SC_ATT)
                else:
                    for j in range(len(sl)):
                        ACT(Pb[pb_i][0:kt, j * 512:(j + 1) * 512], pst[0:kt, j * 512:(j + 1) * 512], AF.Exp, [B(2 * pb_i + j)],
                            [("Pb", pb_i, j)], scale=SC_ATT)

            s_mm(0)
            deferred = []
            for i in range(nu):
                if i + 1 < nu:
                    s_mm(i + 1)
                qb, sl = units[i]
                pb_i = (it + i) % 2
                ob = 4 + qb % 2
                for j, s_ in enumerate(sl):
                    kt = 128 if s_ < 64 else NMETA
                    MM(bank(ob)[:, :], Vh3[0:kt, s_, :], Pb[pb_i][0:kt, j * 512:(j + 1) * 512], s_ == 0, s_ == NSLOT - 1,
                       [Vt_, "Pb%d" % pb_i, ("Pb", pb_i, j)], [B(ob)])
                if pending and i % stride == stride - 1:
                    pending.pop(0)()
                if deferred and deferred[0][0] <= i:
                    deferred.pop(0)[1]()
                if sl[-1] == NSLOT - 1:
                    CP("dve", oT[0:65, :], bank(ob)[0:65, :], [B(ob)], ["oT"])

                    def norm(qb=qb):
                        for qi in range(4):
                            TR(bank(6)[:, qi * 65:(qi + 1) * 65], oT[0:65, qi * 128:(qi + 1) * 128], identf[0:65, 0:65], ["oT", "identf"], [B(6)])
                        pn = bank(6)[:, 0:260].rearrange("p (a b) -> p a b", a=4)
                        S.op("dve", lambda e, o_=rcp[:, 0:4], i_=pn[:, :, 64]: e.reciprocal(out=o_, in_=i_), [B(6)], ["rcp"])
                        TT("dve", Ym3[:, qb * 4:(qb + 1) * 4, h * 64:(h + 1) * 64], pn[:, :, 0:64],
                           rcp[:, 0:4].unsqueeze(2).to_broadcast([128, 4, 64]), ALU.mult, [B(6), "rcp"], [("Ymla", qb)])
                    deferred.append((i + 3, norm))
            while deferred:
                deferred.pop(0)[1]()
            while pending:
                pending.pop(0)()
            it += nu
        if dbg:
            ydbg = nc.dram_tensor("ymla_dbg", [128, NT * 512], BF16, kind="ExternalOutput").ap()
            DMA("sp", ydbg, Ymla, [("Ymla", q_) for q_ in range(8)], ["ydbg"], "ydbg")
        S.barrier()

    def alloc3c():
        A.pos = P0_END
        L = {}
        L["nfin"] = A.alloc(D, F32)
        L["hbuf"] = [A.alloc(D, F32) for _ in range(2)]
        L["hs"] = [A.alloc(D, BF16) for _ in range(2)]
        L["uT2"] = [A.alloc(8 * 512, BF16) for _ in range(2)]
        L["actT"] = A.alloc(NFC * 512, BF16)
        L["sgb"] = [A.alloc(512, F32) for _ in range(2)]
        L["h2"] = A.alloc(D, F32)
        L["ob"] = [A.alloc(D, F32) for _ in range(2)]
        L["st"] = A.alloc(8, F32)
        L["wg0"] = A.pos
        L["wg"] = A.alloc(8 * DFF, BF16)
        L["wu0"] = A.pos
        L["wu"] = A.alloc(8 * DFF, BF16)
        L["wd0"] = A.pos
        L["wd"] = A.alloc(NFC * 1024, BF16)
        return L

    pre3c = set()

    def ffn_weight_loaders(L, min_col):
        out = []
        wg, wu, wd = L["wg"], L["wu"], L["wd"]
        for kc in range(8):
            if ("wg", kc) not in pre3c and L["wg0"] + kc * DFF >= min_col:
                def f(kc=kc):
                    sl = slice(kc * DFF, (kc + 1) * DFF)
                    LOAD("pool", wg[:, sl], wg_d[:, kc, :], ["wg_%d" % kc])
                    TS("dve", wg[:, sl], wg[:, sl], vec[:, V_NFW + kc:V_NFW + kc + 1], ALU.mult, ["wg_%d" % kc, "vec"], ["wg_%d" % kc])
                pre3c.add(("wg", kc))
                out.append(f)
            if ("wu", kc) not in pre3c and L["wu0"] + kc * DFF >= min_col:
                def f(kc=kc):
                    sl = slice(kc * DFF, (kc + 1) * DFF)
                    LOAD("pool", wu[:, sl], wu_d[:, kc, :], ["wu_%d" % kc])
                    TS("dve", wu[:, sl], wu[:, sl], vec[:, V_NFW + kc:V_NFW + kc + 1], ALU.mult, ["wu_%d" % kc, "vec"], ["wu_%d" % kc])
                pre3c.add(("wu", kc))
                out.append(f)
        for fc in range(NFC):
            if ("wd", fc) not in pre3c and L["wd0"] + fc * 1024 >= min_col:
                def f(fc=fc):
                    LOAD("pool", wd[:, fc * 1024:(fc + 1) * 1024], wd_d[:, fc, :], ["wd_%d" % fc])
                pre3c.add(("wd", fc))
                out.append(f)
        return out

    if stages >= 4:
        A.pos = P0_END
        Ymla = A.alloc(NT * 512, BF16)
        Ym3 = v3(Ymla, NT)
        wgm = A.alloc(8 * 1024, BF16)
        wmo = A.alloc(4 * 1024, BF16)
        wo = A.alloc(8 * 1024, BF16)
        wgm3, wmo3, wo3_ = v3(wgm, 8), v3(wmo, 4), v3(wo, 8)
        xbuf = [A.alloc(D, F32) for _ in range(2)]
        m1l = [A.alloc(D, BF16) for _ in range(2)]
        xs = [A.alloc(D, BF16) for _ in range(2)]
        uT = [A.alloc(D, BF16) for _ in range(2)]
        sg = A.alloc(1024, F32)
        ymT = A.alloc(512, BF16)
        mg = A.alloc(1024, F32)
        mg2 = A.alloc(1024, BF16)
        mT = A.alloc(1024, BF16)
        h1b = [A.alloc(D, F32) for _ in range(2)]
        for kc in range(8):
            LOAD("pool", wgm[:, kc * 1024:(kc + 1) * 1024], wgm_d[:, kc, :], ["wgm_%d" % kc])
            LOAD("pool", wo[:, kc * 1024:(kc + 1) * 1024], wo_d[:, kc, :], ["wo_%d" % kc])
        for kc in range(4):
            LOAD("pool", wmo[:, kc * 1024:(kc + 1) * 1024], wmo_d[:, kc, :], ["wmo"])
        for kc in range(8):
            sl = slice(kc * 1024, (kc + 1) * 1024)
            TS("dve", wgm[:, sl], wgm[:, sl], vec[:, V_NMW + kc:V_NMW + kc + 1], ALU.mult, ["wgm_%d" % kc, "vec"], ["wgm_%d" % kc])
        xbuf = xbuf + [A.alloc(D, F32)]

        def front3b(c):
            xb, xbt = xbuf[c % 3], "xbuf%d" % (c % 3)
            ml, mlt = m1l[c % 2], "m1l%d" % (c % 2)
            xsb, xst = xs[c % 2], "xs%d" % (c % 2)
            u, ut = uT[c % 2], "uT%d" % (c % 2)
            DMA("sp", xb, xo[c * 128:(c + 1) * 128, :], (), [xbt], xbt)
            DMA("sp", ml, m1_d[c * 128:(c + 1) * 128, :], [("m1", c)], [mlt], mlt)
            ACT(xsb, xb, AF.Identity, [xbt], [xst], scale=rstd_own[:, c:c + 1])
            tb = bankb(0)
            for kc in range(8):
                TR(tb[:, kc * 128:(kc + 1) * 128], xsb[:, kc * 128:(kc + 1) * 128], identb, [xst, "identb"], [B(0)])
            CP("dve", u, tb[:, 0:1024], [B(0)], [ut])

        def tailT3b(c):
            tb6 = bankb(6)
            for kc in range(8):
                TR(tb6[:, kc * 128:(kc + 1) * 128], mg2[:, kc * 128:(kc + 1) * 128], identb, ["mg2", "identb"], [B(6)])
            CP("dve", mT, tb6[:, 0:1024], [B(6)], ["mT"])

        def tailO3b(c):
            xb, xbt = xbuf[c % 3], "xbuf%d" % (c % 3)
            hb_, hbt = h1b[c % 2], "h1b%d" % (c % 2)
            mT3 = v3(mT, 8)
            for hf in range(2):
                for kc in range(8):
                    MM(bank(3 + hf), mT3[:, kc, :], wo3_[:, kc, hf * 512:(hf + 1) * 512], kc == 0, kc == 7, ["mT", "wo_%d" % kc], [B(3 + hf)])
            for hf in range(2):
                sl = slice(hf * 512, (hf + 1) * 512)
                TT("dve", hb_[:, sl], bank(3 + hf), xb[:, sl], ALU.add, [B(3 + hf), xbt], [hbt])
            DMA("pool", h1_d[c * 128:(c + 1) * 128, :], hb_, [hbt], [("h1", c)], hbt)

        prefetch = []
        if stages >= 5:
            end3b = A.pos
            prefetch = ffn_weight_loaders(alloc3c(), end3b)
            A.pos = end3b
        front3b(0)
        YB = (5, 7)
        for c in range(NT):
            for _ in range(2):
                if prefetch and c >= 1:
                    prefetch.pop(0)()
            ml, mlt = m1l[c % 2], "m1l%d" % (c % 2)
            u, ut = uT[c % 2], "uT%d" % (c % 2)
            u3 = v3(u, 8)
            for hf in range(2):
                for kc in range(8):
                    MM(bank(1 + hf), u3[:, kc, :], wgm3[:, kc, hf * 512:(hf + 1) * 512], kc == 0, kc == 7, [ut, "wgm_%d" % kc], [B(1 + hf)])
            if c >= 1:
                tailT3b(c - 1)
            if c + 1 < NT:
                front3b(c + 1)
            tb7 = bankb(7)
            for kc in range(4):
                TR(tb7[:, kc * 128:(kc + 1) * 128], Ym3[:, c, kc * 128:(kc + 1) * 128], identb, [("Ymla", c // 4), "identb"], [B(7)])
            CP("dve", ymT, tb7[:, 0:512], [B(7)], ["ymT"])
            if c >= 1:
                tailO3b(c - 1)
            ymT3 = v3(ymT, 4)
            for hf in range(2):
                for kc in range(4):
                    MM(bank(YB[hf]), ymT3[:, kc, :], wmo3[:, kc, hf * 512:(hf + 1) * 512], kc == 0, kc == 3, ["ymT", "wmo"], [B(YB[hf])])
            for hf in range(2):
                sl = slice(hf * 512, (hf + 1) * 512)
                ACT(sg[:, sl], bank(1 + hf), AF.Sigmoid, [B(1 + hf)], ["sg"])
                TT("dve", mg[:, sl], bank(YB[hf]), sg[:, sl], ALU.mult, [B(YB[hf]), "sg"], ["mg"])
            TT("pool", mg2, mg, ml, ALU.add, ["mg", mlt], ["mg2"])
        tailT3b(NT - 1)
        tailO3b(NT - 1)
        while prefetch:
            prefetch.pop(0)()
        S.barrier()

    if stages >= 5:
        L = alloc3c()
        wg3, wu3, wd3 = v3(L["wg"], 8), v3(L["wu"], 8), v3(L["wd"], NFC)
        nfin, hbuf_, hs, actT, sgb, ob_, st = L["nfin"], L["hbuf"], L["hs"], L["actT"], L["sgb"], L["ob"], L["st"]
        h2b, h2t = L["h2"], "h2_0"
        uT2 = L["uT2"]
        actT3 = v3(actT, NFC)
        LOAD("sp", nfin, nfin_d, ["nfin"])
        for f_ in ffn_weight_loaders(L, 0):
            f_()
        NBLK = TOK // 512
        hcnt = [0]

        def front3c_a(blk, t):
            c = blk * 4 + t
            k = hcnt[0] % 2
            hcnt[0] += 1
            hbf, hbft = hbuf_[k], "hbuf%d" % k
            hsb, hst = hs[c % 2], "hs%d" % (c % 2)
            DMA("sp", hbf, h1_d[c * 128:(c + 1) * 128, :], [("h1", c)], [hbft], hbft)
            ACT(hsb, hbf, AF.Square, [hbft], [hst, "f_ss"], accum=st[:, 0:1])
            TS("dve", st[:, 0:1], st[:, 0:1], 1.0 / D, ALU.mult, ["f_ss"], ["f_ss"], s2=RMS_EPS, op1=ALU.add)
            TT("pool", st[:, 1:2], st[:, 0:1], mhalf[:, 0:1], ALU.pow, ["f_ss", "mhalf"], ["f_rstd"])
            ACT(hsb, hbf, AF.Identity, [hbft, "f_rstd"], [hst], scale=st[:, 1:2])

        def front3c_b(blk, t):
            c = blk * 4 + t
            hsb, hst = hs[c % 2], "hs%d" % (c % 2)
            u23 = v3(uT2[blk % 2], 8)
            tb = bankb(t % 2)
            for kc in range(8):
                TR(tb[:, kc * 128:(kc + 1) * 128], hsb[:, kc * 128:(kc + 1) * 128], identb, [hst, "identb"], [B(t % 2)])
            CP("dve", u23[:, :, t * 128:(t + 1) * 128], tb[:, 0:1024].rearrange("p (a b) -> p a b", a=8), [B(t % 2)], ["uT2_%d" % (blk % 2)])

        for t in range(4):
            front3c_a(0, t)
            front3c_b(0, t)
        for blk in range(NBLK):
            uT23 = v3(uT2[blk % 2], 8)
            utt = "uT2_%d" % (blk % 2)
            for fc in range(NFC):
                gb_, ub_ = 2 + 2 * (fc % 2), 3 + 2 * (fc % 2)
                for kc in range(8):
                    MM(bank(gb_), wg3[:, kc, fc * 128:(fc + 1) * 128], uT23[:, kc, :], kc == 0, kc == 7, [utt, "wg_%d" % kc], [B(gb_)])
                for kc in range(8):
                    MM(bank(ub_), wu3[:, kc, fc * 128:(fc + 1) * 128], uT23[:, kc, :], kc == 0, kc == 7, [utt, "wu_%d" % kc], [B(ub_)])
                sgt = "sgb%d" % (fc % 2)
                ACT(sgb[fc % 2], bank(gb_), AF.Sigmoid, [B(gb_)], [sgt])
                TT("dve", sgb[fc % 2], bank(gb_), sgb[fc % 2], ALU.mult, [B(gb_), sgt], [sgt])
                TT("dve", actT3[:, fc, :], bank(ub_), sgb[fc % 2], ALU.mult, [B(ub_), sgt], [("actT", fc)])
                if blk + 1 < NBLK and fc >= 4 and fc % 4 == 0 and (fc - 4) // 4 < 4:
                    front3c_a(blk + 1, (fc - 4) // 4)
                if blk + 1 < NBLK and fc >= 6 and fc % 4 == 2 and (fc - 6) // 4 < 4:
                    front3c_b(blk + 1, (fc - 6) // 4)
            for t in range(4):
                c = blk * 4 + t
                d0 = 6
                for hf in range(2):
                    for fc in range(NFC):
                        MM(bank(d0 + hf), actT3[:, fc, t * 128:(t + 1) * 128], wd3[:, fc, hf * 512:(hf + 1) * 512], fc == 0, fc == NFC - 1,
                           [("actT", fc), "wd_%d" % fc], [B(d0 + hf)])
                k = hcnt[0] % 2
                hcnt[0] += 1
                hbf, hbft = hbuf_[k], "hbuf%d" % k
                DMA("sp", hbf, h1_d[c * 128:(c + 1) * 128, :], [("h1", c)], [hbft], hbft)
                obb, obt = ob_[c % 2], "ob%d" % (c % 2)
                for hf in range(2):
                    sl = slice(hf * 512, (hf + 1) * 512)
                    TT("dve", h2b[:, sl], bank(d0 + hf), hbf[:, sl], ALU.add, [B(d0 + hf), hbft], [h2t])
                ACT(obb, h2b, AF.Square, [h2t], [obt, "o_ss"], accum=st[:, 2:3])
                TS("dve", st[:, 2:3], st[:, 2:3], 1.0 / D, ALU.mult, ["o_ss"], ["o_ss"], s2=RMS_EPS, op1=ALU.add)
                TT("pool", st[:, 3:4], st[:, 2:3], mhalf[:, 0:1], ALU.pow, ["o_ss", "mhalf"], ["o_rstd"])
                STT(obb, h2b, st[:, 3:4], nfin, ALU.mult, ALU.mult, [h2t, "o_rstd", "nfin", obt], [obt])
                DMA("pool", out_d[c * 128:(c + 1) * 128, :], obb, [obt], [("out", c)], obt)

    S.emit(nc)
    es.close()
    return nc


def _kc_layout(w):
    K, N = w.shape
    return np.ascontiguousarray(w.reshape(K // 128, 128, N).transpose(1, 0, 2))


def _swap_cols(w, hd):
    K, N = w.shape
    w4 = w.reshape(K, N // hd, 2, hd // 2)
    return np.ascontiguousarray(w4[:, :, ::-1, :]).reshape(K, N)


def _rope_tab(pos, half, base=10000.0):
    inv = (np.float32(base) ** (-(np.arange(half, dtype=np.float32) / np.float32(half)))).astype(np.float32)
    ang = (pos.astype(np.float32)[:, None] * inv[None, :]).astype(np.float32)
    return np.cos(ang.astype(np.float64)).astype(np.float32), np.sin(ang.astype(np.float64)).astype(np.float32)


_PROGRAM = {}


def _prep_inputs(x, meta_tokens, norm_mix_w, w_in, ret_decay_fwd, ret_decay_bwd, ret_gn_w, w_ret_out,
                 mla_q_norm_w, w_uq, mla_kv_norm_w, w_uk, w_uv, w_mla_out, w_o, norm_ffn_w,
                 w_ffn_gate, w_ffn_up, w_ffn_down, norm_final_w):
    f = np.float32
    x = np.asarray(x, f)
    W = np.asarray(w_in, f)[0]
    rq, rk, rv, rg = W[:, 0:512], W[:, 512:1024], W[:, 1024:2048], W[:, 2048:3072]
    cq, ckv, kr = W[:, 3072:3456], W[:, 3456:3712], W[:, 3712:3744]
    gret, gmla = W[:, 3744:4768], W[:, 4768:5792]
    rks, rqs = _swap_cols(rk, 64), _swap_cols(rq, 64)
    shared = {
        "w1": _kc_layout(np.concatenate([cq, ckv, kr, rk, rks, rv], axis=1)),
        "w3a": _kc_layout(np.concatenate([rq, rqs, rk, rks, rv, rg, gret], axis=1)),
        "wgm": _kc_layout(gmla),
        "wmo": _kc_layout(np.asarray(w_mla_out, f)[0]),
        "wo": _kc_layout(np.asarray(w_o, f)[0]),
        "wuq": _kc_layout(np.asarray(w_uq, f)[0]),
        "wuk": _kc_layout(np.asarray(w_uk, f)[0]),
        "wuv": _kc_layout(np.asarray(w_uv, f)[0]),
        "wro": _kc_layout(np.asarray(w_ret_out, f)[0]),
        "wg": _kc_layout(np.asarray(w_ffn_gate, f)[0]),
        "wu": _kc_layout(np.asarray(w_ffn_up, f)[0]),
        "wd": _kc_layout(np.asarray(w_ffn_down, f)[0]),
        "ident": np.eye(128, dtype=f),
        "nfin": np.ascontiguousarray(np.broadcast_to(np.asarray(norm_final_w, f)[None, :], (128, D))),
    }
    vec = np.zeros((128, NVEC), f)

    def pk(v):
        v = np.asarray(v, f).reshape(-1)
        return v.reshape(-1, 128).T

    vec[:, V_NMW:V_NMW + 8] = pk(norm_mix_w)
    vec[:, V_NFW:V_NFW + 8] = pk(norm_ffn_w)
    vec[:, V_GNW:V_GNW + 8] = pk(ret_gn_w)
    vec[:, V_QNW:V_QNW + 3] = pk(mla_q_norm_w)
    vec[:, V_KVNW:V_KVNW + 2] = pk(mla_kv_norm_w)
    df = np.asarray(ret_decay_fwd, f).reshape(8)
    db = np.asarray(ret_decay_bwd, f).reshape(8)
    par = (np.arange(128) >= 64).astype(np.int64)
    for hp in range(4):
        vec[:, V_DFP + hp] = df[2 * hp + par]
        vec[:, V_DBP + hp] = db[2 * hp + par]
    vec[:, V_DF8:V_DF8 + 8] = df[None, :]
    vec[:, V_DB8:V_DB8 + 8] = db[None, :]
    shared["vec"] = vec
    ctab = np.zeros((128, NCTAB), f)
    j = np.arange(128, dtype=f)[:, None]
    i = np.arange(128, dtype=f)[None, :]
    ctab[:, C_POS:C_POS + 128] = np.maximum(i - j, 0)
    ctab[:, C_NEG:C_NEG + 128] = np.maximum(j - i, 0)
    ctab[:, C_I1:C_I1 + 128] = np.broadcast_to(i + 1, (128, 128))
    ctab[:, C_128MI:C_128MI + 128] = np.broadcast_to(128 - i, (128, 128))
    ctab[:, C_127MJ] = 127 - j[:, 0]
    ctab[:, C_J] = j[:, 0]
    shared["ctab"] = ctab

    meta = np.asarray(meta_tokens, f)
    BIG = f(1.0e9)
    sgn = np.where((np.arange(128) % 64) < 32, -1.0, 1.0).astype(f)[:, None]
    fidx = (np.arange(128) % 64) % 32
    in_maps = []
    for core in range(8):
        b, half = core // 2, core % 2
        oth = 1 - half
        m = dict(shared)
        m["xo"] = np.ascontiguousarray(x[b, half * TOK:(half + 1) * TOK])
        xr = np.zeros((NOT_ * 128, D), f)
        xr[0:TOK] = x[b, oth * TOK:(oth + 1) * TOK]
        xr[TOK:TOK + NMETA] = meta
        m["xr"] = xr
        pos_own = (NMETA + half * TOK + np.arange(TOK)).astype(np.int64)
        pos_oth = np.zeros(NOT_ * 128, np.int64)
        pos_oth[0:TOK] = NMETA + oth * TOK + np.arange(TOK)
        pos_oth[TOK:TOK + NMETA] = np.arange(NMETA)
        valid_oth = np.zeros(NOT_ * 128, bool)
        valid_oth[0:TOK + NMETA] = True
        for nm, pos, ntile in (("ropeR_own", pos_own, NT), ("ropeR_oth", pos_oth, NOT_)):
            c, s = _rope_tab(pos, 32)
            cfm = c[:, fidx].T
            sfm = s[:, fidx].T * sgn
            tab = np.stack([cfm.reshape(128, ntile, 128), sfm.reshape(128, ntile, 128)], axis=2)
            m[nm] = np.ascontiguousarray(tab.reshape(128, ntile, 256)).astype(f)
        for nm, pos, ntile in (("tabM_own", pos_own, NT), ("tabM_oth", pos_oth, NOT_)):
            c, s = _rope_tab(pos, 16)
            tab = np.concatenate([c, c, s, s], axis=1).reshape(ntile, 128, 64).transpose(1, 0, 2)
            m[nm] = np.ascontiguousarray(tab).astype(f)
        own_first = NMETA + half * TOK
        own_last = own_first + TOK - 1
        dfw = np.where(valid_oth & (pos_oth < own_first), own_first - 1 - pos_oth, BIG).astype(f)
        dbw = np.where(valid_oth & (pos_oth > own_last), pos_oth - own_last - 1, BIG).astype(f)
        dist = np.concatenate([dfw.reshape(NOT_, 128).T, dbw.reshape(NOT_, 128).T], axis=1)
        m["dist"] = np.ascontiguousarray(dist).astype(f)
        in_maps.append(m)
    return in_maps


def kernel(**inputs):
    in_maps = _prep_inputs(**inputs)
    if "nc" not in _PROGRAM:
        _PROGRAM["nc"] = build_program()
    nc = _PROGRAM["nc"]
    res = run_bass_kernel_spmd(nc, in_maps, core_ids=list(range(8)))
    out = np.zeros((NB, SEQ, D), np.float32)
    for core in range(8):
        b, half = core // 2, core % 2
        out[b, half * TOK:(half + 1) * TOK] = res.results[core]["out"]
    return out
```

```python
from contextlib import ExitStack
import os
import numpy as np
import concourse.bass as bass
import concourse.mybir as mybir
from concourse.bass_utils import run_bass_kernel_spmd

F32 = mybir.dt.float32
BF16 = mybir.dt.bfloat16
AF = mybir.ActivationFunctionType
ALU = mybir.AluOpType
AX = mybir.AxisListType

D = 1024
SEQ = 8192
NB = 4
NMETA = 16
TOK = 4096
NT = 32
NOT_ = 33
NSLOT = 65
NKEY = NSLOT * 128
NKEYP = 17 * 512
DFF = 2816
NFC = DFF // 128
RMS_EPS = 1e-6
GN_EPS = 1e-5
SC_ATT = 96.0 ** -0.5

ENGS = ("pe", "act", "dve", "pool", "sp")
EPOCH = 24000


class _Op:
    __slots__ = ("eng", "fn", "signal", "deps", "dma", "dma_n", "sem", "cnt")

    def __init__(self, eng, fn, dma):
        self.eng = eng
        self.fn = fn
        self.signal = False
        self.deps = []
        self.dma = dma
        self.dma_n = 0
        self.sem = None
        self.cnt = 0


class Sched:
    def __init__(self):
        self.ops = []
        self.last_w = {}
        self.readers = {}
        self.dma_cnt = {}
        self._bar = []
        self._bar_seen = set()

    def op(self, eng, fn, reads=(), writes=(), dma=None):
        o = _Op(eng, fn, dma)
        deps = set()
        if eng not in self._bar_seen:
            self._bar_seen.add(eng)
            deps.update(self._bar)
        for t in reads:
            w = self.last_w.get(t)
            if w is not None:
                deps.add(w)
            if isinstance(t, str) and t[0] == "b" and t[1:].isdigit():
                for k, r in self.readers.get(t, {}).items():
                    if r.eng != eng:
                        deps.add(r)
        for t in writes:
            w = self.last_w.get(t)
            if w is not None:
                deps.add(w)
            for r in self.readers.get(t, {}).values():
                deps.add(r)
        if dma is not None:
            n = self.dma_cnt.get(dma, 0) + 1
            self.dma_cnt[dma] = n
            o.dma_n = n
        for d in deps:
            if d is o:
                continue
            if d.dma is None and d.eng == "pe" and eng == "pe" and dma is None:
                continue
            o.deps.append(d)
            if d.dma is None:
                d.signal = True
        for t in reads:
            rd = self.readers.setdefault(t, {})
            rd[eng if dma is None else ("dma", id(o))] = o
        for t in writes:
            self.last_w[t] = o
            self.readers[t] = {}
        self.ops.append(o)
        return o

    def barrier(self):
        last = {}
        for o in self.ops:
            last[o.eng if o.dma is None else ("dma", o.dma)] = o
        self._bar = list(last.values())
        self._bar_seen = set()
        for o in self._bar:
            if o.dma is None:
                o.signal = True

    def emit(self, nc, final_eng="sp"):
        per = {e: [] for e in ENGS}
        for o in self.ops:
            per[o.eng].append(o)
        nsems = {}
        for e in ENGS:
            c = 0
            ep = 0
            for o in per[e]:
                if o.signal and o.dma is None:
                    c += 1
                    if c > EPOCH:
                        ep += 1
                        c = 1
                    o.sem = (e, ep)
                    o.cnt = c
            nsems[e] = ep + 1
        with ExitStack() as es:
            sems = {}
            for e in ENGS:
                for ep in range(nsems[e]):
                    sems[(e, ep)] = es.enter_context(nc.semaphore(f"s_{e}_{ep}"))
            dsems = {}
            for k in self.dma_cnt:
                dsems[k] = es.enter_context(nc.semaphore("d_" + str(len(dsems))))
            block = es.enter_context(nc.Block())
            dma_cnt = self.dma_cnt

            def run(e, eng):
                waited = {}
                for o in per[e]:
                    need = {}
                    for d in o.deps:
                        if d.dma is not None:
                            key = ("d", d.dma)
                            v = (0, 16 * d.dma_n)
                        else:
                            key = ("e", d.eng)
                            v = (d.sem[1], d.cnt)
                        if v > need.get(key, (-1, -1)):
                            need[key] = v
                    for key, v in need.items():
                        if v <= waited.get(key, (-1, -1)):
                            continue
                        waited[key] = v
                        if key[0] == "d":
                            eng.wait_ge(dsems[key[1]], v[1])
                        else:
                            eng.wait_ge(sems[(key[1], v[0])], v[1])
                    ins = o.fn(eng)
                    if o.dma is not None:
                        ins.then_inc(dsems[o.dma], 16)
                    elif o.signal:
                        ins.then_inc(sems[o.sem], 1)
                if e == final_eng:
                    for k, n in dma_cnt.items():
                        if 16 * n > waited.get(("d", k), (-1, -1))[1]:
                            eng.wait_ge(dsems[k], 16 * n)

            @block.tensor
            def _(eng):
                run("pe", eng)

            @block.scalar
            def _(eng):
                run("act", eng)

            @block.vector
            def _(eng):
                run("dve", eng)

            @block.gpsimd
            def _(eng):
                run("pool", eng)

            @block.sync
            def _(eng):
                run("sp", eng)


class Arena:
    def __init__(self, ap, ncols):
        self.ap = ap
        self.n = ncols
        self.pos = 0

    def alloc(self, cols, dt=BF16):
        n = cols * 2 if dt == F32 else cols
        n = (n + 1) // 2 * 2
        v = self.ap[:, self.pos:self.pos + (cols * 2 if dt == F32 else cols)]
        self.pos += n
        assert self.pos <= self.n, ("arena overflow", self.pos, self.n)
        return v.bitcast(F32) if dt == F32 else v


V_NMW, V_NFW, V_GNW, V_QNW, V_KVNW, V_DFP, V_DBP, V_DF8, V_DB8 = 0, 8, 16, 24, 27, 29, 33, 37, 45
NVEC = 53
C_POS, C_NEG, C_I1, C_128MI, C_127MJ, C_J = 0, 128, 256, 384, 512, 513
NCTAB = 514
W1_CQ, W1_CKV, W1_KR, W1_RK, W1_RKS, W1_RV, W1_N = 0, 384, 640, 672, 1184, 1696, 2720
W3_RQ, W3_RQS, W3_RK, W3_RKS, W3_RV, W3_RG, W3_GR, W3_N = 0, 512, 1024, 1536, 2048, 3072, 4096, 5120
ARENA_COLS = 106400


def build_program(stages=99, dbg=False):
    nc = bass.Bass("TRN2", target_bir_lowering=False)

    def din(name, shape, dt=F32):
        return nc.dram_tensor(name, list(shape), dt, kind="ExternalInput").ap()

    xo = din("xo", [TOK, D])
    xr = din("xr", [NOT_ * 128, D])
    w1_d = din("w1", [128, 8, W1_N])
    w3a_d = din("w3a", [128, 8, W3_N])
    wgm_d = din("wgm", [128, 8, 1024])
    wmo_d = din("wmo", [128, 4, 1024])
    wo_d = din("wo", [128, 8, 1024])
    wuq_d = din("wuq", [128, 3, 768])
    wuk_d = din("wuk", [128, 2, 512])
    wuv_d = din("wuv", [128, 2, 512])
    wro_d = din("wro", [128, 8, 1024])
    wg_d = din("wg", [128, 8, DFF])
    wu_d = din("wu", [128, 8, DFF])
    wd_d = din("wd", [128, NFC, 1024])
    vec_d = din("vec", [128, NVEC])
    ctab_d = din("ctab", [128, NCTAB])
    dist_d = din("dist", [128, 2 * NOT_])
    ident_d = din("ident", [128, 128])
    nfin_d = din("nfin", [128, D])
    ropeR_own_d = din("ropeR_own", [128, NT, 256])
    ropeR_oth_d = din("ropeR_oth", [128, NOT_, 256])
    tabM_own_d = din("tabM_own", [128, NT, 64])
    tabM_oth_d = din("tabM_oth", [128, NOT_, 64])
    out_d = nc.dram_tensor("out", [TOK, D], F32, kind="ExternalOutput").ap()
    SCR_KIND = "ExternalOutput" if dbg else "Internal"
    Qs_d = nc.dram_tensor("Qs", [8, 96, TOK], BF16, kind=SCR_KIND).ap()
    m1_d = nc.dram_tensor("m1s", [TOK, D], BF16, kind=SCR_KIND).ap()
    h1_d = nc.dram_tensor("h1s", [TOK, D], F32, kind=SCR_KIND).ap()
    ckvT_d = nc.dram_tensor("ckvTs", [128, 2 * NKEYP], BF16, kind=SCR_KIND).ap()
    krope_d = nc.dram_tensor("kropes", [32, NKEYP], BF16, kind=SCR_KIND).ap()
    SbAll_d = nc.dram_tensor("SbAlls", [NT, 128, 512], BF16, kind=SCR_KIND).ap()

    S = Sched()
    es = ExitStack()
    arena_t = es.enter_context(nc.sbuf_tensor("arena", [128, ARENA_COLS], BF16))
    A = Arena(arena_t, ARENA_COLS)
    PS2 = [es.enter_context(nc.psum_tensor(f"ps2_{i}", [128, 1024], F32)) for i in range(4)]

    def bank(i):
        return PS2[i // 2][:, (i % 2) * 512:(i % 2 + 1) * 512]

    def bankb(i):
        return bank(i).bitcast(BF16)

    def B(i):
        return "b%d" % i

    def MM(out, lhsT, rhs, start, stop, r, w):
        S.op("pe", lambda e: e.matmul(out, lhsT=lhsT, rhs=rhs, start=start, stop=stop), r, w)

    def TR(out, in_, idn, r, w):
        S.op("pe", lambda e: e.transpose(out=out, in_=in_, identity=idn), r, w)

    def ACT(out, in_, func, r, w, scale=None, bias=None, accum=None):
        kw = {}
        if scale is not None:
            kw["scale"] = scale
        if bias is not None:
            kw["bias"] = bias
        if accum is not None:
            kw["accum_out"] = accum
        S.op("act", lambda e: e.activation(out=out, in_=in_, func=func, **kw), r, w)

    def TT(eng, out, in0, in1, op, r, w):
        S.op(eng, lambda e: e.tensor_tensor(out=out, in0=in0, in1=in1, op=op), r, w)

    def TS(eng, out, in0, s1, op0, r, w, s2=None, op1=None):
        if op1 is None:
            S.op(eng, lambda e: e.tensor_scalar(out=out, in0=in0, scalar1=s1, scalar2=None, op0=op0), r, w)
        else:
            S.op(eng, lambda e: e.tensor_scalar(out=out, in0=in0, scalar1=s1, scalar2=s2, op0=op0, op1=op1), r, w)

    def STT(out, in0, scalar, in1, op0, op1, r, w):
        S.op("dve", lambda e: e.scalar_tensor_tensor(out=out, in0=in0, scalar=scalar, in1=in1, op0=op0, op1=op1), r, w)

    def CP(eng, out, in_, r, w):
        if eng == "act":
            S.op("act", lambda e: e.copy(out=out, in_=in_), r, w)
        else:
            S.op(eng, lambda e: e.tensor_copy(out=out, in_=in_), r, w)

    def RED(out, in_, r, w):
        S.op("dve", lambda e: e.tensor_reduce(out=out, in_=in_, axis=AX.X, op=ALU.add), r, w)

    def MSET(eng, ap, val, w):
        S.op(eng, lambda e: e.memset(ap, val), (), w)

    def DMA(eng, out, in_, r, w, key):
        S.op(eng, lambda e: e.dma_start(out=out, in_=in_), r, w, dma=key)

    chain_i = [0]

    def LOAD(eng, out, in_, w):
        k = "chain_%s%d" % (eng, chain_i[0] % 3)
        chain_i[0] += 1
        S.op(eng, lambda e: e.dma_start(out=out, in_=in_), (), list(w) + [k], dma=k)

    def v3(ap, a):
        return ap.rearrange("p (a b) -> p a b", a=a)

    vec = A.alloc(NVEC + 1, F32)
    identf = A.alloc(128, F32)
    identb = A.alloc(128, BF16)
    lg8 = A.alloc(16, F32)
    lgP = A.alloc(8, F32)
    mhalf = A.alloc(16, F32)
    rstd_own = A.alloc(NT, F32)
    P0_END = A.pos
    ctab = A.alloc(NCTAB, F32)
    c128 = A.alloc(128, F32)
    Sf = A.alloc(512, F32)
    Sb = A.alloc(512, F32)
    P1_END = A.pos

    LOAD("sp", vec[:, 0:NVEC], vec_d, ["vec"])
    LOAD("sp", ctab, ctab_d, ["ctab"])
    LOAD("sp", identf, ident_d, ["identf"])
    CP("dve", identb, identf, ["identf"], ["identb"])
    MSET("pool", c128, 128.0, ["c128"])
    MSET("pool", mhalf, -0.5, ["mhalf"])
    MSET("pool", Sf, 0.0, ["Sf"])
    MSET("pool", Sb, 0.0, ["Sb"])
    ACT(lg8, vec[:, V_DF8:V_DF8 + 16], AF.Exp, ["vec"], ["lg8"])
    TS("dve", lg8, lg8, -1.0, ALU.mult, ["lg8"], ["lg8"])
    ACT(lgP, vec[:, V_DFP:V_DFP + 8], AF.Exp, ["vec"], ["lgP"])
    TS("dve", lgP, lgP, -1.0, ALU.mult, ["lgP"], ["lgP"])

    if stages >= 1:
        A.pos = P1_END
        w1 = A.alloc(8 * W1_N, BF16)
        wuq = A.alloc(3 * 768, BF16)
        w13 = v3(w1, 8)
        wuq3 = v3(wuq, 3)
        wfb = A.alloc(16, F32)
        GbT = A.alloc(512, F32)
        wo_fb = A.alloc(2 * NOT_ * 8, F32)
        dist = A.alloc(2 * NOT_, F32)
        tabM_own = A.alloc(NT * 64, F32)
        tabM_oth = A.alloc(NOT_ * 64, F32)
        NXB = 2
        xbuf = [A.alloc(D, F32) for _ in range(NXB)]
        ropeb = [A.alloc(256, F32) for _ in range(NXB)]
        xs = [A.alloc(D, BF16) for _ in range(2)]
        junk = A.alloc(D, BF16)
        uT = [A.alloc(D, BF16) for _ in range(2)]
        st = A.alloc(8, F32)
        ckvn = A.alloc(256, BF16)
        kaug = A.alloc(96, BF16)
        ropeA = A.alloc(32, F32)
        ropeBv = A.alloc(32, F32)
        t1 = A.alloc(512, F32)
        t2 = A.alloc(512, F32)
        kT = A.alloc(512, BF16)
        kwf = A.alloc(512, BF16)
        kwb = A.alloc(512, BF16)
        vtok = A.alloc(1024, BF16)
        cqn = A.alloc(384, BF16)
        cqT = A.alloc(384, BF16)
        qA = A.alloc(256, F32)
        qB = A.alloc(256, F32)
        qtok = A.alloc(768, BF16)
        Qst = [A.alloc(8 * 512, BF16) for _ in range(2)]
        ckst = [A.alloc(2 * 512, BF16) for _ in range(2)]
        krst = [A.alloc(512, BF16) for _ in range(2)]
        sbst = [A.alloc(512, BF16) for _ in range(2)]

        LOAD("sp", dist, dist_d, ["dist"])
        LOAD("sp", tabM_own, tabM_own_d.rearrange("p a b -> p (a b)"), ["tabM_own"])
        LOAD("sp", tabM_oth, tabM_oth_d.rearrange("p a b -> p (a b)"), ["tabM_oth"])
        TS("dve", wfb[:, 0:8], lg8[:, 0:8], ctab[:, C_127MJ:C_127MJ + 1], ALU.mult, ["lg8", "ctab"], ["wfb"])
        TS("dve", wfb[:, 8:16], lg8[:, 8:16], ctab[:, C_J:C_J + 1], ALU.mult, ["lg8", "ctab", "wfb"], ["wfb"])
        ACT(wfb, wfb, AF.Exp, ["wfb"], ["wfb"])
        TS("dve", wfb, wfb, 0.125, ALU.mult, ["wfb"], ["wfb"])
        GbT3 = v3(GbT, 4)
        for hp in range(4):
            ACT(GbT3[:, hp, :], c128, AF.Exp, ["c128", "lgP"], ["GT"], scale=lgP[:, 4 + hp:5 + hp])
        wo3 = v3(wo_fb, 2 * NOT_)
        for h in range(8):
            TS("dve", wo3[:, 0:NOT_, h], dist[:, 0:NOT_], lg8[:, h:h + 1], ALU.mult, ["dist", "lg8"], ["wo_fb"])
            TS("dve", wo3[:, NOT_:2 * NOT_, h], dist[:, NOT_:2 * NOT_], lg8[:, 8 + h:9 + h], ALU.mult, ["dist", "lg8"], ["wo_fb"])
        ACT(wo_fb, wo_fb, AF.Exp, ["wo_fb"], ["wo_fb"])
        TS("dve", wo_fb, wo_fb, 0.125, ALU.mult, ["wo_fb"], ["wo_fb"])

        for kc in range(8):
            LOAD("pool", w1[:, kc * W1_N:(kc + 1) * W1_N], w1_d[:, kc, :], ["w1_%d" % kc])
        for kc in range(3):
            LOAD("pool", wuq[:, kc * 768:(kc + 1) * 768], wuq_d[:, kc, :], ["wuq"])
        for kc in range(8):
            sl = slice(kc * W1_N, (kc + 1) * W1_N)
            TS("dve", w1[:, sl], w1[:, sl], vec[:, V_NMW + kc:V_NMW + kc + 1], ALU.mult, ["w1_%d" % kc, "vec"], ["w1_%d" % kc])
        for kc in range(3):
            sl = slice(kc * 768, (kc + 1) * 768)
            TS("dve", wuq[:, sl], wuq[:, sl], vec[:, V_QNW + kc:V_QNW + kc + 1], ALU.mult, ["wuq", "vec"], ["wuq"])
        W1T = ["w1_%d" % kc for kc in range(8)]
        MSET("pool", kaug, 0.0, ["kaug"])
        for i in range(2):
            MSET("pool", ckst[i], 0.0, ["ckst%d" % i])
            MSET("pool", krst[i], 0.0, ["krst%d" % i])
        gcount = 0

        seq = [("o", t) for t in range(NOT_)] + [("s", c) for c in range(NT - 1, -1, -1)]
        import os
        if os.environ.get("K_METAFIRST"):
            seq = [("o", NOT_ - 1)] + [("o", t) for t in range(NOT_ - 1)] + [("s", c) for c in range(NT - 1, -1, -1)]
        if os.environ.get("K_S1N"):
            seq = seq[:int(os.environ["K_S1N"])]
        def front1(it):
            kind, ti = seq[it]
            own = kind == "s"
            xsrc = xo[ti * 128:(ti + 1) * 128, :] if own else xr[ti * 128:(ti + 1) * 128, :]
            rsrc = ropeR_own_d[:, ti, :] if own else ropeR_oth_d[:, ti, :]
            xb, xbt = xbuf[it % NXB], "xbuf%d" % (it % NXB)
            rb, rbt = ropeb[it % NXB], "ropeb%d" % (it % NXB)
            xsb, xst = xs[it % 2], "xs%d" % (it % 2)
            u, ut = uT[it % 2], "uT%d" % (it % 2)
            DMA("sp", xb, xsrc, (), [xbt], xbt)
            DMA("sp", rb, rsrc, (), [rbt], rbt)
            ACT(junk, xb, AF.Square, [xbt], ["junk", "x_ss"], accum=st[:, 0:1])
            rs = rstd_own[:, ti:ti + 1] if own else st[:, 1:2]
            TS("dve", st[:, 0:1], st[:, 0:1], 1.0 / D, ALU.mult, ["x_ss"], ["x_ss"], s2=RMS_EPS, op1=ALU.add)
            TT("pool", rs, st[:, 0:1], mhalf[:, 0:1], ALU.pow, ["x_ss", "mhalf"], ["x_rstd"])
            ACT(xsb, xb, AF.Identity, [xbt, "x_rstd"], [xst], scale=rs)
            tb = bankb(0)
            for kc in range(8):
                TR(tb[:, kc * 128:(kc + 1) * 128], xsb[:, kc * 128:(kc + 1) * 128], identb, [xst, "identb"], [B(0)])
            CP("dve", u, tb[:, 0:1024], [B(0)], [ut])

        lat = [A.alloc(672, F32) for _ in range(2)]
        kT2 = [kT, A.alloc(512, BF16)]
        vtok2 = [vtok, A.alloc(1024, BF16)]
        qtok2 = [qtok, A.alloc(768, BF16)]
        gstate = {"gcount": 0}

        def proj1(it):
            kind, ti = seq[it]
            own = kind == "s"
            par = it % 2
            rb, rbt = ropeb[it % NXB], "ropeb%d" % (it % NXB)
            u, ut = uT[par], "uT%d" % par
            u3 = v3(u, 8)
            la, lat_t = lat[par], "lat%d" % par
            for kc in range(8):
                MM(bank(1)[:, 0:288], u3[:, kc, :], w13[:, kc, W1_CKV:W1_CKV + 288], kc == 0, kc == 7, [ut, W1T[kc]], [B(1)])
            if own:
                for kc in range(8):
                    MM(bank(2)[:, 0:384], u3[:, kc, :], w13[:, kc, W1_CQ:W1_CQ + 384], kc == 0, kc == 7, [ut, W1T[kc]], [B(2)])
            for hp in range(4):
                for kc in range(8):
                    MM(bank(3)[:, hp * 128:(hp + 1) * 128], w13[:, kc, W1_RK + hp * 128:W1_RK + (hp + 1) * 128], u3[:, kc, :],
                       kc == 0, kc == 7, [ut, W1T[kc]], [B(3)])
            CP("act", la[:, 0:288], bank(1)[:, 0:288], [B(1)], [lat_t])
            if own:
                CP("act", la[:, 288:672], bank(2)[:, 0:384], [B(2)], [lat_t])
            for hp in range(4):
                for kc in range(8):
                    MM(bank(4)[:, hp * 128:(hp + 1) * 128], w13[:, kc, W1_RKS + hp * 128:W1_RKS + (hp + 1) * 128], u3[:, kc, :],
                       kc == 0, kc == 7, [ut, W1T[kc]], [B(4)])
            cosR = rb[:, 0:128].unsqueeze(1).to_broadcast([128, 4, 128])
            sinR = rb[:, 128:256].unsqueeze(1).to_broadcast([128, 4, 128])
            TT("dve", v3(t1, 4), v3(bank(3), 4), cosR, ALU.mult, [B(3), rbt], ["t1"])
            for hf in range(2):
                for kc in range(8):
                    MM(bank(5 + hf), u3[:, kc, :], w13[:, kc, W1_RV + hf * 512:W1_RV + (hf + 1) * 512], kc == 0, kc == 7,
                       [ut, W1T[kc]], [B(5 + hf)])
            TT("dve", v3(t2, 4), v3(bank(4), 4), sinR, ALU.mult, [B(4), rbt], ["t2"])
            TT("pool", kT2[par], t1, t2, ALU.add, ["t1", "t2"], ["kT%d" % par])
            CP("act", vtok2[par][:, 0:512], bank(5), [B(5)], ["vtok%d" % par])
            CP("act", vtok2[par][:, 512:1024], bank(6), [B(6)], ["vtok%d" % par])

        def back1(it):
            kind, ti = seq[it]
            own = kind == "s"
            par = it % 2
            slot = ti if own else (32 + ti)
            tabM = (tabM_own if own else tabM_oth)[:, ti * 64:(ti + 1) * 64]
            tabMt = "tabM_own" if own else "tabM_oth"
            la, lat_t = lat[par], "lat%d" % par
            kTp, kTt = kT2[par], "kT%d" % par
            vtp, vtt = vtok2[par], "vtok%d" % par
            ACT(junk[:, 0:256], la[:, 0:256], AF.Square, [lat_t], ["junk", "kv_ss"], accum=st[:, 2:3])
            TS("dve", st[:, 2:3], st[:, 2:3], 1.0 / 256, ALU.mult, ["kv_ss"], ["kv_ss"], s2=RMS_EPS, op1=ALU.add)
            TT("pool", st[:, 3:4], st[:, 2:3], mhalf[:, 0:1], ALU.pow, ["kv_ss", "mhalf"], ["kv_rstd"])
            ACT(ckvn, la[:, 0:256], AF.Identity, [lat_t, "kv_rstd"], ["ckvn"], scale=st[:, 3:4])
            if own:
                ACT(junk[:, 0:384], la[:, 288:672], AF.Square, [lat_t], ["junk", "q_ss"], accum=st[:, 4:5])
                TS("dve", st[:, 4:5], st[:, 4:5], 1.0 / 384, ALU.mult, ["q_ss"], ["q_ss"], s2=RMS_EPS, op1=ALU.add)
                TT("pool", st[:, 5:6], st[:, 4:5], mhalf[:, 0:1], ALU.pow, ["q_ss", "mhalf"], ["q_rstd"])
                ACT(cqn, la[:, 288:672], AF.Identity, [lat_t, "q_rstd"], ["cqn"], scale=st[:, 5:6])
            TT("dve", ropeA, la[:, 256:288], tabM[:, 0:32], ALU.mult, [lat_t, tabMt], ["ropeA"])
            TT("dve", ropeBv, la[:, 256:288], tabM[:, 32:64], ALU.mult, [lat_t, tabMt], ["ropeB"])
            TT("pool", kaug[:, 64:80], ropeA[:, 0:16], ropeBv[:, 16:32], ALU.subtract, ["ropeA", "ropeB"], ["kaug"])
            TT("pool", kaug[:, 80:96], ropeBv[:, 0:16], ropeA[:, 16:32], ALU.add, ["ropeA", "ropeB"], ["kaug"])
            hb = bankb(7)
            for hp in range(4):
                TR(hb[:, 384 + hp * 128:384 + (hp + 1) * 128], kTp[:, hp * 128:(hp + 1) * 128], identb, [kTt, "identb"], [B(7)])
            for c2 in range(2):
                TR(hb[:, c2 * 128:(c2 + 1) * 128], ckvn[:, c2 * 128:(c2 + 1) * 128], identb, ["ckvn", "identb"], [B(7)])
            TR(hb[0:96, 256:384], kaug, identb, ["kaug", "identb"], [B(7)])
            if own:
                tb0 = bankb(0)
                for c3 in range(3):
                    TR(tb0[:, c3 * 128:(c3 + 1) * 128], cqn[:, c3 * 128:(c3 + 1) * 128], identb, ["cqn", "identb"], [B(0)])
                CP("dve", cqT, tb0[:, 0:384], [B(0)], ["cqT"])
            ktok3 = hb[:, 384:896].rearrange("p (h d) -> p h d", h=8)
            if own:
                wbb = wfb[:, 8:16].unsqueeze(2).to_broadcast([128, 8, 64])
                TT("dve", kwb.rearrange("p (h d) -> p h d", h=8), ktok3, wbb, ALU.mult, [B(7), "wfb"], ["kwb"])
                dirs = [("b", kwb, "kwb", 3)]
            else:
                wof = wo3[:, ti, :].unsqueeze(2).to_broadcast([128, 8, 64])
                wob = wo3[:, NOT_ + ti, :].unsqueeze(2).to_broadcast([128, 8, 64])
                TT("dve", kwf.rearrange("p (h d) -> p h d", h=8), ktok3, wof, ALU.mult, [B(7), "wo_fb"], ["kwf"])
                TT("dve", kwb.rearrange("p (h d) -> p h d", h=8), ktok3, wob, ALU.mult, [B(7), "wo_fb"], ["kwb"])
                dirs = [("f", kwf, "kwf", 1), ("b", kwb, "kwb", 3)]
            gb_i = gstate["gcount"] % 2
            cks, ckt = ckst[gb_i], "ckst%d" % gb_i
            krs, krt = krst[gb_i], "krst%d" % gb_i
            sp_ = slot % 4
            CP("dve", v3(cks, 2)[:, :, sp_ * 128:(sp_ + 1) * 128], hb[:, 0:256].rearrange("p (a b) -> p a b", a=2), [B(7)], [ckt])
            CP("dve", krs[64:96, sp_ * 128:(sp_ + 1) * 128], hb[64:96, 256:384], [B(7)], [krt])
            flush = (slot == 64) or (own and ti % 4 == 0) or ((not own) and slot < 64 and slot % 4 == 3)
            if flush:
                g0 = (slot // 4) * 512
                DMA("pool", ckvT_d.rearrange("p (a n) -> p a n", a=2)[:, :, g0:g0 + 512], v3(cks, 2)[:, :, 0:512], [ckt], [("ckvT_d", slot // 4)], ckt)
                DMA("pool", krope_d[:, g0:g0 + 512], krs[64:96, 0:512], [krt], [("krope_d", slot // 4)], krt)
                gstate["gcount"] += 1
            if own:
                cqT3 = v3(cqT, 3)
                for kc in range(3):
                    MM(bank(5)[:, 0:480], cqT3[:, kc, :], wuq3[:, kc, 0:480], kc == 0, kc == 2, ["cqT", "wuq"], [B(5)])
                for kc in range(3):
                    MM(bank(6)[:, 0:288], cqT3[:, kc, :], wuq3[:, kc, 480:768], kc == 0, kc == 2, ["cqT", "wuq"], [B(6)])
            for (dname, kw_, kwt, b0) in dirs:
                for hp in range(4):
                    bk = bank(b0 + hp // 2)
                    MM(bk[:, (hp % 2) * 256:(hp % 2) * 256 + 256], kw_[:, hp * 128:(hp + 1) * 128], vtp[:, hp * 256:(hp + 1) * 256],
                       True, True, [kwt, vtt], [B(b0 + hp // 2)])
            Sb3 = v3(Sb, 4)
            Sf3 = v3(Sf, 4)
            if own:
                sbs, sbt = sbst[ti % 2], "sbst%d" % (ti % 2)
                CP("pool", sbs, Sb, ["Sb"], [sbt])
                DMA("pool", SbAll_d[ti, :, :], sbs, [sbt], [("SbAll", ti)], sbt)
                TT("pool", Sb, Sb, GbT, ALU.mult, ["Sb", "GT", sbt], ["Sb"])
            for (dname, kw_, kwt, b0) in dirs:
                Sx3 = Sb3 if dname == "b" else Sf3
                Sxt = "Sb" if dname == "b" else "Sf"
                for hh in range(2):
                    bk = v3(bank(b0 + hh), 2)
                    TT("dve", Sx3[0:64, 2 * hh:2 * hh + 2, :], Sx3[0:64, 2 * hh:2 * hh + 2, :], bk[0:64, :, 0:128], ALU.add,
                       [Sxt, B(b0 + hh)], [Sxt])
                    TT("dve", Sx3[64:128, 2 * hh:2 * hh + 2, :], Sx3[64:128, 2 * hh:2 * hh + 2, :], bk[64:128, :, 128:256], ALU.add,
                       [Sxt, B(b0 + hh)], [Sxt])
            if own:
                qtk, qtt = qtok2[par], "qtok%d" % par
                q3 = qtk.rearrange("p (h c) -> p h c", h=8)
                for (bk_i, h0, nh) in ((5, 0, 5), (6, 5, 3)):
                    src = bank(bk_i)[:, 0:nh * 96].rearrange("p (h c) -> p h c", h=nh)
                    CP("act", q3[:, h0:h0 + nh, 0:64], src[:, :, 0:64], [B(bk_i)], [qtt])
                    ccq = tabM[:, 0:32].unsqueeze(1).to_broadcast([128, nh, 32])
                    ssq = tabM[:, 32:64].unsqueeze(1).to_broadcast([128, nh, 32])
                    qA3 = qA[:, h0 * 32:(h0 + nh) * 32].rearrange("p (h c) -> p h c", h=nh)
                    qB3 = qB[:, h0 * 32:(h0 + nh) * 32].rearrange("p (h c) -> p h c", h=nh)
                    TT("dve", qA3, src[:, :, 64:96], ccq, ALU.mult, [B(bk_i), tabMt], ["qA"])
                    TT("dve", qB3, src[:, :, 64:96], ssq, ALU.mult, [B(bk_i), tabMt], ["qB"])
                qA3 = qA.rearrange("p (h c) -> p h c", h=8)
                qB3 = qB.rearrange("p (h c) -> p h c", h=8)
                TT("pool", q3[:, :, 64:80], qA3[:, :, 0:16], qB3[:, :, 16:32], ALU.subtract, ["qA", "qB"], [qtt])
                TT("pool", q3[:, :, 80:96], qB3[:, :, 0:16], qA3[:, :, 16:32], ALU.add, ["qA", "qB"], [qtt])

        def back2(it):
            kind, ti = seq[it]
            if kind != "s":
                return
            par = it % 2
            qtk, qtt = qtok2[par], "qtok%d" % par
            q3 = qtk.rearrange("p (h c) -> p h c", h=8)
            tb2 = bankb(2)
            for h in range(8):
                TR(tb2[0:96, h * 128:(h + 1) * 128], q3[:, h, :], identb, [qtt, "identb"], [B(2)])
            g = ti // 4
            qs = Qst[g % 2]
            qst = "Qst%d" % (g % 2)
            qs3 = v3(qs, 8)
            CP("dve", qs3[0:96, :, (ti % 4) * 128:(ti % 4 + 1) * 128], tb2[0:96, 0:1024].rearrange("p (h t) -> p h t", h=8),
               [B(2)], [qst])
            if ti % 4 == 0:
                DMA("pool", Qs_d[:, :, g * 512:(g + 1) * 512].rearrange("h d t -> d h t"), qs3[0:96, :, :], [qst], [("Qs", g)], qst)

        n1 = len(seq)
        if n1:
            front1(0)
        for it in range(n1):
            proj1(it)
            if it + 1 < n1:
                front1(it + 1)
            if it >= 1:
                back1(it - 1)
            if it >= 2:
                back2(it - 2)
        if n1:
            back1(n1 - 1)
            if n1 >= 2:
                back2(n1 - 2)
            back2(n1 - 1)
        S.barrier()

    if stages >= 2:
        A.pos = P1_END
        w3 = A.alloc(8 * W3_N, BF16)
        wro = A.alloc(8 * 1024, BF16)
        w33 = v3(w3, 8)
        wro3 = v3(wro, 8)
        DT = A.alloc(1024, F32)
        wfb = A.alloc(16, F32)
        wqfd = A.alloc(1024, F32)
        wqbd = A.alloc(1024, F32)
        GfT = A.alloc(512, F32)
        Sf_bf = A.alloc(512, BF16)
        NXB = 2
        xbuf = [A.alloc(D, F32) for _ in range(NXB)]
        ropeb = [A.alloc(256, F32) for _ in range(NXB)]
        sbl = [A.alloc(512, BF16) for _ in range(2)]
        xs = [A.alloc(D, BF16) for _ in range(2)]
        uT = [A.alloc(D, BF16) for _ in range(2)]
        t1 = A.alloc(512, F32)
        t2 = A.alloc(512, F32)
        qTd = A.alloc(1024, BF16)
        kT = A.alloc(512, BF16)
        qfd = A.alloc(1024, BF16)
        qbd = A.alloc(1024, BF16)
        kwf = A.alloc(512, BF16)
        vtok = A.alloc(1024, BF16)
        sig = A.alloc(1024, F32)
        silu = sig
        sgr = A.alloc(1024, F32)
        PT = A.alloc(1024, BF16)
        sq = A.alloc(1024, F32)
        gst = A.alloc(64, F32)
        zn = sq
        z = A.alloc(1024, BF16)
        zT = A.alloc(1024, BF16)
        m1b = [A.alloc(1024, BF16) for _ in range(2)]
        DT3 = v3(DT, 8)
        for h in range(8):
            TS("dve", DT3[:, h, :], ctab[:, C_POS:C_POS + 128], lg8[:, h:h + 1], ALU.mult, ["ctab", "lg8"], ["DT"])
            STT(DT3[:, h, :], ctab[:, C_NEG:C_NEG + 128], lg8[:, 8 + h:9 + h], DT3[:, h, :], ALU.mult, ALU.add,
                ["ctab", "lg8", "DT"], ["DT"])
        ACT(DT, DT, AF.Exp, ["DT"], ["DT"])
        TS("dve", DT, DT, 0.125, ALU.mult, ["DT"], ["DT"])
        TS("dve", wfb[:, 0:8], lg8[:, 0:8], ctab[:, C_127MJ:C_127MJ + 1], ALU.mult, ["lg8", "ctab"], ["wfb"])
        TS("dve", wfb[:, 8:16], lg8[:, 8:16], ctab[:, C_J:C_J + 1], ALU.mult, ["lg8", "ctab", "wfb"], ["wfb"])
        ACT(wfb, wfb, AF.Exp, ["wfb"], ["wfb"])
        TS("dve", wfb, wfb, 0.125, ALU.mult, ["wfb"], ["wfb"])
        wqfd3, wqbd3, GfT3 = v3(wqfd, 4), v3(wqbd, 4), v3(GfT, 4)
        MSET("pool", wqfd, 0.0, ["wq"])
        MSET("pool", wqbd, 0.0, ["wq"])
        MSET("pool", qTd, 0.0, ["qT"])
        for hp in range(4):
            for par in range(2):
                rs_ = slice(par * 64, (par + 1) * 64)
                cs_ = slice(par * 128, (par + 1) * 128)
                ACT(wqfd3[rs_, hp, cs_], ctab[rs_, C_I1:C_I1 + 128], AF.Exp, ["ctab", "lgP", "wq"], ["wq"], scale=lgP[rs_, hp:hp + 1])
                ACT(wqbd3[rs_, hp, cs_], ctab[rs_, C_128MI:C_128MI + 128], AF.Exp, ["ctab", "lgP", "wq"], ["wq"], scale=lgP[rs_, 4 + hp:5 + hp])
            ACT(GfT3[:, hp, :], c128, AF.Exp, ["c128", "lgP"], ["GT"], scale=lgP[:, hp:hp + 1])
        for kc in range(8):
            LOAD("pool", w3[:, kc * W3_N:(kc + 1) * W3_N], w3a_d[:, kc, :], ["w3_%d" % kc])
            LOAD("pool", wro[:, kc * 1024:(kc + 1) * 1024], wro_d[:, kc, :], ["wro_%d" % kc])
        for kc in range(8):
            sl = slice(kc * W3_N, (kc + 1) * W3_N)
            TS("dve", w3[:, sl], w3[:, sl], vec[:, V_NMW + kc:V_NMW + kc + 1], ALU.mult, ["w3_%d" % kc, "vec"], ["w3_%d" % kc])
            sl = slice(kc * 1024, (kc + 1) * 1024)
            TS("dve", wro[:, sl], wro[:, sl], vec[:, V_GNW + kc:V_GNW + kc + 1], ALU.mult, ["wro_%d" % kc, "vec"], ["wro_%d" % kc])
        W3T = ["w3_%d" % kc for kc in range(8)]
        WRT = ["wro_%d" % kc for kc in range(8)]
        CP("pool", Sf_bf, Sf, ["Sf"], ["Sf_bf"])
        Sf3 = v3(Sf, 4)
        Sfb3 = v3(Sf_bf, 4)
        PT3 = v3(PT, 8)
        N3A = int(os.environ.get("K_S3N", NT))

        def front3a(c):
            xb, xbt = xbuf[c % NXB], "xbuf%d" % (c % NXB)
            rb, rbt = ropeb[c % NXB], "ropeb%d" % (c % NXB)
            xsb, xst = xs[c % 2], "xs%d" % (c % 2)
            u, ut = uT[c % 2], "uT%d" % (c % 2)
            DMA("sp", xb, xo[c * 128:(c + 1) * 128, :], (), [xbt], xbt)
            DMA("sp", rb, ropeR_own_d[:, c, :], (), [rbt], rbt)
            DMA("sp", sbl[c % 2], SbAll_d[c, :, :], [("SbAll", c)], ["sbl%d" % (c % 2)], "sbl%d" % (c % 2))
            ACT(xsb, xb, AF.Identity, [xbt], [xst], scale=rstd_own[:, c:c + 1])
            tb = bankb(0)
            for kc in range(8):
                TR(tb[:, kc * 128:(kc + 1) * 128], xsb[:, kc * 128:(kc + 1) * 128], identb, [xst, "identb"], [B(0)])
            CP("act", u, tb[:, 0:1024], [B(0)], [ut])

        def tail3a(c):
            tb7 = bankb(7)
            for kc in range(8):
                TR(tb7[:, kc * 128:(kc + 1) * 128], z[:, kc * 128:(kc + 1) * 128], identb, ["z", "identb"], [B(7)])
            CP("act", zT, tb7[:, 0:1024], [B(7)], ["zT"])
            zT3 = v3(zT, 8)
            for hf in range(2):
                for kc in range(8):
                    MM(bank(5 + hf), zT3[:, kc, :], wro3[:, kc, hf * 512:(hf + 1) * 512], kc == 0, kc == 7, ["zT", WRT[kc]], [B(5 + hf)])
            mb = m1b[c % 2]
            mbt = "m1b%d" % (c % 2)
            for hf in range(2):
                TT("dve", mb[:, hf * 512:(hf + 1) * 512], bank(5 + hf), sgr[:, hf * 512:(hf + 1) * 512], ALU.mult, [B(5 + hf), "sgr"], [mbt])
            DMA("pool", m1_d[c * 128:(c + 1) * 128, :], mb, [mbt], [("m1", c)], mbt)

        if N3A:
            front3a(0)
        for c in range(N3A):
            rb = ropeb[c % NXB]
            rbt = "ropeb%d" % (c % NXB)
            u = uT[c % 2]
            ut = "uT%d" % (c % 2)
            u3 = v3(u, 8)
            sbc, sbct = sbl[c % 2], "sbl%d" % (c % 2)
            sbc3 = v3(sbc, 4)
            for (bk_i, c0) in ((1, W3_RQ), (2, W3_RQS), (3, W3_RK), (4, W3_RKS)):
                for hp in range(4):
                    for kc in range(8):
                        MM(bank(bk_i)[:, hp * 128:(hp + 1) * 128], w33[:, kc, c0 + hp * 128:c0 + (hp + 1) * 128], u3[:, kc, :],
                           kc == 0, kc == 7, [ut, W3T[kc]], [B(bk_i)])
            for hf in range(2):
                for kc in range(8):
                    MM(bank(5 + hf), u3[:, kc, :], w33[:, kc, W3_RV + hf * 512:W3_RV + (hf + 1) * 512], kc == 0, kc == 7,
                       [ut, W3T[kc]], [B(5 + hf)])
            cosR = rb[:, 0:128].unsqueeze(1).to_broadcast([128, 4, 128])
            sinR = rb[:, 128:256].unsqueeze(1).to_broadcast([128, 4, 128])
            TT("dve", v3(t1, 4), v3(bank(1), 4), cosR, ALU.mult, [B(1), rbt], ["t1"])
            TT("dve", v3(t2, 4), v3(bank(2), 4), sinR, ALU.mult, [B(2), rbt], ["t2"])
            qTd3 = v3(qTd, 4)
            TT("pool", qTd3[0:64, :, 0:128], v3(t1, 4)[0:64, :, :], v3(t2, 4)[0:64, :, :], ALU.add, ["t1", "t2", "qT"], ["qT"])
            TT("pool", qTd3[64:128, :, 128:256], v3(t1, 4)[64:128, :, :], v3(t2, 4)[64:128, :, :], ALU.add, ["t1", "t2", "qT"], ["qT"])
            TT("dve", v3(t1, 4), v3(bank(3), 4), cosR, ALU.mult, [B(3), rbt, "qT"], ["t1"])
            TT("dve", v3(t2, 4), v3(bank(4), 4), sinR, ALU.mult, [B(4), rbt, "qT"], ["t2"])
            TT("pool", kT, t1, t2, ALU.add, ["t1", "t2"], ["kT"])
            TT("dve", qfd, qTd, wqfd, ALU.mult, ["qT", "wq"], ["qf"])
            TT("pool", qbd, qTd, wqbd, ALU.mult, ["qT", "wq"], ["qb"])
            CP("act", vtok[:, 0:512], bank(5), [B(5)], ["vtok"])
            CP("act", vtok[:, 512:1024], bank(6), [B(6)], ["vtok"])
            if c >= 1:
                tail3a(c - 1)
            for hp in range(4):
                MM(bank(1 + hp // 2)[:, (hp % 2) * 256:(hp % 2) * 256 + 256], kT[:, hp * 128:(hp + 1) * 128], qTd3[:, hp, :],
                   True, True, ["kT", "qT"], [B(1 + hp // 2)])
            hb = bankb(7)
            for hp in range(4):
                TR(hb[:, hp * 128:(hp + 1) * 128], kT[:, hp * 128:(hp + 1) * 128], identb, ["kT", "identb"], [B(7)])
            for hf in range(2):
                for kc in range(8):
                    MM(bank(3 + hf), u3[:, kc, :], w33[:, kc, W3_RG + hf * 512:W3_RG + (hf + 1) * 512], kc == 0, kc == 7,
                       [ut, W3T[kc]], [B(3 + hf)])
            for hf in range(2):
                TT("dve", PT[:, hf * 512:(hf + 1) * 512], bank(1 + hf), DT[:, hf * 512:(hf + 1) * 512], ALU.mult, [B(1 + hf), "DT"], ["PT"])
            wfbb = wfb[:, 0:8].unsqueeze(2).to_broadcast([128, 8, 64])
            TT("dve", kwf.rearrange("p (h d) -> p h d", h=8), hb[:, 0:512].rearrange("p (h d) -> p h d", h=8), wfbb, ALU.mult,
               [B(7), "wfb"], ["kwf"])
            for hf in range(2):
                sl = slice(hf * 512, (hf + 1) * 512)
                ACT(sig[:, sl], bank(3 + hf), AF.Sigmoid, [B(3 + hf)], ["sig"])
                TT("dve", silu[:, sl], bank(3 + hf), sig[:, sl], ALU.mult, [B(3 + hf), "sig"], ["sig"])
            for h in range(8):
                hp, hb0 = h // 2, (h % 2) * 64
                o_ap = bank(5 + h // 4)[:, (h % 4) * 128:(h % 4 + 1) * 128]
                MM(o_ap, PT3[:, h, :], vtok[:, h * 128:(h + 1) * 128], True, False, ["PT", "vtok"], [B(5 + h // 4)])
                pc = slice((h % 2) * 128, (h % 2) * 128 + 128)
                MM(o_ap, v3(qfd, 4)[:, hp, pc], Sfb3[:, hp, :], False, False, ["qf", "Sf_bf"], [B(5 + h // 4)])
                MM(o_ap, v3(qbd, 4)[:, hp, pc], sbc3[:, hp, :], False, True, ["qb", sbct], [B(5 + h // 4)])
            for hp in range(4):
                bk = bank(1 + hp // 2)
                MM(bk[:, (hp % 2) * 256:(hp % 2) * 256 + 256], kwf[:, hp * 128:(hp + 1) * 128], vtok[:, hp * 256:(hp + 1) * 256],
                   True, True, ["kwf", "vtok"], [B(1 + hp // 2)])
            TT("pool", Sf, Sf, GfT, ALU.mult, ["Sf", "GT", "Sf_bf"], ["Sf"])
            for hh in range(2):
                bk = v3(bank(1 + hh), 2)
                TT("dve", Sf3[0:64, 2 * hh:2 * hh + 2, :], Sf3[0:64, 2 * hh:2 * hh + 2, :], bk[0:64, :, 0:128], ALU.add, ["Sf", B(1 + hh)], ["Sf"])
                TT("dve", Sf3[64:128, 2 * hh:2 * hh + 2, :], Sf3[64:128, 2 * hh:2 * hh + 2, :], bk[64:128, :, 128:256], ALU.add, ["Sf", B(1 + hh)], ["Sf"])
            CP("pool", Sf_bf, Sf, ["Sf"], ["Sf_bf"])
            for hf in range(2):
                for kc in range(8):
                    MM(bank(3 + hf), u3[:, kc, :], w33[:, kc, W3_GR + hf * 512:W3_GR + (hf + 1) * 512], kc == 0, kc == 7,
                       [ut, W3T[kc]], [B(3 + hf)])
            if c + 1 < N3A:
                front3a(c + 1)
            for hf in range(2):
                RED(gst[:, hf * 4:(hf + 1) * 4], v3(bank(5 + hf), 4), [B(5 + hf)], ["gs1"])
                ACT(sq[:, hf * 512:(hf + 1) * 512], bank(5 + hf), AF.Square, [B(5 + hf)], ["sq"])
            RED(gst[:, 8:16], v3(sq, 8), ["sq"], ["gs2"])
            TS("dve", gst[:, 16:24], gst[:, 0:8], 1.0 / 128, ALU.mult, ["gs1"], ["gmean"])
            TT("dve", gst[:, 48:56], gst[:, 16:24], gst[:, 16:24], ALU.mult, ["gmean"], ["gmsq"])
            TS("dve", gst[:, 24:32], gst[:, 8:16], 1.0 / 128, ALU.mult, ["gs2"], ["gvar"], s2=GN_EPS, op1=ALU.add)
            TT("dve", gst[:, 24:32], gst[:, 24:32], gst[:, 48:56], ALU.subtract, ["gvar", "gmsq"], ["gvar"])
            TT("pool", gst[:, 32:40], gst[:, 24:32], mhalf[:, 0:8], ALU.pow, ["gvar", "mhalf"], ["grstd"])
            TT("pool", gst[:, 40:48], gst[:, 16:24], gst[:, 32:40], ALU.mult, ["gmean", "grstd"], ["gnmr"])
            TS("pool", gst[:, 40:48], gst[:, 40:48], -1.0, ALU.mult, ["gnmr"], ["gnmr"], s2=1.0, op1=ALU.mult)
            for h in range(8):
                ACT(zn[:, h * 128:(h + 1) * 128], bank(5 + h // 4)[:, (h % 4) * 128:(h % 4 + 1) * 128], AF.Identity,
                    [B(5 + h // 4), "grstd", "gnmr", "gs2"], ["sq"], scale=gst[:, 32 + h:33 + h], bias=gst[:, 40 + h:41 + h])
            TT("dve", z, zn, silu, ALU.mult, ["sq", "sig"], ["z"])
            for hf in range(2):
                ACT(sgr[:, hf * 512:(hf + 1) * 512], bank(3 + hf), AF.Sigmoid, [B(3 + hf)], ["sgr"])
        if N3A:
            tail3a(N3A - 1)
        S.barrier()

    if stages >= 3:
        A.pos = P0_END
        Ymla = A.alloc(NT * 512, BF16)
        ckvT = A.alloc(2 * NKEY, BF16)
        ckvT3 = v3(ckvT, 2)
        Kb = [A.alloc(NKEY, BF16), A.alloc(NKEY, BF16)]
        wuk = A.alloc(2 * 512, BF16)
        wuv = A.alloc(2 * 512, BF16)
        Vb = [A.alloc(NSLOT * 128, BF16) for _ in range(2)]
        Qb = [A.alloc(TOK, BF16) for _ in range(2)]
        Pb = [A.alloc(1024, BF16) for _ in range(2)]
        oT = A.alloc(512, F32)
        rcp = A.alloc(4, F32)
        CKD = [("ckvT_d", g) for g in range(17)]
        KRD = [("krope_d", g) for g in range(17)]
        for kc in range(2):
            DMA("sp", ckvT[:, kc * NKEY:(kc + 1) * NKEY], ckvT_d[:, kc * NKEYP:kc * NKEYP + NKEY], CKD, ["ckvT%d" % kc], "ckvT%d" % kc)
        for i in range(2):
            DMA("sp", Kb[i][64:96, :], krope_d[:, 0:NKEY], KRD, [("Kr", i)], "Kr%d" % i)
            Vi3 = v3(Vb[i], NSLOT)
            MSET("pool", Vb[i], 0.0, [("V", i)])
            MSET("pool", Vi3[:, :, 64:65], 1.0, [("V", i)])
        for kc in range(2):
            LOAD("pool", wuk[:, kc * 512:(kc + 1) * 512], wuk_d[:, kc, :], ["wuk"])
            LOAD("pool", wuv[:, kc * 512:(kc + 1) * 512], wuv_d[:, kc, :], ["wuv"])
        for kc in range(2):
            sl = slice(kc * 512, (kc + 1) * 512)
            TS("dve", wuk[:, sl], wuk[:, sl], vec[:, V_KVNW + kc:V_KVNW + kc + 1], ALU.mult, ["wuk", "vec"], ["wuk"])
            TS("dve", wuv[:, sl], wuv[:, sl], vec[:, V_KVNW + kc:V_KVNW + kc + 1], ALU.mult, ["wuv", "vec"], ["wuv"])
        Ym3 = v3(Ymla, NT)

        def gen_groups(h):
            hbuf = h % 2
            Kh, Vh, Qh = Kb[hbuf], Vb[hbuf], Qb[hbuf]
            Kt_, Vt_, Qt_ = ("K", hbuf), ("V", hbuf), ("Q", hbuf)
            Vh3 = v3(Vh, NSLOT)
            out = []

            def qload():
                DMA("sp", Qh[0:96, :], Qs_d[h, :, :], [("Qs", g) for g in range(8)], [Qt_], "Qb%d" % hbuf)
            out.append(qload)
            for n in range(17):
                def kgen(n=n):
                    n0 = n * 512
                    w = 512 if n < 16 else 128
                    for kc in range(2):
                        MM(bank(7)[0:64, 0:w], wuk[:, kc * 512 + h * 64:kc * 512 + (h + 1) * 64], ckvT3[:, kc, n0:n0 + w],
                           kc == 0, kc == 1, ["wuk", "ckvT%d" % kc], [B(7)])
                    CP("dve", Kh[0:64, n0:n0 + w], bank(7)[0:64, 0:w], [B(7)], [Kt_])
                out.append(kgen)
            for g0 in range(0, NSLOT, 8):
                def vgen(g0=g0):
                    ng = min(8, NSLOT - g0)
                    for j in range(ng):
                        s_ = g0 + j
                        for kc in range(2):
                            MM(bank(7)[:, j * 64:(j + 1) * 64], ckvT3[:, kc, s_ * 128:(s_ + 1) * 128],
                               wuv[:, kc * 512 + h * 64:kc * 512 + (h + 1) * 64], kc == 0, kc == 1, ["wuv", "ckvT%d" % kc], [B(7)])
                    CP("dve", Vh3[:, g0:g0 + ng, 0:64], bank(7)[:, 0:ng * 64].rearrange("p (a b) -> p a b", a=ng), [B(7)], [Vt_])
                out.append(vgen)
            return out

        for g_ in gen_groups(0):
            g_()
        units = []
        for qb in range(8):
            for s0 in range(0, 64, 2):
                units.append((qb, (s0, s0 + 1)))
            units.append((qb, (64,)))
        nu = len(units)
        it = 0
        for h in range(8):
            hbuf = h % 2
            Kh, Vh, Qh = Kb[hbuf], Vb[hbuf], Qb[hbuf]
            Kt_, Vt_, Qt_ = ("K", hbuf), ("V", hbuf), ("Q", hbuf)
            Vh3 = v3(Vh, NSLOT)
            pending = gen_groups(h + 1) if h < 7 else []
            stride = max(1, nu // (len(pending) + 1)) if pending else nu

            def s_mm(i):
                qb, sl = units[i]
                pb_i = (it + i) % 2
                pst = PS2[pb_i]
                for j, s_ in enumerate(sl):
                    kt = 128 if s_ < 64 else NMETA
                    MM(pst[0:kt, j * 512:(j + 1) * 512], Kh[0:96, s_ * 128:s_ * 128 + kt], Qh[0:96, qb * 512:(qb + 1) * 512], True, True,
                       [Kt_, Qt_, ("Kr", hbuf)], [B(2 * pb_i + j)])
                kt = 128 if sl[0] < 64 else NMETA
                if os.environ.get("K_WIDEEXP"):
                    w = 512 * len(sl)
                    ACT(Pb[pb_i][0:kt, 0:w], pst[0:kt, 0:w], AF.Exp, [B(2 * pb_i + j) for j in range(len(sl))], ["Pb%d" % pb_i], scale=SC_ATT)
                else:
                    for j in range(len(sl)):
                        ACT(Pb[pb_i][0:kt, j * 512:(j + 1) * 512], pst[0:kt, j * 512:(j + 1) * 512], AF.Exp, [B(2 * pb_i + j)],
                            [("Pb", pb_i, j)], scale=SC_ATT)

            s_mm(0)
            deferred = []
            for i in range(nu):
                if i + 1 < nu:
                    s_mm(i + 1)
                qb, sl = units[i]
                pb_i = (it + i) % 2
                ob = 4 + qb % 2
                for j, s_ in enumerate(sl):
                    kt = 128 if s_ < 64 else NMETA
                    MM(bank(ob)[:, :], Vh3[0:kt, s_, :], Pb[pb_i][0:kt, j * 512:(j + 1) * 512], s_ == 0, s_ == NSLOT - 1,
                       [Vt_, "Pb%d" % pb_i, ("Pb", pb_i, j)], [B(ob)])
                if pending and i % stride == stride - 1:
                    pending.pop(0)()
                if deferred and deferred[0][0] <= i:
                    deferred.pop(0)[1]()
                if sl[-1] == NSLOT - 1:
                    CP("dve", oT[0:65, :], bank(ob)[0:65, :], [B(ob)], ["oT"])

                    def norm(qb=qb):
                        for qi in range(4):
                            TR(bank(6)[:, qi * 65:(qi + 1) * 65], oT[0:65, qi * 128:(qi + 1) * 128], identf[0:65, 0:65], ["oT", "identf"], [B(6)])
                        pn = bank(6)[:, 0:260].rearrange("p (a b) -> p a b", a=4)
                        S.op("dve", lambda e, o_=rcp[:, 0:4], i_=pn[:, :, 64]: e.reciprocal(out=o_, in_=i_), [B(6)], ["rcp"])
                        TT("dve", Ym3[:, qb * 4:(qb + 1) * 4, h * 64:(h + 1) * 64], pn[:, :, 0:64],
                           rcp[:, 0:4].unsqueeze(2).to_broadcast([128, 4, 64]), ALU.mult, [B(6), "rcp"], [("Ymla", qb)])
                    deferred.append((i + 3, norm))
            while deferred:
                deferred.pop(0)[1]()
            while pending:
                pending.pop(0)()
            it += nu
        if dbg:
            ydbg = nc.dram_tensor("ymla_dbg", [128, NT * 512], BF16, kind="ExternalOutput").ap()
            DMA("sp", ydbg, Ymla, [("Ymla", q_) for q_ in range(8)], ["ydbg"], "ydbg")
        S.barrier()

    def alloc3c():
        A.pos = P0_END
        L = {}
        L["nfin"] = A.alloc(D, F32)
        L["hbuf"] = [A.alloc(D, F32) for _ in range(2)]
        L["hs"] = [A.alloc(D, BF16) for _ in range(2)]
        L["uT2"] = [A.alloc(8 * 512, BF16) for _ in range(2)]
        L["actT"] = A.alloc(NFC * 512, BF16)
        L["sgb"] = [A.alloc(512, F32) for _ in range(2)]
        L["h2"] = A.alloc(D, F32)
        L["ob"] = [A.alloc(D, F32) for _ in range(2)]
        L["st"] = A.alloc(8, F32)
        L["wg0"] = A.pos
        L["wg"] = A.alloc(8 * DFF, BF16)
        L["wu0"] = A.pos
        L["wu"] = A.alloc(8 * DFF, BF16)
        L["wd0"] = A.pos
        L["wd"] = A.alloc(NFC * 1024, BF16)
        return L

    pre3c = set()

    def ffn_weight_loaders(L, min_col):
        out = []
        wg, wu, wd = L["wg"], L["wu"], L["wd"]
        for kc in range(8):
            if ("wg", kc) not in pre3c and L["wg0"] + kc * DFF >= min_col:
                def f(kc=kc):
                    sl = slice(kc * DFF, (kc + 1) * DFF)
                    LOAD("pool", wg[:, sl], wg_d[:, kc, :], ["wg_%d" % kc])
                    TS("dve", wg[:, sl], wg[:, sl], vec[:, V_NFW + kc:V_NFW + kc + 1], ALU.mult, ["wg_%d" % kc, "vec"], ["wg_%d" % kc])
                pre3c.add(("wg", kc))
                out.append(f)
            if ("wu", kc) not in pre3c and L["wu0"] + kc * DFF >= min_col:
                def f(kc=kc):
                    sl = slice(kc * DFF, (kc + 1) * DFF)
                    LOAD("pool", wu[:, sl], wu_d[:, kc, :], ["wu_%d" % kc])
                    TS("dve", wu[:, sl], wu[:, sl], vec[:, V_NFW + kc:V_NFW + kc + 1], ALU.mult, ["wu_%d" % kc, "vec"], ["wu_%d" % kc])
                pre3c.add(("wu", kc))
                out.append(f)
        for fc in range(NFC):
            if ("wd", fc) not in pre3c and L["wd0"] + fc * 1024 >= min_col:
                def f(fc=fc):
                    LOAD("pool", wd[:, fc * 1024:(fc + 1) * 1024], wd_d[:, fc, :], ["wd_%d" % fc])
                pre3c.add(("wd", fc))
                out.append(f)
        return out

    if stages >= 4:
        A.pos = P0_END
        Ymla = A.alloc(NT * 512, BF16)
        Ym3 = v3(Ymla, NT)
        wgm = A.alloc(8 * 1024, BF16)
        wmo = A.alloc(4 * 1024, BF16)
        wo = A.alloc(8 * 1024, BF16)
        wgm3, wmo3, wo3_ = v3(wgm, 8), v3(wmo, 4), v3(wo, 8)
        xbuf = [A.alloc(D, F32) for _ in range(2)]
        m1l = [A.alloc(D, BF16) for _ in range(2)]
        xs = [A.alloc(D, BF16) for _ in range(2)]
        uT = [A.alloc(D, BF16) for _ in range(2)]
        sg = A.alloc(1024, F32)
        ymT = A.alloc(512, BF16)
        mg = A.alloc(1024, F32)
        mg2 = A.alloc(1024, BF16)
        mT = A.alloc(1024, BF16)
        h1b = [A.alloc(D, F32) for _ in range(2)]
        for kc in range(8):
            LOAD("pool", wgm[:, kc * 1024:(kc + 1) * 1024], wgm_d[:, kc, :], ["wgm_%d" % kc])
            LOAD("pool", wo[:, kc * 1024:(kc + 1) * 1024], wo_d[:, kc, :], ["wo_%d" % kc])
        for kc in range(4):
            LOAD("pool", wmo[:, kc * 1024:(kc + 1) * 1024], wmo_d[:, kc, :], ["wmo"])
        for kc in range(8):
            sl = slice(kc * 1024, (kc + 1) * 1024)
            TS("dve", wgm[:, sl], wgm[:, sl], vec[:, V_NMW + kc:V_NMW + kc + 1], ALU.mult, ["wgm_%d" % kc, "vec"], ["wgm_%d" % kc])
        xbuf = xbuf + [A.alloc(D, F32)]

        def front3b(c):
            xb, xbt = xbuf[c % 3], "xbuf%d" % (c % 3)
            ml, mlt = m1l[c % 2], "m1l%d" % (c % 2)
            xsb, xst = xs[c % 2], "xs%d" % (c % 2)
            u, ut = uT[c % 2], "uT%d" % (c % 2)
            DMA("sp", xb, xo[c * 128:(c + 1) * 128, :], (), [xbt], xbt)
            DMA("sp", ml, m1_d[c * 128:(c + 1) * 128, :], [("m1", c)], [mlt], mlt)
            ACT(xsb, xb, AF.Identity, [xbt], [xst], scale=rstd_own[:, c:c + 1])
            tb = bankb(0)
            for kc in range(8):
                TR(tb[:, kc * 128:(kc + 1) * 128], xsb[:, kc * 128:(kc + 1) * 128], identb, [xst, "identb"], [B(0)])
            CP("dve", u, tb[:, 0:1024], [B(0)], [ut])

        def tailT3b(c):
            tb6 = bankb(6)
            for kc in range(8):
                TR(tb6[:, kc * 128:(kc + 1) * 128], mg2[:, kc * 128:(kc + 1) * 128], identb, ["mg2", "identb"], [B(6)])
            CP("dve", mT, tb6[:, 0:1024], [B(6)], ["mT"])

        def tailO3b(c):
            xb, xbt = xbuf[c % 3], "xbuf%d" % (c % 3)
            hb_, hbt = h1b[c % 2], "h1b%d" % (c % 2)
            mT3 = v3(mT, 8)
            for hf in range(2):
                for kc in range(8):
                    MM(bank(3 + hf), mT3[:, kc, :], wo3_[:, kc, hf * 512:(hf + 1) * 512], kc == 0, kc == 7, ["mT", "wo_%d" % kc], [B(3 + hf)])
            for hf in range(2):
                sl = slice(hf * 512, (hf + 1) * 512)
                TT("dve", hb_[:, sl], bank(3 + hf), xb[:, sl], ALU.add, [B(3 + hf), xbt], [hbt])
            DMA("pool", h1_d[c * 128:(c + 1) * 128, :], hb_, [hbt], [("h1", c)], hbt)

        prefetch = []
        if stages >= 5:
            end3b = A.pos
            prefetch = ffn_weight_loaders(alloc3c(), end3b)
            A.pos = end3b
        front3b(0)
        YB = (5, 7)
        for c in range(NT):
            for _ in range(2):
                if prefetch and c >= 1:
                    prefetch.pop(0)()
            ml, mlt = m1l[c % 2], "m1l%d" % (c % 2)
            u, ut = uT[c % 2], "uT%d" % (c % 2)
            u3 = v3(u, 8)
            for hf in range(2):
                for kc in range(8):
                    MM(bank(1 + hf), u3[:, kc, :], wgm3[:, kc, hf * 512:(hf + 1) * 512], kc == 0, kc == 7, [ut, "wgm_%d" % kc], [B(1 + hf)])
            if c >= 1:
                tailT3b(c - 1)
            if c + 1 < NT:
                front3b(c + 1)
            tb7 = bankb(7)
            for kc in range(4):
                TR(tb7[:, kc * 128:(kc + 1) * 128], Ym3[:, c, kc * 128:(kc + 1) * 128], identb, [("Ymla", c // 4), "identb"], [B(7)])
            CP("dve", ymT, tb7[:, 0:512], [B(7)], ["ymT"])
            if c >= 1:
                tailO3b(c - 1)
            ymT3 = v3(ymT, 4)
            for hf in range(2):
                for kc in range(4):
                    MM(bank(YB[hf]), ymT3[:, kc, :], wmo3[:, kc, hf * 512:(hf + 1) * 512], kc == 0, kc == 3, ["ymT", "wmo"], [B(YB[hf])])
            for hf in range(2):
                sl = slice(hf * 512, (hf + 1) * 512)
                ACT(sg[:, sl], bank(1 + hf), AF.Sigmoid, [B(1 + hf)], ["sg"])
                TT("dve", mg[:, sl], bank(YB[hf]), sg[:, sl], ALU.mult, [B(YB[hf]), "sg"], ["mg"])
            TT("pool", mg2, mg, ml, ALU.add, ["mg", mlt], ["mg2"])
        tailT3b(NT - 1)
        tailO3b(NT - 1)
        while prefetch:
            prefetch.pop(0)()
        S.barrier()

    if stages >= 5:
        L = alloc3c()
        wg3, wu3, wd3 = v3(L["wg"], 8), v3(L["wu"], 8), v3(L["wd"], NFC)
        nfin, hbuf_, hs, actT, sgb, ob_, st = L["nfin"], L["hbuf"], L["hs"], L["actT"], L["sgb"], L["ob"], L["st"]
        h2b, h2t = L["h2"], "h2_0"
        uT2 = L["uT2"]
        actT3 = v3(actT, NFC)
        LOAD("sp", nfin, nfin_d, ["nfin"])
        for f_ in ffn_weight_loaders(L, 0):
            f_()
        NBLK = TOK // 512
        hcnt = [0]

        def front3c_a(blk, t):
            c = blk * 4 + t
            k = hcnt[0] % 2
            hcnt[0] += 1
            hbf, hbft = hbuf_[k], "hbuf%d" % k
            hsb, hst = hs[c % 2], "hs%d" % (c % 2)
            DMA("sp", hbf, h1_d[c * 128:(c + 1) * 128, :], [("h1", c)], [hbft], hbft)
            ACT(hsb, hbf, AF.Square, [hbft], [hst, "f_ss"], accum=st[:, 0:1])
            TS("dve", st[:, 0:1], st[:, 0:1], 1.0 / D, ALU.mult, ["f_ss"], ["f_ss"], s2=RMS_EPS, op1=ALU.add)
            TT("pool", st[:, 1:2], st[:, 0:1], mhalf[:, 0:1], ALU.pow, ["f_ss", "mhalf"], ["f_rstd"])
            ACT(hsb, hbf, AF.Identity, [hbft, "f_rstd"], [hst], scale=st[:, 1:2])

        def front3c_b(blk, t):
            c = blk * 4 + t
            hsb, hst = hs[c % 2], "hs%d" % (c % 2)
            u23 = v3(uT2[blk % 2], 8)
            tb = bankb(t % 2)
            for kc in range(8):
                TR(tb[:, kc * 128:(kc + 1) * 128], hsb[:, kc * 128:(kc + 1) * 128], identb, [hst, "identb"], [B(t % 2)])
            CP("dve", u23[:, :, t * 128:(t + 1) * 128], tb[:, 0:1024].rearrange("p (a b) -> p a b", a=8), [B(t % 2)], ["uT2_%d" % (blk % 2)])

        for t in range(4):
            front3c_a(0, t)
            front3c_b(0, t)
        for blk in range(NBLK):
            uT23 = v3(uT2[blk % 2], 8)
            utt = "uT2_%d" % (blk % 2)
            for fc in range(NFC):
                gb_, ub_ = 2 + 2 * (fc % 2), 3 + 2 * (fc % 2)
                for kc in range(8):
                    MM(bank(gb_), wg3[:, kc, fc * 128:(fc + 1) * 128], uT23[:, kc, :], kc == 0, kc == 7, [utt, "wg_%d" % kc], [B(gb_)])
                for kc in range(8):
                    MM(bank(ub_), wu3[:, kc, fc * 128:(fc + 1) * 128], uT23[:, kc, :], kc == 0, kc == 7, [utt, "wu_%d" % kc], [B(ub_)])
                sgt = "sgb%d" % (fc % 2)
                ACT(sgb[fc % 2], bank(gb_), AF.Sigmoid, [B(gb_)], [sgt])
                TT("dve", sgb[fc % 2], bank(gb_), sgb[fc % 2], ALU.mult, [B(gb_), sgt], [sgt])
                TT("dve", actT3[:, fc, :], bank(ub_), sgb[fc % 2], ALU.mult, [B(ub_), sgt], [("actT", fc)])
                if blk + 1 < NBLK and fc >= 4 and fc % 4 == 0 and (fc - 4) // 4 < 4:
                    front3c_a(blk + 1, (fc - 4) // 4)
                if blk + 1 < NBLK and fc >= 6 and fc % 4 == 2 and (fc - 6) // 4 < 4:
                    front3c_b(blk + 1, (fc - 6) // 4)
            for t in range(4):
                c = blk * 4 + t
                d0 = 6
                for hf in range(2):
                    for fc in range(NFC):
                        MM(bank(d0 + hf), actT3[:, fc, t * 128:(t + 1) * 128], wd3[:, fc, hf * 512:(hf + 1) * 512], fc == 0, fc == NFC - 1,
                           [("actT", fc), "wd_%d" % fc], [B(d0 + hf)])
                k = hcnt[0] % 2
                hcnt[0] += 1
                hbf, hbft = hbuf_[k], "hbuf%d" % k
                DMA("sp", hbf, h1_d[c * 128:(c + 1) * 128, :], [("h1", c)], [hbft], hbft)
                obb, obt = ob_[c % 2], "ob%d" % (c % 2)
                for hf in range(2):
                    sl = slice(hf * 512, (hf + 1) * 512)
                    TT("dve", h2b[:, sl], bank(d0 + hf), hbf[:, sl], ALU.add, [B(d0 + hf), hbft], [h2t])
                ACT(obb, h2b, AF.Square, [h2t], [obt, "o_ss"], accum=st[:, 2:3])
                TS("dve", st[:, 2:3], st[:, 2:3], 1.0 / D, ALU.mult, ["o_ss"], ["o_ss"], s2=RMS_EPS, op1=ALU.add)
                TT("pool", st[:, 3:4], st[:, 2:3], mhalf[:, 0:1], ALU.pow, ["o_ss", "mhalf"], ["o_rstd"])
                STT(obb, h2b, st[:, 3:4], nfin, ALU.mult, ALU.mult, [h2t, "o_rstd", "nfin", obt], [obt])
                DMA("pool", out_d[c * 128:(c + 1) * 128, :], obb, [obt], [("out", c)], obt)

    S.emit(nc)
    es.close()
    return nc


def _kc_layout(w):
    K, N = w.shape
    return np.ascontiguousarray(w.reshape(K // 128, 128, N).transpose(1, 0, 2))


def _swap_cols(w, hd):
    K, N = w.shape
    w4 = w.reshape(K, N // hd, 2, hd // 2)
    return np.ascontiguousarray(w4[:, :, ::-1, :]).reshape(K, N)


def _rope_tab(pos, half, base=10000.0):
    inv = (np.float32(base) ** (-(np.arange(half, dtype=np.float32) / np.float32(half)))).astype(np.float32)
    ang = (pos.astype(np.float32)[:, None] * inv[None, :]).astype(np.float32)
    return np.cos(ang.astype(np.float64)).astype(np.float32), np.sin(ang.astype(np.float64)).astype(np.float32)


_PROGRAM = {}


def _prep_inputs(x, meta_tokens, norm_mix_w, w_in, ret_decay_fwd, ret_decay_bwd, ret_gn_w, w_ret_out,
                 mla_q_norm_w, w_uq, mla_kv_norm_w, w_uk, w_uv, w_mla_out, w_o, norm_ffn_w,
                 w_ffn_gate, w_ffn_up, w_ffn_down, norm_final_w):
    f = np.float32
    x = np.asarray(x, f)
    W = np.asarray(w_in, f)[0]
    rq, rk, rv, rg = W[:, 0:512], W[:, 512:1024], W[:, 1024:2048], W[:, 2048:3072]
    cq, ckv, kr = W[:, 3072:3456], W[:, 3456:3712], W[:, 3712:3744]
    gret, gmla = W[:, 3744:4768], W[:, 4768:5792]
    rks, rqs = _swap_cols(rk, 64), _swap_cols(rq, 64)
    shared = {
        "w1": _kc_layout(np.concatenate([cq, ckv, kr, rk, rks, rv], axis=1)),
        "w3a": _kc_layout(np.concatenate([rq, rqs, rk, rks, rv, rg, gret], axis=1)),
        "wgm": _kc_layout(gmla),
        "wmo": _kc_layout(np.asarray(w_mla_out, f)[0]),
        "wo": _kc_layout(np.asarray(w_o, f)[0]),
        "wuq": _kc_layout(np.asarray(w_uq, f)[0]),
        "wuk": _kc_layout(np.asarray(w_uk, f)[0]),
        "wuv": _kc_layout(np.asarray(w_uv, f)[0]),
        "wro": _kc_layout(np.asarray(w_ret_out, f)[0]),
        "wg": _kc_layout(np.asarray(w_ffn_gate, f)[0]),
        "wu": _kc_layout(np.asarray(w_ffn_up, f)[0]),
        "wd": _kc_layout(np.asarray(w_ffn_down, f)[0]),
        "ident": np.eye(128, dtype=f),
        "nfin": np.ascontiguousarray(np.broadcast_to(np.asarray(norm_final_w, f)[None, :], (128, D))),
    }
    vec = np.zeros((128, NVEC), f)

    def pk(v):
        v = np.asarray(v, f).reshape(-1)
        return v.reshape(-1, 128).T

    vec[:, V_NMW:V_NMW + 8] = pk(norm_mix_w)
    vec[:, V_NFW:V_NFW + 8] = pk(norm_ffn_w)
    vec[:, V_GNW:V_GNW + 8] = pk(ret_gn_w)
    vec[:, V_QNW:V_QNW + 3] = pk(mla_q_norm_w)
    vec[:, V_KVNW:V_KVNW + 2] = pk(mla_kv_norm_w)
    df = np.asarray(ret_decay_fwd, f).reshape(8)
    db = np.asarray(ret_decay_bwd, f).reshape(8)
    par = (np.arange(128) >= 64).astype(np.int64)
    for hp in range(4):
        vec[:, V_DFP + hp] = df[2 * hp + par]
        vec[:, V_DBP + hp] = db[2 * hp + par]
    vec[:, V_DF8:V_DF8 + 8] = df[None, :]
    vec[:, V_DB8:V_DB8 + 8] = db[None, :]
    shared["vec"] = vec
    ctab = np.zeros((128, NCTAB), f)
    j = np.arange(128, dtype=f)[:, None]
    i = np.arange(128, dtype=f)[None, :]
    ctab[:, C_POS:C_POS + 128] = np.maximum(i - j, 0)
    ctab[:, C_NEG:C_NEG + 128] = np.maximum(j - i, 0)
    ctab[:, C_I1:C_I1 + 128] = np.broadcast_to(i + 1, (128, 128))
    ctab[:, C_128MI:C_128MI + 128] = np.broadcast_to(128 - i, (128, 128))
    ctab[:, C_127MJ] = 127 - j[:, 0]
    ctab[:, C_J] = j[:, 0]
    shared["ctab"] = ctab

    meta = np.asarray(meta_tokens, f)
    BIG = f(1.0e9)
    sgn = np.where((np.arange(128) % 64) < 32, -1.0, 1.0).astype(f)[:, None]
    fidx = (np.arange(128) % 64) % 32
    in_maps = []
    for core in range(8):
        b, half = core // 2, core % 2
        oth = 1 - half
        m = dict(shared)
        m["xo"] = np.ascontiguousarray(x[b, half * TOK:(half + 1) * TOK])
        xr = np.zeros((NOT_ * 128, D), f)
        xr[0:TOK] = x[b, oth * TOK:(oth + 1) * TOK]
        xr[TOK:TOK + NMETA] = meta
        m["xr"] = xr
        pos_own = (NMETA + half * TOK + np.arange(TOK)).astype(np.int64)
        pos_oth = np.zeros(NOT_ * 128, np.int64)
        pos_oth[0:TOK] = NMETA + oth * TOK + np.arange(TOK)
        pos_oth[TOK:TOK + NMETA] = np.arange(NMETA)
        valid_oth = np.zeros(NOT_ * 128, bool)
        valid_oth[0:TOK + NMETA] = True
        for nm, pos, ntile in (("ropeR_own", pos_own, NT), ("ropeR_oth", pos_oth, NOT_)):
            c, s = _rope_tab(pos, 32)
            cfm = c[:, fidx].T
            sfm = s[:, fidx].T * sgn
            tab = np.stack([cfm.reshape(128, ntile, 128), sfm.reshape(128, ntile, 128)], axis=2)
            m[nm] = np.ascontiguousarray(tab.reshape(128, ntile, 256)).astype(f)
        for nm, pos, ntile in (("tabM_own", pos_own, NT), ("tabM_oth", pos_oth, NOT_)):
            c, s = _rope_tab(pos, 16)
            tab = np.concatenate([c, c, s, s], axis=1).reshape(ntile, 128, 64).transpose(1, 0, 2)
            m[nm] = np.ascontiguousarray(tab).astype(f)
        own_first = NMETA + half * TOK
        own_last = own_first + TOK - 1
        dfw = np.where(valid_oth & (pos_oth < own_first), own_first - 1 - pos_oth, BIG).astype(f)
        dbw = np.where(valid_oth & (pos_oth > own_last), pos_oth - own_last - 1, BIG).astype(f)
        dist = np.concatenate([dfw.reshape(NOT_, 128).T, dbw.reshape(NOT_, 128).T], axis=1)
        m["dist"] = np.ascontiguousarray(dist).astype(f)
        in_maps.append(m)
    return in_maps


def kernel(**inputs):
    in_maps = _prep_inputs(**inputs)
    if "nc" not in _PROGRAM:
        _PROGRAM["nc"] = build_program()
    nc = _PROGRAM["nc"]
    res = run_bass_kernel_spmd(nc, in_maps, core_ids=list(range(8)))
    out = np.zeros((NB, SEQ, D), np.float32)
    for core in range(8):
        b, half = core // 2, core % 2
        out[b, half * TOK:(half + 1) * TOK] = res.results[core]["out"]
    return out
```

```python
from contextlib import ExitStack
import os
import numpy as np
import concourse.bass as bass
import concourse.mybir as mybir
from concourse.bass_utils import run_bass_kernel_spmd

F32 = mybir.dt.float32
BF16 = mybir.dt.bfloat16
AF = mybir.ActivationFunctionType
ALU = mybir.AluOpType
AX = mybir.AxisListType

D = 1024
SEQ = 8192
NB = 4
NMETA = 16
TOK = 4096
NT = 32
NOT_ = 33
NSLOT = 65
NKEY = NSLOT * 128
NKEYP = 17 * 512
DFF = 2816
NFC = DFF // 128
RMS_EPS = 1e-6
GN_EPS = 1e-5
SC_ATT = 96.0 ** -0.5

ENGS = ("pe", "act", "dve", "pool", "sp")
EPOCH = 24000


class _Op:
    __slots__ = ("eng", "fn", "signal", "deps", "dma", "dma_n", "sem", "cnt")

    def __init__(self, eng, fn, dma):
        self.eng = eng
        self.fn = fn
        self.signal = False
        self.deps = []
        self.dma = dma
        self.dma_n = 0
        self.sem = None
        self.cnt = 0


class Sched:
    def __init__(self):
        self.ops = []
        self.last_w = {}
        self.readers = {}
        self.dma_cnt = {}
        self._bar = []
        self._bar_seen = set()

    def op(self, eng, fn, reads=(), writes=(), dma=None):
        o = _Op(eng, fn, dma)
        deps = set()
        if eng not in self._bar_seen:
            self._bar_seen.add(eng)
            deps.update(self._bar)
        for t in reads:
            w = self.last_w.get(t)
            if w is not None:
                deps.add(w)
            if isinstance(t, str) and t[0] == "b" and t[1:].isdigit():
                for k, r in self.readers.get(t, {}).items():
                    if r.eng != eng:
                        deps.add(r)
        for t in writes:
            w = self.last_w.get(t)
            if w is not None:
                deps.add(w)
            for r in self.readers.get(t, {}).values():
                deps.add(r)
        if dma is not None:
            n = self.dma_cnt.get(dma, 0) + 1
            self.dma_cnt[dma] = n
            o.dma_n = n
        for d in deps:
            if d is o:
                continue
            if d.dma is None and d.eng == "pe" and eng == "pe" and dma is None:
                continue
            o.deps.append(d)
            if d.dma is None:
                d.signal = True
        for t in reads:
            rd = self.readers.setdefault(t, {})
            rd[eng if dma is None else ("dma", id(o))] = o
        for t in writes:
            self.last_w[t] = o
            self.readers[t] = {}
        self.ops.append(o)
        return o

    def barrier(self):
        last = {}
        for o in self.ops:
            last[o.eng if o.dma is None else ("dma", o.dma)] = o
        self._bar = list(last.values())
        self._bar_seen = set()
        for o in self._bar:
            if o.dma is None:
                o.signal = True

    def emit(self, nc, final_eng="sp"):
        per = {e: [] for e in ENGS}
        for o in self.ops:
            per[o.eng].append(o)
        nsems = {}
        for e in ENGS:
            c = 0
            ep = 0
            for o in per[e]:
                if o.signal and o.dma is None:
                    c += 1
                    if c > EPOCH:
                        ep += 1
                        c = 1
                    o.sem = (e, ep)
                    o.cnt = c
            nsems[e] = ep + 1
        with ExitStack() as es:
            sems = {}
            for e in ENGS:
                for ep in range(nsems[e]):
                    sems[(e, ep)] = es.enter_context(nc.semaphore(f"s_{e}_{ep}"))
            dsems = {}
            for k in self.dma_cnt:
                dsems[k] = es.enter_context(nc.semaphore("d_" + str(len(dsems))))
            block = es.enter_context(nc.Block())
            dma_cnt = self.dma_cnt

            def run(e, eng):
                waited = {}
                for o in per[e]:
                    need = {}
                    for d in o.deps:
                        if d.dma is not None:
                            key = ("d", d.dma)
                            v = (0, 16 * d.dma_n)
                        else:
                            key = ("e", d.eng)
                            v = (d.sem[1], d.cnt)
                        if v > need.get(key, (-1, -1)):
                            need[key] = v
                    for key, v in need.items():
                        if v <= waited.get(key, (-1, -1)):
                            continue
                        waited[key] = v
                        if key[0] == "d":
                            eng.wait_ge(dsems[key[1]], v[1])
                        else:
                            eng.wait_ge(sems[(key[1], v[0])], v[1])
                    ins = o.fn(eng)
                    if o.dma is not None:
                        ins.then_inc(dsems[o.dma], 16)
                    elif o.signal:
                        ins.then_inc(sems[o.sem], 1)
                if e == final_eng:
                    for k, n in dma_cnt.items():
                        if 16 * n > waited.get(("d", k), (-1, -1))[1]:
                            eng.wait_ge(dsems[k], 16 * n)

            @block.tensor
            def _(eng):
                run("pe", eng)

            @block.scalar
            def _(eng):
                run("act", eng)

            @block.vector
            def _(eng):
                run("dve", eng)

            @block.gpsimd
            def _(eng):
                run("pool", eng)

            @block.sync
            def _(eng):
                run("sp", eng)


class Arena:
    def __init__(self, ap, ncols):
        self.ap = ap
        self.n = ncols
        self.pos = 0

    def alloc(self, cols, dt=BF16):
        n = cols * 2 if dt == F32 else cols
        n = (n + 1) // 2 * 2
        v = self.ap[:, self.pos:self.pos + (cols * 2 if dt == F32 else cols)]
        self.pos += n
        assert self.pos <= self.n, ("arena overflow", self.pos, self.n)
        return v.bitcast(F32) if dt == F32 else v


V_NMW, V_NFW, V_GNW, V_QNW, V_KVNW, V_DFP, V_DBP, V_DF8, V_DB8 = 0, 8, 16, 24, 27, 29, 33, 37, 45
NVEC = 53
C_POS, C_NEG, C_I1, C_128MI, C_127MJ, C_J = 0, 128, 256, 384, 512, 513
NCTAB = 514
W1_CQ, W1_CKV, W1_KR, W1_RK, W1_RKS, W1_RV, W1_N = 0, 384, 640, 672, 1184, 1696, 2720
W3_RQ, W3_RQS, W3_RK, W3_RKS, W3_RV, W3_RG, W3_GR, W3_N = 0, 512, 1024, 1536, 2048, 3072, 4096, 5120
ARENA_COLS = 106400


def build_program(stages=99, dbg=False):
    nc = bass.Bass("TRN2", target_bir_lowering=False)

    def din(name, shape, dt=F32):
        return nc.dram_tensor(name, list(shape), dt, kind="ExternalInput").ap()

    xo = din("xo", [TOK, D])
    xr = din("xr", [NOT_ * 128, D])
    w1_d = din("w1", [128, 8, W1_N])
    w3a_d = din("w3a", [128, 8, W3_N])
    wgm_d = din("wgm", [128, 8, 1024])
    wmo_d = din("wmo", [128, 4, 1024])
    wo_d = din("wo", [128, 8, 1024])
    wuq_d = din("wuq", [128, 3, 768])
    wuk_d = din("wuk", [128, 2, 512])
    wuv_d = din("wuv", [128, 2, 512])
    wro_d = din("wro", [128, 8, 1024])
    wg_d = din("wg", [128, 8, DFF])
    wu_d = din("wu", [128, 8, DFF])
    wd_d = din("wd", [128, NFC, 1024])
    vec_d = din("vec", [128, NVEC])
    ctab_d = din("ctab", [128, NCTAB])
    dist_d = din("dist", [128, 2 * NOT_])
    ident_d = din("ident", [128, 128])
    nfin_d = din("nfin", [128, D])
    ropeR_own_d = din("ropeR_own", [128, NT, 256])
    ropeR_oth_d = din("ropeR_oth", [128, NOT_, 256])
    tabM_own_d = din("tabM_own", [128, NT, 64])
    tabM_oth_d = din("tabM_oth", [128, NOT_, 64])
    out_d = nc.dram_tensor("out", [TOK, D], F32, kind="ExternalOutput").ap()
    SCR_KIND = "ExternalOutput" if dbg else "Internal"
    Qs_d = nc.dram_tensor("Qs", [8, 96, TOK], BF16, kind=SCR_KIND).ap()
    m1_d = nc.dram_tensor("m1s", [TOK, D], BF16, kind=SCR_KIND).ap()
    h1_d = nc.dram_tensor("h1s", [TOK, D], F32, kind=SCR_KIND).ap()
    ckvT_d = nc.dram_tensor("ckvTs", [128, 2 * NKEYP], BF16, kind=SCR_KIND).ap()
    krope_d = nc.dram_tensor("kropes", [32, NKEYP], BF16, kind=SCR_KIND).ap()
    SbAll_d = nc.dram_tensor("SbAlls", [NT, 128, 512], BF16, kind=SCR_KIND).ap()

    S = Sched()
    es = ExitStack()
    arena_t = es.enter_context(nc.sbuf_tensor("arena", [128, ARENA_COLS], BF16))
    A = Arena(arena_t, ARENA_COLS)
    PS2 = [es.enter_context(nc.psum_tensor(f"ps2_{i}", [128, 1024], F32)) for i in range(4)]

    def bank(i):
        return PS2[i // 2][:, (i % 2) * 512:(i % 2 + 1) * 512]

    def bankb(i):
        return bank(i).bitcast(BF16)

    def B(i):
        return "b%d" % i

    def MM(out, lhsT, rhs, start, stop, r, w):
        S.op("pe", lambda e: e.matmul(out, lhsT=lhsT, rhs=rhs, start=start, stop=stop), r, w)

    def TR(out, in_, idn, r, w):
        S.op("pe", lambda e: e.transpose(out=out, in_=in_, identity=idn), r, w)

    def ACT(out, in_, func, r, w, scale=None, bias=None, accum=None):
        kw = {}
        if scale is not None:
            kw["scale"] = scale
        if bias is not None:
            kw["bias"] = bias
        if accum is not None:
            kw["accum_out"] = accum
        S.op("act", lambda e: e.activation(out=out, in_=in_, func=func, **kw), r, w)

    def TT(eng, out, in0, in1, op, r, w):
        S.op(eng, lambda e: e.tensor_tensor(out=out, in0=in0, in1=in1, op=op), r, w)

    def TS(eng, out, in0, s1, op0, r, w, s2=None, op1=None):
        if op1 is None:
            S.op(eng, lambda e: e.tensor_scalar(out=out, in0=in0, scalar1=s1, scalar2=None, op0=op0), r, w)
        else:
            S.op(eng, lambda e: e.tensor_scalar(out=out, in0=in0, scalar1=s1, scalar2=s2, op0=op0, op1=op1), r, w)

    def STT(out, in0, scalar, in1, op0, op1, r, w):
        S.op("dve", lambda e: e.scalar_tensor_tensor(out=out, in0=in0, scalar=scalar, in1=in1, op0=op0, op1=op1), r, w)

    def CP(eng, out, in_, r, w):
        if eng == "act":
            S.op("act", lambda e: e.copy(out=out, in_=in_), r, w)
        else:
            S.op(eng, lambda e: e.tensor_copy(out=out, in_=in_), r, w)

    def RED(out, in_, r, w):
        S.op("dve", lambda e: e.tensor_reduce(out=out, in_=in_, axis=AX.X, op=ALU.add), r, w)

    def MSET(eng, ap, val, w):
        S.op(eng, lambda e: e.memset(ap, val), (), w)

    def DMA(eng, out, in_, r, w, key):
        S.op(eng, lambda e: e.dma_start(out=out, in_=in_), r, w, dma=key)

    chain_i = [0]

    def LOAD(eng, out, in_, w):
        k = "chain_%s%d" % (eng, chain_i[0] % 3)
        chain_i[0] += 1
        S.op(eng, lambda e: e.dma_start(out=out, in_=in_), (), list(w) + [k], dma=k)

    def v3(ap, a):
        return ap.rearrange("p (a b) -> p a b", a=a)

    vec = A.alloc(NVEC + 1, F32)
    identf = A.alloc(128, F32)
    identb = A.alloc(128, BF16)
    lg8 = A.alloc(16, F32)
    lgP = A.alloc(8, F32)
    mhalf = A.alloc(16, F32)
    rstd_own = A.alloc(NT, F32)
    P0_END = A.pos
    ctab = A.alloc(NCTAB, F32)
    c128 = A.alloc(128, F32)
    Sf = A.alloc(512, F32)
    Sb = A.alloc(512, F32)
    P1_END = A.pos

    LOAD("sp", vec[:, 0:NVEC], vec_d, ["vec"])
    LOAD("sp", ctab, ctab_d, ["ctab"])
    LOAD("sp", identf, ident_d, ["identf"])
    CP("dve", identb, identf, ["identf"], ["identb"])
    MSET("pool", c128, 128.0, ["c128"])
    MSET("pool", mhalf, -0.5, ["mhalf"])
    MSET("pool", Sf, 0.0, ["Sf"])
    MSET("pool", Sb, 0.0, ["Sb"])
    ACT(lg8, vec[:, V_DF8:V_DF8 + 16], AF.Exp, ["vec"], ["lg8"])
    TS("dve", lg8, lg8, -1.0, ALU.mult, ["lg8"], ["lg8"])
    ACT(lgP, vec[:, V_DFP:V_DFP + 8], AF.Exp, ["vec"], ["lgP"])
    TS("dve", lgP, lgP, -1.0, ALU.mult, ["lgP"], ["lgP"])

    if stages >= 1:
        A.pos = P1_END
        w1 = A.alloc(8 * W1_N, BF16)
        wuq = A.alloc(3 * 768, BF16)
        w13 = v3(w1, 8)
        wuq3 = v3(wuq, 3)
        wfb = A.alloc(16, F32)
        GbT = A.alloc(512, F32)
        wo_fb = A.alloc(2 * NOT_ * 8, F32)
        dist = A.alloc(2 * NOT_, F32)
        tabM_own = A.alloc(NT * 64, F32)
        tabM_oth = A.alloc(NOT_ * 64, F32)
        NXB = 2
        xbuf = [A.alloc(D, F32) for _ in range(NXB)]
        ropeb = [A.alloc(256, F32) for _ in range(NXB)]
        xs = [A.alloc(D, BF16) for _ in range(2)]
        junk = A.alloc(D, BF16)
        uT = [A.alloc(D, BF16) for _ in range(2)]
        st = A.alloc(8, F32)
        ckvn = A.alloc(256, BF16)
        kaug = A.alloc(96, BF16)
        ropeA = A.alloc(32, F32)
        ropeBv = A.alloc(32, F32)
        t1 = A.alloc(512, F32)
        t2 = A.alloc(512, F32)
        kT = A.alloc(512, BF16)
        kwf = A.alloc(512, BF16)
        kwb = A.alloc(512, BF16)
        vtok = A.alloc(1024, BF16)
        cqn = A.alloc(384, BF16)
        cqT = A.alloc(384, BF16)
        qA = A.alloc(256, F32)
        qB = A.alloc(256, F32)
        qtok = A.alloc(768, BF16)
        Qst = [A.alloc(8 * 512, BF16) for _ in range(2)]
        ckst = [A.alloc(2 * 512, BF16) for _ in range(2)]
        krst = [A.alloc(512, BF16) for _ in range(2)]
        sbst = [A.alloc(512, BF16) for _ in range(2)]

        LOAD("sp", dist, dist_d, ["dist"])
        LOAD("sp", tabM_own, tabM_own_d.rearrange("p a b -> p (a b)"), ["tabM_own"])
        LOAD("sp", tabM_oth, tabM_oth_d.rearrange("p a b -> p (a b)"), ["tabM_oth"])
        TS("dve", wfb[:, 0:8], lg8[:, 0:8], ctab[:, C_127MJ:C_127MJ + 1], ALU.mult, ["lg8", "ctab"], ["wfb"])
        TS("dve", wfb[:, 8:16], lg8[:, 8:16], ctab[:, C_J:C_J + 1], ALU.mult, ["lg8", "ctab", "wfb"], ["wfb"])
        ACT(wfb, wfb, AF.Exp, ["wfb"], ["wfb"])
        TS("dve", wfb, wfb, 0.125, ALU.mult, ["wfb"], ["wfb"])
        GbT3 = v3(GbT, 4)
        for hp in range(4):
            ACT(GbT3[:, hp, :], c128, AF.Exp, ["c128", "lgP"], ["GT"], scale=lgP[:, 4 + hp:5 + hp])
        wo3 = v3(wo_fb, 2 * NOT_)
        for h in range(8):
            TS("dve", wo3[:, 0:NOT_, h], dist[:, 0:NOT_], lg8[:, h:h + 1], ALU.mult, ["dist", "lg8"], ["wo_fb"])
            TS("dve", wo3[:, NOT_:2 * NOT_, h], dist[:, NOT_:2 * NOT_], lg8[:, 8 + h:9 + h], ALU.mult, ["dist", "lg8"], ["wo_fb"])
        ACT(wo_fb, wo_fb, AF.Exp, ["wo_fb"], ["wo_fb"])
        TS("dve", wo_fb, wo_fb, 0.125, ALU.mult, ["wo_fb"], ["wo_fb"])

        for kc in range(8):
            LOAD("pool", w1[:, kc * W1_N:(kc + 1) * W1_N], w1_d[:, kc, :], ["w1_%d" % kc])
        for kc in range(3):
            LOAD("pool", wuq[:, kc * 768:(kc + 1) * 768], wuq_d[:, kc, :], ["wuq"])
        for kc in range(8):
            sl = slice(kc * W1_N, (kc + 1) * W1_N)
            TS("dve", w1[:, sl], w1[:, sl], vec[:, V_NMW + kc:V_NMW + kc + 1], ALU.mult, ["w1_%d" % kc, "vec"], ["w1_%d" % kc])
        for kc in range(3):
            sl = slice(kc * 768, (kc + 1) * 768)
            TS("dve", wuq[:, sl], wuq[:, sl], vec[:, V_QNW + kc:V_QNW + kc + 1], ALU.mult, ["wuq", "vec"], ["wuq"])
        W1T = ["w1_%d" % kc for kc in range(8)]
        MSET("pool", kaug, 0.0, ["kaug"])
        for i in range(2):
            MSET("pool", ckst[i], 0.0, ["ckst%d" % i])
            MSET("pool", krst[i], 0.0, ["krst%d" % i])
        gcount = 0

        seq = [("o", t) for t in range(NOT_)] + [("s", c) for c in range(NT - 1, -1, -1)]
        import os
        if os.environ.get("K_METAFIRST"):
            seq = [("o", NOT_ - 1)] + [("o", t) for t in range(NOT_ - 1)] + [("s", c) for c in range(NT - 1, -1, -1)]
        if os.environ.get("K_S1N"):
            seq = seq[:int(os.environ["K_S1N"])]
        def front1_a(it):
            kind, ti = seq[it]
            own = kind == "s"
            xsrc = xo[ti * 128:(ti + 1) * 128, :] if own else xr[ti * 128:(ti + 1) * 128, :]
            rsrc = ropeR_own_d[:, ti, :] if own else ropeR_oth_d[:, ti, :]
            xb, xbt = xbuf[it % NXB], "xbuf%d" % (it % NXB)
            rb, rbt = ropeb[it % NXB], "ropeb%d" % (it % NXB)
            xsb, xst = xs[it % 2], "xs%d" % (it % 2)
            DMA("sp", xb, xsrc, (), [xbt], xbt)
            DMA("sp", rb, rsrc, (), [rbt], rbt)
            ACT(junk, xb, AF.Square, [xbt], ["junk", "x_ss"], accum=st[:, 0:1])
            rs = rstd_own[:, ti:ti + 1] if own else st[:, 1:2]
            TS("dve", st[:, 0:1], st[:, 0:1], 1.0 / D, ALU.mult, ["x_ss"], ["x_ss"], s2=RMS_EPS, op1=ALU.add)
            TT("pool", rs, st[:, 0:1], mhalf[:, 0:1], ALU.pow, ["x_ss", "mhalf"], ["x_rstd"])
            ACT(xsb, xb, AF.Identity, [xbt, "x_rstd"], [xst], scale=rs)

        def front1_b(it):
            xsb, xst = xs[it % 2], "xs%d" % (it % 2)
            u, ut = uT[it % 2], "uT%d" % (it % 2)
            tb = bankb(0)
            for kc in range(8):
                TR(tb[:, kc * 128:(kc + 1) * 128], xsb[:, kc * 128:(kc + 1) * 128], identb, [xst, "identb"], [B(0)])
            CP("act", u, tb[:, 0:1024], [B(0)], [ut])

        lat = [A.alloc(672, F32) for _ in range(2)]
        kT2 = [kT, A.alloc(512, BF16)]
        vtok2 = [vtok, A.alloc(1024, BF16)]
        qtok2 = [qtok, A.alloc(768, BF16)]
        gstate = {"gcount": 0}

        def proj1(it):
            kind, ti = seq[it]
            own = kind == "s"
            par = it % 2
            rb, rbt = ropeb[it % NXB], "ropeb%d" % (it % NXB)
            u, ut = uT[par], "uT%d" % par
            u3 = v3(u, 8)
            la, lat_t = lat[par], "lat%d" % par
            for kc in range(8):
                MM(bank(1)[:, 0:288], u3[:, kc, :], w13[:, kc, W1_CKV:W1_CKV + 288], kc == 0, kc == 7, [ut, W1T[kc]], [B(1)])
            if own:
                for kc in range(8):
                    MM(bank(2)[:, 0:384], u3[:, kc, :], w13[:, kc, W1_CQ:W1_CQ + 384], kc == 0, kc == 7, [ut, W1T[kc]], [B(2)])
            for hp in range(4):
                for kc in range(8):
                    MM(bank(3)[:, hp * 128:(hp + 1) * 128], w13[:, kc, W1_RK + hp * 128:W1_RK + (hp + 1) * 128], u3[:, kc, :],
                       kc == 0, kc == 7, [ut, W1T[kc]], [B(3)])
            CP("act", la[:, 0:288], bank(1)[:, 0:288], [B(1)], [lat_t])
            if own:
                CP("act", la[:, 288:672], bank(2)[:, 0:384], [B(2)], [lat_t])
            for hp in range(4):
                for kc in range(8):
                    MM(bank(4)[:, hp * 128:(hp + 1) * 128], w13[:, kc, W1_RKS + hp * 128:W1_RKS + (hp + 1) * 128], u3[:, kc, :],
                       kc == 0, kc == 7, [ut, W1T[kc]], [B(4)])
            cosR = rb[:, 0:128].unsqueeze(1).to_broadcast([128, 4, 128])
            sinR = rb[:, 128:256].unsqueeze(1).to_broadcast([128, 4, 128])
            TT("dve", v3(t1, 4), v3(bank(3), 4), cosR, ALU.mult, [B(3), rbt], ["t1"])
            for hf in range(2):
                for kc in range(8):
                    MM(bank(5 + hf), u3[:, kc, :], w13[:, kc, W1_RV + hf * 512:W1_RV + (hf + 1) * 512], kc == 0, kc == 7,
                       [ut, W1T[kc]], [B(5 + hf)])
            TT("dve", v3(t2, 4), v3(bank(4), 4), sinR, ALU.mult, [B(4), rbt], ["t2"])
            TT("pool", kT2[par], t1, t2, ALU.add, ["t1", "t2"], ["kT%d" % par])
            CP("act", vtok2[par][:, 0:512], bank(5), [B(5)], ["vtok%d" % par])
            CP("act", vtok2[par][:, 512:1024], bank(6), [B(6)], ["vtok%d" % par])

        def back1_a(it):
            kind, ti = seq[it]
            own = kind == "s"
            par = it % 2
            tabM = (tabM_own if own else tabM_oth)[:, ti * 64:(ti + 1) * 64]
            tabMt = "tabM_own" if own else "tabM_oth"
            la, lat_t = lat[par], "lat%d" % par
            ACT(junk[:, 0:256], la[:, 0:256], AF.Square, [lat_t], ["junk", "kv_ss"], accum=st[:, 2:3])
            TS("dve", st[:, 2:3], st[:, 2:3], 1.0 / 256, ALU.mult, ["kv_ss"], ["kv_ss"], s2=RMS_EPS, op1=ALU.add)
            TT("pool", st[:, 3:4], st[:, 2:3], mhalf[:, 0:1], ALU.pow, ["kv_ss", "mhalf"], ["kv_rstd"])
            ACT(ckvn, la[:, 0:256], AF.Identity, [lat_t, "kv_rstd"], ["ckvn"], scale=st[:, 3:4])
            if own:
                ACT(junk[:, 0:384], la[:, 288:672], AF.Square, [lat_t], ["junk", "q_ss"], accum=st[:, 4:5])
                TS("dve", st[:, 4:5], st[:, 4:5], 1.0 / 384, ALU.mult, ["q_ss"], ["q_ss"], s2=RMS_EPS, op1=ALU.add)
                TT("pool", st[:, 5:6], st[:, 4:5], mhalf[:, 0:1], ALU.pow, ["q_ss", "mhalf"], ["q_rstd"])
                ACT(cqn, la[:, 288:672], AF.Identity, [lat_t, "q_rstd"], ["cqn"], scale=st[:, 5:6])
            TT("dve", ropeA, la[:, 256:288], tabM[:, 0:32], ALU.mult, [lat_t, tabMt], ["ropeA"])
            TT("dve", ropeBv, la[:, 256:288], tabM[:, 32:64], ALU.mult, [lat_t, tabMt], ["ropeB"])
            TT("pool", kaug[:, 64:80], ropeA[:, 0:16], ropeBv[:, 16:32], ALU.subtract, ["ropeA", "ropeB"], ["kaug"])
            TT("pool", kaug[:, 80:96], ropeBv[:, 0:16], ropeA[:, 16:32], ALU.add, ["ropeA", "ropeB"], ["kaug"])

        def back1_b(it):
            kind, ti = seq[it]
            own = kind == "s"
            par = it % 2
            slot = ti if own else (32 + ti)
            tabM = (tabM_own if own else tabM_oth)[:, ti * 64:(ti + 1) * 64]
            tabMt = "tabM_own" if own else "tabM_oth"
            kTp, kTt = kT2[par], "kT%d" % par
            vtp, vtt = vtok2[par], "vtok%d" % par
            hb = bankb(7)
            for hp in range(4):
                TR(hb[:, 384 + hp * 128:384 + (hp + 1) * 128], kTp[:, hp * 128:(hp + 1) * 128], identb, [kTt, "identb"], [B(7)])
            for c2 in range(2):
                TR(hb[:, c2 * 128:(c2 + 1) * 128], ckvn[:, c2 * 128:(c2 + 1) * 128], identb, ["ckvn", "identb"], [B(7)])
            TR(hb[0:96, 256:384], kaug, identb, ["kaug", "identb"], [B(7)])
            if own:
                tb0 = bankb(0)
                for c3 in range(3):
                    TR(tb0[:, c3 * 128:(c3 + 1) * 128], cqn[:, c3 * 128:(c3 + 1) * 128], identb, ["cqn", "identb"], [B(0)])
                CP("dve", cqT, tb0[:, 0:384], [B(0)], ["cqT"])
            ktok3 = hb[:, 384:896].rearrange("p (h d) -> p h d", h=8)
            if own:
                wbb = wfb[:, 8:16].unsqueeze(2).to_broadcast([128, 8, 64])
                TT("dve", kwb.rearrange("p (h d) -> p h d", h=8), ktok3, wbb, ALU.mult, [B(7), "wfb"], ["kwb"])
                dirs = [("b", kwb, "kwb", 3)]
            else:
                wof = wo3[:, ti, :].unsqueeze(2).to_broadcast([128, 8, 64])
                wob = wo3[:, NOT_ + ti, :].unsqueeze(2).to_broadcast([128, 8, 64])
                TT("dve", kwf.rearrange("p (h d) -> p h d", h=8), ktok3, wof, ALU.mult, [B(7), "wo_fb"], ["kwf"])
                TT("dve", kwb.rearrange("p (h d) -> p h d", h=8), ktok3, wob, ALU.mult, [B(7), "wo_fb"], ["kwb"])
                dirs = [("f", kwf, "kwf", 1), ("b", kwb, "kwb", 3)]
            gb_i = gstate["gcount"] % 2
            cks, ckt = ckst[gb_i], "ckst%d" % gb_i
            krs, krt = krst[gb_i], "krst%d" % gb_i
            sp_ = slot % 4
            CP("dve", v3(cks, 2)[:, :, sp_ * 128:(sp_ + 1) * 128], hb[:, 0:256].rearrange("p (a b) -> p a b", a=2), [B(7)], [ckt])
            CP("dve", krs[64:96, sp_ * 128:(sp_ + 1) * 128], hb[64:96, 256:384], [B(7)], [krt])
            flush = (slot == 64) or (own and ti % 4 == 0) or ((not own) and slot < 64 and slot % 4 == 3)
            if flush:
                g0 = (slot // 4) * 512
                DMA("pool", ckvT_d.rearrange("p (a n) -> p a n", a=2)[:, :, g0:g0 + 512], v3(cks, 2)[:, :, 0:512], [ckt], [("ckvT_d", slot // 4)], ckt)
                DMA("pool", krope_d[:, g0:g0 + 512], krs[64:96, 0:512], [krt], [("krope_d", slot // 4)], krt)
                gstate["gcount"] += 1
            if own:
                cqT3 = v3(cqT, 3)
                for kc in range(3):
                    MM(bank(5)[:, 0:480], cqT3[:, kc, :], wuq3[:, kc, 0:480], kc == 0, kc == 2, ["cqT", "wuq"], [B(5)])
                for kc in range(3):
                    MM(bank(6)[:, 0:288], cqT3[:, kc, :], wuq3[:, kc, 480:768], kc == 0, kc == 2, ["cqT", "wuq"], [B(6)])
            for (dname, kw_, kwt, b0) in dirs:
                for hp in range(4):
                    bk = bank(b0 + hp // 2)
                    MM(bk[:, (hp % 2) * 256:(hp % 2) * 256 + 256], kw_[:, hp * 128:(hp + 1) * 128], vtp[:, hp * 256:(hp + 1) * 256],
                       True, True, [kwt, vtt], [B(b0 + hp // 2)])
            Sb3 = v3(Sb, 4)
            Sf3 = v3(Sf, 4)
            if own:
                sbs, sbt = sbst[ti % 2], "sbst%d" % (ti % 2)
                CP("pool", sbs, Sb, ["Sb"], [sbt])
                DMA("pool", SbAll_d[ti, :, :], sbs, [sbt], [("SbAll", ti)], sbt)
                TT("pool", Sb, Sb, GbT, ALU.mult, ["Sb", "GT", sbt], ["Sb"])
            for (dname, kw_, kwt, b0) in dirs:
                Sx3 = Sb3 if dname == "b" else Sf3
                Sxt = "Sb" if dname == "b" else "Sf"
                for hh in range(2):
                    bk = v3(bank(b0 + hh), 2)
                    TT("dve", Sx3[0:64, 2 * hh:2 * hh + 2, :], Sx3[0:64, 2 * hh:2 * hh + 2, :], bk[0:64, :, 0:128], ALU.add,
                       [Sxt, B(b0 + hh)], [Sxt])
                    TT("dve", Sx3[64:128, 2 * hh:2 * hh + 2, :], Sx3[64:128, 2 * hh:2 * hh + 2, :], bk[64:128, :, 128:256], ALU.add,
                       [Sxt, B(b0 + hh)], [Sxt])
            if own:
                qtk, qtt = qtok2[par], "qtok%d" % par
                q3 = qtk.rearrange("p (h c) -> p h c", h=8)
                for (bk_i, h0, nh) in ((5, 0, 5), (6, 5, 3)):
                    src = bank(bk_i)[:, 0:nh * 96].rearrange("p (h c) -> p h c", h=nh)
                    CP("act", q3[:, h0:h0 + nh, 0:64], src[:, :, 0:64], [B(bk_i)], [qtt])
                    ccq = tabM[:, 0:32].unsqueeze(1).to_broadcast([128, nh, 32])
                    ssq = tabM[:, 32:64].unsqueeze(1).to_broadcast([128, nh, 32])
                    qA3 = qA[:, h0 * 32:(h0 + nh) * 32].rearrange("p (h c) -> p h c", h=nh)
                    qB3 = qB[:, h0 * 32:(h0 + nh) * 32].rearrange("p (h c) -> p h c", h=nh)
                    TT("dve", qA3, src[:, :, 64:96], ccq, ALU.mult, [B(bk_i), tabMt], ["qA"])
                    TT("dve", qB3, src[:, :, 64:96], ssq, ALU.mult, [B(bk_i), tabMt], ["qB"])
                qA3 = qA.rearrange("p (h c) -> p h c", h=8)
                qB3 = qB.rearrange("p (h c) -> p h c", h=8)
                TT("pool", q3[:, :, 64:80], qA3[:, :, 0:16], qB3[:, :, 16:32], ALU.subtract, ["qA", "qB"], [qtt])
                TT("pool", q3[:, :, 80:96], qB3[:, :, 0:16], qA3[:, :, 16:32], ALU.add, ["qA", "qB"], [qtt])

        def back2(it):
            kind, ti = seq[it]
            if kind != "s":
                return
            par = it % 2
            qtk, qtt = qtok2[par], "qtok%d" % par
            q3 = qtk.rearrange("p (h c) -> p h c", h=8)
            tb2 = bankb(2)
            for h in range(8):
                TR(tb2[0:96, h * 128:(h + 1) * 128], q3[:, h, :], identb, [qtt, "identb"], [B(2)])
            g = ti // 4
            qs = Qst[g % 2]
            qst = "Qst%d" % (g % 2)
            qs3 = v3(qs, 8)
            CP("dve", qs3[0:96, :, (ti % 4) * 128:(ti % 4 + 1) * 128], tb2[0:96, 0:1024].rearrange("p (h t) -> p h t", h=8),
               [B(2)], [qst])
            if ti % 4 == 0:
                DMA("pool", Qs_d[:, :, g * 512:(g + 1) * 512].rearrange("h d t -> d h t"), qs3[0:96, :, :], [qst], [("Qs", g)], qst)

        n1 = len(seq)
        if n1:
            front1_a(0)
            front1_b(0)
        for it in range(n1):
            if it + 1 < n1:
                front1_a(it + 1)
            if it >= 1:
                back1_a(it - 1)
            proj1(it)
            if it + 1 < n1:
                front1_b(it + 1)
            if it >= 1:
                back1_b(it - 1)
            if it >= 2:
                back2(it - 2)
        if n1:
            back1_a(n1 - 1)
            back1_b(n1 - 1)
            if n1 >= 2:
                back2(n1 - 2)
            back2(n1 - 1)
        S.barrier()

    if stages >= 2:
        A.pos = P1_END
        w3 = A.alloc(8 * W3_N, BF16)
        wro = A.alloc(8 * 1024, BF16)
        w33 = v3(w3, 8)
        wro3 = v3(wro, 8)
        DT = A.alloc(1024, F32)
        wfb = A.alloc(16, F32)
        wqfd = A.alloc(1024, F32)
        wqbd = A.alloc(1024, F32)
        GfT = A.alloc(512, F32)
        Sf_bf = A.alloc(512, BF16)
        NXB = 2
        xbuf = [A.alloc(D, F32) for _ in range(NXB)]
        ropeb = [A.alloc(256, F32) for _ in range(NXB)]
        sbl = [A.alloc(512, BF16) for _ in range(2)]
        xs = [A.alloc(D, BF16) for _ in range(2)]
        uT = [A.alloc(D, BF16) for _ in range(2)]
        t1 = A.alloc(512, F32)
        t2 = A.alloc(512, F32)
        qTd = A.alloc(1024, BF16)
        kT = A.alloc(512, BF16)
        qfd = A.alloc(1024, BF16)
        qbd = A.alloc(1024, BF16)
        kwf = A.alloc(512, BF16)
        vtok = A.alloc(1024, BF16)
        sig = A.alloc(1024, F32)
        silu = sig
        sgr = A.alloc(1024, F32)
        PT = A.alloc(1024, BF16)
        sq = A.alloc(1024, F32)
        gst = A.alloc(64, F32)
        zn = sq
        z = A.alloc(1024, BF16)
        zT = A.alloc(1024, BF16)
        m1b = [A.alloc(1024, BF16) for _ in range(2)]
        DT3 = v3(DT, 8)
        for h in range(8):
            TS("dve", DT3[:, h, :], ctab[:, C_POS:C_POS + 128], lg8[:, h:h + 1], ALU.mult, ["ctab", "lg8"], ["DT"])
            STT(DT3[:, h, :], ctab[:, C_NEG:C_NEG + 128], lg8[:, 8 + h:9 + h], DT3[:, h, :], ALU.mult, ALU.add,
                ["ctab", "lg8", "DT"], ["DT"])
        ACT(DT, DT, AF.Exp, ["DT"], ["DT"])
        TS("dve", DT, DT, 0.125, ALU.mult, ["DT"], ["DT"])
        TS("dve", wfb[:, 0:8], lg8[:, 0:8], ctab[:, C_127MJ:C_127MJ + 1], ALU.mult, ["lg8", "ctab"], ["wfb"])
        TS("dve", wfb[:, 8:16], lg8[:, 8:16], ctab[:, C_J:C_J + 1], ALU.mult, ["lg8", "ctab", "wfb"], ["wfb"])
        ACT(wfb, wfb, AF.Exp, ["wfb"], ["wfb"])
        TS("dve", wfb, wfb, 0.125, ALU.mult, ["wfb"], ["wfb"])
        wqfd3, wqbd3, GfT3 = v3(wqfd, 4), v3(wqbd, 4), v3(GfT, 4)
        MSET("pool", wqfd, 0.0, ["wq"])
        MSET("pool", wqbd, 0.0, ["wq"])
        MSET("pool", qTd, 0.0, ["qT"])
        for hp in range(4):
            for par in range(2):
                rs_ = slice(par * 64, (par + 1) * 64)
                cs_ = slice(par * 128, (par + 1) * 128)
                ACT(wqfd3[rs_, hp, cs_], ctab[rs_, C_I1:C_I1 + 128], AF.Exp, ["ctab", "lgP", "wq"], ["wq"], scale=lgP[rs_, hp:hp + 1])
                ACT(wqbd3[rs_, hp, cs_], ctab[rs_, C_128MI:C_128MI + 128], AF.Exp, ["ctab", "lgP", "wq"], ["wq"], scale=lgP[rs_, 4 + hp:5 + hp])
            ACT(GfT3[:, hp, :], c128, AF.Exp, ["c128", "lgP"], ["GT"], scale=lgP[:, hp:hp + 1])
        for kc in range(8):
            LOAD("pool", w3[:, kc * W3_N:(kc + 1) * W3_N], w3a_d[:, kc, :], ["w3_%d" % kc])
            LOAD("pool", wro[:, kc * 1024:(kc + 1) * 1024], wro_d[:, kc, :], ["wro_%d" % kc])
        for kc in range(8):
            sl = slice(kc * W3_N, (kc + 1) * W3_N)
            TS("dve", w3[:, sl], w3[:, sl], vec[:, V_NMW + kc:V_NMW + kc + 1], ALU.mult, ["w3_%d" % kc, "vec"], ["w3_%d" % kc])
            sl = slice(kc * 1024, (kc + 1) * 1024)
            TS("dve", wro[:, sl], wro[:, sl], vec[:, V_GNW + kc:V_GNW + kc + 1], ALU.mult, ["wro_%d" % kc, "vec"], ["wro_%d" % kc])
        W3T = ["w3_%d" % kc for kc in range(8)]
        WRT = ["wro_%d" % kc for kc in range(8)]
        CP("pool", Sf_bf, Sf, ["Sf"], ["Sf_bf"])
        Sf3 = v3(Sf, 4)
        Sfb3 = v3(Sf_bf, 4)
        PT3 = v3(PT, 8)
        N3A = int(os.environ.get("K_S3N", NT))

        def front3a(c):
            xb, xbt = xbuf[c % NXB], "xbuf%d" % (c % NXB)
            rb, rbt = ropeb[c % NXB], "ropeb%d" % (c % NXB)
            xsb, xst = xs[c % 2], "xs%d" % (c % 2)
            u, ut = uT[c % 2], "uT%d" % (c % 2)
            DMA("sp", xb, xo[c * 128:(c + 1) * 128, :], (), [xbt], xbt)
            DMA("sp", rb, ropeR_own_d[:, c, :], (), [rbt], rbt)
            DMA("sp", sbl[c % 2], SbAll_d[c, :, :], [("SbAll", c)], ["sbl%d" % (c % 2)], "sbl%d" % (c % 2))
            ACT(xsb, xb, AF.Identity, [xbt], [xst], scale=rstd_own[:, c:c + 1])
            tb = bankb(0)
            for kc in range(8):
                TR(tb[:, kc * 128:(kc + 1) * 128], xsb[:, kc * 128:(kc + 1) * 128], identb, [xst, "identb"], [B(0)])
            CP("act", u, tb[:, 0:1024], [B(0)], [ut])

        def tail3a(c):
            tb7 = bankb(7)
            for kc in range(8):
                TR(tb7[:, kc * 128:(kc + 1) * 128], z[:, kc * 128:(kc + 1) * 128], identb, ["z", "identb"], [B(7)])
            CP("act", zT, tb7[:, 0:1024], [B(7)], ["zT"])
            zT3 = v3(zT, 8)
            for hf in range(2):
                for kc in range(8):
                    MM(bank(5 + hf), zT3[:, kc, :], wro3[:, kc, hf * 512:(hf + 1) * 512], kc == 0, kc == 7, ["zT", WRT[kc]], [B(5 + hf)])
            mb = m1b[c % 2]
            mbt = "m1b%d" % (c % 2)
            for hf in range(2):
                TT("dve", mb[:, hf * 512:(hf + 1) * 512], bank(5 + hf), sgr[:, hf * 512:(hf + 1) * 512], ALU.mult, [B(5 + hf), "sgr"], [mbt])
            DMA("pool", m1_d[c * 128:(c + 1) * 128, :], mb, [mbt], [("m1", c)], mbt)

        if N3A:
            front3a(0)
        for c in range(N3A):
            rb = ropeb[c % NXB]
            rbt = "ropeb%d" % (c % NXB)
            u = uT[c % 2]
            ut = "uT%d" % (c % 2)
            u3 = v3(u, 8)
            sbc, sbct = sbl[c % 2], "sbl%d" % (c % 2)
            sbc3 = v3(sbc, 4)
            for (bk_i, c0) in ((1, W3_RQ), (2, W3_RQS), (3, W3_RK), (4, W3_RKS)):
                for hp in range(4):
                    for kc in range(8):
                        MM(bank(bk_i)[:, hp * 128:(hp + 1) * 128], w33[:, kc, c0 + hp * 128:c0 + (hp + 1) * 128], u3[:, kc, :],
                           kc == 0, kc == 7, [ut, W3T[kc]], [B(bk_i)])
            for hf in range(2):
                for kc in range(8):
                    MM(bank(5 + hf), u3[:, kc, :], w33[:, kc, W3_RV + hf * 512:W3_RV + (hf + 1) * 512], kc == 0, kc == 7,
                       [ut, W3T[kc]], [B(5 + hf)])
            cosR = rb[:, 0:128].unsqueeze(1).to_broadcast([128, 4, 128])
            sinR = rb[:, 128:256].unsqueeze(1).to_broadcast([128, 4, 128])
            TT("dve", v3(t1, 4), v3(bank(1), 4), cosR, ALU.mult, [B(1), rbt], ["t1"])
            TT("dve", v3(t2, 4), v3(bank(2), 4), sinR, ALU.mult, [B(2), rbt], ["t2"])
            qTd3 = v3(qTd, 4)
            TT("pool", qTd3[0:64, :, 0:128], v3(t1, 4)[0:64, :, :], v3(t2, 4)[0:64, :, :], ALU.add, ["t1", "t2", "qT"], ["qT"])
            TT("pool", qTd3[64:128, :, 128:256], v3(t1, 4)[64:128, :, :], v3(t2, 4)[64:128, :, :], ALU.add, ["t1", "t2", "qT"], ["qT"])
            TT("dve", v3(t1, 4), v3(bank(3), 4), cosR, ALU.mult, [B(3), rbt, "qT"], ["t1"])
            TT("dve", v3(t2, 4), v3(bank(4), 4), sinR, ALU.mult, [B(4), rbt, "qT"], ["t2"])
            TT("pool", kT, t1, t2, ALU.add, ["t1", "t2"], ["kT"])
            TT("dve", qfd, qTd, wqfd, ALU.mult, ["qT", "wq"], ["qf"])
            TT("pool", qbd, qTd, wqbd, ALU.mult, ["qT", "wq"], ["qb"])
            CP("act", vtok[:, 0:512], bank(5), [B(5)], ["vtok"])
            CP("act", vtok[:, 512:1024], bank(6), [B(6)], ["vtok"])
            if c >= 1:
                tail3a(c - 1)
            for hp in range(4):
                MM(bank(1 + hp // 2)[:, (hp % 2) * 256:(hp % 2) * 256 + 256], kT[:, hp * 128:(hp + 1) * 128], qTd3[:, hp, :],
                   True, True, ["kT", "qT"], [B(1 + hp // 2)])
            hb = bankb(7)
            for hp in range(4):
                TR(hb[:, hp * 128:(hp + 1) * 128], kT[:, hp * 128:(hp + 1) * 128], identb, ["kT", "identb"], [B(7)])
            for hf in range(2):
                for kc in range(8):
                    MM(bank(3 + hf), u3[:, kc, :], w33[:, kc, W3_RG + hf * 512:W3_RG + (hf + 1) * 512], kc == 0, kc == 7,
                       [ut, W3T[kc]], [B(3 + hf)])
            for hf in range(2):
                TT("dve", PT[:, hf * 512:(hf + 1) * 512], bank(1 + hf), DT[:, hf * 512:(hf + 1) * 512], ALU.mult, [B(1 + hf), "DT"], ["PT"])
            wfbb = wfb[:, 0:8].unsqueeze(2).to_broadcast([128, 8, 64])
            TT("dve", kwf.rearrange("p (h d) -> p h d", h=8), hb[:, 0:512].rearrange("p (h d) -> p h d", h=8), wfbb, ALU.mult,
               [B(7), "wfb"], ["kwf"])
            for hf in range(2):
                sl = slice(hf * 512, (hf + 1) * 512)
                ACT(sig[:, sl], bank(3 + hf), AF.Sigmoid, [B(3 + hf)], ["sig"])
                TT("dve", silu[:, sl], bank(3 + hf), sig[:, sl], ALU.mult, [B(3 + hf), "sig"], ["sig"])
            for h in range(8):
                hp, hb0 = h // 2, (h % 2) * 64
                o_ap = bank(5 + h // 4)[:, (h % 4) * 128:(h % 4 + 1) * 128]
                MM(o_ap, PT3[:, h, :], vtok[:, h * 128:(h + 1) * 128], True, False, ["PT", "vtok"], [B(5 + h // 4)])
                pc = slice((h % 2) * 128, (h % 2) * 128 + 128)
                MM(o_ap, v3(qfd, 4)[:, hp, pc], Sfb3[:, hp, :], False, False, ["qf", "Sf_bf"], [B(5 + h // 4)])
                MM(o_ap, v3(qbd, 4)[:, hp, pc], sbc3[:, hp, :], False, True, ["qb", sbct], [B(5 + h // 4)])
            for hp in range(4):
                bk = bank(1 + hp // 2)
                MM(bk[:, (hp % 2) * 256:(hp % 2) * 256 + 256], kwf[:, hp * 128:(hp + 1) * 128], vtok[:, hp * 256:(hp + 1) * 256],
                   True, True, ["kwf", "vtok"], [B(1 + hp // 2)])
            TT("pool", Sf, Sf, GfT, ALU.mult, ["Sf", "GT", "Sf_bf"], ["Sf"])
            for hh in range(2):
                bk = v3(bank(1 + hh), 2)
                TT("dve", Sf3[0:64, 2 * hh:2 * hh + 2, :], Sf3[0:64, 2 * hh:2 * hh + 2, :], bk[0:64, :, 0:128], ALU.add, ["Sf", B(1 + hh)], ["Sf"])
                TT("dve", Sf3[64:128, 2 * hh:2 * hh + 2, :], Sf3[64:128, 2 * hh:2 * hh + 2, :], bk[64:128, :, 128:256], ALU.add, ["Sf", B(1 + hh)], ["Sf"])
            CP("pool", Sf_bf, Sf, ["Sf"], ["Sf_bf"])
            for hf in range(2):
                for kc in range(8):
                    MM(bank(3 + hf), u3[:, kc, :], w33[:, kc, W3_GR + hf * 512:W3_GR + (hf + 1) * 512], kc == 0, kc == 7,
                       [ut, W3T[kc]], [B(3 + hf)])
            if c + 1 < N3A:
                front3a(c + 1)
            for hf in range(2):
                RED(gst[:, hf * 4:(hf + 1) * 4], v3(bank(5 + hf), 4), [B(5 + hf)], ["gs1"])
                ACT(sq[:, hf * 512:(hf + 1) * 512], bank(5 + hf), AF.Square, [B(5 + hf)], ["sq"])
            RED(gst[:, 8:16], v3(sq, 8), ["sq"], ["gs2"])
            TS("dve", gst[:, 16:24], gst[:, 0:8], 1.0 / 128, ALU.mult, ["gs1"], ["gmean"])
            TT("dve", gst[:, 48:56], gst[:, 16:24], gst[:, 16:24], ALU.mult, ["gmean"], ["gmsq"])
            TS("dve", gst[:, 24:32], gst[:, 8:16], 1.0 / 128, ALU.mult, ["gs2"], ["gvar"], s2=GN_EPS, op1=ALU.add)
            TT("dve", gst[:, 24:32], gst[:, 24:32], gst[:, 48:56], ALU.subtract, ["gvar", "gmsq"], ["gvar"])
            TT("pool", gst[:, 32:40], gst[:, 24:32], mhalf[:, 0:8], ALU.pow, ["gvar", "mhalf"], ["grstd"])
            TT("pool", gst[:, 40:48], gst[:, 16:24], gst[:, 32:40], ALU.mult, ["gmean", "grstd"], ["gnmr"])
            TS("pool", gst[:, 40:48], gst[:, 40:48], -1.0, ALU.mult, ["gnmr"], ["gnmr"], s2=1.0, op1=ALU.mult)
            for h in range(8):
                ACT(zn[:, h * 128:(h + 1) * 128], bank(5 + h // 4)[:, (h % 4) * 128:(h % 4 + 1) * 128], AF.Identity,
                    [B(5 + h // 4), "grstd", "gnmr", "gs2"], ["sq"], scale=gst[:, 32 + h:33 + h], bias=gst[:, 40 + h:41 + h])
            TT("dve", z, zn, silu, ALU.mult, ["sq", "sig"], ["z"])
            for hf in range(2):
                ACT(sgr[:, hf * 512:(hf + 1) * 512], bank(3 + hf), AF.Sigmoid, [B(3 + hf)], ["sgr"])
        if N3A:
            tail3a(N3A - 1)
        S.barrier()

    if stages >= 3:
        A.pos = P0_END
        Ymla = A.alloc(NT * 512, BF16)
        ckvT = A.alloc(2 * NKEY, BF16)
        ckvT3 = v3(ckvT, 2)
        Kb = [A.alloc(NKEY, BF16), A.alloc(NKEY, BF16)]
        wuk = A.alloc(2 * 512, BF16)
        wuv = A.alloc(2 * 512, BF16)
        Vb = [A.alloc(NSLOT * 128, BF16) for _ in range(2)]
        Qb = [A.alloc(TOK, BF16) for _ in range(2)]
        Pb = [A.alloc(1024, BF16) for _ in range(2)]
        oT = A.alloc(512, F32)
        rcp = A.alloc(4, F32)
        CKD = [("ckvT_d", g) for g in range(17)]
        KRD = [("krope_d", g) for g in range(17)]
        for kc in range(2):
            DMA("sp", ckvT[:, kc * NKEY:(kc + 1) * NKEY], ckvT_d[:, kc * NKEYP:kc * NKEYP + NKEY], CKD, ["ckvT%d" % kc], "ckvT%d" % kc)
        for i in range(2):
            DMA("sp", Kb[i][64:96, :], krope_d[:, 0:NKEY], KRD, [("Kr", i)], "Kr%d" % i)
            Vi3 = v3(Vb[i], NSLOT)
            MSET("pool", Vb[i], 0.0, [("V", i)])
            MSET("pool", Vi3[:, :, 64:65], 1.0, [("V", i)])
        for kc in range(2):
            LOAD("pool", wuk[:, kc * 512:(kc + 1) * 512], wuk_d[:, kc, :], ["wuk"])
            LOAD("pool", wuv[:, kc * 512:(kc + 1) * 512], wuv_d[:, kc, :], ["wuv"])
        for kc in range(2):
            sl = slice(kc * 512, (kc + 1) * 512)
            TS("dve", wuk[:, sl], wuk[:, sl], vec[:, V_KVNW + kc:V_KVNW + kc + 1], ALU.mult, ["wuk", "vec"], ["wuk"])
            TS("dve", wuv[:, sl], wuv[:, sl], vec[:, V_KVNW + kc:V_KVNW + kc + 1], ALU.mult, ["wuv", "vec"], ["wuv"])
        Ym3 = v3(Ymla, NT)

        def gen_groups(h):
            hbuf = h % 2
            Kh, Vh, Qh = Kb[hbuf], Vb[hbuf], Qb[hbuf]
            Kt_, Vt_, Qt_ = ("K", hbuf), ("V", hbuf), ("Q", hbuf)
            Vh3 = v3(Vh, NSLOT)
            out = []

            def qload():
                DMA("sp", Qh[0:96, :], Qs_d[h, :, :], [("Qs", g) for g in range(8)], [Qt_], "Qb%d" % hbuf)
            out.append(qload)
            for n in range(17):
                def kgen(n=n):
                    n0 = n * 512
                    w = 512 if n < 16 else 128
                    for kc in range(2):
                        MM(bank(7)[0:64, 0:w], wuk[:, kc * 512 + h * 64:kc * 512 + (h + 1) * 64], ckvT3[:, kc, n0:n0 + w],
                           kc == 0, kc == 1, ["wuk", "ckvT%d" % kc], [B(7)])
                    CP("dve", Kh[0:64, n0:n0 + w], bank(7)[0:64, 0:w], [B(7)], [Kt_])
                out.append(kgen)
            for g0 in range(0, NSLOT, 8):
                def vgen(g0=g0):
                    ng = min(8, NSLOT - g0)
                    for j in range(ng):
                        s_ = g0 + j
                        for kc in range(2):
                            MM(bank(7)[:, j * 64:(j + 1) * 64], ckvT3[:, kc, s_ * 128:(s_ + 1) * 128],
                               wuv[:, kc * 512 + h * 64:kc * 512 + (h + 1) * 64], kc == 0, kc == 1, ["wuv", "ckvT%d" % kc], [B(7)])
                    CP("dve", Vh3[:, g0:g0 + ng, 0:64], bank(7)[:, 0:ng * 64].rearrange("p (a b) -> p a b", a=ng), [B(7)], [Vt_])
                out.append(vgen)
            return out

        for g_ in gen_groups(0):
            g_()
        units = []
        for qb in range(8):
            for s0 in range(0, 64, 2):
                units.append((qb, (s0, s0 + 1)))
            units.append((qb, (64,)))
        nu = len(units)
        it = 0
        for h in range(8):
            hbuf = h % 2
            Kh, Vh, Qh = Kb[hbuf], Vb[hbuf], Qb[hbuf]
            Kt_, Vt_, Qt_ = ("K", hbuf), ("V", hbuf), ("Q", hbuf)
            Vh3 = v3(Vh, NSLOT)
            pending = gen_groups(h + 1) if h < 7 else []
            stride = max(1, nu // (len(pending) + 1)) if pending else nu

            def s_mm(i):
                qb, sl = units[i]
                pb_i = (it + i) % 2
                pst = PS2[pb_i]
                for j, s_ in enumerate(sl):
                    kt = 128 if s_ < 64 else NMETA
                    MM(pst[0:kt, j * 512:(j + 1) * 512], Kh[0:96, s_ * 128:s_ * 128 + kt], Qh[0:96, qb * 512:(qb + 1) * 512], True, True,
                       [Kt_, Qt_, ("Kr", hbuf)], [B(2 * pb_i + j)])
                kt = 128 if sl[0] < 64 else NMETA
                if os.environ.get("K_WIDEEXP"):
                    w = 512 * len(sl)
                    ACT(Pb[pb_i][0:kt, 0:w], pst[0:kt, 0:w], AF.Exp, [B(2 * pb_i + j) for j in range(len(sl))], ["Pb%d" % pb_i], scale=SC_ATT)
                else:
                    for j in range(len(sl)):
                        ACT(Pb[pb_i][0:kt, j * 512:(j + 1) * 512], pst[0:kt, j * 512:(j + 1) * 512], AF.Exp, [B(2 * pb_i + j)],
                            [("Pb", pb_i, j)], scale=SC_ATT)

            s_mm(0)
            deferred = []
            for i in range(nu):
                if i + 1 < nu:
                    s_mm(i + 1)
                qb, sl = units[i]
                pb_i = (it + i) % 2
                ob = 4 + qb % 2
                for j, s_ in enumerate(sl):
                    kt = 128 if s_ < 64 else NMETA
                    MM(bank(ob)[:, :], Vh3[0:kt, s_, :], Pb[pb_i][0:kt, j * 512:(j + 1) * 512], s_ == 0, s_ == NSLOT - 1,
                       [Vt_, "Pb%d" % pb_i, ("Pb", pb_i, j)], [B(ob)])
                if pending and i % stride == stride - 1:
                    pending.pop(0)()
                if deferred and deferred[0][0] <= i:
                    deferred.pop(0)[1]()
                if sl[-1] == NSLOT - 1:
                    CP("dve", oT[0:65, :], bank(ob)[0:65, :], [B(ob)], ["oT"])

                    def norm(qb=qb):
                        for qi in range(4):
                            TR(bank(6)[:, qi * 65:(qi + 1) * 65], oT[0:65, qi * 128:(qi + 1) * 128], identf[0:65, 0:65], ["oT", "identf"], [B(6)])
                        pn = bank(6)[:, 0:260].rearrange("p (a b) -> p a b", a=4)
                        S.op("dve", lambda e, o_=rcp[:, 0:4], i_=pn[:, :, 64]: e.reciprocal(out=o_, in_=i_), [B(6)], ["rcp"])
                        TT("dve", Ym3[:, qb * 4:(qb + 1) * 4, h * 64:(h + 1) * 64], pn[:, :, 0:64],
                           rcp[:, 0:4].unsqueeze(2).to_broadcast([128, 4, 64]), ALU.mult, [B(6), "rcp"], [("Ymla", qb)])
                    deferred.append((i + 3, norm))
            while deferred:
                deferred.pop(0)[1]()
            while pending:
                pending.pop(0)()
            it += nu
        if dbg:
            ydbg = nc.dram_tensor("ymla_dbg", [128, NT * 512], BF16, kind="ExternalOutput").ap()
            DMA("sp", ydbg, Ymla, [("Ymla", q_) for q_ in range(8)], ["ydbg"], "ydbg")
        S.barrier()

    def alloc3c():
        A.pos = P0_END
        L = {}
        L["nfin"] = A.alloc(D, F32)
        L["hbuf"] = [A.alloc(D, F32) for _ in range(2)]
        L["hs"] = [A.alloc(D, BF16) for _ in range(2)]
        L["uT2"] = [A.alloc(8 * 512, BF16) for _ in range(2)]
        L["actT"] = A.alloc(NFC * 512, BF16)
        L["sgb"] = [A.alloc(512, F32) for _ in range(2)]
        L["h2"] = A.alloc(D, F32)
        L["ob"] = [A.alloc(D, F32) for _ in range(2)]
        L["st"] = A.alloc(8, F32)
        L["wg0"] = A.pos
        L["wg"] = A.alloc(8 * DFF, BF16)
        L["wu0"] = A.pos
        L["wu"] = A.alloc(8 * DFF, BF16)
        L["wd0"] = A.pos
        L["wd"] = A.alloc(NFC * 1024, BF16)
        return L

    pre3c = set()

    def ffn_weight_loaders(L, min_col):
        out = []
        wg, wu, wd = L["wg"], L["wu"], L["wd"]
        for kc in range(8):
            if ("wg", kc) not in pre3c and L["wg0"] + kc * DFF >= min_col:
                def f(kc=kc):
                    sl = slice(kc * DFF, (kc + 1) * DFF)
                    LOAD("pool", wg[:, sl], wg_d[:, kc, :], ["wg_%d" % kc])
                    TS("dve", wg[:, sl], wg[:, sl], vec[:, V_NFW + kc:V_NFW + kc + 1], ALU.mult, ["wg_%d" % kc, "vec"], ["wg_%d" % kc])
                pre3c.add(("wg", kc))
                out.append(f)
            if ("wu", kc) not in pre3c and L["wu0"] + kc * DFF >= min_col:
                def f(kc=kc):
                    sl = slice(kc * DFF, (kc + 1) * DFF)
                    LOAD("pool", wu[:, sl], wu_d[:, kc, :], ["wu_%d" % kc])
                    TS("dve", wu[:, sl], wu[:, sl], vec[:, V_NFW + kc:V_NFW + kc + 1], ALU.mult, ["wu_%d" % kc, "vec"], ["wu_%d" % kc])
                pre3c.add(("wu", kc))
                out.append(f)
        for fc in range(NFC):
            if ("wd", fc) not in pre3c and L["wd0"] + fc * 1024 >= min_col:
                def f(fc=fc):
                    LOAD("pool", wd[:, fc * 1024:(fc + 1) * 1024], wd_d[:, fc, :], ["wd_%d" % fc])
                pre3c.add(("wd", fc))
                out.append(f)
        return out

    if stages >= 4:
        A.pos = P0_END
        Ymla = A.alloc(NT * 512, BF16)
        Ym3 = v3(Ymla, NT)
        wgm = A.alloc(8 * 1024, BF16)
        wmo = A.alloc(4 * 1024, BF16)
        wo = A.alloc(8 * 1024, BF16)
        wgm3, wmo3, wo3_ = v3(wgm, 8), v3(wmo, 4), v3(wo, 8)
        xbuf = [A.alloc(D, F32) for _ in range(2)]
        m1l = [A.alloc(D, BF16) for _ in range(2)]
        xs = [A.alloc(D, BF16) for _ in range(2)]
        uT = [A.alloc(D, BF16) for _ in range(2)]
        sg = A.alloc(1024, F32)
        ymT = A.alloc(512, BF16)
        mg = A.alloc(1024, F32)
        mg2 = A.alloc(1024, BF16)
        mT = A.alloc(1024, BF16)
        h1b = [A.alloc(D, F32) for _ in range(2)]
        for kc in range(8):
            LOAD("pool", wgm[:, kc * 1024:(kc + 1) * 1024], wgm_d[:, kc, :], ["wgm_%d" % kc])
            LOAD("pool", wo[:, kc * 1024:(kc + 1) * 1024], wo_d[:, kc, :], ["wo_%d" % kc])
        for kc in range(4):
            LOAD("pool", wmo[:, kc * 1024:(kc + 1) * 1024], wmo_d[:, kc, :], ["wmo"])
        for kc in range(8):
            sl = slice(kc * 1024, (kc + 1) * 1024)
            TS("dve", wgm[:, sl], wgm[:, sl], vec[:, V_NMW + kc:V_NMW + kc + 1], ALU.mult, ["wgm_%d" % kc, "vec"], ["wgm_%d" % kc])
        xbuf = xbuf + [A.alloc(D, F32)]

        def front3b(c):
            xb, xbt = xbuf[c % 3], "xbuf%d" % (c % 3)
            ml, mlt = m1l[c % 2], "m1l%d" % (c % 2)
            xsb, xst = xs[c % 2], "xs%d" % (c % 2)
            u, ut = uT[c % 2], "uT%d" % (c % 2)
            DMA("sp", xb, xo[c * 128:(c + 1) * 128, :], (), [xbt], xbt)
            DMA("sp", ml, m1_d[c * 128:(c + 1) * 128, :], [("m1", c)], [mlt], mlt)
            ACT(xsb, xb, AF.Identity, [xbt], [xst], scale=rstd_own[:, c:c + 1])
            tb = bankb(0)
            for kc in range(8):
                TR(tb[:, kc * 128:(kc + 1) * 128], xsb[:, kc * 128:(kc + 1) * 128], identb, [xst, "identb"], [B(0)])
            CP("dve", u, tb[:, 0:1024], [B(0)], [ut])

        def tailT3b(c):
            tb6 = bankb(6)
            for kc in range(8):
                TR(tb6[:, kc * 128:(kc + 1) * 128], mg2[:, kc * 128:(kc + 1) * 128], identb, ["mg2", "identb"], [B(6)])
            CP("dve", mT, tb6[:, 0:1024], [B(6)], ["mT"])

        def tailO3b(c):
            xb, xbt = xbuf[c % 3], "xbuf%d" % (c % 3)
            hb_, hbt = h1b[c % 2], "h1b%d" % (c % 2)
            mT3 = v3(mT, 8)
            for hf in range(2):
                for kc in range(8):
                    MM(bank(3 + hf), mT3[:, kc, :], wo3_[:, kc, hf * 512:(hf + 1) * 512], kc == 0, kc == 7, ["mT", "wo_%d" % kc], [B(3 + hf)])
            for hf in range(2):
                sl = slice(hf * 512, (hf + 1) * 512)
                TT("dve", hb_[:, sl], bank(3 + hf), xb[:, sl], ALU.add, [B(3 + hf), xbt], [hbt])
            DMA("pool", h1_d[c * 128:(c + 1) * 128, :], hb_, [hbt], [("h1", c)], hbt)

        prefetch = []
        if stages >= 5:
            end3b = A.pos
            prefetch = ffn_weight_loaders(alloc3c(), end3b)
            A.pos = end3b
        front3b(0)
        YB = (5, 7)
        for c in range(NT):
            for _ in range(2):
                if prefetch and c >= 1:
                    prefetch.pop(0)()
            ml, mlt = m1l[c % 2], "m1l%d" % (c % 2)
            u, ut = uT[c % 2], "uT%d" % (c % 2)
            u3 = v3(u, 8)
            for hf in range(2):
                for kc in range(8):
                    MM(bank(1 + hf), u3[:, kc, :], wgm3[:, kc, hf * 512:(hf + 1) * 512], kc == 0, kc == 7, [ut, "wgm_%d" % kc], [B(1 + hf)])
            if c >= 1:
                tailT3b(c - 1)
            if c + 1 < NT:
                front3b(c + 1)
            tb7 = bankb(7)
            for kc in range(4):
                TR(tb7[:, kc * 128:(kc + 1) * 128], Ym3[:, c, kc * 128:(kc + 1) * 128], identb, [("Ymla", c // 4), "identb"], [B(7)])
            CP("dve", ymT, tb7[:, 0:512], [B(7)], ["ymT"])
            if c >= 1:
                tailO3b(c - 1)
            ymT3 = v3(ymT, 4)
            for hf in range(2):
                for kc in range(4):
                    MM(bank(YB[hf]), ymT3[:, kc, :], wmo3[:, kc, hf * 512:(hf + 1) * 512], kc == 0, kc == 3, ["ymT", "wmo"], [B(YB[hf])])
            for hf in range(2):
                sl = slice(hf * 512, (hf + 1) * 512)
                ACT(sg[:, sl], bank(1 + hf), AF.Sigmoid, [B(1 + hf)], ["sg"])
                TT("dve", mg[:, sl], bank(YB[hf]), sg[:, sl], ALU.mult, [B(YB[hf]), "sg"], ["mg"])
            TT("pool", mg2, mg, ml, ALU.add, ["mg", mlt], ["mg2"])
        tailT3b(NT - 1)
        tailO3b(NT - 1)
        while prefetch:
            prefetch.pop(0)()
        S.barrier()

    if stages >= 5:
        L = alloc3c()
        wg3, wu3, wd3 = v3(L["wg"], 8), v3(L["wu"], 8), v3(L["wd"], NFC)
        nfin, hbuf_, hs, actT, sgb, ob_, st = L["nfin"], L["hbuf"], L["hs"], L["actT"], L["sgb"], L["ob"], L["st"]
        h2b, h2t = L["h2"], "h2_0"
        uT2 = L["uT2"]
        actT3 = v3(actT, NFC)
        LOAD("sp", nfin, nfin_d, ["nfin"])
        for f_ in ffn_weight_loaders(L, 0):
            f_()
        NBLK = TOK // 512
        hcnt = [0]

        def front3c_a(blk, t):
            c = blk * 4 + t
            k = hcnt[0] % 2
            hcnt[0] += 1
            hbf, hbft = hbuf_[k], "hbuf%d" % k
            hsb, hst = hs[c % 2], "hs%d" % (c % 2)
            DMA("sp", hbf, h1_d[c * 128:(c + 1) * 128, :], [("h1", c)], [hbft], hbft)
            ACT(hsb, hbf, AF.Square, [hbft], [hst, "f_ss"], accum=st[:, 0:1])
            TS("dve", st[:, 0:1], st[:, 0:1], 1.0 / D, ALU.mult, ["f_ss"], ["f_ss"], s2=RMS_EPS, op1=ALU.add)
            TT("pool", st[:, 1:2], st[:, 0:1], mhalf[:, 0:1], ALU.pow, ["f_ss", "mhalf"], ["f_rstd"])
            ACT(hsb, hbf, AF.Identity, [hbft, "f_rstd"], [hst], scale=st[:, 1:2])

        def front3c_b(blk, t):
            c = blk * 4 + t
            hsb, hst = hs[c % 2], "hs%d" % (c % 2)
            u23 = v3(uT2[blk % 2], 8)
            tb = bankb(t % 2)
            for kc in range(8):
                TR(tb[:, kc * 128:(kc + 1) * 128], hsb[:, kc * 128:(kc + 1) * 128], identb, [hst, "identb"], [B(t % 2)])
            CP("dve", u23[:, :, t * 128:(t + 1) * 128], tb[:, 0:1024].rearrange("p (a b) -> p a b", a=8), [B(t % 2)], ["uT2_%d" % (blk % 2)])

        for t in range(4):
            front3c_a(0, t)
            front3c_b(0, t)
        for blk in range(NBLK):
            uT23 = v3(uT2[blk % 2], 8)
            utt = "uT2_%d" % (blk % 2)
            for fc in range(NFC):
                gb_, ub_ = 2 + 2 * (fc % 2), 3 + 2 * (fc % 2)
                for kc in range(8):
                    MM(bank(gb_), wg3[:, kc, fc * 128:(fc + 1) * 128], uT23[:, kc, :], kc == 0, kc == 7, [utt, "wg_%d" % kc], [B(gb_)])
                for kc in range(8):
                    MM(bank(ub_), wu3[:, kc, fc * 128:(fc + 1) * 128], uT23[:, kc, :], kc == 0, kc == 7, [utt, "wu_%d" % kc], [B(ub_)])
                sgt = "sgb%d" % (fc % 2)
                ACT(sgb[fc % 2], bank(gb_), AF.Sigmoid, [B(gb_)], [sgt])
                TT("dve", sgb[fc % 2], bank(gb_), sgb[fc % 2], ALU.mult, [B(gb_), sgt], [sgt])
                TT("dve", actT3[:, fc, :], bank(ub_), sgb[fc % 2], ALU.mult, [B(ub_), sgt], [("actT", fc)])
                if blk + 1 < NBLK and fc >= 4 and fc % 4 == 0 and (fc - 4) // 4 < 4:
                    front3c_a(blk + 1, (fc - 4) // 4)
                if blk + 1 < NBLK and fc >= 6 and fc % 4 == 2 and (fc - 6) // 4 < 4:
                    front3c_b(blk + 1, (fc - 6) // 4)
            for t in range(4):
                c = blk * 4 + t
                d0 = 6
                for hf in range(2):
                    for fc in range(NFC):
                        MM(bank(d0 + hf), actT3[:, fc, t * 128:(t + 1) * 128], wd3[:, fc, hf * 512:(hf + 1) * 512], fc == 0, fc == NFC - 1,
                           [("actT", fc), "wd_%d" % fc], [B(d0 + hf)])
                k = hcnt[0] % 2
                hcnt[0] += 1
                hbf, hbft = hbuf_[k], "hbuf%d" % k
                DMA("sp", hbf, h1_d[c * 128:(c + 1) * 128, :], [("h1", c)], [hbft], hbft)
                obb, obt = ob_[c % 2], "ob%d" % (c % 2)
                for hf in range(2):
                    sl = slice(hf * 512, (hf + 1) * 512)
                    TT("dve", h2b[:, sl], bank(d0 + hf), hbf[:, sl], ALU.add, [B(d0 + hf), hbft], [h2t])
                ACT(obb, h2b, AF.Square, [h2t], [obt, "o_ss"], accum=st[:, 2:3])
                TS("dve", st[:, 2:3], st[:, 2:3], 1.0 / D, ALU.mult, ["o_ss"], ["o_ss"], s2=RMS_EPS, op1=ALU.add)
                TT("pool", st[:, 3:4], st[:, 2:3], mhalf[:, 0:1], ALU.pow, ["o_ss", "mhalf"], ["o_rstd"])
                STT(obb, h2b, st[:, 3:4], nfin, ALU.mult, ALU.mult, [h2t, "o_rstd", "nfin", obt], [obt])
                DMA("pool", out_d[c * 128:(c + 1) * 128, :], obb, [obt], [("out", c)], obt)

    S.emit(nc)
    es.close()
    return nc


def _kc_layout(w):
    K, N = w.shape
    return np.ascontiguousarray(w.reshape(K // 128, 128, N).transpose(1, 0, 2))


def _swap_cols(w, hd):
    K, N = w.shape
    w4 = w.reshape(K, N // hd, 2, hd // 2)
    return np.ascontiguousarray(w4[:, :, ::-1, :]).reshape(K, N)


def _rope_tab(pos, half, base=10000.0):
    inv = (np.float32(base) ** (-(np.arange(half, dtype=np.float32) / np.float32(half)))).astype(np.float32)
    ang = (pos.astype(np.float32)[:, None] * inv[None, :]).astype(np.float32)
    return np.cos(ang.astype(np.float64)).astype(np.float32), np.sin(ang.astype(np.float64)).astype(np.float32)


_PROGRAM = {}


def _prep_inputs(x, meta_tokens, norm_mix_w, w_in, ret_decay_fwd, ret_decay_bwd, ret_gn_w, w_ret_out,
                 mla_q_norm_w, w_uq, mla_kv_norm_w, w_uk, w_uv, w_mla_out, w_o, norm_ffn_w,
                 w_ffn_gate, w_ffn_up, w_ffn_down, norm_final_w):
    f = np.float32
    x = np.asarray(x, f)
    W = np.asarray(w_in, f)[0]
    rq, rk, rv, rg = W[:, 0:512], W[:, 512:1024], W[:, 1024:2048], W[:, 2048:3072]
    cq, ckv, kr = W[:, 3072:3456], W[:, 3456:3712], W[:, 3712:3744]
    gret, gmla = W[:, 3744:4768], W[:, 4768:5792]
    rks, rqs = _swap_cols(rk, 64), _swap_cols(rq, 64)
    shared = {
        "w1": _kc_layout(np.concatenate([cq, ckv, kr, rk, rks, rv], axis=1)),
        "w3a": _kc_layout(np.concatenate([rq, rqs, rk, rks, rv, rg, gret], axis=1)),
        "wgm": _kc_layout(gmla),
        "wmo": _kc_layout(np.asarray(w_mla_out, f)[0]),
        "wo": _kc_layout(np.asarray(w_o, f)[0]),
        "wuq": _kc_layout(np.asarray(w_uq, f)[0]),
        "wuk": _kc_layout(np.asarray(w_uk, f)[0]),
        "wuv": _kc_layout(np.asarray(w_uv, f)[0]),
        "wro": _kc_layout(np.asarray(w_ret_out, f)[0]),
        "wg": _kc_layout(np.asarray(w_ffn_gate, f)[0]),
        "wu": _kc_layout(np.asarray(w_ffn_up, f)[0]),
        "wd": _kc_layout(np.asarray(w_ffn_down, f)[0]),
        "ident": np.eye(128, dtype=f),
        "nfin": np.ascontiguousarray(np.broadcast_to(np.asarray(norm_final_w, f)[None, :], (128, D))),
    }
    vec = np.zeros((128, NVEC), f)

    def pk(v):
        v = np.asarray(v, f).reshape(-1)
        return v.reshape(-1, 128).T

    vec[:, V_NMW:V_NMW + 8] = pk(norm_mix_w)
    vec[:, V_NFW:V_NFW + 8] = pk(norm_ffn_w)
    vec[:, V_GNW:V_GNW + 8] = pk(ret_gn_w)
    vec[:, V_QNW:V_QNW + 3] = pk(mla_q_norm_w)
    vec[:, V_KVNW:V_KVNW + 2] = pk(mla_kv_norm_w)
    df = np.asarray(ret_decay_fwd, f).reshape(8)
    db = np.asarray(ret_decay_bwd, f).reshape(8)
    par = (np.arange(128) >= 64).astype(np.int64)
    for hp in range(4):
        vec[:, V_DFP + hp] = df[2 * hp + par]
        vec[:, V_DBP + hp] = db[2 * hp + par]
    vec[:, V_DF8:V_DF8 + 8] = df[None, :]
    vec[:, V_DB8:V_DB8 + 8] = db[None, :]
    shared["vec"] = vec
    ctab = np.zeros((128, NCTAB), f)
    j = np.arange(128, dtype=f)[:, None]
    i = np.arange(128, dtype=f)[None, :]
    ctab[:, C_POS:C_POS + 128] = np.maximum(i - j, 0)
    ctab[:, C_NEG:C_NEG + 128] = np.maximum(j - i, 0)
    ctab[:, C_I1:C_I1 + 128] = np.broadcast_to(i + 1, (128, 128))
    ctab[:, C_128MI:C_128MI + 128] = np.broadcast_to(128 - i, (128, 128))
    ctab[:, C_127MJ] = 127 - j[:, 0]
    ctab[:, C_J] = j[:, 0]
    shared["ctab"] = ctab

    meta = np.asarray(meta_tokens, f)
    BIG = f(1.0e9)
    sgn = np.where((np.arange(128) % 64) < 32, -1.0, 1.0).astype(f)[:, None]
    fidx = (np.arange(128) % 64) % 32
    in_maps = []
    for core in range(8):
        b, half = core // 2, core % 2
        oth = 1 - half
        m = dict(shared)
        m["xo"] = np.ascontiguousarray(x[b, half * TOK:(half + 1) * TOK])
        xr = np.zeros((NOT_ * 128, D), f)
        xr[0:TOK] = x[b, oth * TOK:(oth + 1) * TOK]
        xr[TOK:TOK + NMETA] = meta
        m["xr"] = xr
        pos_own = (NMETA + half * TOK + np.arange(TOK)).astype(np.int64)
        pos_oth = np.zeros(NOT_ * 128, np.int64)
        pos_oth[0:TOK] = NMETA + oth * TOK + np.arange(TOK)
        pos_oth[TOK:TOK + NMETA] = np.arange(NMETA)
        valid_oth = np.zeros(NOT_ * 128, bool)
        valid_oth[0:TOK + NMETA] = True
        for nm, pos, ntile in (("ropeR_own", pos_own, NT), ("ropeR_oth", pos_oth, NOT_)):
            c, s = _rope_tab(pos, 32)
            cfm = c[:, fidx].T
            sfm = s[:, fidx].T * sgn
            tab = np.stack([cfm.reshape(128, ntile, 128), sfm.reshape(128, ntile, 128)], axis=2)
            m[nm] = np.ascontiguousarray(tab.reshape(128, ntile, 256)).astype(f)
        for nm, pos, ntile in (("tabM_own", pos_own, NT), ("tabM_oth", pos_oth, NOT_)):
            c, s = _rope_tab(pos, 16)
            tab = np.concatenate([c, c, s, s], axis=1).reshape(ntile, 128, 64).transpose(1, 0, 2)
            m[nm] = np.ascontiguousarray(tab).astype(f)
        own_first = NMETA + half * TOK
        own_last = own_first + TOK - 1
        dfw = np.where(valid_oth & (pos_oth < own_first), own_first - 1 - pos_oth, BIG).astype(f)
        dbw = np.where(valid_oth & (pos_oth > own_last), pos_oth - own_last - 1, BIG).astype(f)
        dist = np.concatenate([dfw.reshape(NOT_, 128).T, dbw.reshape(NOT_, 128).T], axis=1)
        m["dist"] = np.ascontiguousarray(dist).astype(f)
        in_maps.append(m)
    return in_maps


def kernel(**inputs):
    in_maps = _prep_inputs(**inputs)
    if "nc" not in _PROGRAM:
        _PROGRAM["nc"] = build_program()
    nc = _PROGRAM["nc"]
    res = run_bass_kernel_spmd(nc, in_maps, core_ids=list(range(8)))
    out = np.zeros((NB, SEQ, D), np.float32)
    for core in range(8):
        b, half = core // 2, core % 2
        out[b, half * TOK:(half + 1) * TOK] = res.results[core]["out"]
    return out
```

```python
from contextlib import ExitStack
import os
import numpy as np
import concourse.bass as bass
import concourse.mybir as mybir
from concourse.bass_utils import run_bass_kernel_spmd

F32 = mybir.dt.float32
BF16 = mybir.dt.bfloat16
AF = mybir.ActivationFunctionType
ALU = mybir.AluOpType
AX = mybir.AxisListType

D = 1024
SEQ = 8192
NB = 4
NMETA = 16
TOK = 4096
NT = 32
NOT_ = 33
NSLOT = 65
NKEY = NSLOT * 128
NKEYP = 17 * 512
DFF = 2816
NFC = DFF // 128
RMS_EPS = 1e-6
GN_EPS = 1e-5
SC_ATT = 96.0 ** -0.5

ENGS = ("pe", "act", "dve", "pool", "sp")
EPOCH = 24000


class _Op:
    __slots__ = ("eng", "fn", "signal", "deps", "dma", "dma_n", "sem", "cnt")

    def __init__(self, eng, fn, dma):
        self.eng = eng
        self.fn = fn
        self.signal = False
        self.deps = []
        self.dma = dma
        self.dma_n = 0
        self.sem = None
        self.cnt = 0


class Sched:
    def __init__(self):
        self.ops = []
        self.last_w = {}
        self.readers = {}
        self.dma_cnt = {}
        self._bar = []
        self._bar_seen = set()

    def op(self, eng, fn, reads=(), writes=(), dma=None):
        o = _Op(eng, fn, dma)
        deps = set()
        if eng not in self._bar_seen:
            self._bar_seen.add(eng)
            deps.update(self._bar)
        for t in reads:
            w = self.last_w.get(t)
            if w is not None:
                deps.add(w)
            if isinstance(t, str) and t[0] == "b" and t[1:].isdigit():
                for k, r in self.readers.get(t, {}).items():
                    if r.eng != eng:
                        deps.add(r)
        for t in writes:
            w = self.last_w.get(t)
            if w is not None:
                deps.add(w)
            for r in self.readers.get(t, {}).values():
                deps.add(r)
        if dma is not None:
            n = self.dma_cnt.get(dma, 0) + 1
            self.dma_cnt[dma] = n
            o.dma_n = n
        for d in deps:
            if d is o:
                continue
            if d.dma is None and d.eng == "pe" and eng == "pe" and dma is None:
                continue
            o.deps.append(d)
            if d.dma is None:
                d.signal = True
        for t in reads:
            rd = self.readers.setdefault(t, {})
            rd[eng if dma is None else ("dma", id(o))] = o
        for t in writes:
            self.last_w[t] = o
            self.readers[t] = {}
        self.ops.append(o)
        return o

    def barrier(self):
        last = {}
        for o in self.ops:
            last[o.eng if o.dma is None else ("dma", o.dma)] = o
        self._bar = list(last.values())
        self._bar_seen = set()
        for o in self._bar:
            if o.dma is None:
                o.signal = True

    def emit(self, nc, final_eng="sp"):
        per = {e: [] for e in ENGS}
        for o in self.ops:
            per[o.eng].append(o)
        nsems = {}
        for e in ENGS:
            c = 0
            ep = 0
            for o in per[e]:
                if o.signal and o.dma is None:
                    c += 1
                    if c > EPOCH:
                        ep += 1
                        c = 1
                    o.sem = (e, ep)
                    o.cnt = c
            nsems[e] = ep + 1
        with ExitStack() as es:
            sems = {}
            for e in ENGS:
                for ep in range(nsems[e]):
                    sems[(e, ep)] = es.enter_context(nc.semaphore(f"s_{e}_{ep}"))
            dsems = {}
            for k in self.dma_cnt:
                dsems[k] = es.enter_context(nc.semaphore("d_" + str(len(dsems))))
            block = es.enter_context(nc.Block())
            dma_cnt = self.dma_cnt

            def run(e, eng):
                waited = {}
                for o in per[e]:
                    need = {}
                    for d in o.deps:
                        if d.dma is not None:
                            key = ("d", d.dma)
                            v = (0, 16 * d.dma_n)
                        else:
                            key = ("e", d.eng)
                            v = (d.sem[1], d.cnt)
                        if v > need.get(key, (-1, -1)):
                            need[key] = v
                    for key, v in need.items():
                        if v <= waited.get(key, (-1, -1)):
                            continue
                        waited[key] = v
                        if key[0] == "d":
                            eng.wait_ge(dsems[key[1]], v[1])
                        else:
                            eng.wait_ge(sems[(key[1], v[0])], v[1])
                    ins = o.fn(eng)
                    if o.dma is not None:
                        ins.then_inc(dsems[o.dma], 16)
                    elif o.signal:
                        ins.then_inc(sems[o.sem], 1)
                if e == final_eng:
                    for k, n in dma_cnt.items():
                        if 16 * n > waited.get(("d", k), (-1, -1))[1]:
                            eng.wait_ge(dsems[k], 16 * n)

            @block.tensor
            def _(eng):
                run("pe", eng)

            @block.scalar
            def _(eng):
                run("act", eng)

            @block.vector
            def _(eng):
                run("dve", eng)

            @block.gpsimd
            def _(eng):
                run("pool", eng)

            @block.sync
            def _(eng):
                run("sp", eng)


class Arena:
    def __init__(self, ap, ncols):
        self.ap = ap
        self.n = ncols
        self.pos = 0

    def alloc(self, cols, dt=BF16):
        n = cols * 2 if dt == F32 else cols
        n = (n + 1) // 2 * 2
        v = self.ap[:, self.pos:self.pos + (cols * 2 if dt == F32 else cols)]
        self.pos += n
        assert self.pos <= self.n, ("arena overflow", self.pos, self.n)
        return v.bitcast(F32) if dt == F32 else v


V_NMW, V_NFW, V_GNW, V_QNW, V_KVNW, V_DFP, V_DBP, V_DF8, V_DB8 = 0, 8, 16, 24, 27, 29, 33, 37, 45
NVEC = 53
C_POS, C_NEG, C_I1, C_128MI, C_127MJ, C_J = 0, 128, 256, 384, 512, 513
NCTAB = 514
W1_CQ, W1_CKV, W1_KR, W1_RK, W1_RKS, W1_RV, W1_N = 0, 384, 640, 672, 1184, 1696, 2720
W3_RQ, W3_RQS, W3_RK, W3_RKS, W3_RV, W3_RG, W3_GR, W3_N = 0, 512, 1024, 1536, 2048, 3072, 4096, 5120
ARENA_COLS = 106400


def build_program(stages=99, dbg=False):
    nc = bass.Bass("TRN2", target_bir_lowering=False)

    def din(name, shape, dt=F32):
        return nc.dram_tensor(name, list(shape), dt, kind="ExternalInput").ap()

    xo = din("xo", [TOK, D])
    xr = din("xr", [NOT_ * 128, D])
    w1_d = din("w1", [128, 8, W1_N])
    w3a_d = din("w3a", [128, 8, W3_N])
    wgm_d = din("wgm", [128, 8, 1024])
    wmo_d = din("wmo", [128, 4, 1024])
    wo_d = din("wo", [128, 8, 1024])
    wuq_d = din("wuq", [128, 3, 768])
    wuk_d = din("wuk", [128, 2, 512])
    wuv_d = din("wuv", [128, 2, 512])
    wro_d = din("wro", [128, 8, 1024])
    wg_d = din("wg", [128, 8, DFF])
    wu_d = din("wu", [128, 8, DFF])
    wd_d = din("wd", [128, NFC, 1024])
    vec_d = din("vec", [128, NVEC])
    ctab_d = din("ctab", [128, NCTAB])
    dist_d = din("dist", [128, 2 * NOT_])
    ident_d = din("ident", [128, 128])
    nfin_d = din("nfin", [128, D])
    ropeR_own_d = din("ropeR_own", [128, NT, 256])
    ropeR_oth_d = din("ropeR_oth", [128, NOT_, 256])
    tabM_own_d = din("tabM_own", [128, NT, 64])
    tabM_oth_d = din("tabM_oth", [128, NOT_, 64])
    out_d = nc.dram_tensor("out", [TOK, D], F32, kind="ExternalOutput").ap()
    SCR_KIND = "ExternalOutput" if dbg else "Internal"
    Qs_d = nc.dram_tensor("Qs", [8, 96, TOK], BF16, kind=SCR_KIND).ap()
    m1_d = nc.dram_tensor("m1s", [TOK, D], BF16, kind=SCR_KIND).ap()
    h1_d = nc.dram_tensor("h1s", [TOK, D], F32, kind=SCR_KIND).ap()
    ckvT_d = nc.dram_tensor("ckvTs", [128, 2 * NKEYP], BF16, kind=SCR_KIND).ap()
    krope_d = nc.dram_tensor("kropes", [32, NKEYP], BF16, kind=SCR_KIND).ap()
    SbAll_d = nc.dram_tensor("SbAlls", [NT, 128, 512], BF16, kind=SCR_KIND).ap()

    S = Sched()
    es = ExitStack()
    arena_t = es.enter_context(nc.sbuf_tensor("arena", [128, ARENA_COLS], BF16))
    A = Arena(arena_t, ARENA_COLS)
    PS2 = [es.enter_context(nc.psum_tensor(f"ps2_{i}", [128, 1024], F32)) for i in range(4)]

    def bank(i):
        return PS2[i // 2][:, (i % 2) * 512:(i % 2 + 1) * 512]

    def bankb(i):
        return bank(i).bitcast(BF16)

    def B(i):
        return "b%d" % i

    def MM(out, lhsT, rhs, start, stop, r, w):
        S.op("pe", lambda e: e.matmul(out, lhsT=lhsT, rhs=rhs, start=start, stop=stop), r, w)

    def TR(out, in_, idn, r, w):
        S.op("pe", lambda e: e.transpose(out=out, in_=in_, identity=idn), r, w)

    def ACT(out, in_, func, r, w, scale=None, bias=None, accum=None):
        kw = {}
        if scale is not None:
            kw["scale"] = scale
        if bias is not None:
            kw["bias"] = bias
        if accum is not None:
            kw["accum_out"] = accum
        S.op("act", lambda e: e.activation(out=out, in_=in_, func=func, **kw), r, w)

    def TT(eng, out, in0, in1, op, r, w):
        S.op(eng, lambda e: e.tensor_tensor(out=out, in0=in0, in1=in1, op=op), r, w)

    def TS(eng, out, in0, s1, op0, r, w, s2=None, op1=None):
        if op1 is None:
            S.op(eng, lambda e: e.tensor_scalar(out=out, in0=in0, scalar1=s1, scalar2=None, op0=op0), r, w)
        else:
            S.op(eng, lambda e: e.tensor_scalar(out=out, in0=in0, scalar1=s1, scalar2=s2, op0=op0, op1=op1), r, w)

    def STT(out, in0, scalar, in1, op0, op1, r, w):
        S.op("dve", lambda e: e.scalar_tensor_tensor(out=out, in0=in0, scalar=scalar, in1=in1, op0=op0, op1=op1), r, w)

    def CP(eng, out, in_, r, w):
        if eng == "act":
            S.op("act", lambda e: e.copy(out=out, in_=in_), r, w)
        else:
            S.op(eng, lambda e: e.tensor_copy(out=out, in_=in_), r, w)

    def RED(out, in_, r, w):
        S.op("dve", lambda e: e.tensor_reduce(out=out, in_=in_, axis=AX.X, op=ALU.add), r, w)

    def MSET(eng, ap, val, w):
        S.op(eng, lambda e: e.memset(ap, val), (), w)

    def DMA(eng, out, in_, r, w, key):
        S.op(eng, lambda e: e.dma_start(out=out, in_=in_), r, w, dma=key)

    chain_i = [0]

    def LOAD(eng, out, in_, w):
        k = "chain_%s%d" % (eng, chain_i[0] % 3)
        chain_i[0] += 1
        S.op(eng, lambda e: e.dma_start(out=out, in_=in_), (), list(w) + [k], dma=k)

    def v3(ap, a):
        return ap.rearrange("p (a b) -> p a b", a=a)

    vec = A.alloc(NVEC + 1, F32)
    identf = A.alloc(128, F32)
    identb = A.alloc(128, BF16)
    lg8 = A.alloc(16, F32)
    lgP = A.alloc(8, F32)
    mhalf = A.alloc(16, F32)
    rstd_own = A.alloc(NT, F32)
    P0_END = A.pos
    ctab = A.alloc(NCTAB, F32)
    c128 = A.alloc(128, F32)
    Sf = A.alloc(512, F32)
    Sb = A.alloc(512, F32)
    P1_END = A.pos

    LOAD("sp", vec[:, 0:NVEC], vec_d, ["vec"])
    LOAD("sp", ctab, ctab_d, ["ctab"])
    LOAD("sp", identf, ident_d, ["identf"])
    CP("dve", identb, identf, ["identf"], ["identb"])
    MSET("pool", c128, 128.0, ["c128"])
    MSET("pool", mhalf, -0.5, ["mhalf"])
    MSET("pool", Sf, 0.0, ["Sf"])
    MSET("pool", Sb, 0.0, ["Sb"])
    ACT(lg8, vec[:, V_DF8:V_DF8 + 16], AF.Exp, ["vec"], ["lg8"])
    TS("dve", lg8, lg8, -1.0, ALU.mult, ["lg8"], ["lg8"])
    ACT(lgP, vec[:, V_DFP:V_DFP + 8], AF.Exp, ["vec"], ["lgP"])
    TS("dve", lgP, lgP, -1.0, ALU.mult, ["lgP"], ["lgP"])

    if stages >= 1:
        A.pos = P1_END
        w1 = A.alloc(8 * W1_N, BF16)
        wuq = A.alloc(3 * 768, BF16)
        w13 = v3(w1, 8)
        wuq3 = v3(wuq, 3)
        wfb = A.alloc(16, F32)
        GbT = A.alloc(512, F32)
        wo_fb = A.alloc(2 * NOT_ * 8, F32)
        dist = A.alloc(2 * NOT_, F32)
        tabM_own = A.alloc(NT * 64, F32)
        tabM_oth = A.alloc(NOT_ * 64, F32)
        NXB = 2
        xbuf = [A.alloc(D, F32) for _ in range(NXB)]
        ropeb = [A.alloc(256, F32) for _ in range(NXB)]
        xs = [A.alloc(D, BF16) for _ in range(2)]
        junk = A.alloc(D, BF16)
        uT = [A.alloc(D, BF16) for _ in range(2)]
        st = A.alloc(8, F32)
        ckvn = A.alloc(256, BF16)
        kaug = A.alloc(96, BF16)
        ropeA = A.alloc(32, F32)
        ropeBv = A.alloc(32, F32)
        t1 = A.alloc(512, F32)
        t2 = A.alloc(512, F32)
        kT = A.alloc(512, BF16)
        kwf = A.alloc(512, BF16)
        kwb = A.alloc(512, BF16)
        vtok = A.alloc(1024, BF16)
        cqn = A.alloc(384, BF16)
        cqT = A.alloc(384, BF16)
        qA = A.alloc(256, F32)
        qB = A.alloc(256, F32)
        qtok = A.alloc(768, BF16)
        Qst = [A.alloc(8 * 512, BF16) for _ in range(2)]
        ckst = [A.alloc(2 * 512, BF16) for _ in range(2)]
        krst = [A.alloc(512, BF16) for _ in range(2)]
        sbst = [A.alloc(512, BF16) for _ in range(2)]

        LOAD("sp", dist, dist_d, ["dist"])
        LOAD("sp", tabM_own, tabM_own_d.rearrange("p a b -> p (a b)"), ["tabM_own"])
        LOAD("sp", tabM_oth, tabM_oth_d.rearrange("p a b -> p (a b)"), ["tabM_oth"])
        TS("dve", wfb[:, 0:8], lg8[:, 0:8], ctab[:, C_127MJ:C_127MJ + 1], ALU.mult, ["lg8", "ctab"], ["wfb"])
        TS("dve", wfb[:, 8:16], lg8[:, 8:16], ctab[:, C_J:C_J + 1], ALU.mult, ["lg8", "ctab", "wfb"], ["wfb"])
        ACT(wfb, wfb, AF.Exp, ["wfb"], ["wfb"])
        TS("dve", wfb, wfb, 0.125, ALU.mult, ["wfb"], ["wfb"])
        GbT3 = v3(GbT, 4)
        for hp in range(4):
            ACT(GbT3[:, hp, :], c128, AF.Exp, ["c128", "lgP"], ["GT"], scale=lgP[:, 4 + hp:5 + hp])
        wo3 = v3(wo_fb, 2 * NOT_)
        for h in range(8):
            TS("dve", wo3[:, 0:NOT_, h], dist[:, 0:NOT_], lg8[:, h:h + 1], ALU.mult, ["dist", "lg8"], ["wo_fb"])
            TS("dve", wo3[:, NOT_:2 * NOT_, h], dist[:, NOT_:2 * NOT_], lg8[:, 8 + h:9 + h], ALU.mult, ["dist", "lg8"], ["wo_fb"])
        ACT(wo_fb, wo_fb, AF.Exp, ["wo_fb"], ["wo_fb"])
        TS("dve", wo_fb, wo_fb, 0.125, ALU.mult, ["wo_fb"], ["wo_fb"])

        for kc in range(8):
            LOAD("pool", w1[:, kc * W1_N:(kc + 1) * W1_N], w1_d[:, kc, :], ["w1_%d" % kc])
        for kc in range(3):
            LOAD("pool", wuq[:, kc * 768:(kc + 1) * 768], wuq_d[:, kc, :], ["wuq"])
        for kc in range(8):
            sl = slice(kc * W1_N, (kc + 1) * W1_N)
            TS("dve", w1[:, sl], w1[:, sl], vec[:, V_NMW + kc:V_NMW + kc + 1], ALU.mult, ["w1_%d" % kc, "vec"], ["w1_%d" % kc])
        for kc in range(3):
            sl = slice(kc * 768, (kc + 1) * 768)
            TS("dve", wuq[:, sl], wuq[:, sl], vec[:, V_QNW + kc:V_QNW + kc + 1], ALU.mult, ["wuq", "vec"], ["wuq"])
        W1T = ["w1_%d" % kc for kc in range(8)]
        MSET("pool", kaug, 0.0, ["kaug"])
        for i in range(2):
            MSET("pool", ckst[i], 0.0, ["ckst%d" % i])
            MSET("pool", krst[i], 0.0, ["krst%d" % i])
        gcount = 0

        seq = [("o", t) for t in range(NOT_)] + [("s", c) for c in range(NT - 1, -1, -1)]
        import os
        if os.environ.get("K_METAFIRST"):
            seq = [("o", NOT_ - 1)] + [("o", t) for t in range(NOT_ - 1)] + [("s", c) for c in range(NT - 1, -1, -1)]
        if os.environ.get("K_S1N"):
            seq = seq[:int(os.environ["K_S1N"])]
        def front1_a(it):
            kind, ti = seq[it]
            own = kind == "s"
            xsrc = xo[ti * 128:(ti + 1) * 128, :] if own else xr[ti * 128:(ti + 1) * 128, :]
            rsrc = ropeR_own_d[:, ti, :] if own else ropeR_oth_d[:, ti, :]
            xb, xbt = xbuf[it % NXB], "xbuf%d" % (it % NXB)
            rb, rbt = ropeb[it % NXB], "ropeb%d" % (it % NXB)
            xsb, xst = xs[it % 2], "xs%d" % (it % 2)
            DMA("sp", xb, xsrc, (), [xbt], xbt)
            DMA("sp", rb, rsrc, (), [rbt], rbt)
            ACT(junk, xb, AF.Square, [xbt], ["junk", "x_ss"], accum=st[:, 0:1])
            rs = rstd_own[:, ti:ti + 1] if own else st[:, 1:2]
            TS("dve", st[:, 0:1], st[:, 0:1], 1.0 / D, ALU.mult, ["x_ss"], ["x_ss"], s2=RMS_EPS, op1=ALU.add)
            TT("pool", rs, st[:, 0:1], mhalf[:, 0:1], ALU.pow, ["x_ss", "mhalf"], ["x_rstd"])
            ACT(xsb, xb, AF.Identity, [xbt, "x_rstd"], [xst], scale=rs)

        def front1_b(it):
            xsb, xst = xs[it % 2], "xs%d" % (it % 2)
            u, ut = uT[it % 2], "uT%d" % (it % 2)
            tb = bankb(0)
            for kc in range(8):
                TR(tb[:, kc * 128:(kc + 1) * 128], xsb[:, kc * 128:(kc + 1) * 128], identb, [xst, "identb"], [B(0)])
            CP("act", u, tb[:, 0:1024], [B(0)], [ut])

        lat = [A.alloc(672, F32) for _ in range(2)]
        kT2 = [kT, A.alloc(512, BF16)]
        vtok2 = [vtok, A.alloc(1024, BF16)]
        qtok2 = [qtok, A.alloc(768, BF16)]
        gstate = {"gcount": 0}

        def proj1(it):
            kind, ti = seq[it]
            own = kind == "s"
            par = it % 2
            rb, rbt = ropeb[it % NXB], "ropeb%d" % (it % NXB)
            u, ut = uT[par], "uT%d" % par
            u3 = v3(u, 8)
            la, lat_t = lat[par], "lat%d" % par
            for kc in range(8):
                MM(bank(1)[:, 0:288], u3[:, kc, :], w13[:, kc, W1_CKV:W1_CKV + 288], kc == 0, kc == 7, [ut, W1T[kc]], [B(1)])
            if own:
                for kc in range(8):
                    MM(bank(2)[:, 0:384], u3[:, kc, :], w13[:, kc, W1_CQ:W1_CQ + 384], kc == 0, kc == 7, [ut, W1T[kc]], [B(2)])
            for hp in range(4):
                for kc in range(8):
                    MM(bank(3)[:, hp * 128:(hp + 1) * 128], w13[:, kc, W1_RK + hp * 128:W1_RK + (hp + 1) * 128], u3[:, kc, :],
                       kc == 0, kc == 7, [ut, W1T[kc]], [B(3)])
            CP("act", la[:, 0:288], bank(1)[:, 0:288], [B(1)], [lat_t])
            if own:
                CP("act", la[:, 288:672], bank(2)[:, 0:384], [B(2)], [lat_t])
            for hp in range(4):
                for kc in range(8):
                    MM(bank(4)[:, hp * 128:(hp + 1) * 128], w13[:, kc, W1_RKS + hp * 128:W1_RKS + (hp + 1) * 128], u3[:, kc, :],
                       kc == 0, kc == 7, [ut, W1T[kc]], [B(4)])
            cosR = rb[:, 0:128].unsqueeze(1).to_broadcast([128, 4, 128])
            sinR = rb[:, 128:256].unsqueeze(1).to_broadcast([128, 4, 128])
            TT("dve", v3(t1, 4), v3(bank(3), 4), cosR, ALU.mult, [B(3), rbt], ["t1"])
            for hf in range(2):
                for kc in range(8):
                    MM(bank(5 + hf), u3[:, kc, :], w13[:, kc, W1_RV + hf * 512:W1_RV + (hf + 1) * 512], kc == 0, kc == 7,
                       [ut, W1T[kc]], [B(5 + hf)])
            TT("dve", v3(t2, 4), v3(bank(4), 4), sinR, ALU.mult, [B(4), rbt], ["t2"])
            TT("pool", kT2[par], t1, t2, ALU.add, ["t1", "t2"], ["kT%d" % par])
            CP("act", vtok2[par][:, 0:512], bank(5), [B(5)], ["vtok%d" % par])
            CP("act", vtok2[par][:, 512:1024], bank(6), [B(6)], ["vtok%d" % par])

        def back1_a(it):
            kind, ti = seq[it]
            own = kind == "s"
            par = it % 2
            tabM = (tabM_own if own else tabM_oth)[:, ti * 64:(ti + 1) * 64]
            tabMt = "tabM_own" if own else "tabM_oth"
            la, lat_t = lat[par], "lat%d" % par
            ACT(junk[:, 0:256], la[:, 0:256], AF.Square, [lat_t], ["junk", "kv_ss"], accum=st[:, 2:3])
            TS("dve", st[:, 2:3], st[:, 2:3], 1.0 / 256, ALU.mult, ["kv_ss"], ["kv_ss"], s2=RMS_EPS, op1=ALU.add)
            TT("pool", st[:, 3:4], st[:, 2:3], mhalf[:, 0:1], ALU.pow, ["kv_ss", "mhalf"], ["kv_rstd"])
            ACT(ckvn, la[:, 0:256], AF.Identity, [lat_t, "kv_rstd"], ["ckvn"], scale=st[:, 3:4])
            if own:
                ACT(junk[:, 0:384], la[:, 288:672], AF.Square, [lat_t], ["junk", "q_ss"], accum=st[:, 4:5])
                TS("dve", st[:, 4:5], st[:, 4:5], 1.0 / 384, ALU.mult, ["q_ss"], ["q_ss"], s2=RMS_EPS, op1=ALU.add)
                TT("pool", st[:, 5:6], st[:, 4:5], mhalf[:, 0:1], ALU.pow, ["q_ss", "mhalf"], ["q_rstd"])
                ACT(cqn, la[:, 288:672], AF.Identity, [lat_t, "q_rstd"], ["cqn"], scale=st[:, 5:6])
            TT("dve", ropeA, la[:, 256:288], tabM[:, 0:32], ALU.mult, [lat_t, tabMt], ["ropeA"])
            TT("dve", ropeBv, la[:, 256:288], tabM[:, 32:64], ALU.mult, [lat_t, tabMt], ["ropeB"])
            TT("pool", kaug[:, 64:80], ropeA[:, 0:16], ropeBv[:, 16:32], ALU.subtract, ["ropeA", "ropeB"], ["kaug"])
            TT("pool", kaug[:, 80:96], ropeBv[:, 0:16], ropeA[:, 16:32], ALU.add, ["ropeA", "ropeB"], ["kaug"])

        def back1_b(it):
            kind, ti = seq[it]
            own = kind == "s"
            par = it % 2
            slot = ti if own else (32 + ti)
            tabM = (tabM_own if own else tabM_oth)[:, ti * 64:(ti + 1) * 64]
            tabMt = "tabM_own" if own else "tabM_oth"
            kTp, kTt = kT2[par], "kT%d" % par
            vtp, vtt = vtok2[par], "vtok%d" % par
            hb = bankb(7)
            for hp in range(4):
                TR(hb[:, 384 + hp * 128:384 + (hp + 1) * 128], kTp[:, hp * 128:(hp + 1) * 128], identb, [kTt, "identb"], [B(7)])
            for c2 in range(2):
                TR(hb[:, c2 * 128:(c2 + 1) * 128], ckvn[:, c2 * 128:(c2 + 1) * 128], identb, ["ckvn", "identb"], [B(7)])
            TR(hb[0:96, 256:384], kaug, identb, ["kaug", "identb"], [B(7)])
            if own:
                tb0 = bankb(0)
                for c3 in range(3):
                    TR(tb0[:, c3 * 128:(c3 + 1) * 128], cqn[:, c3 * 128:(c3 + 1) * 128], identb, ["cqn", "identb"], [B(0)])
                CP("dve", cqT, tb0[:, 0:384], [B(0)], ["cqT"])
            ktok3 = hb[:, 384:896].rearrange("p (h d) -> p h d", h=8)
            if own:
                wbb = wfb[:, 8:16].unsqueeze(2).to_broadcast([128, 8, 64])
                TT("dve", kwb.rearrange("p (h d) -> p h d", h=8), ktok3, wbb, ALU.mult, [B(7), "wfb"], ["kwb"])
                dirs = [("b", kwb, "kwb", 3)]
            else:
                wof = wo3[:, ti, :].unsqueeze(2).to_broadcast([128, 8, 64])
                wob = wo3[:, NOT_ + ti, :].unsqueeze(2).to_broadcast([128, 8, 64])
                TT("dve", kwf.rearrange("p (h d) -> p h d", h=8), ktok3, wof, ALU.mult, [B(7), "wo_fb"], ["kwf"])
                TT("dve", kwb.rearrange("p (h d) -> p h d", h=8), ktok3, wob, ALU.mult, [B(7), "wo_fb"], ["kwb"])
                dirs = [("f", kwf, "kwf", 1), ("b", kwb, "kwb", 3)]
            gb_i = gstate["gcount"] % 2
            cks, ckt = ckst[gb_i], "ckst%d" % gb_i
            krs, krt = krst[gb_i], "krst%d" % gb_i
            sp_ = slot % 4
            CP("dve", v3(cks, 2)[:, :, sp_ * 128:(sp_ + 1) * 128], hb[:, 0:256].rearrange("p (a b) -> p a b", a=2), [B(7)], [ckt])
            CP("dve", krs[64:96, sp_ * 128:(sp_ + 1) * 128], hb[64:96, 256:384], [B(7)], [krt])
            flush = (slot == 64) or (own and ti % 4 == 0) or ((not own) and slot < 64 and slot % 4 == 3)
            if flush:
                g0 = (slot // 4) * 512
                DMA("pool", ckvT_d.rearrange("p (a n) -> p a n", a=2)[:, :, g0:g0 + 512], v3(cks, 2)[:, :, 0:512], [ckt], [("ckvT_d", slot // 4)], ckt)
                DMA("pool", krope_d[:, g0:g0 + 512], krs[64:96, 0:512], [krt], [("krope_d", slot // 4)], krt)
                gstate["gcount"] += 1
            if own:
                cqT3 = v3(cqT, 3)
                for kc in range(3):
                    MM(bank(5)[:, 0:480], cqT3[:, kc, :], wuq3[:, kc, 0:480], kc == 0, kc == 2, ["cqT", "wuq"], [B(5)])
                for kc in range(3):
                    MM(bank(6)[:, 0:288], cqT3[:, kc, :], wuq3[:, kc, 480:768], kc == 0, kc == 2, ["cqT", "wuq"], [B(6)])
            for (dname, kw_, kwt, b0) in dirs:
                for hp in range(4):
                    bk = bank(b0 + hp // 2)
                    MM(bk[:, (hp % 2) * 256:(hp % 2) * 256 + 256], kw_[:, hp * 128:(hp + 1) * 128], vtp[:, hp * 256:(hp + 1) * 256],
                       True, True, [kwt, vtt], [B(b0 + hp // 2)])
            Sb3 = v3(Sb, 4)
            Sf3 = v3(Sf, 4)
            if own:
                sbs, sbt = sbst[ti % 2], "sbst%d" % (ti % 2)
                CP("pool", sbs, Sb, ["Sb"], [sbt])
                DMA("pool", SbAll_d[ti, :, :], sbs, [sbt], [("SbAll", ti)], sbt)
                TT("pool", Sb, Sb, GbT, ALU.mult, ["Sb", "GT", sbt], ["Sb"])
            for (dname, kw_, kwt, b0) in dirs:
                Sx3 = Sb3 if dname == "b" else Sf3
                Sxt = "Sb" if dname == "b" else "Sf"
                for hh in range(2):
                    bk = v3(bank(b0 + hh), 2)
                    TT("dve", Sx3[0:64, 2 * hh:2 * hh + 2, :], Sx3[0:64, 2 * hh:2 * hh + 2, :], bk[0:64, :, 0:128], ALU.add,
                       [Sxt, B(b0 + hh)], [Sxt])
                    TT("dve", Sx3[64:128, 2 * hh:2 * hh + 2, :], Sx3[64:128, 2 * hh:2 * hh + 2, :], bk[64:128, :, 128:256], ALU.add,
                       [Sxt, B(b0 + hh)], [Sxt])
            if own:
                qtk, qtt = qtok2[par], "qtok%d" % par
                q3 = qtk.rearrange("p (h c) -> p h c", h=8)
                for (bk_i, h0, nh) in ((5, 0, 5), (6, 5, 3)):
                    src = bank(bk_i)[:, 0:nh * 96].rearrange("p (h c) -> p h c", h=nh)
                    CP("act", q3[:, h0:h0 + nh, 0:64], src[:, :, 0:64], [B(bk_i)], [qtt])
                    ccq = tabM[:, 0:32].unsqueeze(1).to_broadcast([128, nh, 32])
                    ssq = tabM[:, 32:64].unsqueeze(1).to_broadcast([128, nh, 32])
                    qA3 = qA[:, h0 * 32:(h0 + nh) * 32].rearrange("p (h c) -> p h c", h=nh)
                    qB3 = qB[:, h0 * 32:(h0 + nh) * 32].rearrange("p (h c) -> p h c", h=nh)
                    TT("dve", qA3, src[:, :, 64:96], ccq, ALU.mult, [B(bk_i), tabMt], ["qA"])
                    TT("dve", qB3, src[:, :, 64:96], ssq, ALU.mult, [B(bk_i), tabMt], ["qB"])
                qA3 = qA.rearrange("p (h c) -> p h c", h=8)
                qB3 = qB.rearrange("p (h c) -> p h c", h=8)
                TT("pool", q3[:, :, 64:80], qA3[:, :, 0:16], qB3[:, :, 16:32], ALU.subtract, ["qA", "qB"], [qtt])
                TT("pool", q3[:, :, 80:96], qB3[:, :, 0:16], qA3[:, :, 16:32], ALU.add, ["qA", "qB"], [qtt])

        def back2(it):
            kind, ti = seq[it]
            if kind != "s":
                return
            par = it % 2
            qtk, qtt = qtok2[par], "qtok%d" % par
            q3 = qtk.rearrange("p (h c) -> p h c", h=8)
            tb2 = bankb(2)
            for h in range(8):
                TR(tb2[0:96, h * 128:(h + 1) * 128], q3[:, h, :], identb, [qtt, "identb"], [B(2)])
            g = ti // 4
            qs = Qst[g % 2]
            qst = "Qst%d" % (g % 2)
            qs3 = v3(qs, 8)
            CP("dve", qs3[0:96, :, (ti % 4) * 128:(ti % 4 + 1) * 128], tb2[0:96, 0:1024].rearrange("p (h t) -> p h t", h=8),
               [B(2)], [qst])
            if ti % 4 == 0:
                DMA("pool", Qs_d[:, :, g * 512:(g + 1) * 512].rearrange("h d t -> d h t"), qs3[0:96, :, :], [qst], [("Qs", g)], qst)

        n1 = len(seq)
        if n1:
            front1_a(0)
            front1_b(0)
        for it in range(n1):
            if it + 1 < n1:
                front1_a(it + 1)
            if it >= 1:
                back1_a(it - 1)
            proj1(it)
            if it + 1 < n1:
                front1_b(it + 1)
            if it >= 1:
                back1_b(it - 1)
            if it >= 2:
                back2(it - 2)
        if n1:
            back1_a(n1 - 1)
            back1_b(n1 - 1)
            if n1 >= 2:
                back2(n1 - 2)
            back2(n1 - 1)
        S.barrier()

    if stages >= 2:
        A.pos = P1_END
        w3 = A.alloc(8 * W3_N, BF16)
        wro = A.alloc(8 * 1024, BF16)
        w33 = v3(w3, 8)
        wro3 = v3(wro, 8)
        DT = A.alloc(1024, F32)
        wfb = A.alloc(16, F32)
        wqfd = A.alloc(1024, F32)
        wqbd = A.alloc(1024, F32)
        GfT = A.alloc(512, F32)
        Sf_bf = A.alloc(512, BF16)
        NXB = 2
        xbuf = [A.alloc(D, F32) for _ in range(NXB)]
        ropeb = [A.alloc(256, F32) for _ in range(NXB)]
        sbl = [A.alloc(512, BF16) for _ in range(2)]
        xs = [A.alloc(D, BF16) for _ in range(2)]
        uT = [A.alloc(D, BF16) for _ in range(2)]
        t1 = A.alloc(512, F32)
        t2 = A.alloc(512, F32)
        qTd = A.alloc(1024, BF16)
        kT = A.alloc(512, BF16)
        qfd = A.alloc(1024, BF16)
        qbd = A.alloc(1024, BF16)
        kwf = A.alloc(512, BF16)
        vtok = A.alloc(1024, BF16)
        sig = A.alloc(1024, F32)
        silu = sig
        sgr = A.alloc(1024, F32)
        PT = A.alloc(1024, BF16)
        sq = A.alloc(1024, F32)
        gst = A.alloc(64, F32)
        zn = sq
        z = A.alloc(1024, BF16)
        zT = A.alloc(1024, BF16)
        m1b = [A.alloc(1024, BF16) for _ in range(2)]
        DT3 = v3(DT, 8)
        for h in range(8):
            TS("dve", DT3[:, h, :], ctab[:, C_POS:C_POS + 128], lg8[:, h:h + 1], ALU.mult, ["ctab", "lg8"], ["DT"])
            STT(DT3[:, h, :], ctab[:, C_NEG:C_NEG + 128], lg8[:, 8 + h:9 + h], DT3[:, h, :], ALU.mult, ALU.add,
                ["ctab", "lg8", "DT"], ["DT"])
        ACT(DT, DT, AF.Exp, ["DT"], ["DT"])
        TS("dve", DT, DT, 0.125, ALU.mult, ["DT"], ["DT"])
        TS("dve", wfb[:, 0:8], lg8[:, 0:8], ctab[:, C_127MJ:C_127MJ + 1], ALU.mult, ["lg8", "ctab"], ["wfb"])
        TS("dve", wfb[:, 8:16], lg8[:, 8:16], ctab[:, C_J:C_J + 1], ALU.mult, ["lg8", "ctab", "wfb"], ["wfb"])
        ACT(wfb, wfb, AF.Exp, ["wfb"], ["wfb"])
        TS("dve", wfb, wfb, 0.125, ALU.mult, ["wfb"], ["wfb"])
        wqfd3, wqbd3, GfT3 = v3(wqfd, 4), v3(wqbd, 4), v3(GfT, 4)
        MSET("pool", wqfd, 0.0, ["wq"])
        MSET("pool", wqbd, 0.0, ["wq"])
        MSET("pool", qTd, 0.0, ["qT"])
        for hp in range(4):
            for par in range(2):
                rs_ = slice(par * 64, (par + 1) * 64)
                cs_ = slice(par * 128, (par + 1) * 128)
                ACT(wqfd3[rs_, hp, cs_], ctab[rs_, C_I1:C_I1 + 128], AF.Exp, ["ctab", "lgP", "wq"], ["wq"], scale=lgP[rs_, hp:hp + 1])
                ACT(wqbd3[rs_, hp, cs_], ctab[rs_, C_128MI:C_128MI + 128], AF.Exp, ["ctab", "lgP", "wq"], ["wq"], scale=lgP[rs_, 4 + hp:5 + hp])
            ACT(GfT3[:, hp, :], c128, AF.Exp, ["c128", "lgP"], ["GT"], scale=lgP[:, hp:hp + 1])
        for kc in range(8):
            LOAD("pool", w3[:, kc * W3_N:(kc + 1) * W3_N], w3a_d[:, kc, :], ["w3_%d" % kc])
            LOAD("pool", wro[:, kc * 1024:(kc + 1) * 1024], wro_d[:, kc, :], ["wro_%d" % kc])
        for kc in range(8):
            sl = slice(kc * W3_N, (kc + 1) * W3_N)
            TS("dve", w3[:, sl], w3[:, sl], vec[:, V_NMW + kc:V_NMW + kc + 1], ALU.mult, ["w3_%d" % kc, "vec"], ["w3_%d" % kc])
            sl = slice(kc * 1024, (kc + 1) * 1024)
            TS("dve", wro[:, sl], wro[:, sl], vec[:, V_GNW + kc:V_GNW + kc + 1], ALU.mult, ["wro_%d" % kc, "vec"], ["wro_%d" % kc])
        W3T = ["w3_%d" % kc for kc in range(8)]
        WRT = ["wro_%d" % kc for kc in range(8)]
        CP("pool", Sf_bf, Sf, ["Sf"], ["Sf_bf"])
        Sf3 = v3(Sf, 4)
        Sfb3 = v3(Sf_bf, 4)
        PT3 = v3(PT, 8)
        N3A = int(os.environ.get("K_S3N", NT))

        def front3a(c):
            xb, xbt = xbuf[c % NXB], "xbuf%d" % (c % NXB)
            rb, rbt = ropeb[c % NXB], "ropeb%d" % (c % NXB)
            xsb, xst = xs[c % 2], "xs%d" % (c % 2)
            u, ut = uT[c % 2], "uT%d" % (c % 2)
            DMA("sp", xb, xo[c * 128:(c + 1) * 128, :], (), [xbt], xbt)
            DMA("sp", rb, ropeR_own_d[:, c, :], (), [rbt], rbt)
            DMA("sp", sbl[c % 2], SbAll_d[c, :, :], [("SbAll", c)], ["sbl%d" % (c % 2)], "sbl%d" % (c % 2))
            ACT(xsb, xb, AF.Identity, [xbt], [xst], scale=rstd_own[:, c:c + 1])
            tb = bankb(0)
            for kc in range(8):
                TR(tb[:, kc * 128:(kc + 1) * 128], xsb[:, kc * 128:(kc + 1) * 128], identb, [xst, "identb"], [B(0)])
            CP("act", u, tb[:, 0:1024], [B(0)], [ut])

        def tail3a(c):
            tb7 = bankb(7)
            for kc in range(8):
                TR(tb7[:, kc * 128:(kc + 1) * 128], z[:, kc * 128:(kc + 1) * 128], identb, ["z", "identb"], [B(7)])
            CP("act", zT, tb7[:, 0:1024], [B(7)], ["zT"])
            zT3 = v3(zT, 8)
            for hf in range(2):
                for kc in range(8):
                    MM(bank(5 + hf), zT3[:, kc, :], wro3[:, kc, hf * 512:(hf + 1) * 512], kc == 0, kc == 7, ["zT", WRT[kc]], [B(5 + hf)])
            mb = m1b[c % 2]
            mbt = "m1b%d" % (c % 2)
            for hf in range(2):
                TT("dve", mb[:, hf * 512:(hf + 1) * 512], bank(5 + hf), sgr[:, hf * 512:(hf + 1) * 512], ALU.mult, [B(5 + hf), "sgr"], [mbt])
            DMA("pool", m1_d[c * 128:(c + 1) * 128, :], mb, [mbt], [("m1", c)], mbt)

        if N3A:
            front3a(0)
        for c in range(N3A):
            rb = ropeb[c % NXB]
            rbt = "ropeb%d" % (c % NXB)
            u = uT[c % 2]
            ut = "uT%d" % (c % 2)
            u3 = v3(u, 8)
            sbc, sbct = sbl[c % 2], "sbl%d" % (c % 2)
            sbc3 = v3(sbc, 4)
            for (bk_i, c0) in ((1, W3_RQ), (2, W3_RQS), (3, W3_RK), (4, W3_RKS)):
                for hp in range(4):
                    for kc in range(8):
                        MM(bank(bk_i)[:, hp * 128:(hp + 1) * 128], w33[:, kc, c0 + hp * 128:c0 + (hp + 1) * 128], u3[:, kc, :],
                           kc == 0, kc == 7, [ut, W3T[kc]], [B(bk_i)])
            for hf in range(2):
                for kc in range(8):
                    MM(bank(5 + hf), u3[:, kc, :], w33[:, kc, W3_RV + hf * 512:W3_RV + (hf + 1) * 512], kc == 0, kc == 7,
                       [ut, W3T[kc]], [B(5 + hf)])
            cosR = rb[:, 0:128].unsqueeze(1).to_broadcast([128, 4, 128])
            sinR = rb[:, 128:256].unsqueeze(1).to_broadcast([128, 4, 128])
            TT("dve", v3(t1, 4), v3(bank(1), 4), cosR, ALU.mult, [B(1), rbt], ["t1"])
            TT("dve", v3(t2, 4), v3(bank(2), 4), sinR, ALU.mult, [B(2), rbt], ["t2"])
            qTd3 = v3(qTd, 4)
            TT("pool", qTd3[0:64, :, 0:128], v3(t1, 4)[0:64, :, :], v3(t2, 4)[0:64, :, :], ALU.add, ["t1", "t2", "qT"], ["qT"])
            TT("pool", qTd3[64:128, :, 128:256], v3(t1, 4)[64:128, :, :], v3(t2, 4)[64:128, :, :], ALU.add, ["t1", "t2", "qT"], ["qT"])
            TT("dve", v3(t1, 4), v3(bank(3), 4), cosR, ALU.mult, [B(3), rbt, "qT"], ["t1"])
            TT("dve", v3(t2, 4), v3(bank(4), 4), sinR, ALU.mult, [B(4), rbt, "qT"], ["t2"])
            TT("pool", kT, t1, t2, ALU.add, ["t1", "t2"], ["kT"])
            TT("dve", qfd, qTd, wqfd, ALU.mult, ["qT", "wq"], ["qf"])
            TT("pool", qbd, qTd, wqbd, ALU.mult, ["qT", "wq"], ["qb"])
            CP("act", vtok[:, 0:512], bank(5), [B(5)], ["vtok"])
            CP("act", vtok[:, 512:1024], bank(6), [B(6)], ["vtok"])
            if c >= 1:
                tail3a(c - 1)
            for hp in range(4):
                MM(bank(1 + hp // 2)[:, (hp % 2) * 256:(hp % 2) * 256 + 256], kT[:, hp * 128:(hp + 1) * 128], qTd3[:, hp, :],
                   True, True, ["kT", "qT"], [B(1 + hp // 2)])
            hb = bankb(7)
            for hp in range(4):
                TR(hb[:, hp * 128:(hp + 1) * 128], kT[:, hp * 128:(hp + 1) * 128], identb, ["kT", "identb"], [B(7)])
            for hf in range(2):
                for kc in range(8):
                    MM(bank(3 + hf), u3[:, kc, :], w33[:, kc, W3_RG + hf * 512:W3_RG + (hf + 1) * 512], kc == 0, kc == 7,
                       [ut, W3T[kc]], [B(3 + hf)])
            for hf in range(2):
                TT("dve", PT[:, hf * 512:(hf + 1) * 512], bank(1 + hf), DT[:, hf * 512:(hf + 1) * 512], ALU.mult, [B(1 + hf), "DT"], ["PT"])
            wfbb = wfb[:, 0:8].unsqueeze(2).to_broadcast([128, 8, 64])
            TT("dve", kwf.rearrange("p (h d) -> p h d", h=8), hb[:, 0:512].rearrange("p (h d) -> p h d", h=8), wfbb, ALU.mult,
               [B(7), "wfb"], ["kwf"])
            for hf in range(2):
                sl = slice(hf * 512, (hf + 1) * 512)
                ACT(sig[:, sl], bank(3 + hf), AF.Sigmoid, [B(3 + hf)], ["sig"])
                TT("dve", silu[:, sl], bank(3 + hf), sig[:, sl], ALU.mult, [B(3 + hf), "sig"], ["sig"])
            for h in range(8):
                hp, hb0 = h // 2, (h % 2) * 64
                o_ap = bank(5 + h // 4)[:, (h % 4) * 128:(h % 4 + 1) * 128]
                MM(o_ap, PT3[:, h, :], vtok[:, h * 128:(h + 1) * 128], True, False, ["PT", "vtok"], [B(5 + h // 4)])
                pc = slice((h % 2) * 128, (h % 2) * 128 + 128)
                MM(o_ap, v3(qfd, 4)[:, hp, pc], Sfb3[:, hp, :], False, False, ["qf", "Sf_bf"], [B(5 + h // 4)])
                MM(o_ap, v3(qbd, 4)[:, hp, pc], sbc3[:, hp, :], False, True, ["qb", sbct], [B(5 + h // 4)])
            for hp in range(4):
                bk = bank(1 + hp // 2)
                MM(bk[:, (hp % 2) * 256:(hp % 2) * 256 + 256], kwf[:, hp * 128:(hp + 1) * 128], vtok[:, hp * 256:(hp + 1) * 256],
                   True, True, ["kwf", "vtok"], [B(1 + hp // 2)])
            TT("pool", Sf, Sf, GfT, ALU.mult, ["Sf", "GT", "Sf_bf"], ["Sf"])
            for hh in range(2):
                bk = v3(bank(1 + hh), 2)
                TT("dve", Sf3[0:64, 2 * hh:2 * hh + 2, :], Sf3[0:64, 2 * hh:2 * hh + 2, :], bk[0:64, :, 0:128], ALU.add, ["Sf", B(1 + hh)], ["Sf"])
                TT("dve", Sf3[64:128, 2 * hh:2 * hh + 2, :], Sf3[64:128, 2 * hh:2 * hh + 2, :], bk[64:128, :, 128:256], ALU.add, ["Sf", B(1 + hh)], ["Sf"])
            CP("pool", Sf_bf, Sf, ["Sf"], ["Sf_bf"])
            for hf in range(2):
                for kc in range(8):
                    MM(bank(3 + hf), u3[:, kc, :], w33[:, kc, W3_GR + hf * 512:W3_GR + (hf + 1) * 512], kc == 0, kc == 7,
                       [ut, W3T[kc]], [B(3 + hf)])
            for hf in range(2):
                ACT(sgr[:, hf * 512:(hf + 1) * 512], bank(3 + hf), AF.Sigmoid, [B(3 + hf)], ["sgr"])
            if c + 1 < N3A:
                front3a(c + 1)
            for hf in range(2):
                RED(gst[:, hf * 4:(hf + 1) * 4], v3(bank(5 + hf), 4), [B(5 + hf)], ["gs1"])
                ACT(sq[:, hf * 512:(hf + 1) * 512], bank(5 + hf), AF.Square, [B(5 + hf)], ["sq"])
            RED(gst[:, 8:16], v3(sq, 8), ["sq"], ["gs2"])
            TS("dve", gst[:, 16:24], gst[:, 0:8], 1.0 / 128, ALU.mult, ["gs1"], ["gmean"])
            TT("dve", gst[:, 48:56], gst[:, 16:24], gst[:, 16:24], ALU.mult, ["gmean"], ["gmsq"])
            TS("dve", gst[:, 24:32], gst[:, 8:16], 1.0 / 128, ALU.mult, ["gs2"], ["gvar"], s2=GN_EPS, op1=ALU.add)
            TT("dve", gst[:, 24:32], gst[:, 24:32], gst[:, 48:56], ALU.subtract, ["gvar", "gmsq"], ["gvar"])
            TT("pool", gst[:, 32:40], gst[:, 24:32], mhalf[:, 0:8], ALU.pow, ["gvar", "mhalf"], ["grstd"])
            TT("pool", gst[:, 40:48], gst[:, 16:24], gst[:, 32:40], ALU.mult, ["gmean", "grstd"], ["gnmr"])
            TS("pool", gst[:, 40:48], gst[:, 40:48], -1.0, ALU.mult, ["gnmr"], ["gnmr"], s2=1.0, op1=ALU.mult)
            for h in range(8):
                ACT(zn[:, h * 128:(h + 1) * 128], bank(5 + h // 4)[:, (h % 4) * 128:(h % 4 + 1) * 128], AF.Identity,
                    [B(5 + h // 4), "grstd", "gnmr", "gs2"], ["sq"], scale=gst[:, 32 + h:33 + h], bias=gst[:, 40 + h:41 + h])
            TT("dve", z, zn, silu, ALU.mult, ["sq", "sig"], ["z"])
        if N3A:
            tail3a(N3A - 1)
        S.barrier()

    if stages >= 3:
        A.pos = P0_END
        Ymla = A.alloc(NT * 512, BF16)
        ckvT = A.alloc(2 * NKEY, BF16)
        ckvT3 = v3(ckvT, 2)
        Kb = [A.alloc(NKEY, BF16), A.alloc(NKEY, BF16)]
        wuk = A.alloc(2 * 512, BF16)
        wuv = A.alloc(2 * 512, BF16)
        Vb = [A.alloc(NSLOT * 128, BF16) for _ in range(2)]
        Qb = [A.alloc(TOK, BF16) for _ in range(2)]
        Pb = [A.alloc(1024, BF16) for _ in range(2)]
        oT = A.alloc(512, F32)
        rcp = A.alloc(4, F32)
        CKD = [("ckvT_d", g) for g in range(17)]
        KRD = [("krope_d", g) for g in range(17)]
        for kc in range(2):
            DMA("sp", ckvT[:, kc * NKEY:(kc + 1) * NKEY], ckvT_d[:, kc * NKEYP:kc * NKEYP + NKEY], CKD, ["ckvT%d" % kc], "ckvT%d" % kc)
        for i in range(2):
            DMA("sp", Kb[i][64:96, :], krope_d[:, 0:NKEY], KRD, [("Kr", i)], "Kr%d" % i)
            Vi3 = v3(Vb[i], NSLOT)
            MSET("pool", Vb[i], 0.0, [("V", i)])
            MSET("pool", Vi3[:, :, 64:65], 1.0, [("V", i)])
        for kc in range(2):
            LOAD("pool", wuk[:, kc * 512:(kc + 1) * 512], wuk_d[:, kc, :], ["wuk"])
            LOAD("pool", wuv[:, kc * 512:(kc + 1) * 512], wuv_d[:, kc, :], ["wuv"])
        for kc in range(2):
            sl = slice(kc * 512, (kc + 1) * 512)
            TS("dve", wuk[:, sl], wuk[:, sl], vec[:, V_KVNW + kc:V_KVNW + kc + 1], ALU.mult, ["wuk", "vec"], ["wuk"])
            TS("dve", wuv[:, sl], wuv[:, sl], vec[:, V_KVNW + kc:V_KVNW + kc + 1], ALU.mult, ["wuv", "vec"], ["wuv"])
        Ym3 = v3(Ymla, NT)

        def gen_groups(h):
            hbuf = h % 2
            Kh, Vh, Qh = Kb[hbuf], Vb[hbuf], Qb[hbuf]
            Kt_, Vt_, Qt_ = ("K", hbuf), ("V", hbuf), ("Q", hbuf)
            Vh3 = v3(Vh, NSLOT)
            out = []

            def qload():
                DMA("sp", Qh[0:96, :], Qs_d[h, :, :], [("Qs", g) for g in range(8)], [Qt_], "Qb%d" % hbuf)
            out.append(qload)
            for n in range(17):
                def kgen(n=n):
                    n0 = n * 512
                    w = 512 if n < 16 else 128
                    for kc in range(2):
                        MM(bank(7)[0:64, 0:w], wuk[:, kc * 512 + h * 64:kc * 512 + (h + 1) * 64], ckvT3[:, kc, n0:n0 + w],
                           kc == 0, kc == 1, ["wuk", "ckvT%d" % kc], [B(7)])
                    CP("dve", Kh[0:64, n0:n0 + w], bank(7)[0:64, 0:w], [B(7)], [Kt_])
                out.append(kgen)
            for g0 in range(0, NSLOT, 8):
                def vgen(g0=g0):
                    ng = min(8, NSLOT - g0)
                    for j in range(ng):
                        s_ = g0 + j
                        for kc in range(2):
                            MM(bank(7)[:, j * 64:(j + 1) * 64], ckvT3[:, kc, s_ * 128:(s_ + 1) * 128],
                               wuv[:, kc * 512 + h * 64:kc * 512 + (h + 1) * 64], kc == 0, kc == 1, ["wuv", "ckvT%d" % kc], [B(7)])
                    CP("dve", Vh3[:, g0:g0 + ng, 0:64], bank(7)[:, 0:ng * 64].rearrange("p (a b) -> p a b", a=ng), [B(7)], [Vt_])
                out.append(vgen)
            return out

        for g_ in gen_groups(0):
            g_()
        units = []
        for qb in range(8):
            for s0 in range(0, 64, 2):
                units.append((qb, (s0, s0 + 1)))
            units.append((qb, (64,)))
        nu = len(units)
        it = 0
        for h in range(8):
            hbuf = h % 2
            Kh, Vh, Qh = Kb[hbuf], Vb[hbuf], Qb[hbuf]
            Kt_, Vt_, Qt_ = ("K", hbuf), ("V", hbuf), ("Q", hbuf)
            Vh3 = v3(Vh, NSLOT)
            pending = gen_groups(h + 1) if h < 7 else []
            stride = max(1, nu // (len(pending) + 1)) if pending else nu

            def s_mm(i):
                qb, sl = units[i]
                pb_i = (it + i) % 2
                pst = PS2[pb_i]
                for j, s_ in enumerate(sl):
                    kt = 128 if s_ < 64 else NMETA
                    MM(pst[0:kt, j * 512:(j + 1) * 512], Kh[0:96, s_ * 128:s_ * 128 + kt], Qh[0:96, qb * 512:(qb + 1) * 512], True, True,
                       [Kt_, Qt_, ("Kr", hbuf)], [B(2 * pb_i + j)])
                kt = 128 if sl[0] < 64 else NMETA
                if os.environ.get("K_WIDEEXP"):
                    w = 512 * len(sl)
                    ACT(Pb[pb_i][0:kt, 0:w], pst[0:kt, 0:w], AF.Exp, [B(2 * pb_i + j) for j in range(len(sl))], ["Pb%d" % pb_i], scale=SC_ATT)
                else:
                    for j in range(len(sl)):
                        ACT(Pb[pb_i][0:kt, j * 512:(j + 1) * 512], pst[0:kt, j * 512:(j + 1) * 512], AF.Exp, [B(2 * pb_i + j)],
                            [("Pb", pb_i, j)], scale=SC_ATT)

            s_mm(0)
            deferred = []
            for i in range(nu):
                if i + 1 < nu:
                    s_mm(i + 1)
                qb, sl = units[i]
                pb_i = (it + i) % 2
                ob = 4 + qb % 2
                for j, s_ in enumerate(sl):
                    kt = 128 if s_ < 64 else NMETA
                    MM(bank(ob)[:, :], Vh3[0:kt, s_, :], Pb[pb_i][0:kt, j * 512:(j + 1) * 512], s_ == 0, s_ == NSLOT - 1,
                       [Vt_, "Pb%d" % pb_i, ("Pb", pb_i, j)], [B(ob)])
                if pending and i % stride == stride - 1:
                    pending.pop(0)()
                if deferred and deferred[0][0] <= i:
                    deferred.pop(0)[1]()
                if sl[-1] == NSLOT - 1:
                    CP("dve", oT[0:65, :], bank(ob)[0:65, :], [B(ob)], ["oT"])

                    def norm(qb=qb):
                        for qi in range(4):
                            TR(bank(6)[:, qi * 65:(qi + 1) * 65], oT[0:65, qi * 128:(qi + 1) * 128], identf[0:65, 0:65], ["oT", "identf"], [B(6)])
                        pn = bank(6)[:, 0:260].rearrange("p (a b) -> p a b", a=4)
                        S.op("dve", lambda e, o_=rcp[:, 0:4], i_=pn[:, :, 64]: e.reciprocal(out=o_, in_=i_), [B(6)], ["rcp"])
                        TT("dve", Ym3[:, qb * 4:(qb + 1) * 4, h * 64:(h + 1) * 64], pn[:, :, 0:64],
                           rcp[:, 0:4].unsqueeze(2).to_broadcast([128, 4, 64]), ALU.mult, [B(6), "rcp"], [("Ymla", qb)])
                    deferred.append((i + 3, norm))
            while deferred:
                deferred.pop(0)[1]()
            while pending:
                pending.pop(0)()
            it += nu
        if dbg:
            ydbg = nc.dram_tensor("ymla_dbg", [128, NT * 512], BF16, kind="ExternalOutput").ap()
            DMA("sp", ydbg, Ymla, [("Ymla", q_) for q_ in range(8)], ["ydbg"], "ydbg")
        S.barrier()

    def alloc3c():
        A.pos = P0_END
        L = {}
        L["nfin"] = A.alloc(D, F32)
        L["hbuf"] = [A.alloc(D, F32) for _ in range(2)]
        L["hs"] = [A.alloc(D, BF16) for _ in range(2)]
        L["uT2"] = [A.alloc(8 * 512, BF16) for _ in range(2)]
        L["actT"] = A.alloc(NFC * 512, BF16)
        L["sgb"] = [A.alloc(512, F32) for _ in range(2)]
        L["h2"] = A.alloc(D, F32)
        L["ob"] = [A.alloc(D, F32) for _ in range(2)]
        L["st"] = A.alloc(8, F32)
        L["wg0"] = A.pos
        L["wg"] = A.alloc(8 * DFF, BF16)
        L["wu0"] = A.pos
        L["wu"] = A.alloc(8 * DFF, BF16)
        L["wd0"] = A.pos
        L["wd"] = A.alloc(NFC * 1024, BF16)
        return L

    pre3c = set()

    def ffn_weight_loaders(L, min_col):
        out = []
        wg, wu, wd = L["wg"], L["wu"], L["wd"]
        for kc in range(8):
            if ("wg", kc) not in pre3c and L["wg0"] + kc * DFF >= min_col:
                def f(kc=kc):
                    sl = slice(kc * DFF, (kc + 1) * DFF)
                    LOAD("pool", wg[:, sl], wg_d[:, kc, :], ["wg_%d" % kc])
                    TS("dve", wg[:, sl], wg[:, sl], vec[:, V_NFW + kc:V_NFW + kc + 1], ALU.mult, ["wg_%d" % kc, "vec"], ["wg_%d" % kc])
                pre3c.add(("wg", kc))
                out.append(f)
            if ("wu", kc) not in pre3c and L["wu0"] + kc * DFF >= min_col:
                def f(kc=kc):
                    sl = slice(kc * DFF, (kc + 1) * DFF)
                    LOAD("pool", wu[:, sl], wu_d[:, kc, :], ["wu_%d" % kc])
                    TS("dve", wu[:, sl], wu[:, sl], vec[:, V_NFW + kc:V_NFW + kc + 1], ALU.mult, ["wu_%d" % kc, "vec"], ["wu_%d" % kc])
                pre3c.add(("wu", kc))
                out.append(f)
        for fc in range(NFC):
            if ("wd", fc) not in pre3c and L["wd0"] + fc * 1024 >= min_col:
                def f(fc=fc):
                    LOAD("pool", wd[:, fc * 1024:(fc + 1) * 1024], wd_d[:, fc, :], ["wd_%d" % fc])
                pre3c.add(("wd", fc))
                out.append(f)
        return out

    if stages >= 4:
        A.pos = P0_END
        Ymla = A.alloc(NT * 512, BF16)
        Ym3 = v3(Ymla, NT)
        wgm = A.alloc(8 * 1024, BF16)
        wmo = A.alloc(4 * 1024, BF16)
        wo = A.alloc(8 * 1024, BF16)
        wgm3, wmo3, wo3_ = v3(wgm, 8), v3(wmo, 4), v3(wo, 8)
        xbuf = [A.alloc(D, F32) for _ in range(2)]
        m1l = [A.alloc(D, BF16) for _ in range(2)]
        xs = [A.alloc(D, BF16) for _ in range(2)]
        uT = [A.alloc(D, BF16) for _ in range(2)]
        sg = A.alloc(1024, F32)
        ymT = A.alloc(512, BF16)
        mg = A.alloc(1024, F32)
        mg2 = A.alloc(1024, BF16)
        mT = A.alloc(1024, BF16)
        h1b = [A.alloc(D, F32) for _ in range(2)]
        for kc in range(8):
            LOAD("pool", wgm[:, kc * 1024:(kc + 1) * 1024], wgm_d[:, kc, :], ["wgm_%d" % kc])
            LOAD("pool", wo[:, kc * 1024:(kc + 1) * 1024], wo_d[:, kc, :], ["wo_%d" % kc])
        for kc in range(4):
            LOAD("pool", wmo[:, kc * 1024:(kc + 1) * 1024], wmo_d[:, kc, :], ["wmo"])
        for kc in range(8):
            sl = slice(kc * 1024, (kc + 1) * 1024)
            TS("dve", wgm[:, sl], wgm[:, sl], vec[:, V_NMW + kc:V_NMW + kc + 1], ALU.mult, ["wgm_%d" % kc, "vec"], ["wgm_%d" % kc])
        xbuf = xbuf + [A.alloc(D, F32)]

        def front3b(c):
            xb, xbt = xbuf[c % 3], "xbuf%d" % (c % 3)
            ml, mlt = m1l[c % 2], "m1l%d" % (c % 2)
            xsb, xst = xs[c % 2], "xs%d" % (c % 2)
            u, ut = uT[c % 2], "uT%d" % (c % 2)
            DMA("sp", xb, xo[c * 128:(c + 1) * 128, :], (), [xbt], xbt)
            DMA("sp", ml, m1_d[c * 128:(c + 1) * 128, :], [("m1", c)], [mlt], mlt)
            ACT(xsb, xb, AF.Identity, [xbt], [xst], scale=rstd_own[:, c:c + 1])
            tb = bankb(0)
            for kc in range(8):
                TR(tb[:, kc * 128:(kc + 1) * 128], xsb[:, kc * 128:(kc + 1) * 128], identb, [xst, "identb"], [B(0)])
            CP("dve", u, tb[:, 0:1024], [B(0)], [ut])

        def tailT3b(c):
            tb6 = bankb(6)
            for kc in range(8):
                TR(tb6[:, kc * 128:(kc + 1) * 128], mg2[:, kc * 128:(kc + 1) * 128], identb, ["mg2", "identb"], [B(6)])
            CP("dve", mT, tb6[:, 0:1024], [B(6)], ["mT"])

        def tailO3b(c):
            xb, xbt = xbuf[c % 3], "xbuf%d" % (c % 3)
            hb_, hbt = h1b[c % 2], "h1b%d" % (c % 2)
            mT3 = v3(mT, 8)
            for hf in range(2):
                for kc in range(8):
                    MM(bank(3 + hf), mT3[:, kc, :], wo3_[:, kc, hf * 512:(hf + 1) * 512], kc == 0, kc == 7, ["mT", "wo_%d" % kc], [B(3 + hf)])
            for hf in range(2):
                sl = slice(hf * 512, (hf + 1) * 512)
                TT("dve", hb_[:, sl], bank(3 + hf), xb[:, sl], ALU.add, [B(3 + hf), xbt], [hbt])
            DMA("pool", h1_d[c * 128:(c + 1) * 128, :], hb_, [hbt], [("h1", c)], hbt)

        prefetch = []
        if stages >= 5:
            end3b = A.pos
            prefetch = ffn_weight_loaders(alloc3c(), end3b)
            A.pos = end3b
        front3b(0)
        YB = (5, 7)
        for c in range(NT):
            for _ in range(2):
                if prefetch and c >= 1:
                    prefetch.pop(0)()
            ml, mlt = m1l[c % 2], "m1l%d" % (c % 2)
            u, ut = uT[c % 2], "uT%d" % (c % 2)
            u3 = v3(u, 8)
            for hf in range(2):
                for kc in range(8):
                    MM(bank(1 + hf), u3[:, kc, :], wgm3[:, kc, hf * 512:(hf + 1) * 512], kc == 0, kc == 7, [ut, "wgm_%d" % kc], [B(1 + hf)])
            if c >= 1:
                tailT3b(c - 1)
            if c + 1 < NT:
                front3b(c + 1)
            tb7 = bankb(7)
            for kc in range(4):
                TR(tb7[:, kc * 128:(kc + 1) * 128], Ym3[:, c, kc * 128:(kc + 1) * 128], identb, [("Ymla", c // 4), "identb"], [B(7)])
            CP("dve", ymT, tb7[:, 0:512], [B(7)], ["ymT"])
            if c >= 1:
                tailO3b(c - 1)
            ymT3 = v3(ymT, 4)
            for hf in range(2):
                for kc in range(4):
                    MM(bank(YB[hf]), ymT3[:, kc, :], wmo3[:, kc, hf * 512:(hf + 1) * 512], kc == 0, kc == 3, ["ymT", "wmo"], [B(YB[hf])])
            for hf in range(2):
                sl = slice(hf * 512, (hf + 1) * 512)
                ACT(sg[:, sl], bank(1 + hf), AF.Sigmoid, [B(1 + hf)], ["sg"])
                TT("dve", mg[:, sl], bank(YB[hf]), sg[:, sl], ALU.mult, [B(YB[hf]), "sg"], ["mg"])
            TT("pool", mg2, mg, ml, ALU.add, ["mg", mlt], ["mg2"])
        tailT3b(NT - 1)
        tailO3b(NT - 1)
        while prefetch:
            prefetch.pop(0)()
        S.barrier()

    if stages >= 5:
        L = alloc3c()
        wg3, wu3, wd3 = v3(L["wg"], 8), v3(L["wu"], 8), v3(L["wd"], NFC)
        nfin, hbuf_, hs, actT, sgb, ob_, st = L["nfin"], L["hbuf"], L["hs"], L["actT"], L["sgb"], L["ob"], L["st"]
        h2b, h2t = L["h2"], "h2_0"
        uT2 = L["uT2"]
        actT3 = v3(actT, NFC)
        LOAD("sp", nfin, nfin_d, ["nfin"])
        for f_ in ffn_weight_loaders(L, 0):
            f_()
        NBLK = TOK // 512
        hcnt = [0]

        def front3c_a(blk, t):
            c = blk * 4 + t
            k = hcnt[0] % 2
            hcnt[0] += 1
            hbf, hbft = hbuf_[k], "hbuf%d" % k
            hsb, hst = hs[c % 2], "hs%d" % (c % 2)
            DMA("sp", hbf, h1_d[c * 128:(c + 1) * 128, :], [("h1", c)], [hbft], hbft)
            ACT(hsb, hbf, AF.Square, [hbft], [hst, "f_ss"], accum=st[:, 0:1])
            TS("dve", st[:, 0:1], st[:, 0:1], 1.0 / D, ALU.mult, ["f_ss"], ["f_ss"], s2=RMS_EPS, op1=ALU.add)
            TT("pool", st[:, 1:2], st[:, 0:1], mhalf[:, 0:1], ALU.pow, ["f_ss", "mhalf"], ["f_rstd"])
            ACT(hsb, hbf, AF.Identity, [hbft, "f_rstd"], [hst], scale=st[:, 1:2])

        def front3c_b(blk, t):
            c = blk * 4 + t
            hsb, hst = hs[c % 2], "hs%d" % (c % 2)
            u23 = v3(uT2[blk % 2], 8)
            tb = bankb(t % 2)
            for kc in range(8):
                TR(tb[:, kc * 128:(kc + 1) * 128], hsb[:, kc * 128:(kc + 1) * 128], identb, [hst, "identb"], [B(t % 2)])
            CP("dve", u23[:, :, t * 128:(t + 1) * 128], tb[:, 0:1024].rearrange("p (a b) -> p a b", a=8), [B(t % 2)], ["uT2_%d" % (blk % 2)])

        for t in range(4):
            front3c_a(0, t)
            front3c_b(0, t)
        for blk in range(NBLK):
            uT23 = v3(uT2[blk % 2], 8)
            utt = "uT2_%d" % (blk % 2)
            for fc in range(NFC):
                gb_, ub_ = 2 + 2 * (fc % 2), 3 + 2 * (fc % 2)
                for kc in range(8):
                    MM(bank(gb_), wg3[:, kc, fc * 128:(fc + 1) * 128], uT23[:, kc, :], kc == 0, kc == 7, [utt, "wg_%d" % kc], [B(gb_)])
                for kc in range(8):
                    MM(bank(ub_), wu3[:, kc, fc * 128:(fc + 1) * 128], uT23[:, kc, :], kc == 0, kc == 7, [utt, "wu_%d" % kc], [B(ub_)])
                sgt = "sgb%d" % (fc % 2)
                ACT(sgb[fc % 2], bank(gb_), AF.Sigmoid, [B(gb_)], [sgt])
                TT("dve", sgb[fc % 2], bank(gb_), sgb[fc % 2], ALU.mult, [B(gb_), sgt], [sgt])
                TT("dve", actT3[:, fc, :], bank(ub_), sgb[fc % 2], ALU.mult, [B(ub_), sgt], [("actT", fc)])
                if blk + 1 < NBLK and fc >= 4 and fc % 4 == 0 and (fc - 4) // 4 < 4:
                    front3c_a(blk + 1, (fc - 4) // 4)
                if blk + 1 < NBLK and fc >= 6 and fc % 4 == 2 and (fc - 6) // 4 < 4:
                    front3c_b(blk + 1, (fc - 6) // 4)
            for t in range(4):
                c = blk * 4 + t
                d0 = 6
                for hf in range(2):
                    for fc in range(NFC):
                        MM(bank(d0 + hf), actT3[:, fc, t * 128:(t + 1) * 128], wd3[:, fc, hf * 512:(hf + 1) * 512], fc == 0, fc == NFC - 1,
                           [("actT", fc), "wd_%d" % fc], [B(d0 + hf)])
                k = hcnt[0] % 2
                hcnt[0] += 1
                hbf, hbft = hbuf_[k], "hbuf%d" % k
                DMA("sp", hbf, h1_d[c * 128:(c + 1) * 128, :], [("h1", c)], [hbft], hbft)
                obb, obt = ob_[c % 2], "ob%d" % (c % 2)
                for hf in range(2):
                    sl = slice(hf * 512, (hf + 1) * 512)
                    TT("dve", h2b[:, sl], bank(d0 + hf), hbf[:, sl], ALU.add, [B(d0 + hf), hbft], [h2t])
                ACT(obb, h2b, AF.Square, [h2t], [obt, "o_ss"], accum=st[:, 2:3])
                TS("dve", st[:, 2:3], st[:, 2:3], 1.0 / D, ALU.mult, ["o_ss"], ["o_ss"], s2=RMS_EPS, op1=ALU.add)
                TT("pool", st[:, 3:4], st[:, 2:3], mhalf[:, 0:1], ALU.pow, ["o_ss", "mhalf"], ["o_rstd"])
                STT(obb, h2b, st[:, 3:4], nfin, ALU.mult, ALU.mult, [h2t, "o_rstd", "nfin", obt], [obt])
                DMA("pool", out_d[c * 128:(c + 1) * 128, :], obb, [obt], [("out", c)], obt)

    S.emit(nc)
    es.close()
    return nc


def _kc_layout(w):
    K, N = w.shape
    return np.ascontiguousarray(w.reshape(K // 128, 128, N).transpose(1, 0, 2))


def _swap_cols(w, hd):
    K, N = w.shape
    w4 = w.reshape(K, N // hd, 2, hd // 2)
    return np.ascontiguousarray(w4[:, :, ::-1, :]).reshape(K, N)


def _rope_tab(pos, half, base=10000.0):
    inv = (np.float32(base) ** (-(np.arange(half, dtype=np.float32) / np.float32(half)))).astype(np.float32)
    ang = (pos.astype(np.float32)[:, None] * inv[None, :]).astype(np.float32)
    return np.cos(ang.astype(np.float64)).astype(np.float32), np.sin(ang.astype(np.float64)).astype(np.float32)


_PROGRAM = {}


def _prep_inputs(x, meta_tokens, norm_mix_w, w_in, ret_decay_fwd, ret_decay_bwd, ret_gn_w, w_ret_out,
                 mla_q_norm_w, w_uq, mla_kv_norm_w, w_uk, w_uv, w_mla_out, w_o, norm_ffn_w,
                 w_ffn_gate, w_ffn_up, w_ffn_down, norm_final_w):
    f = np.float32
    x = np.asarray(x, f)
    W = np.asarray(w_in, f)[0]
    rq, rk, rv, rg = W[:, 0:512], W[:, 512:1024], W[:, 1024:2048], W[:, 2048:3072]
    cq, ckv, kr = W[:, 3072:3456], W[:, 3456:3712], W[:, 3712:3744]
    gret, gmla = W[:, 3744:4768], W[:, 4768:5792]
    rks, rqs = _swap_cols(rk, 64), _swap_cols(rq, 64)
    shared = {
        "w1": _kc_layout(np.concatenate([cq, ckv, kr, rk, rks, rv], axis=1)),
        "w3a": _kc_layout(np.concatenate([rq, rqs, rk, rks, rv, rg, gret], axis=1)),
        "wgm": _kc_layout(gmla),
        "wmo": _kc_layout(np.asarray(w_mla_out, f)[0]),
        "wo": _kc_layout(np.asarray(w_o, f)[0]),
        "wuq": _kc_layout(np.asarray(w_uq, f)[0]),
        "wuk": _kc_layout(np.asarray(w_uk, f)[0]),
        "wuv": _kc_layout(np.asarray(w_uv, f)[0]),
        "wro": _kc_layout(np.asarray(w_ret_out, f)[0]),
        "wg": _kc_layout(np.asarray(w_ffn_gate, f)[0]),
        "wu": _kc_layout(np.asarray(w_ffn_up, f)[0]),
        "wd": _kc_layout(np.asarray(w_ffn_down, f)[0]),
        "ident": np.eye(128, dtype=f),
        "nfin": np.ascontiguousarray(np.broadcast_to(np.asarray(norm_final_w, f)[None, :], (128, D))),
    }
    vec = np.zeros((128, NVEC), f)

    def pk(v):
        v = np.asarray(v, f).reshape(-1)
        return v.reshape(-1, 128).T

    vec[:, V_NMW:V_NMW + 8] = pk(norm_mix_w)
    vec[:, V_NFW:V_NFW + 8] = pk(norm_ffn_w)
    vec[:, V_GNW:V_GNW + 8] = pk(ret_gn_w)
    vec[:, V_QNW:V_QNW + 3] = pk(mla_q_norm_w)
    vec[:, V_KVNW:V_KVNW + 2] = pk(mla_kv_norm_w)
    df = np.asarray(ret_decay_fwd, f).reshape(8)
    db = np.asarray(ret_decay_bwd, f).reshape(8)
    par = (np.arange(128) >= 64).astype(np.int64)
    for hp in range(4):
        vec[:, V_DFP + hp] = df[2 * hp + par]
        vec[:, V_DBP + hp] = db[2 * hp + par]
    vec[:, V_DF8:V_DF8 + 8] = df[None, :]
    vec[:, V_DB8:V_DB8 + 8] = db[None, :]
    shared["vec"] = vec
    ctab = np.zeros((128, NCTAB), f)
    j = np.arange(128, dtype=f)[:, None]
    i = np.arange(128, dtype=f)[None, :]
    ctab[:, C_POS:C_POS + 128] = np.maximum(i - j, 0)
    ctab[:, C_NEG:C_NEG + 128] = np.maximum(j - i, 0)
    ctab[:, C_I1:C_I1 + 128] = np.broadcast_to(i + 1, (128, 128))
    ctab[:, C_128MI:C_128MI + 128] = np.broadcast_to(128 - i, (128, 128))
    ctab[:, C_127MJ] = 127 - j[:, 0]
    ctab[:, C_J] = j[:, 0]
    shared["ctab"] = ctab

    meta = np.asarray(meta_tokens, f)
    BIG = f(1.0e9)
    sgn = np.where((np.arange(128) % 64) < 32, -1.0, 1.0).astype(f)[:, None]
    fidx = (np.arange(128) % 64) % 32
    in_maps = []
    for core in range(8):
        b, half = core // 2, core % 2
        oth = 1 - half
        m = dict(shared)
        m["xo"] = np.ascontiguousarray(x[b, half * TOK:(half + 1) * TOK])
        xr = np.zeros((NOT_ * 128, D), f)
        xr[0:TOK] = x[b, oth * TOK:(oth + 1) * TOK]
        xr[TOK:TOK + NMETA] = meta
        m["xr"] = xr
        pos_own = (NMETA + half * TOK + np.arange(TOK)).astype(np.int64)
        pos_oth = np.zeros(NOT_ * 128, np.int64)
        pos_oth[0:TOK] = NMETA + oth * TOK + np.arange(TOK)
        pos_oth[TOK:TOK + NMETA] = np.arange(NMETA)
        valid_oth = np.zeros(NOT_ * 128, bool)
        valid_oth[0:TOK + NMETA] = True
        for nm, pos, ntile in (("ropeR_own", pos_own, NT), ("ropeR_oth", pos_oth, NOT_)):
            c, s = _rope_tab(pos, 32)
            cfm = c[:, fidx].T
            sfm = s[:, fidx].T * sgn
            tab = np.stack([cfm.reshape(128, ntile, 128), sfm.reshape(128, ntile, 128)], axis=2)
            m[nm] = np.ascontiguousarray(tab.reshape(128, ntile, 256)).astype(f)
        for nm, pos, ntile in (("tabM_own", pos_own, NT), ("tabM_oth", pos_oth, NOT_)):
            c, s = _rope_tab(pos, 16)
            tab = np.concatenate([c, c, s, s], axis=1).reshape(ntile, 128, 64).transpose(1, 0, 2)
            m[nm] = np.ascontiguousarray(tab).astype(f)
        own_first = NMETA + half * TOK
        own_last = own_first + TOK - 1
        dfw = np.where(valid_oth & (pos_oth < own_first), own_first - 1 - pos_oth, BIG).astype(f)
        dbw = np.where(valid_oth & (pos_oth > own_last), pos_oth - own_last - 1, BIG).astype(f)
        dist = np.concatenate([dfw.reshape(NOT_, 128).T, dbw.reshape(NOT_, 128).T], axis=1)
        m["dist"] = np.ascontiguousarray(dist).astype(f)
        in_maps.append(m)
    return in_maps


def kernel(**inputs):
    in_maps = _prep_inputs(**inputs)
    if "nc" not in _PROGRAM:
        _PROGRAM["nc"] = build_program()
    nc = _PROGRAM["nc"]
    res = run_bass_kernel_spmd(nc, in_maps, core_ids=list(range(8)))
    out = np.zeros((NB, SEQ, D), np.float32)
    for core in range(8):
        b, half = core // 2, core % 2
        out[b, half * TOK:(half + 1) * TOK] = res.results[core]["out"]
    return out
```
